# Optimizing a Trainium2 kernel written in Bass

```python
import math, functools
import jax, jax.numpy as jnp
from jax import lax
import numpy as np

D_MODEL = 1024
BATCH = 8
SEQ = 2048
DEPTH = 2

N_A_LAYERS = DEPTH // 2
N_B_LAYERS = DEPTH - N_A_LAYERS
GDN_HEADS = D_MODEL // 128
GDN_DK = 128
GDN_DV = 128
GDN_WIDTH = GDN_HEADS * GDN_DV
CONV_WIDTH = 4
CHUNK = 64
DIFF_HEADS = D_MODEL // 128
DIFF_DK = 64
DIFF_DV = 2 * DIFF_DK
Q_BLOCK = 128
D_FF = 4 * D_MODEL
ROPE_THETA = 10000.0
EPS = 1e-6
GDN_IN_WIDTH = 4 * GDN_WIDTH + 2 * GDN_HEADS
Q_WIDTH = DIFF_HEADS * 2 * DIFF_DK
KV_WIDTH = DIFF_HEADS * 2 * DIFF_DK + DIFF_HEADS * DIFF_DV

kernel_name = 'yoco_gdn_diffattn_hybrid'


def rms_norm(x, w):
    xf = x.astype(jnp.float32)
    y = xf * lax.rsqrt(jnp.mean(xf * xf, axis=-1, keepdims=True) + EPS)
    return (y * w.astype(jnp.float32)).astype(x.dtype)


def l2_norm(x):
    return x * lax.rsqrt(jnp.sum(x * x, axis=-1, keepdims=True) + EPS)


def rope(t, positions):
    half = t.shape[-1] // 2
    freqs = ROPE_THETA ** (-jnp.arange(half, dtype=jnp.float32) / half)
    ang = positions.astype(jnp.float32)[:, :, None] * freqs
    ang = ang.reshape(ang.shape[:2] + (1,) * (t.ndim - 3) + (half,))
    cos, sin = jnp.cos(ang), jnp.sin(ang)
    tf = t.astype(jnp.float32)
    t1, t2 = tf[..., :half], tf[..., half:]
    return jnp.concatenate([t1 * cos - t2 * sin, t2 * cos + t1 * sin], axis=-1).astype(t.dtype)


def causal_conv(x, w):
    width = w.shape[0]
    seq = x.shape[1]
    xp = jnp.pad(x, ((0, 0), (width - 1, 0), (0, 0)))
    return sum(w[j] * xp[:, j:j + seq] for j in range(width))


def chunked_gated_delta_rule(q, k, v, beta, g):
    b, s, h, dk = q.shape
    dv = v.shape[-1]
    nc = s // CHUNK

    def chunks(t):
        return t.reshape((b, nc, CHUNK, h) + t.shape[3:]).swapaxes(2, 3)

    q, k, v, beta, g = chunks(q), chunks(k), chunks(v), chunks(beta), chunks(g)
    g_cum = jnp.cumsum(g, axis=-1)
    causal = jnp.tril(jnp.ones((CHUNK, CHUNK), dtype=bool))
    strict = jnp.tril(jnp.ones((CHUNK, CHUNK), dtype=bool), k=-1)
    decay = jnp.where(causal, jnp.exp(jnp.where(causal, g_cum[..., :, None] - g_cum[..., None, :], 0.0)), 0.0)
    k_beta = k * beta[..., None]
    kk = jnp.einsum('bnhcd,bnhed->bnhce', k_beta, k) * decay
    tri = jnp.where(strict, kk, 0.0) + jnp.eye(CHUNK, dtype=jnp.float32)
    solve = functools.partial(lax.linalg.triangular_solve, left_side=True, lower=True, unit_diagonal=True)
    u = solve(tri, v * beta[..., None])
    w = solve(tri, k_beta * jnp.exp(g_cum)[..., None])
    intra = jnp.einsum('bnhcd,bnhed->bnhce', q, k) * decay
    q_dec = q * jnp.exp(g_cum)[..., None]
    k_dec = k * jnp.exp(g_cum[..., -1:] - g_cum)[..., None]
    chunk_decay = jnp.exp(g_cum[..., -1])
    xs = tuple(jnp.moveaxis(t, 1, 0) for t in (q_dec, k_dec, w, u, intra, chunk_decay))

    def step(state, inp):
        qd, kd, wc, uc, ic, cd = inp
        v_new = uc - jnp.einsum('bhcd,bhdv->bhcv', wc, state)
        out = jnp.einsum('bhcd,bhdv->bhcv', qd, state) + jnp.einsum('bhce,bhev->bhcv', ic, v_new)
        state = state * cd[..., None, None] + jnp.einsum('bhcd,bhcv->bhdv', kd, v_new)
        return state, out

    state0 = jnp.zeros((b, h, dk, dv), jnp.float32)
    _, o = lax.scan(step, state0, xs)
    return o.transpose(1, 0, 3, 2, 4).reshape(b, s, h, dv)


def gated_deltanet(x, norm_w, w_in, conv_w, a_log, dt_bias, out_norm, w_out):
    b, s, _ = x.shape
    f32 = jnp.float32
    proj = rms_norm(x, norm_w) @ w_in
    qkv = jax.nn.silu(causal_conv(proj[..., :3 * GDN_WIDTH], conv_w)).astype(f32)
    z = proj[..., 3 * GDN_WIDTH:4 * GDN_WIDTH].astype(f32)
    b_raw = proj[..., 4 * GDN_WIDTH:4 * GDN_WIDTH + GDN_HEADS].astype(f32)
    a_raw = proj[..., 4 * GDN_WIDTH + GDN_HEADS:].astype(f32)
    q, k, v = jnp.split(qkv, 3, axis=-1)
    q = l2_norm(q.reshape(b, s, GDN_HEADS, GDN_DK)) * (GDN_DK ** -0.5)
    k = l2_norm(k.reshape(b, s, GDN_HEADS, GDN_DK))
    v = v.reshape(b, s, GDN_HEADS, GDN_DV)
    beta = jax.nn.sigmoid(b_raw)
    g = -jnp.exp(a_log.astype(f32)) * jax.nn.softplus(a_raw + dt_bias.astype(f32))
    o = chunked_gated_delta_rule(q, k, v, beta, g)
    o = rms_norm(o, out_norm) * jax.nn.silu(z.reshape(b, s, GDN_HEADS, GDN_DV))
    return o.reshape(b, s, GDN_WIDTH).astype(x.dtype) @ w_out


def shared_kv(x, kv_norm, w_kv, k_norm, positions):
    b, s, _ = x.shape
    kv = rms_norm(x, kv_norm) @ w_kv
    k = kv[..., :Q_WIDTH].reshape(b, s, DIFF_HEADS, 2, DIFF_DK)
    v = kv[..., Q_WIDTH:].reshape(b, s, DIFF_HEADS, DIFF_DV)
    k = rope(rms_norm(k, k_norm), positions)
    return k.transpose(0, 2, 1, 3, 4), v.transpose(0, 2, 1, 3)


def diff_attention(x, positions, k, v, norm_w, w_q, q_norm, lam_params, sub_norm, w_out, lam_init):
    b, s, _ = x.shape
    f32 = jnp.float32
    q = (rms_norm(x, norm_w) @ w_q).reshape(b, s, DIFF_HEADS, 2, DIFF_DK)
    q = rope(rms_norm(q, q_norm), positions).transpose(0, 2, 1, 3, 4)
    lp = lam_params.astype(f32)
    lam = jnp.exp(jnp.sum(lp[0] * lp[1])) - jnp.exp(jnp.sum(lp[2] * lp[3])) + lam_init
    scale = DIFF_DK ** -0.5
    outs = []
    for blk in range(s // Q_BLOCK):
        start, end = blk * Q_BLOCK, (blk + 1) * Q_BLOCK
        scores = jnp.einsum('bhqmd,bhkmd->bhmqk', q[:, :, start:end], k[:, :, :end]).astype(f32) * scale
        causal = jnp.arange(end)[None, :] <= (start + jnp.arange(Q_BLOCK))[:, None]
        probs = jax.nn.softmax(jnp.where(causal, scores, -1e30), axis=-1)
        diff = probs[:, :, 0] - lam * probs[:, :, 1]
        outs.append(jnp.einsum('bhqk,bhkv->bhqv', diff, v[:, :, :end].astype(f32)))
    o = jnp.concatenate(outs, axis=2)
    o = rms_norm(o, sub_norm) * (1.0 - lam_init)
    return o.transpose(0, 2, 1, 3).reshape(b, s, DIFF_HEADS * DIFF_DV).astype(x.dtype) @ w_out


def sq_relu_mlp(x, norm_w, w1, w2):
    h = jax.nn.relu(rms_norm(x, norm_w) @ w1)
    return (h * h) @ w2


def setup_inputs(seed: int = 0) -> dict:
    key = jax.random.key(seed)
    keys = jax.random.split(key, 21)
    f32 = jnp.float32

    def normal(k, shape, scale):
        return jax.random.normal(k, shape, f32) * scale

    def gain(k, shape):
        return 1.0 + 0.05 * jax.random.normal(k, shape, f32)

    x = normal(keys[0], (BATCH, SEQ, D_MODEL), 1.0)
    offset = jax.random.randint(keys[1], (BATCH, 1), 0, 4096, dtype=jnp.int32)
    positions = offset + jnp.arange(SEQ, dtype=jnp.int32)[None, :]
    a_norm = gain(keys[2], (N_A_LAYERS, D_MODEL))
    a_w_in = normal(keys[3], (N_A_LAYERS, D_MODEL, GDN_IN_WIDTH), D_MODEL ** -0.5)
    a_conv_w = normal(keys[4], (N_A_LAYERS, CONV_WIDTH, 3 * GDN_WIDTH), CONV_WIDTH ** -0.5)
    a_a_log = jnp.log(jax.random.uniform(keys[5], (N_A_LAYERS, GDN_HEADS), f32, 1.0, 16.0))
    dt = jnp.exp(jax.random.uniform(keys[6], (N_A_LAYERS, GDN_HEADS), f32, math.log(1e-3), math.log(1e-1)))
    a_dt_bias = dt + jnp.log(-jnp.expm1(-dt))
    a_out_norm = gain(keys[7], (N_A_LAYERS, GDN_DV))
    a_w_out = normal(keys[8], (N_A_LAYERS, GDN_WIDTH, D_MODEL), GDN_WIDTH ** -0.5)
    kv_norm = gain(keys[9], (D_MODEL,))
    w_kv = normal(keys[10], (D_MODEL, KV_WIDTH), D_MODEL ** -0.5)
    k_norm = gain(keys[11], (DIFF_DK,))
    b_norm = gain(keys[12], (N_B_LAYERS, D_MODEL))
    b_w_q = normal(keys[13], (N_B_LAYERS, D_MODEL, Q_WIDTH), D_MODEL ** -0.5)
    b_q_norm = gain(keys[14], (N_B_LAYERS, DIFF_DK))
    b_lambda = normal(keys[15], (N_B_LAYERS, 4, DIFF_DK), 0.1)
    b_sub_norm = gain(keys[16], (N_B_LAYERS, DIFF_DV))
    b_w_out = normal(keys[17], (N_B_LAYERS, DIFF_HEADS * DIFF_DV, D_MODEL), (DIFF_HEADS * DIFF_DV) ** -0.5)
    mlp_norm = gain(keys[18], (DEPTH, D_MODEL))
    mlp_w1 = normal(keys[19], (DEPTH, D_MODEL, D_FF), D_MODEL ** -0.5)
    mlp_w2 = normal(keys[20], (DEPTH, D_FF, D_MODEL), 0.5 * D_FF ** -0.5)
    return {'x': x, 'positions': positions,
            'a_norm': a_norm, 'a_w_in': a_w_in, 'a_conv_w': a_conv_w, 'a_a_log': a_a_log,
            'a_dt_bias': a_dt_bias, 'a_out_norm': a_out_norm, 'a_w_out': a_w_out,
            'kv_norm': kv_norm, 'w_kv': w_kv, 'k_norm': k_norm,
            'b_norm': b_norm, 'b_w_q': b_w_q, 'b_q_norm': b_q_norm, 'b_lambda': b_lambda,
            'b_sub_norm': b_sub_norm, 'b_w_out': b_w_out,
            'mlp_norm': mlp_norm, 'mlp_w1': mlp_w1, 'mlp_w2': mlp_w2}


def reference(x, positions, a_norm, a_w_in, a_conv_w, a_a_log, a_dt_bias, a_out_norm, a_w_out,
              kv_norm, w_kv, k_norm, b_norm, b_w_q, b_q_norm, b_lambda, b_sub_norm, b_w_out,
              mlp_norm, mlp_w1, mlp_w2):
    k_shared, v_shared = None, None
    for layer in range(DEPTH):
        if layer < N_A_LAYERS:
            x = x + gated_deltanet(x, a_norm[layer], a_w_in[layer], a_conv_w[layer], a_a_log[layer],
                                   a_dt_bias[layer], a_out_norm[layer], a_w_out[layer])
        else:
            if layer == N_A_LAYERS:
                k_shared, v_shared = shared_kv(x, kv_norm, w_kv, k_norm, positions)
            j = layer - N_A_LAYERS
            lam_init = 0.8 - 0.6 * math.exp(-0.3 * layer)
            x = x + diff_attention(x, positions, k_shared, v_shared, b_norm[j], b_w_q[j], b_q_norm[j],
                                   b_lambda[j], b_sub_norm[j], b_w_out[j], lam_init)
        x = x + sq_relu_mlp(x, mlp_norm[layer], mlp_w1[layer], mlp_w2[layer])
    return x
```

```python
import numpy as np
import concourse.bass as bass
import concourse.mybir as mybir
from concourse.bass_utils import run_bass_kernel_spmd
from contextlib import ExitStack

F32 = mybir.dt.float32
BF16 = mybir.dt.bfloat16
I32 = mybir.dt.int32
AF = mybir.ActivationFunctionType
ALU = mybir.AluOpType
AX = mybir.AxisListType

S = 2048
D = 1024
NT = 16
NB = 4
DFF = 4096
EPS = 1e-6


INHERIT = {}


class Tk:
    __slots__ = ("name", "w", "r", "excl")

    def __init__(self, name="", excl=False):
        self.name = name
        self.w = None
        self.r = dict(INHERIT)
        self.excl = excl


def retire(tks):
    for t in tks:
        if t.w is not None:
            k, v = t.w
            if INHERIT.get(k, 0) < v:
                INHERIT[k] = v
        for k, v in t.r.items():
            if INHERIT.get(k, 0) < v:
                INHERIT[k] = v


class Prog:
    ENG = ("pe", "act", "dve", "pool", "sp")

    def __init__(self, nc, es):
        self.nc = nc
        self.es = es
        self.ops = {e: [] for e in self.ENG}
        self.sems = {}
        self.cnt = {}
        for e in self.ENG:
            self.sems[e] = es.enter_context(nc.semaphore("s_" + e))
            self.cnt[e] = 0
        self.waited = {e: {} for e in self.ENG}
        self.pending_nosig = {e: False for e in self.ENG}

    def new_dma_sem(self, name):
        key = "dma_" + name
        self.sems[key] = self.es.enter_context(self.nc.semaphore(key))
        self.cnt[key] = 0
        return key

    def _deps(self, eng, reads, writes):
        deps = {}

        def add(d):
            if d is None:
                return
            k, v = d
            if deps.get(k, 0) < v:
                deps[k] = v
        for t in reads:
            add(t.w)
            if t.excl:
                for k, v in t.r.items():
                    if k != eng:
                        add((k, v))
        for t in writes:
            add(t.w)
            for k, v in t.r.items():
                add((k, v))
        waits = []
        wd = self.waited[eng]
        for k, v in deps.items():
            if k == "pe" and eng == "pe":
                continue
            if wd.get(k, 0) < v:
                wd[k] = v
                waits.append((k, v))
        return waits

    def op(self, eng, fn, reads=(), writes=(), sig=True):
        assert sig or eng == "pe"
        waits = self._deps(eng, reads, writes)
        if sig:
            self.cnt[eng] += 1
            val = self.cnt[eng]
        else:
            val = self.cnt[eng] + 1
        self.pending_nosig[eng] = not sig
        self.ops[eng].append((waits, fn, (eng, 1) if sig else None))
        for t in reads:
            if t.r.get(eng, 0) < val:
                t.r[eng] = val
        for t in writes:
            t.w = (eng, val)
            t.r = {}

    def dma(self, q, semkey, fn, reads=(), writes=()):
        assert q == "sp"
        waits = self._deps(q, reads, writes)
        self.cnt[semkey] += 1
        val = self.cnt[semkey]
        self.ops[q].append((waits, fn, (semkey, 16)))
        for t in reads:
            if t.r.get(semkey, 0) < val:
                t.r[semkey] = val
        for t in writes:
            t.w = (semkey, val)
            t.r = {}

    def final_wait(self, eng, semkey):
        if self.cnt[semkey] > 0:
            self.ops[eng].append(([(semkey, self.cnt[semkey])], None, None))

    def emit(self):
        nc = self.nc
        for e in self.ENG:
            assert not self.pending_nosig[e], e
        actual = {k: [0] for k in self.sems if k.startswith("dma_")}
        self.split_log = []
        with nc.Block() as block:
            def run(engname):
                def body(eng):
                    for waits, fn, inc in self.ops[engname]:
                        for k, v in waits:
                            eng.wait_ge(self.sems[k], actual[k][v] if k in actual else v)
                        if fn is not None:
                            n0 = nc.n_instructions()
                            ins = fn(eng)
                            if inc is not None:
                                ins.then_inc(self.sems[inc[0]], inc[1])
                                if inc[0] in actual:
                                    k_ = max(1, nc.n_instructions() - n0)
                                    if k_ != 1:
                                        self.split_log.append((inc[0], len(actual[inc[0]]), k_))
                                    actual[inc[0]].append(actual[inc[0]][-1] + 16 * k_)
                return body
            block.sync(run("sp"))
            block.tensor(run("pe"))
            block.scalar(run("act"))
            block.vector(run("dve"))
            block.gpsimd(run("pool"))


class Ring:
    def __init__(self, bufs, tks=None):
        self.bufs = bufs
        self.tks = tks if tks is not None else [Tk() for _ in bufs]
        self.i = 0

    def next(self):
        b, t = self.bufs[self.i], self.tks[self.i]
        self.i = (self.i + 1) % len(self.bufs)
        return b, t


def host_consts():
    c = {}
    i = np.arange(128)
    c["ident"] = np.eye(128, dtype=np.float32)
    c["ones"] = np.ones((128, 128), dtype=np.float32)
    bo = np.zeros((128, 128), np.float32); bo[:64, :64] = 1; bo[64:, 64:] = 1
    c["blockones"] = bo
    rm = np.zeros((128, 128), np.float32)
    for p in range(128):
        if p % 64 < 32:
            rm[p + 32, p] = -1.0
        else:
            rm[p - 32, p] = 1.0
    c["rot"] = rm
    fr = np.zeros((128, 128), np.float32)
    fr[:, 0] = (10000.0 ** (-(np.arange(128) % 32).astype(np.float32) / 32.0)).astype(np.float32)
    c["freq"] = fr
    c["trimask"] = (i[:, None] <= i[None, :]).astype(np.float32)
    c["mnegL"] = np.where(i[:, None] >= i[None, :], 0.0, -30000.0).astype(np.float32)
    c["mnegU"] = np.ascontiguousarray(c["mnegL"].T)
    c["strictL"] = (i[:, None] > i[None, :]).astype(np.float32)
    c["strictU"] = np.ascontiguousarray(c["strictL"].T)
    return c


CONST_NAMES = ["ident", "ones", "blockones", "rot", "freq", "trimask", "mnegL", "mnegU", "strictL", "strictU"]


def build_program(stages=("gdn", "mlp0", "attn", "mlp1"), debug=False):
    INHERIT.clear()
    nc = bass.Bass("TRN2", target_bir_lowering=False)
    dr = {}

    def din(name, shape, dt=F32):
        dr[name] = nc.dram_tensor(name, list(shape), dt, kind="ExternalInput").ap()
        return dr[name]
    x_d = din("x", [S, D])
    pos_d = din("positions", [1, S], I32)
    din("a_norm", [1, D]); din("a_w_in", [D, 4112]); din("a_conv_w", [4, 3072]); din("a_a_log", [1, 8])
    din("a_dt_bias", [1, 8]); din("a_out_norm", [1, 128]); din("a_w_out", [D, D])
    din("kv_norm", [1, D]); din("w_kv", [D, 2048]); din("k_norm", [1, 64])
    din("b_norm", [1, D]); din("b_w_q", [D, D]); din("b_q_norm", [1, 64]); din("b_lambda", [4, 64])
    din("b_sub_norm", [1, 128]); din("b_w_out", [D, D])
    din("mlp_norm", [2, D]); din("mlp_w1", [2 * D, DFF]); din("mlp_w2", [2 * DFF, D])
    for n in CONST_NAMES:
        din("c_" + n, [128, 128])
    out_d = nc.dram_tensor("out", [S, D], F32, kind="ExternalOutput").ap()

    with ExitStack() as es:
        P = Prog(nc, es)

        def sb(name, shape, dt=F32):
            return es.enter_context(nc.sbuf_tensor(name, list(shape), dt))

        xT = sb("xT", [128, 8, S])
        xT_tk = [[Tk() for _ in range(NB)] for _ in range(8)]
        xnT = sb("xnT", [128, 8, S], BF16)
        xnT_tk = [Tk() for _ in range(NB)]
        banks = [es.enter_context(nc.psum_tensor("bank%d" % i, [128, 512], F32)) for i in range(8)]
        PS = Ring(banks, [Tk("bank%d" % i, excl=True) for i in range(8)])
        stage = Ring([sb("stage%d" % i, [128, 2048]) for i in range(2)])
        stage_sem = [P.new_dma_sem("stage%d" % i) for i in range(2)]
        tmpf = Ring([sb("tmpf%d" % i, [128, 512]) for i in range(3)])
        rstd_r = Ring([sb("rstd%d" % i, [128, 512]) for i in range(1)])
        cst = {}
        cst_tk = Tk()
        csem = P.new_dma_sem("const")
        for n in CONST_NAMES:
            cst[n] = sb("cs_" + n, [128, 128])
            P.dma("sp", csem, (lambda n=n: lambda e: e.dma_start(out=cst[n][:], in_=dr["c_" + n]))(), writes=[cst_tk])
        gains = sb("gains", [128, 5, 8])
        gain_src = [dr["a_norm"][0, :], dr["mlp_norm"][0, :], dr["kv_norm"][0, :], dr["b_norm"][0, :], dr["mlp_norm"][1, :]]
        for gi, src in enumerate(gain_src):
            P.dma("sp", csem, (lambda gi=gi, src=src: lambda e: e.dma_start(
                out=gains[:, gi, :], in_=src.rearrange("(c p) -> p c", p=128), allow_slow_non_contiguous=True))(), writes=[cst_tk])
        epsb = sb("epsb", [128, 1])
        P.op("pool", lambda e: e.memset(epsb[:], EPS), writes=[cst_tk])

        evac_flip = [0]

        def evac_copy(out_ap, in_ap, reads, writes):
            evac_flip[0] ^= 1
            if evac_flip[0]:
                P.op("act", lambda e: e.activation(out=out_ap, in_=in_ap, func=AF.Copy), reads=reads, writes=writes)
            else:
                P.op("dve", lambda e: e.tensor_copy(out=out_ap, in_=in_ap), reads=reads, writes=writes)

        for t in range(NT):
            st, st_tk = stage.next()
            si = (stage.i - 1) % 2
            P.dma("sp", stage_sem[si], (lambda st=st, t=t: lambda e: e.dma_start(
                out=st[:, 0:1024], in_=x_d[t * 128:(t + 1) * 128, :]))(), writes=[st_tk])
            for cg in range(2):
                ps, ps_tk = PS.next()
                for j in range(4):
                    c = cg * 4 + j
                    P.op("pe", (lambda ps=ps, st=st, c=c, j=j: lambda e: e.transpose(
                        ps[:, j * 128:(j + 1) * 128], st[:, c * 128:(c + 1) * 128], cst["ident"][:]))(),
                        reads=[st_tk, cst_tk], writes=[ps_tk], sig=(j == 3))
                tb = t // 4
                evac_copy(xT[:, cg * 4:(cg + 1) * 4, t * 128:(t + 1) * 128],
                          ps[:].rearrange("p (j n) -> p j n", j=4),
                          reads=[ps_tk], writes=[xT_tk[c][tb] for c in range(cg * 4, cg * 4 + 4)])

        def rms_to_xnT():
            for tb in range(NB):
                blk = slice(tb * 512, (tb + 1) * 512)
                ps, ps_tk = PS.next()
                for c in range(8):
                    sq, sq_tk = tmpf.next()
                    P.op("act", (lambda sq=sq, c=c, blk=blk: lambda e: e.activation(out=sq[:], in_=xT[:, c, blk], func=AF.Square))(),
                         reads=[xT_tk[c][tb]], writes=[sq_tk])
                    P.op("pe", (lambda ps=ps, sq=sq, c=c: lambda e: e.matmul(ps[:], cst["ones"][:], sq[:], start=(c == 0), stop=(c == 7)))(),
                         reads=[sq_tk, cst_tk], writes=[ps_tk], sig=True)
                rs, rs_tk = tmpf.next()
                P.op("act", (lambda rs=rs, ps=ps: lambda e: e.activation(out=rs[:], in_=ps[:], func=AF.Sqrt, bias=epsb[:, 0:1], scale=1.0 / D))(),
                     reads=[ps_tk, cst_tk], writes=[rs_tk])
                rstd_bc, rstd_tk = rstd_r.next()
                P.op("dve", (lambda rs=rs, rstd_bc=rstd_bc: lambda e: e.reciprocal(out=rstd_bc[:], in_=rs[:]))(),
                     reads=[rs_tk], writes=[rstd_tk])
                for c in range(8):
                    eng = "dve" if c % 2 == 0 else "pool"
                    P.op(eng, (lambda c=c, blk=blk, rstd_bc=rstd_bc: lambda e: e.tensor_tensor(out=xnT[:, c, blk], in0=xT[:, c, blk], in1=rstd_bc[:], op=ALU.mult))(),
                         reads=[xT_tk[c][tb], rstd_tk], writes=[xnT_tk[tb]])

        wsem = [P.new_dma_sem("w%d" % i) for i in range(2)]

        def load_w(dst, dst_tk, src, kc, ncols, gain_idx, gain_c0=0):
            per = max(1, min(2048 // ncols, 512 // 128 if ncols <= 128 else 99))
            c0 = 0
            while c0 < kc:
                n = min(per, kc - c0)
                st, st_tk = stage.next()
                si = (stage.i - 1) % 2
                stv = st[:, 0:n * ncols].rearrange("p (c n) -> p c n", c=n)
                srcv = src[c0 * 128:(c0 + n) * 128, :].rearrange("(c p) n -> p c n", p=128)
                P.dma("sp", stage_sem[si], (lambda stv=stv, srcv=srcv: lambda e: e.dma_start(out=stv, in_=srcv))(), writes=[st_tk])
                dv = dst[:, c0:c0 + n, :]
                if gain_idx is None:
                    P.op("pool", (lambda dv=dv, stv=stv: lambda e: e.tensor_copy(out=dv, in_=stv))(), reads=[st_tk], writes=[dst_tk])
                else:
                    gv = gains[:, gain_idx, gain_c0 + c0:gain_c0 + c0 + n].unsqueeze(2).to_broadcast([128, n, ncols])
                    P.op("pool", (lambda dv=dv, stv=stv, gv=gv: lambda e: e.tensor_tensor(out=dv, in0=stv, in1=gv, op=ALU.mult))(),
                         reads=[st_tk, cst_tk], writes=[dst_tk])
                c0 += n

        def mlp(layer, gain_idx):
            with ExitStack() as ph:
                mlp_body(layer, gain_idx, ph)

        def mlp_body(layer, gain_idx, ph):
            FG = 512
            psb = lambda name, shape, dt=F32: ph.enter_context(nc.sbuf_tensor(name, list(shape), dt))
            w1r = Ring([psb("w1g%d_%d" % (layer, i), [128, 8, FG], BF16) for i in range(2)])
            w2r = Ring([psb("w2g%d_%d" % (layer, i), [128, FG // 128, D], BF16) for i in range(2)])
            hr = Ring([psb("hT%d_%d" % (layer, i), [128, FG // 128, 512], BF16) for i in range(2)])
            w1_d = dr["mlp_w1"][layer * D:(layer + 1) * D, :]
            w2_d = dr["mlp_w2"][layer * DFF:(layer + 1) * DFF, :]
            for g in range(DFF // FG):
                w1, w1_tk = w1r.next()
                w2, w2_tk = w2r.next()
                load_w(w1, w1_tk, w1_d[:, g * FG:(g + 1) * FG], 8, FG, gain_idx)
                load_w(w2, w2_tk, w2_d[g * FG:(g + 1) * FG, :], FG // 128, D, None)
                for tb in range(NB):
                    blk = slice(tb * 512, (tb + 1) * 512)
                    hT, hT_tk = hr.next()
                    for fc in range(FG // 128):
                        ps, ps_tk = PS.next()
                        for c in range(8):
                            P.op("pe", (lambda ps=ps, w1=w1, c=c, fc=fc, blk=blk: lambda e: e.matmul(
                                ps[:], w1[:, c, fc * 128:(fc + 1) * 128], xnT[:, c, blk], start=(c == 0), stop=(c == 7)))(),
                                reads=[w1_tk, xnT_tk[tb]], writes=[ps_tk], sig=(c == 7))
                        h1, h1_tk = tmpf.next()
                        P.op("act", (lambda h1=h1, ps=ps: lambda e: e.activation(out=h1[:], in_=ps[:], func=AF.Relu))(),
                             reads=[ps_tk], writes=[h1_tk])
                        P.op("pool", (lambda h1=h1, hT=hT, fc=fc: lambda e: e.tensor_tensor(out=hT[:, fc, :], in0=h1[:], in1=h1[:], op=ALU.mult))(),
                             reads=[h1_tk], writes=[hT_tk])
                    for n in range(8):
                        ps, ps_tk = PS.next()
                        for fc in range(FG // 128):
                            P.op("pe", (lambda ps=ps, w2=w2, n=n, fc=fc, hT=hT: lambda e: e.matmul(
                                ps[:], w2[:, fc, n * 128:(n + 1) * 128], hT[:, fc, :], start=(fc == 0), stop=(fc == FG // 128 - 1)))(),
                                reads=[w2_tk, hT_tk], writes=[ps_tk], sig=(fc == FG // 128 - 1))
                        P.op("dve", (lambda ps=ps, n=n, blk=blk: lambda e: e.tensor_tensor(out=xT[:, n, blk], in0=ps[:], in1=xT[:, n, blk], op=ALU.add))(),
                             reads=[ps_tk], writes=[xT_tk[n][tb]])
            retire(w1r.tks + w2r.tks + hr.tks)


        def gdn():
            with ExitStack() as ph:
                gdn_body(ph)

        def gdn_body(ph):
            psb = lambda name, shape, dt=F32: ph.enter_context(nc.sbuf_tensor(name, list(shape), dt))
            mytks = []

            def tk():
                t = Tk(); mytks.append(t); return t
            PSr = Ring(banks, PS.tks)
            ident = cst["ident"]; onesf = cst["ones"]
            ident_bf = psb("g_identbf", [128, 128], BF16); cb_tk = tk()
            P.op("dve", lambda e: e.tensor_copy(out=ident_bf[:], in_=ident[:]), reads=[cst_tk], writes=[cb_tk])
            one_b = psb("g_oneb", [128, 1]); P.op("pool", lambda e: e.memset(one_b[:], 1.0), writes=[cb_tk])
            s1 = P.new_dma_sem("g_p1"); s2 = P.new_dma_sem("g_p2"); s3 = P.new_dma_sem("g_p3"); s4 = P.new_dma_sem("g_p4")
            alog = psb("g_alog", [128, 8]); dtb = psb("g_dtb", [128, 8]); onw = psb("g_onw", [128, 128]); cw = psb("g_cw", [128, 24, 4])
            alog_tk = tk(); dtb_tk = tk(); onw_tk = tk(); cw_tk = tk()
            P.dma("sp", s1, lambda e: e.dma_start(out=alog[:], in_=dr["a_a_log"][0, :].partition_broadcast(128)), writes=[alog_tk])
            P.dma("sp", s2, lambda e: e.dma_start(out=dtb[:], in_=dr["a_dt_bias"][0, :].partition_broadcast(128)), writes=[dtb_tk])
            P.dma("sp", s3, lambda e: e.dma_start(out=onw[:], in_=dr["a_out_norm"][0, :].partition_broadcast(128)), writes=[onw_tk])
            for j in range(4):
                P.dma("sp", s4, (lambda j=j: lambda e: e.dma_start(out=cw[:, :, j], in_=dr["a_conv_w"][j, :].rearrange("(c p) -> p c", p=128),
                      allow_slow_non_contiguous=True))(), writes=[cw_tk])
            P.op("act", lambda e: e.activation(out=alog[:], in_=alog[:], func=AF.Exp), reads=[alog_tk], writes=[alog_tk])
            P.op("dve", lambda e: e.tensor_scalar(out=alog[:], in0=alog[:], scalar1=-1.0, scalar2=None, op0=ALU.mult), reads=[alog_tk], writes=[alog_tk])
            wbg = psb("g_wbg", [128, 8, 16], BF16); wbg_tk = tk()
            load_w(wbg, wbg_tk, dr["a_w_in"][:, 4096:4112], 8, 16, 0)
            bg = psb("g_bg", [128, NT, 16]); bg_tk = tk()
            ps, ps_tk = PSr.next()
            for t in range(NT):
                for c in range(8):
                    P.op("pe", (lambda ps=ps, t=t, c=c: lambda e: e.matmul(ps[:, t * 16:(t + 1) * 16], xnT[:, c, t * 128:(t + 1) * 128], wbg[:, c, :],
                         start=(c == 0), stop=(c == 7)))(), reads=[xnT_tk[t // 4], wbg_tk], writes=[ps_tk], sig=(t == NT - 1 and c == 7))
            P.op("dve", (lambda ps=ps: lambda e: e.tensor_copy(out=bg[:].rearrange("p t k -> p (t k)"), in_=ps[:, 0:256]))(), reads=[ps_tk], writes=[bg_tk])
            beta = psb("g_beta", [128, NT, 8]); gg = psb("g_g", [128, NT, 8]); par_tk = tk()
            P.op("act", lambda e: e.activation(out=beta[:], in_=bg[:, :, 0:8], func=AF.Sigmoid), reads=[bg_tk], writes=[par_tk])
            P.op("dve", lambda e: e.tensor_tensor(out=gg[:], in0=bg[:, :, 8:16], in1=dtb[:].unsqueeze(1).to_broadcast([128, NT, 8]), op=ALU.add),
                 reads=[bg_tk, dtb_tk, par_tk], writes=[par_tk])
            P.op("act", lambda e: e.activation(out=gg[:], in_=gg[:], func=AF.Exp), reads=[par_tk], writes=[par_tk])
            P.op("act", lambda e: e.activation(out=gg[:], in_=gg[:], func=AF.Ln, bias=one_b[:, 0:1], scale=1.0), reads=[par_tk, cb_tk], writes=[par_tk])
            P.op("dve", lambda e: e.tensor_tensor(out=gg[:], in0=gg[:], in1=alog[:].unsqueeze(1).to_broadcast([128, NT, 8]), op=ALU.mult),
                 reads=[par_tk, alog_tk], writes=[par_tk])
            gc = psb("g_gc", [128, NT, 8]); ngc = psb("g_ngc", [128, NT, 8]); bgam = psb("g_bgam", [128, NT, 8])
            kds = psb("g_kds", [128, NT, 8]); cd = psb("g_cd", [128, NT, 8])
            ps, ps_tk = PSr.next()
            P.op("pe", (lambda ps=ps: lambda e: e.matmul(ps[:, 0:128], cst["trimask"][:], gg[:].rearrange("p t k -> p (t k)"), start=True, stop=True))(),
                 reads=[par_tk, cst_tk], writes=[ps_tk], sig=False)
            P.op("pe", (lambda ps=ps: lambda e: e.matmul(ps[:, 128:256], onesf[:], gg[:].rearrange("p t k -> p (t k)"), start=True, stop=True))(),
                 reads=[par_tk, cst_tk], writes=[ps_tk])
            fl = lambda a: a[:].rearrange("p t k -> p (t k)")
            P.op("dve", (lambda ps=ps: lambda e: e.tensor_copy(out=fl(gc), in_=ps[:, 0:128]))(), reads=[ps_tk], writes=[par_tk])
            P.op("dve", (lambda ps=ps: lambda e: e.tensor_scalar(out=fl(ngc), in0=ps[:, 0:128], scalar1=-1.0, scalar2=None, op0=ALU.mult))(), reads=[ps_tk, par_tk], writes=[par_tk])
            P.op("act", (lambda ps=ps: lambda e: e.activation(out=fl(cd), in_=ps[:, 128:256], func=AF.Exp))(), reads=[ps_tk, par_tk], writes=[par_tk])
            P.op("dve", (lambda ps=ps: lambda e: e.tensor_tensor(out=fl(kds), in0=ps[:, 128:256], in1=fl(gc), op=ALU.subtract))(), reads=[ps_tk, par_tk], writes=[par_tk])
            P.op("act", lambda e: e.activation(out=fl(kds), in_=fl(kds), func=AF.Exp), reads=[par_tk], writes=[par_tk])
            P.op("act", lambda e: e.activation(out=fl(bgam), in_=fl(gc), func=AF.Exp), reads=[par_tk], writes=[par_tk])
            P.op("dve", lambda e: e.tensor_tensor(out=fl(bgam), in0=fl(bgam), in1=fl(beta), op=ALU.mult), reads=[par_tk], writes=[par_tk])
            w_r = [Ring([psb("g_w%d_%d" % (k, i), [128, 8, 128], BF16) for i in range(1)]) for k in range(4)]
            wo_r = Ring([psb("g_wo%d" % i, [128, 1, D], BF16) for i in range(1)])
            prj = psb("g_prj", [128, 3 + S]); prj_tk = tk()
            P.op("pool", lambda e: e.memset(prj[:, 0:3], 0.0), writes=[prj_tk])
            cacc = psb("g_cacc", [128, S]); cacc_tk = tk()
            qk_f = cacc; qkf_tk = cacc_tk
            qhT = psb("g_qhT", [128, S], BF16); khT = psb("g_khT", [128, S], BF16); vT = cacc
            qhT_tk = tk(); khT_tk = tk(); vT_tk = cacc_tk
            zs = psb("g_zs", [128, NT, 128], BF16); zs_tk = tk()
            kbg = psb("g_kbg", [128, NT, 128], BF16); kdec = psb("g_kdec", [128, NT, 128], BF16); bv = psb("g_bv", [128, NT, 128], BF16)
            tok_tk = [tk() for _ in range(NT)]
            u_sb = prj[:, 3:3 + S].rearrange("p (t n) -> p t n", t=NT); wT = psb("g_wT", [128, NT, 128], BF16); qdT = psb("g_qdT", [128, NT, 128], BF16); inT = psb("g_inT", [128, NT, 128], BF16)
            prep_tk = [tk() for _ in range(NT)]
            goT = khT; goT_tk = [khT_tk for _ in range(NB)]
            S_f = psb("g_Sf", [128, 128]); S_b = psb("g_Sb", [128, 128], BF16); S_tk = tk()
            GT = 2
            Pb = [[psb("g_P%d_%d" % (a, i), [128, 128]) for i in range(GT)] for a in range(2)]
            PTb = [[psb("g_PT%d_%d" % (a, i), [128, 128]) for i in range(GT)] for a in range(2)]
            Rb = [psb("g_R%d" % i, [128, 128]) for i in range(GT)]
            Rbf = [psb("g_Rbf%d" % i, [128, 128], BF16) for i in range(GT)]
            P_tk = [[tk() for i in range(GT)] for a in range(2)]
            PT_tk = [[tk() for i in range(GT)] for a in range(2)]
            R_tk = [tk() for i in range(GT)]
            sm = Ring([psb("g_sm%d" % i, [128, 256]) for i in range(4)])
            smb = Ring([psb("g_smb%d" % i, [128, 128], BF16) for i in range(3)])
            st1 = Ring([psb("g_st%d" % i, [128, 4]) for i in range(3)])

            for h in range(8):
                ws = []
                for k in range(4):
                    w, w_tk = w_r[k].next()
                    load_w(w, w_tk, dr["a_w_in"][:, k * 1024 + h * 128:k * 1024 + (h + 1) * 128], 8, 128, 0)
                    ws.append((w, w_tk))
                wo, wo_tk = wo_r.next()
                load_w(wo, wo_tk, dr["a_w_out"][h * 128:(h + 1) * 128, :], 1, D, None)
                for sec in range(3):
                    w, w_tk = ws[sec]
                    ch = sec * 8 + h
                    for tb in range(NB):
                        blk = slice(tb * 512, (tb + 1) * 512)
                        ps, ps_tk = PSr.next()
                        for c in range(8):
                            P.op("pe", (lambda ps=ps, w=w, c=c, blk=blk: lambda e: e.matmul(ps[:], w[:, c, :], xnT[:, c, blk], start=(c == 0), stop=(c == 7)))(),
                                 reads=[w_tk, xnT_tk[tb]], writes=[ps_tk], sig=(c == 7))
                        evac_copy(prj[:, 3 + tb * 512:3 + (tb + 1) * 512], ps[:], reads=[ps_tk], writes=[prj_tk])
                    P.op("dve", (lambda ch=ch: lambda e: e.tensor_scalar(out=cacc[:], in0=prj[:, 0:S], scalar1=cw[:, ch, 0:1], scalar2=None, op0=ALU.mult))(),
                         reads=[prj_tk, cw_tk], writes=[cacc_tk])
                    for j in range(1, 4):
                        P.op("dve", (lambda ch=ch, j=j: lambda e: e.scalar_tensor_tensor(out=cacc[:], in0=prj[:, j:j + S], scalar=cw[:, ch, j:j + 1], in1=cacc[:], op0=ALU.mult, op1=ALU.add))(),
                             reads=[prj_tk, cw_tk, cacc_tk], writes=[cacc_tk])
                    if sec == 2:
                        P.op("act", lambda e: e.activation(out=cacc[:], in_=cacc[:], func=AF.Silu), reads=[cacc_tk], writes=[cacc_tk])
                    else:
                        dst, dst_tk = (qhT, qhT_tk) if sec == 0 else (khT, khT_tk)
                        P.op("act", lambda e: e.activation(out=cacc[:], in_=cacc[:], func=AF.Silu), reads=[cacc_tk], writes=[cacc_tk])
                        for tb in range(NB):
                            blk = slice(tb * 512, (tb + 1) * 512)
                            sq, sq_tk = tmpf.next()
                            P.op("act", (lambda sq=sq, blk=blk: lambda e: e.activation(out=sq[:], in_=qk_f[:, blk], func=AF.Square))(), reads=[qkf_tk], writes=[sq_tk])
                            ps2, ps2_tk = PSr.next()
                            P.op("pe", (lambda ps2=ps2, sq=sq: lambda e: e.matmul(ps2[:], onesf[:], sq[:], start=True, stop=True))(), reads=[sq_tk, cst_tk], writes=[ps2_tk])
                            rs, rs_tk = tmpf.next()
                            P.op("act", (lambda rs=rs, ps2=ps2: lambda e: e.activation(out=rs[:], in_=ps2[:], func=AF.Sqrt, bias=epsb[:, 0:1], scale=1.0))(),
                                 reads=[ps2_tk, cst_tk], writes=[rs_tk])
                            rs0, rs0_tk = rs, rs_tk
                            rs, rs_tk = tmpf.next()
                            P.op("dve", (lambda rs=rs, rs0=rs0: lambda e: e.reciprocal(out=rs[:], in_=rs0[:]))(), reads=[rs0_tk], writes=[rs_tk])
                            sc = float(128 ** -0.5) if sec == 0 else 1.0
                            P.op("dve", (lambda dst=dst, rs=rs, blk=blk, sc=sc: lambda e: e.scalar_tensor_tensor(out=dst[:, blk], in0=qk_f[:, blk], scalar=sc, in1=rs[:], op0=ALU.mult, op1=ALU.mult))(),
                                 reads=[qkf_tk, rs_tk], writes=[dst_tk])
                w, w_tk = ws[3]
                for tg in range(4):
                    ps, ps_tk = PSr.next()
                    for j in range(4):
                        t = tg * 4 + j
                        for c in range(8):
                            P.op("pe", (lambda ps=ps, j=j, t=t, c=c, w=w: lambda e: e.matmul(ps[:, j * 128:(j + 1) * 128], xnT[:, c, t * 128:(t + 1) * 128], w[:, c, :],
                                 start=(c == 0), stop=(c == 7)))(), reads=[xnT_tk[tg], w_tk], writes=[ps_tk], sig=(j == 3 and c == 7))
                    zf, zf_tk = tmpf.next()
                    P.op("act", (lambda zf=zf, ps=ps: lambda e: e.activation(out=zf[:], in_=ps[:], func=AF.Silu))(), reads=[ps_tk], writes=[zf_tk])
                    P.op("pool", (lambda zf=zf, tg=tg: lambda e: e.tensor_tensor(out=zs[:, tg * 4:(tg + 1) * 4, :], in0=zf[:].rearrange("p (j n) -> p j n", j=4),
                         in1=onw[:].unsqueeze(1).to_broadcast([128, 4, 128]), op=ALU.mult))(), reads=[zf_tk, onw_tk], writes=[zs_tk])
                for t in range(NT):
                    ts_ = slice(t * 128, (t + 1) * 128)
                    ps, ps_tk = PSr.next()
                    P.op("pe", (lambda ps=ps, ts_=ts_: lambda e: e.matmul(ps[:, 0:128], khT[:, ts_], ident_bf[:], start=True, stop=True))(), reads=[khT_tk, cb_tk], writes=[ps_tk], sig=False)
                    P.op("pe", (lambda ps=ps, ts_=ts_: lambda e: e.matmul(ps[:, 128:256], vT[:, ts_], ident[:], start=True, stop=True))(), reads=[vT_tk, cst_tk], writes=[ps_tk])
                    P.op("act", (lambda ps=ps, t=t, h=h: lambda e: e.activation(out=kbg[:, t, :], in_=ps[:, 0:128], func=AF.Copy, scale=bgam[:, t, h:h + 1]))(), reads=[ps_tk, par_tk], writes=[tok_tk[t]])
                    P.op("dve", (lambda ps=ps, t=t, h=h: lambda e: e.tensor_scalar(out=kdec[:, t, :], in0=ps[:, 0:128], scalar1=kds[:, t, h:h + 1], scalar2=None, op0=ALU.mult))(), reads=[ps_tk, par_tk], writes=[tok_tk[t]])
                    P.op("dve", (lambda ps=ps, t=t, h=h: lambda e: e.tensor_scalar(out=bv[:, t, :], in0=ps[:, 128:256], scalar1=beta[:, t, h:h + 1], scalar2=None, op0=ALU.mult))(), reads=[ps_tk, par_tk], writes=[tok_tk[t]])
                for g0 in range(0, NT, GT):
                    tiles = list(range(g0, g0 + GT))
                    for i, t in enumerate(tiles):
                        ts_ = slice(t * 128, (t + 1) * 128)
                        z, z_tk = sm.next()
                        P.op("pool", (lambda z=z, t=t, h=h: lambda e: e.tensor_scalar(out=z[:, 0:128], in0=ident[:], scalar1=gc[:, t, h:h + 1], scalar2=None, op0=ALU.mult))(), reads=[cst_tk, par_tk], writes=[z_tk])
                        P.op("pool", (lambda z=z, t=t, h=h: lambda e: e.tensor_scalar(out=z[:, 128:256], in0=ident[:], scalar1=beta[:, t, h:h + 1], scalar2=None, op0=ALU.mult))(), reads=[cst_tk, par_tk, z_tk], writes=[z_tk])
                        bc, bc_tk = PSr.next()
                        P.op("pe", (lambda bc=bc, z=z: lambda e: e.matmul(bc[:, 0:256], onesf[:], z[:], start=True, stop=True))(), reads=[z_tk, cst_tk], writes=[bc_tk])
                        d1, d1_tk = sm.next()
                        P.op("dve", (lambda d1=d1, bc=bc: lambda e: e.scalar_tensor_tensor(out=d1[:, 0:128], in0=bc[:, 0:128], scalar=-1.0, in1=cst["mnegL"][:], op0=ALU.mult, op1=ALU.add))(), reads=[bc_tk, cst_tk], writes=[d1_tk])
                        P.op("dve", (lambda d1=d1, bc=bc: lambda e: e.tensor_tensor(out=d1[:, 128:256], in0=bc[:, 0:128], in1=cst["mnegU"][:], op=ALU.add))(), reads=[bc_tk, cst_tk, d1_tk], writes=[d1_tk])
                        P.op("act", (lambda d1=d1, t=t, h=h: lambda e: e.activation(out=d1[:, 0:128], in_=d1[:, 0:128], func=AF.Exp, bias=gc[:, t, h:h + 1], scale=1.0))(), reads=[d1_tk, par_tk], writes=[d1_tk])
                        P.op("act", (lambda d1=d1, t=t, h=h: lambda e: e.activation(out=d1[:, 128:256], in_=d1[:, 128:256], func=AF.Exp, bias=ngc[:, t, h:h + 1], scale=1.0))(), reads=[d1_tk, par_tk], writes=[d1_tk])
                        d2, d2_tk = sm.next()
                        P.op("act", (lambda d2=d2, bc=bc: lambda e: e.activation(out=d2[:, 0:128], in_=bc[:, 0:128], func=AF.Exp))(), reads=[bc_tk], writes=[d2_tk])
                        P.op("act", (lambda d2=d2, bc=bc: lambda e: e.activation(out=d2[:, 128:256], in_=bc[:, 128:256], func=AF.Copy))(), reads=[bc_tk, d2_tk], writes=[d2_tk])
                        gq, gq_tk = PSr.next()
                        P.op("pe", (lambda gq=gq, ts_=ts_: lambda e: e.matmul(gq[:, 0:128], khT[:, ts_], khT[:, ts_], start=True, stop=True))(), reads=[khT_tk], writes=[gq_tk], sig=False)
                        P.op("pe", (lambda gq=gq, ts_=ts_: lambda e: e.matmul(gq[:, 128:256], khT[:, ts_], qhT[:, ts_], start=True, stop=True))(), reads=[khT_tk, qhT_tk], writes=[gq_tk])
                        P.op("dve", (lambda gq=gq, d1=d1, t=t: lambda e: e.tensor_tensor(out=inT[:, t, :], in0=gq[:, 128:256], in1=d1[:, 128:256], op=ALU.mult))(), reads=[gq_tk, d1_tk], writes=[prep_tk[t]])
                        P.op("pool", (lambda d2=d2, t=t, ts_=ts_: lambda e: e.tensor_tensor(out=qdT[:, t, :], in0=qhT[:, ts_], in1=d2[:, 0:128], op=ALU.mult))(), reads=[qhT_tk, d2_tk], writes=[prep_tk[t]])
                        P.op("pool", (lambda d1=d1: lambda e: e.tensor_tensor(out=d1[:, 0:128], in0=d1[:, 0:128], in1=cst["strictL"][:], op=ALU.mult))(), reads=[d1_tk, cst_tk, prep_tk[t]], writes=[d1_tk])
                        P.op("pool", (lambda d1=d1: lambda e: e.tensor_tensor(out=d1[:, 128:256], in0=d1[:, 128:256], in1=cst["strictU"][:], op=ALU.mult))(), reads=[d1_tk, cst_tk], writes=[d1_tk])
                        P.op("dve", (lambda gq=gq, d1=d1, i=i, t=t, h=h: lambda e: e.scalar_tensor_tensor(out=Pb[0][i][:], in0=gq[:, 0:128], scalar=beta[:, t, h:h + 1], in1=d1[:, 0:128], op0=ALU.mult, op1=ALU.mult))(),
                             reads=[gq_tk, d1_tk, par_tk], writes=[P_tk[0][i]])
                        t1, t1_tk = sm.next()
                        P.op("dve", (lambda gq=gq, d2=d2, t1=t1: lambda e: e.tensor_tensor(out=t1[:, 0:128], in0=gq[:, 0:128], in1=d2[:, 128:256], op=ALU.mult))(), reads=[gq_tk, d2_tk], writes=[t1_tk])
                        P.op("pool", (lambda t1=t1, d1=d1, i=i: lambda e: e.tensor_tensor(out=PTb[0][i][:], in0=t1[:, 0:128], in1=d1[:, 128:256], op=ALU.mult))(), reads=[t1_tk, d1_tk], writes=[PT_tk[0][i]])
                        P.op("pool", (lambda i=i: lambda e: e.tensor_tensor(out=Rb[i][:], in0=ident[:], in1=PTb[0][i][:], op=ALU.subtract))(), reads=[cst_tk, PT_tk[0][i]], writes=[R_tk[i]])
                    cur = 0
                    for lvl in range(1, 7):
                        nxt = 1 - cur
                        for i, t in enumerate(tiles):
                            pp, pp_tk = PSr.next()
                            P.op("pe", (lambda pp=pp, i=i, cur=cur: lambda e: e.matmul(pp[:, 0:128], PTb[cur][i][:], Pb[cur][i][:], start=True, stop=True))(),
                                 reads=[PT_tk[cur][i], P_tk[cur][i]], writes=[pp_tk], sig=(lvl == 6))
                            if lvl < 6:
                                P.op("pe", (lambda pp=pp, i=i, cur=cur: lambda e: e.matmul(pp[:, 128:256], Pb[cur][i][:], PTb[cur][i][:], start=True, stop=True))(),
                                     reads=[PT_tk[cur][i], P_tk[cur][i]], writes=[pp_tk])
                            P.op("act", (lambda pp=pp, i=i, nxt=nxt: lambda e: e.activation(out=Pb[nxt][i][:], in_=pp[:, 0:128], func=AF.Copy))(), reads=[pp_tk], writes=[P_tk[nxt][i]])
                            if lvl < 6:
                                P.op("dve", (lambda pp=pp, i=i, nxt=nxt: lambda e: e.tensor_copy(out=PTb[nxt][i][:], in_=pp[:, 128:256]))(), reads=[pp_tk], writes=[PT_tk[nxt][i]])
                        for i, t in enumerate(tiles):
                            ru, ru_tk = PSr.next()
                            P.op("pe", (lambda ru=ru, i=i, nxt=nxt: lambda e: e.matmul(ru[:, 0:128], Pb[nxt][i][:], Rb[i][:], start=True, stop=True))(),
                                 reads=[P_tk[nxt][i], R_tk[i]], writes=[ru_tk])
                            P.op("dve", (lambda ru=ru, i=i: lambda e: e.tensor_tensor(out=Rb[i][:], in0=ru[:, 0:128], in1=Rb[i][:], op=ALU.add))(), reads=[ru_tk, R_tk[i]], writes=[R_tk[i]])
                        cur = nxt
                    for i, t in enumerate(tiles):
                        P.op("act", (lambda i=i: lambda e: e.activation(out=Rbf[i][:], in_=Rb[i][:], func=AF.Copy))(), reads=[R_tk[i]], writes=[R_tk[i]])
                        uw, uw_tk = PSr.next()
                        P.op("pe", (lambda uw=uw, i=i, t=t: lambda e: e.matmul(uw[:, 0:128], Rbf[i][:], bv[:, t, :], start=True, stop=True))(), reads=[R_tk[i], tok_tk[t]], writes=[uw_tk], sig=False)
                        P.op("pe", (lambda uw=uw, i=i, t=t: lambda e: e.matmul(uw[:, 128:256], kbg[:, t, :], Rbf[i][:], start=True, stop=True))(), reads=[R_tk[i], tok_tk[t]], writes=[uw_tk])
                        P.op("act", (lambda uw=uw, t=t: lambda e: e.activation(out=u_sb[:, t, :], in_=uw[:, 0:128], func=AF.Copy))(), reads=[uw_tk], writes=[prep_tk[t], prj_tk])
                        P.op("dve", (lambda uw=uw, t=t: lambda e: e.tensor_copy(out=wT[:, t, :], in_=uw[:, 128:256]))(), reads=[uw_tk], writes=[prep_tk[t]])
                P.op("pool", lambda e: e.memset(S_f[:], 0.0), writes=[S_tk])
                P.op("pool", lambda e: e.memset(S_b[:], 0.0), reads=[S_tk], writes=[S_tk])
                for t in range(NT):
                    ts_ = slice(t * 128, (t + 1) * 128)
                    wsp, wsp_tk = PSr.next()
                    P.op("pe", (lambda wsp=wsp, t=t: lambda e: e.matmul(wsp[:, 0:128], wT[:, t, :], S_b[:], start=True, stop=True))(), reads=[prep_tk[t], S_tk], writes=[wsp_tk])
                    vn, vn_tk = smb.next()
                    P.op("dve", (lambda vn=vn, wsp=wsp, t=t: lambda e: e.tensor_tensor(out=vn[:], in0=u_sb[:, t, :], in1=wsp[:, 0:128], op=ALU.subtract))(), reads=[prep_tk[t], wsp_tk, prj_tk], writes=[vn_tk])
                    op_, op_tk = PSr.next()
                    P.op("pe", (lambda op_=op_, t=t: lambda e: e.matmul(op_[:, 0:128], qdT[:, t, :], S_b[:], start=True, stop=False))(), reads=[prep_tk[t], S_tk], writes=[op_tk], sig=False)
                    P.op("pe", (lambda op_=op_, t=t, vn=vn: lambda e: e.matmul(op_[:, 0:128], inT[:, t, :], vn[:], start=False, stop=True))(), reads=[prep_tk[t], vn_tk], writes=[op_tk], sig=False)
                    P.op("pe", (lambda op_=op_, t=t, vn=vn: lambda e: e.matmul(op_[:, 128:256], kdec[:, t, :], vn[:], start=True, stop=True))(), reads=[tok_tk[t], vn_tk], writes=[op_tk])
                    P.op("dve", (lambda op_=op_, t=t, h=h: lambda e: e.scalar_tensor_tensor(out=S_f[:], in0=S_f[:], scalar=cd[:, t, h:h + 1], in1=op_[:, 128:256], op0=ALU.mult, op1=ALU.add))(),
                         reads=[op_tk, par_tk, S_tk], writes=[S_tk])
                    P.op("act", lambda e: e.activation(out=S_b[:], in_=S_f[:], func=AF.Copy), reads=[S_tk], writes=[S_tk])
                    o_sb, o_tk = sm.next()
                    sst, sst_tk = st1.next()
                    P.op("act", (lambda o_sb=o_sb, op_=op_, sst=sst: lambda e: e.activation(out=o_sb[:, 128:256], in_=op_[:, 0:128], func=AF.Square, accum_out=sst[:, 0:1]))(), reads=[op_tk], writes=[o_tk, sst_tk])
                    P.op("dve", (lambda o_sb=o_sb, op_=op_: lambda e: e.tensor_copy(out=o_sb[:, 0:128], in_=op_[:, 0:128]))(), reads=[op_tk, o_tk], writes=[o_tk])
                    P.op("act", (lambda sst=sst: lambda e: e.activation(out=sst[:, 1:2], in_=sst[:, 0:1], func=AF.Sqrt, bias=epsb[:, 0:1], scale=1.0 / 128))(), reads=[sst_tk, cst_tk], writes=[sst_tk])
                    P.op("dve", (lambda sst=sst: lambda e: e.reciprocal(out=sst[:, 2:3], in_=sst[:, 1:2]))(), reads=[sst_tk], writes=[sst_tk])
                    go, go_tk = smb.next()
                    P.op("dve", (lambda go=go, o_sb=o_sb, sst=sst, t=t: lambda e: e.scalar_tensor_tensor(out=go[:], in0=o_sb[:, 0:128], scalar=sst[:, 2:3], in1=zs[:, t, :], op0=ALU.mult, op1=ALU.mult))(),
                         reads=[o_tk, sst_tk, zs_tk], writes=[go_tk])
                    gp, gp_tk = PSr.next()
                    P.op("pe", (lambda gp=gp, go=go: lambda e: e.matmul(gp[:, 0:128], go[:], ident_bf[:], start=True, stop=True))(), reads=[go_tk, cb_tk], writes=[gp_tk])
                    evac_copy(goT[:, ts_], gp[:, 0:128], reads=[gp_tk], writes=[goT_tk[t // 4]])
                for tb in range(NB):
                    blk = slice(tb * 512, (tb + 1) * 512)
                    for n_ in range(8):
                        ps, ps_tk = PSr.next()
                        P.op("pe", (lambda ps=ps, wo=wo, n_=n_, blk=blk: lambda e: e.matmul(ps[:], wo[:, 0, n_ * 128:(n_ + 1) * 128], goT[:, blk], start=True, stop=True))(),
                             reads=[wo_tk, goT_tk[tb]], writes=[ps_tk])
                        P.op("dve", (lambda ps=ps, n_=n_, blk=blk: lambda e: e.tensor_tensor(out=xT[:, n_, blk], in0=ps[:], in1=xT[:, n_, blk], op=ALU.add))(),
                             reads=[ps_tk], writes=[xT_tk[n_][tb]])
            for r in w_r + [wo_r, sm, smb, st1]:
                mytks.extend(r.tks)
            retire(mytks)

        def attention():
            with ExitStack() as ph:
                attention_body(ph)

        def attention_body(ph):
            import math
            lam_init = 0.8 - 0.6 * math.exp(-0.3 * 1)
            psb = lambda name, shape, dt=F32: ph.enter_context(nc.sbuf_tensor(name, list(shape), dt))
            mytks = []

            def tk():
                t = Tk(); mytks.append(t); return t
            PSr = Ring(banks[0:4], PS.tks[0:4])
            acc_b, acc_tk = banks[4:8], PS.tks[4:8]
            prm = psb("at_prm", [128, 8]); prm_tk = tk()
            psem = P.new_dma_sem("at_prm"); lsem = P.new_dma_sem("at_lamb"); possem = P.new_dma_sem("at_pos")
            for half in range(2):
                P.dma("sp", psem, (lambda half=half: lambda e: e.dma_start(out=prm[half * 64:(half + 1) * 64, 0:1],
                      in_=dr["b_q_norm"][0, :].rearrange("(p o) -> p o", o=1), allow_slow_non_contiguous=True))(), writes=[prm_tk])
                P.dma("sp", psem, (lambda half=half: lambda e: e.dma_start(out=prm[half * 64:(half + 1) * 64, 1:2],
                      in_=dr["k_norm"][0, :].rearrange("(p o) -> p o", o=1), allow_slow_non_contiguous=True))(), writes=[prm_tk])
            P.dma("sp", psem, lambda e: e.dma_start(out=prm[:, 2:3], in_=dr["b_sub_norm"][0, :].rearrange("(p o) -> p o", o=1),
                  allow_slow_non_contiguous=True), writes=[prm_tk])
            lamb = psb("at_lamb", [128, 4, 64]); lamb_tk = tk()
            P.dma("sp", lsem, lambda e: e.dma_start(out=lamb[:].rearrange("p a b -> p (a b)"),
                  in_=dr["b_lambda"].rearrange("a b -> (a b)").partition_broadcast(128)), writes=[lamb_tk])
            P.op("dve", lambda e: e.tensor_scalar(out=prm[:, 0:1], in0=prm[:, 0:1], scalar1=0.125, scalar2=None, op0=ALU.mult), reads=[prm_tk], writes=[prm_tk])
            P.op("dve", lambda e: e.tensor_scalar(out=prm[:, 2:3], in0=prm[:, 2:3], scalar1=float(1.0 - lam_init), scalar2=None, op0=ALU.mult), reads=[prm_tk], writes=[prm_tk])
            lp = psb("at_lp", [128, 2, 64]); lp_tk = tk()
            P.op("dve", lambda e: e.tensor_tensor(out=lp[:, 0, :], in0=lamb[:, 0, :], in1=lamb[:, 1, :], op=ALU.mult), reads=[lamb_tk], writes=[lp_tk])
            P.op("dve", lambda e: e.tensor_tensor(out=lp[:, 1, :], in0=lamb[:, 2, :], in1=lamb[:, 3, :], op=ALU.mult), reads=[lamb_tk, lp_tk], writes=[lp_tk])
            P.op("dve", lambda e: e.tensor_reduce(out=prm[:, 4:6], in_=lp[:], axis=AX.X, op=ALU.add), reads=[lp_tk, prm_tk], writes=[prm_tk])
            P.op("act", lambda e: e.activation(out=prm[:, 4:6], in_=prm[:, 4:6], func=AF.Exp), reads=[prm_tk], writes=[prm_tk])
            P.op("dve", lambda e: e.tensor_tensor(out=prm[:, 3:4], in0=prm[:, 5:6], in1=prm[:, 4:5], op=ALU.subtract), reads=[prm_tk], writes=[prm_tk])
            P.op("dve", lambda e: e.tensor_scalar(out=prm[:, 3:4], in0=prm[:, 3:4], scalar1=float(-lam_init), scalar2=None, op0=ALU.add), reads=[prm_tk], writes=[prm_tk])
            ones_bf = psb("at_ones_bf", [128, 128], BF16); rot_bf = psb("at_rot_bf", [128, 128], BF16); tri_bf = psb("at_tri_bf", [128, 128], BF16)
            cb_tk = tk()
            P.op("dve", lambda e: e.tensor_copy(out=ones_bf[:], in_=cst["ones"][:]), reads=[cst_tk], writes=[cb_tk])
            P.op("dve", lambda e: e.tensor_copy(out=rot_bf[:], in_=cst["rot"][:]), reads=[cst_tk, cb_tk], writes=[cb_tk])
            P.op("dve", lambda e: e.tensor_copy(out=tri_bf[:], in_=cst["trimask"][:]), reads=[cst_tk, cb_tk], writes=[cb_tk])
            cosT = psb("at_cosT", [128, S]); sinT = psb("at_sinT", [128, S]); cs_tk = tk()
            with ExitStack() as ph2:
                posi = ph2.enter_context(nc.sbuf_tensor("at_posi", [128, S], I32))
                ang = ph2.enter_context(nc.sbuf_tensor("at_ang", [128, S], F32))
                kf = ph2.enter_context(nc.sbuf_tensor("at_kf", [128, S], F32))
                t2 = [Tk() for _ in range(3)]
                P.dma("sp", possem, lambda e: e.dma_start(out=posi[:], in_=pos_d[0, :].partition_broadcast(128)), writes=[t2[0]])
                for which, dst in ((0, sinT), (1, cosT)):
                    P.op("dve", lambda e: e.tensor_copy(out=ang[:], in_=posi[:]), reads=[t2[0]], writes=[t2[1]])
                    P.op("dve", (lambda which=which: lambda e: e.tensor_scalar(out=ang[:], in0=ang[:], scalar1=cst["freq"][:, 0:1], scalar2=float(which * np.pi / 2), op0=ALU.mult, op1=ALU.add))(),
                         reads=[t2[1], cst_tk], writes=[t2[1]])
                    P.op("dve", lambda e: e.tensor_scalar(out=kf[:], in0=ang[:], scalar1=float(1.0 / (2 * np.pi)), scalar2=None, op0=ALU.mult), reads=[t2[1]], writes=[t2[2]])
                    P.op("dve", lambda e: e.tensor_copy(out=posi[:].bitcast(I32) if False else kf[:].bitcast(I32), in_=kf[:]), reads=[t2[2]], writes=[t2[2]])
                    P.op("dve", lambda e: e.tensor_copy(out=kf[:], in_=kf[:].bitcast(I32)), reads=[t2[2]], writes=[t2[2]])
                    P.op("dve", lambda e: e.scalar_tensor_tensor(out=ang[:], in0=kf[:], scalar=-6.28125, in1=ang[:], op0=ALU.mult, op1=ALU.add), reads=[t2[1], t2[2]], writes=[t2[1]])
                    P.op("dve", lambda e: e.scalar_tensor_tensor(out=ang[:], in0=kf[:], scalar=-float(2 * np.pi - 6.28125), in1=ang[:], op0=ALU.mult, op1=ALU.add), reads=[t2[1], t2[2]], writes=[t2[1]])
                    P.op("dve", lambda e: e.tensor_scalar(out=ang[:], in0=ang[:], scalar1=3.14159, scalar2=-3.14159, op0=ALU.min, op1=ALU.max), reads=[t2[1]], writes=[t2[1]])
                    P.op("act", (lambda dst=dst: lambda e: e.activation(out=dst[:], in_=ang[:], func=AF.Sin))(), reads=[t2[1]], writes=[cs_tk])
                retire(t2)
            wq_r = Ring([psb("at_wq%d" % i, [128, 8, 128], BF16) for i in range(2)])
            wk_r = Ring([psb("at_wk%d" % i, [128, 8, 128], BF16) for i in range(2)])
            wv_r = Ring([psb("at_wv%d" % i, [128, 8, 128], BF16) for i in range(2)])
            wo_r = Ring([psb("at_wo%d" % i, [128, 2, D], BF16) for i in range(2)])
            qT_r = Ring([psb("at_qT%d" % i, [128, S], BF16) for i in range(1)])
            kT_r = Ring([psb("at_kT%d" % i, [128, S], BF16) for i in range(1)])
            vt_r = Ring([psb("at_vt%d" % i, [128, NT, 128], BF16) for i in range(1)])
            aoT_r = Ring([psb("at_aoT%d" % i, [128, 2, S], BF16) for i in range(1)])
            pT_r = Ring([psb("at_pT%d" % i, [128, 512], BF16) for i in range(4)])
            raw_r = Ring([psb("at_raw%d" % i, [128, 512]) for i in range(2)])
            qn_r = Ring([psb("at_qn%d" % i, [128, 512]) for i in range(2)])
            qnb_r = Ring([psb("at_qnb%d" % i, [128, 512], BF16) for i in range(2)])
            om_r = Ring([psb("at_om%d" % i, [128, 512]) for i in range(3)])
            rings = [wq_r, wk_r, wv_r, wo_r, qT_r, kT_r, vt_r, aoT_r, pT_r, raw_r, qn_r, qnb_r, om_r]
            flip = [0]

            def any_eng():
                flip[0] ^= 1
                return "dve" if flip[0] else "pool"

            import os
            LIM = int(os.environ.get("ATT_LIMIT", "99"))
            for h in range(8 if LIM >= 99 else (1 if LIM >= 1 else 0)):
                hs = slice(h * 128, (h + 1) * 128)
                wq, wq_tk = wq_r.next(); wk, wk_tk = wk_r.next(); wv, wv_tk = wv_r.next()
                load_w(wq, wq_tk, dr["b_w_q"][:, hs], 8, 128, 3)
                load_w(wk, wk_tk, dr["w_kv"][:, hs], 8, 128, 2)
                load_w(wv, wv_tk, dr["w_kv"][:, 1024 + h * 128:1024 + (h + 1) * 128], 8, 128, 2)
                if h % 2 == 0:
                    wo, wo_tk = wo_r.next()
                    load_w(wo, wo_tk, dr["b_w_out"][h * 128:(h + 2) * 128, :], 2, D, None)
                    aoT, aoT_tk = aoT_r.next()
                qT, qT_tk = qT_r.next(); kT, kT_tk = kT_r.next(); vt, vt_tk = vt_r.next()
                for tg in range(4 if not os.environ.get("ATT_SKIPV") else 0):
                    ps, ps_tk = PSr.next()
                    for j in range(4):
                        t = tg * 4 + j
                        for c in range(8):
                            P.op("pe", (lambda ps=ps, j=j, t=t, c=c, wv=wv: lambda e: e.matmul(
                                ps[:, j * 128:(j + 1) * 128], xnT[:, c, t * 128:(t + 1) * 128], wv[:, c, :], start=(c == 0), stop=(c == 7)))(),
                                reads=[xnT_tk[tg], wv_tk], writes=[ps_tk], sig=(j == 3 and c == 7))
                    evac_copy(vt[:, tg * 4:(tg + 1) * 4, :], ps[:].rearrange("p (j n) -> p j n", j=4), reads=[ps_tk], writes=[vt_tk])
                for (w, w_tk, dst, dst_tk, gcol) in (((wq, wq_tk, qT, qT_tk, 0), (wk, wk_tk, kT, kT_tk, 1)) if not os.environ.get("ATT_SKIPQK") else ()):
                    for tb in range(NB):
                        blk = slice(tb * 512, (tb + 1) * 512)
                        ps, ps_tk = PSr.next()
                        for c in range(8):
                            P.op("pe", (lambda ps=ps, w=w, c=c, blk=blk: lambda e: e.matmul(ps[:], w[:, c, :], xnT[:, c, blk], start=(c == 0), stop=(c == 7)))(),
                                 reads=[w_tk, xnT_tk[tb]], writes=[ps_tk], sig=(c == 7))
                        sq, sq_tk = tmpf.next()
                        P.op("act", (lambda sq=sq, ps=ps: lambda e: e.activation(out=sq[:], in_=ps[:], func=AF.Square))(), reads=[ps_tk], writes=[sq_tk])
                        raw, raw_tk = raw_r.next()
                        P.op("dve", (lambda raw=raw, ps=ps: lambda e: e.tensor_copy(out=raw[:], in_=ps[:]))(), reads=[ps_tk], writes=[raw_tk])
                        ps2, ps2_tk = PSr.next()
                        P.op("pe", (lambda ps2=ps2, sq=sq: lambda e: e.matmul(ps2[:], cst["blockones"][:], sq[:], start=True, stop=True))(),
                             reads=[sq_tk, cst_tk], writes=[ps2_tk])
                        rs, rs_tk = tmpf.next()
                        P.op("act", (lambda rs=rs, ps2=ps2: lambda e: e.activation(out=rs[:], in_=ps2[:], func=AF.Sqrt, bias=epsb[:, 0:1], scale=1.0 / 64))(),
                             reads=[ps2_tk, cst_tk], writes=[rs_tk])
                        rs0, rs0_tk = rs, rs_tk
                        rs, rs_tk = tmpf.next()
                        P.op("dve", (lambda rs=rs, rs0=rs0: lambda e: e.reciprocal(out=rs[:], in_=rs0[:]))(), reads=[rs0_tk], writes=[rs_tk])
                        qn, qn_tk = qn_r.next()
                        P.op("dve", (lambda raw=raw, gcol=gcol: lambda e: e.tensor_scalar(
                            out=raw[:], in0=raw[:], scalar1=prm[:, gcol:gcol + 1], scalar2=None, op0=ALU.mult))(),
                            reads=[raw_tk, prm_tk], writes=[raw_tk])
                        P.op("dve", (lambda qn=qn, raw=raw, rs=rs: lambda e: e.tensor_tensor(out=qn[:], in0=raw[:], in1=rs[:], op=ALU.mult))(),
                            reads=[raw_tk, rs_tk], writes=[qn_tk])
                        qnb, qnb_tk = qnb_r.next()
                        P.op("act", (lambda qnb=qnb, qn=qn: lambda e: e.activation(out=qnb[:], in_=qn[:], func=AF.Copy))(), reads=[qn_tk], writes=[qnb_tk])
                        ps3, ps3_tk = PSr.next()
                        P.op("pe", (lambda ps3=ps3, qnb=qnb: lambda e: e.matmul(ps3[:], rot_bf[:], qnb[:], start=True, stop=True))(),
                             reads=[qnb_tk, cb_tk], writes=[ps3_tk])
                        t1, t1_tk = tmpf.next()
                        P.op("pool", (lambda t1=t1, qn=qn, blk=blk: lambda e: e.tensor_tensor(out=t1[:], in0=qn[:], in1=cosT[:, blk], op=ALU.mult))(),
                             reads=[qn_tk, cs_tk], writes=[t1_tk])
                        t2_, t2_tk = tmpf.next()
                        P.op("dve", (lambda t2_=t2_, ps3=ps3, blk=blk: lambda e: e.tensor_tensor(out=t2_[:], in0=ps3[:], in1=sinT[:, blk], op=ALU.mult))(),
                             reads=[ps3_tk, cs_tk], writes=[t2_tk])
                        P.op("pool", (lambda dst=dst, t1=t1, t2_=t2_, blk=blk: lambda e: e.tensor_tensor(out=dst[:, blk], in0=t1[:], in1=t2_[:], op=ALU.add))(),
                             reads=[t1_tk, t2_tk], writes=[dst_tk])
                for qb in range(NB if LIM >= 2 else 0):
                    blk = slice(qb * 512, (qb + 1) * 512)
                    nkt = 4 * qb + 4
                    for kt in range(nkt):
                        j = kt - 4 * qb
                        q0 = 0 if j < 0 else 128 * j
                        n = 512 - q0
                        for m in range(2):
                            ms = slice(m * 64, (m + 1) * 64)
                            ps, ps_tk = PSr.next()
                            P.op("pe", (lambda ps=ps, ms=ms, kt=kt, qb=qb, q0=q0, n=n, kT=kT, qT=qT: lambda e: e.matmul(
                                ps[:, 0:n], kT[ms, kt * 128:(kt + 1) * 128], qT[ms, qb * 512 + q0:(qb + 1) * 512], start=True, stop=True))(),
                                reads=[kT_tk, qT_tk], writes=[ps_tk])
                            pT, pT_tk = pT_r.next()
                            P.op("act", (lambda pT=pT, ps=ps, n=n: lambda e: e.activation(out=pT[:, 0:n], in_=ps[:, 0:n], func=AF.Exp))(),
                                 reads=[ps_tk], writes=[pT_tk])
                            if j >= 0:
                                P.op("pool", (lambda pT=pT: lambda e: e.tensor_tensor(out=pT[:, 0:128], in0=pT[:, 0:128], in1=tri_bf[:], op=ALU.mult))(),
                                     reads=[pT_tk, cb_tk], writes=[pT_tk])
                            P.op("pe", (lambda pT=pT, m=m, kt=kt, q0=q0, n=n, vt=vt, nkt=nkt: lambda e: e.matmul(
                                acc_b[2 * m][:, q0:512], vt[:, kt, :], pT[:, 0:n], start=(kt == 0), stop=(kt == nkt - 1)))(),
                                reads=[pT_tk, vt_tk], writes=[acc_tk[2 * m]], sig=False)
                            P.op("pe", (lambda pT=pT, m=m, kt=kt, q0=q0, n=n, nkt=nkt: lambda e: e.matmul(
                                acc_b[2 * m + 1][:, q0:512], ones_bf[:], pT[:, 0:n], start=(kt == 0), stop=(kt == nkt - 1)))(),
                                reads=[pT_tk, cb_tk], writes=[acc_tk[2 * m + 1]], sig=True)
                    oms = []
                    for m in range(2):
                        rd, rd_tk = tmpf.next()
                        P.op("dve", (lambda rd=rd, m=m: lambda e: e.reciprocal(out=rd[:], in_=acc_b[2 * m + 1][:]))(), reads=[acc_tk[2 * m + 1]], writes=[rd_tk])
                        om, om_tk = om_r.next()
                        P.op("dve", (lambda om=om, rd=rd, m=m: lambda e: e.tensor_tensor(out=om[:], in0=acc_b[2 * m][:], in1=rd[:], op=ALU.mult))(),
                             reads=[acc_tk[2 * m], rd_tk], writes=[om_tk])
                        oms.append((om, om_tk))
                    df, df_tk = om_r.next()
                    P.op("dve", (lambda df=df, oms=oms: lambda e: e.scalar_tensor_tensor(out=df[:], in0=oms[1][0][:], scalar=prm[:, 3:4], in1=oms[0][0][:], op0=ALU.mult, op1=ALU.add))(),
                         reads=[oms[0][1], oms[1][1], prm_tk], writes=[df_tk])
                    sq, sq_tk = tmpf.next()
                    P.op("act", (lambda sq=sq, df=df: lambda e: e.activation(out=sq[:], in_=df[:], func=AF.Square))(), reads=[df_tk], writes=[sq_tk])
                    ps2, ps2_tk = PSr.next()
                    P.op("pe", (lambda ps2=ps2, sq=sq: lambda e: e.matmul(ps2[:], cst["ones"][:], sq[:], start=True, stop=True))(), reads=[sq_tk, cst_tk], writes=[ps2_tk])
                    rs, rs_tk = tmpf.next()
                    P.op("act", (lambda rs=rs, ps2=ps2: lambda e: e.activation(out=rs[:], in_=ps2[:], func=AF.Sqrt, bias=epsb[:, 0:1], scale=1.0 / 128))(),
                         reads=[ps2_tk, cst_tk], writes=[rs_tk])
                    rs0, rs0_tk = rs, rs_tk
                    rs, rs_tk = tmpf.next()
                    P.op("dve", (lambda rs=rs, rs0=rs0: lambda e: e.reciprocal(out=rs[:], in_=rs0[:]))(), reads=[rs0_tk], writes=[rs_tk])
                    P.op("dve", (lambda df=df, rs=rs, h=h, blk=blk, aoT=aoT: lambda e: e.scalar_tensor_tensor(
                        out=aoT[:, h % 2, blk], in0=df[:], scalar=prm[:, 2:3], in1=rs[:], op0=ALU.mult, op1=ALU.mult))(),
                        reads=[df_tk, rs_tk, prm_tk], writes=[aoT_tk])
                if h % 2 == 1 and LIM >= 3:
                    for tb in range(NB):
                        blk = slice(tb * 512, (tb + 1) * 512)
                        for n_ in range(8):
                            ps, ps_tk = PSr.next()
                            for hh in range(2):
                                P.op("pe", (lambda ps=ps, wo=wo, hh=hh, n_=n_, blk=blk, aoT=aoT: lambda e: e.matmul(
                                    ps[:], wo[:, hh, n_ * 128:(n_ + 1) * 128], aoT[:, hh, blk], start=(hh == 0), stop=(hh == 1)))(),
                                    reads=[wo_tk, aoT_tk], writes=[ps_tk], sig=(hh == 1))
                            P.op("dve", (lambda ps=ps, n_=n_, blk=blk: lambda e: e.tensor_tensor(out=xT[:, n_, blk], in0=ps[:], in1=xT[:, n_, blk], op=ALU.add))(),
                                 reads=[ps_tk], writes=[xT_tk[n_][tb]])
            for r in rings:
                mytks.extend(r.tks)
            retire(mytks)

        if "gdn" in stages:
            rms_to_xnT()
            gdn()
        if "mlp0" in stages:
            rms_to_xnT()
            mlp(0, 1)
        if "attn" in stages:
            rms_to_xnT()
            attention()
        if "mlp1" in stages:
            rms_to_xnT()
            mlp(1, 4)

        osem = [P.new_dma_sem("out%d" % i) for i in range(2)]
        for t in range(NT):
            tb = t // 4
            o, o_tk = stage.next()
            osi = (stage.i - 1) % 2
            for cg in range(2):
                ps, ps_tk = PS.next()
                for j in range(4):
                    c = cg * 4 + j
                    P.op("pe", (lambda ps=ps, c=c, j=j, t=t: lambda e: e.transpose(
                        ps[:, j * 128:(j + 1) * 128], xT[:, c, t * 128:(t + 1) * 128], cst["ident"][:]))(),
                        reads=[xT_tk[c][tb], cst_tk], writes=[ps_tk], sig=(j == 3))
                evac_copy(o[:, cg * 512:(cg + 1) * 512], ps[:], reads=[ps_tk], writes=[o_tk])
            P.dma("sp", osem[osi], (lambda o=o, t=t: lambda e: e.dma_start(out=out_d[t * 128:(t + 1) * 128, :], in_=o[:, 0:1024]))(), reads=[o_tk])
        P.final_wait("sp", osem[0])
        P.final_wait("sp", osem[1])
        P.emit()
        nops = {e: len(P.ops[e]) for e in P.ENG}
        nops["splits"] = P.split_log
    return nc, nops


_CACHE = {}


def kernel(**inputs):
    B = inputs["x"].shape[0]
    consts = host_consts()
    if "nc" not in _CACHE:
        _CACHE["nc"] = build_program()
    nc, _ = _CACHE["nc"]
    shared = {}
    f32 = lambda a: np.ascontiguousarray(np.asarray(a, dtype=np.float32))
    shared["a_norm"] = f32(inputs["a_norm"]).reshape(1, D)
    shared["a_w_in"] = f32(inputs["a_w_in"]).reshape(D, 4112)
    shared["a_conv_w"] = f32(inputs["a_conv_w"]).reshape(4, 3072)
    shared["a_a_log"] = f32(inputs["a_a_log"]).reshape(1, 8)
    shared["a_dt_bias"] = f32(inputs["a_dt_bias"]).reshape(1, 8)
    shared["a_out_norm"] = f32(inputs["a_out_norm"]).reshape(1, 128)
    shared["a_w_out"] = f32(inputs["a_w_out"]).reshape(D, D)
    shared["kv_norm"] = f32(inputs["kv_norm"]).reshape(1, D)
    shared["w_kv"] = f32(inputs["w_kv"]).reshape(D, 2048)
    shared["k_norm"] = f32(inputs["k_norm"]).reshape(1, 64)
    shared["b_norm"] = f32(inputs["b_norm"]).reshape(1, D)
    shared["b_w_q"] = f32(inputs["b_w_q"]).reshape(D, D)
    shared["b_q_norm"] = f32(inputs["b_q_norm"]).reshape(1, 64)
    shared["b_lambda"] = f32(inputs["b_lambda"]).reshape(4, 64)
    shared["b_sub_norm"] = f32(inputs["b_sub_norm"]).reshape(1, 128)
    shared["b_w_out"] = f32(inputs["b_w_out"]).reshape(D, D)
    shared["mlp_norm"] = f32(inputs["mlp_norm"]).reshape(2, D)
    shared["mlp_w1"] = f32(inputs["mlp_w1"]).reshape(2 * D, DFF)
    shared["mlp_w2"] = f32(inputs["mlp_w2"]).reshape(2 * DFF, D)
    for n in CONST_NAMES:
        shared["c_" + n] = consts[n]
    x = f32(inputs["x"])
    pos = np.ascontiguousarray(np.asarray(inputs["positions"], dtype=np.int32))
    in_maps = []
    for b in range(B):
        m = dict(shared)
        m["x"] = x[b]
        m["positions"] = pos[b].reshape(1, S)
        in_maps.append(m)
    res = run_bass_kernel_spmd(nc, in_maps, core_ids=list(range(B)))
    return np.stack([np.asarray(r["out"], dtype=np.float32) for r in res.results], axis=0)
```

```python
import numpy as np
import concourse.bass as bass
import concourse.mybir as mybir
from concourse.bass_utils import run_bass_kernel_spmd
from contextlib import ExitStack

F32 = mybir.dt.float32
BF16 = mybir.dt.bfloat16
I32 = mybir.dt.int32
AF = mybir.ActivationFunctionType
ALU = mybir.AluOpType
AX = mybir.AxisListType

S = 2048
D = 1024
NT = 16
NB = 4
DFF = 4096
EPS = 1e-6


INHERIT = {}


class Tk:
    __slots__ = ("name", "w", "r", "excl")

    def __init__(self, name="", excl=False):
        self.name = name
        self.w = None
        self.r = dict(INHERIT)
        self.excl = excl


def retire(tks):
    for t in tks:
        if t.w is not None:
            k, v = t.w
            if INHERIT.get(k, 0) < v:
                INHERIT[k] = v
        for k, v in t.r.items():
            if INHERIT.get(k, 0) < v:
                INHERIT[k] = v


class Prog:
    ENG = ("pe", "act", "dve", "pool", "sp")

    def __init__(self, nc, es):
        self.nc = nc
        self.es = es
        self.ops = {e: [] for e in self.ENG}
        self.sems = {}
        self.cnt = {}
        for e in self.ENG:
            self.sems[e] = es.enter_context(nc.semaphore("s_" + e))
            self.cnt[e] = 0
        self.waited = {e: {} for e in self.ENG}
        self.pending_nosig = {e: False for e in self.ENG}

    def new_dma_sem(self, name):
        key = "dma_" + name
        self.sems[key] = self.es.enter_context(self.nc.semaphore(key))
        self.cnt[key] = 0
        return key

    def _deps(self, eng, reads, writes):
        deps = {}

        def add(d):
            if d is None:
                return
            k, v = d
            if deps.get(k, 0) < v:
                deps[k] = v
        for t in reads:
            add(t.w)
            if t.excl:
                for k, v in t.r.items():
                    if k != eng:
                        add((k, v))
        for t in writes:
            add(t.w)
            for k, v in t.r.items():
                add((k, v))
        waits = []
        wd = self.waited[eng]
        for k, v in deps.items():
            if k == "pe" and eng == "pe":
                continue
            if wd.get(k, 0) < v:
                wd[k] = v
                waits.append((k, v))
        return waits

    def op(self, eng, fn, reads=(), writes=(), sig=True):
        assert sig or eng == "pe"
        waits = self._deps(eng, reads, writes)
        if sig:
            self.cnt[eng] += 1
            val = self.cnt[eng]
        else:
            val = self.cnt[eng] + 1
        self.pending_nosig[eng] = not sig
        self.ops[eng].append((waits, fn, (eng, 1) if sig else None))
        for t in reads:
            if t.r.get(eng, 0) < val:
                t.r[eng] = val
        for t in writes:
            t.w = (eng, val)
            t.r = {}

    def dma(self, q, semkey, fn, reads=(), writes=()):
        assert q == "sp"
        waits = self._deps(q, reads, writes)
        self.cnt[semkey] += 1
        val = self.cnt[semkey]
        self.ops[q].append((waits, fn, (semkey, 16)))
        for t in reads:
            if t.r.get(semkey, 0) < val:
                t.r[semkey] = val
        for t in writes:
            t.w = (semkey, val)
            t.r = {}

    def final_wait(self, eng, semkey):
        if self.cnt[semkey] > 0:
            self.ops[eng].append(([(semkey, self.cnt[semkey])], None, None))

    def emit(self):
        nc = self.nc
        for e in self.ENG:
            assert not self.pending_nosig[e], e
        actual = {k: [0] for k in self.sems if k.startswith("dma_")}
        self.split_log = []
        with nc.Block() as block:
            def run(engname):
                def body(eng):
                    for waits, fn, inc in self.ops[engname]:
                        for k, v in waits:
                            eng.wait_ge(self.sems[k], actual[k][v] if k in actual else v)
                        if fn is not None:
                            n0 = nc.n_instructions()
                            ins = fn(eng)
                            if inc is not None:
                                ins.then_inc(self.sems[inc[0]], inc[1])
                                if inc[0] in actual:
                                    k_ = max(1, nc.n_instructions() - n0)
                                    if k_ != 1:
                                        self.split_log.append((inc[0], len(actual[inc[0]]), k_))
                                    actual[inc[0]].append(actual[inc[0]][-1] + 16 * k_)
                return body
            block.sync(run("sp"))
            block.tensor(run("pe"))
            block.scalar(run("act"))
            block.vector(run("dve"))
            block.gpsimd(run("pool"))


class Ring:
    def __init__(self, bufs, tks=None):
        self.bufs = bufs
        self.tks = tks if tks is not None else [Tk() for _ in bufs]
        self.i = 0

    def next(self):
        b, t = self.bufs[self.i], self.tks[self.i]
        self.i = (self.i + 1) % len(self.bufs)
        return b, t


def host_consts():
    c = {}
    i = np.arange(128)
    c["ident"] = np.eye(128, dtype=np.float32)
    c["ones"] = np.ones((128, 128), dtype=np.float32)
    bo = np.zeros((128, 128), np.float32); bo[:64, :64] = 1; bo[64:, 64:] = 1
    c["blockones"] = bo
    rm = np.zeros((128, 128), np.float32)
    for p in range(128):
        if p % 64 < 32:
            rm[p + 32, p] = -1.0
        else:
            rm[p - 32, p] = 1.0
    c["rot"] = rm
    fr = np.zeros((128, 128), np.float32)
    fr[:, 0] = (10000.0 ** (-(np.arange(128) % 32).astype(np.float32) / 32.0)).astype(np.float32)
    c["freq"] = fr
    c["trimask"] = (i[:, None] <= i[None, :]).astype(np.float32)
    c["mnegL"] = np.where(i[:, None] >= i[None, :], 0.0, -30000.0).astype(np.float32)
    c["mnegU"] = np.ascontiguousarray(c["mnegL"].T)
    c["strictL"] = (i[:, None] > i[None, :]).astype(np.float32)
    c["strictU"] = np.ascontiguousarray(c["strictL"].T)
    return c


CONST_NAMES = ["ident", "ones", "blockones", "rot", "freq", "trimask", "mnegL", "mnegU", "strictL", "strictU"]


def build_program(stages=("gdn", "mlp0", "attn", "mlp1"), debug=False):
    INHERIT.clear()
    nc = bass.Bass("TRN2", target_bir_lowering=False)
    dr = {}

    def din(name, shape, dt=F32):
        dr[name] = nc.dram_tensor(name, list(shape), dt, kind="ExternalInput").ap()
        return dr[name]
    x_d = din("x", [S, D])
    pos_d = din("positions", [1, S], I32)
    din("a_norm", [1, D]); din("a_w_in", [D, 4112]); din("a_conv_w", [4, 3072]); din("a_a_log", [1, 8])
    din("a_dt_bias", [1, 8]); din("a_out_norm", [1, 128]); din("a_w_out", [D, D])
    din("kv_norm", [1, D]); din("w_kv", [D, 2048]); din("k_norm", [1, 64])
    din("b_norm", [1, D]); din("b_w_q", [D, D]); din("b_q_norm", [1, 64]); din("b_lambda", [4, 64])
    din("b_sub_norm", [1, 128]); din("b_w_out", [D, D])
    din("mlp_norm", [2, D]); din("mlp_w1", [2 * D, DFF]); din("mlp_w2", [2 * DFF, D])
    for n in CONST_NAMES:
        din("c_" + n, [128, 128])
    out_d = nc.dram_tensor("out", [S, D], F32, kind="ExternalOutput").ap()

    with ExitStack() as es:
        P = Prog(nc, es)

        def sb(name, shape, dt=F32):
            return es.enter_context(nc.sbuf_tensor(name, list(shape), dt))

        xT = sb("xT", [128, 8, S])
        xT_tk = [[Tk() for _ in range(NB)] for _ in range(8)]
        xnT = sb("xnT", [128, 8, S], BF16)
        xnT_tk = [Tk() for _ in range(NB)]
        banks = [es.enter_context(nc.psum_tensor("bank%d" % i, [128, 512], F32)) for i in range(8)]
        PS = Ring(banks, [Tk("bank%d" % i, excl=True) for i in range(8)])
        stage = Ring([sb("stage%d" % i, [128, 2048]) for i in range(2)])
        stage_sem = [P.new_dma_sem("stage%d" % i) for i in range(2)]
        tmpf = Ring([sb("tmpf%d" % i, [128, 512]) for i in range(3)])
        rstd_r = Ring([sb("rstd%d" % i, [128, 512]) for i in range(1)])
        cst = {}
        cst_tk = Tk()
        csem = P.new_dma_sem("const")
        for n in CONST_NAMES:
            cst[n] = sb("cs_" + n, [128, 128])
            P.dma("sp", csem, (lambda n=n: lambda e: e.dma_start(out=cst[n][:], in_=dr["c_" + n]))(), writes=[cst_tk])
        gains = sb("gains", [128, 5, 8])
        gain_src = [dr["a_norm"][0, :], dr["mlp_norm"][0, :], dr["kv_norm"][0, :], dr["b_norm"][0, :], dr["mlp_norm"][1, :]]
        for gi, src in enumerate(gain_src):
            P.dma("sp", csem, (lambda gi=gi, src=src: lambda e: e.dma_start(
                out=gains[:, gi, :], in_=src.rearrange("(c p) -> p c", p=128), allow_slow_non_contiguous=True))(), writes=[cst_tk])
        epsb = sb("epsb", [128, 1])
        P.op("pool", lambda e: e.memset(epsb[:], EPS), writes=[cst_tk])

        evac_flip = [0]

        def evac_copy(out_ap, in_ap, reads, writes):
            evac_flip[0] ^= 1
            if evac_flip[0]:
                P.op("act", lambda e: e.activation(out=out_ap, in_=in_ap, func=AF.Copy), reads=reads, writes=writes)
            else:
                P.op("dve", lambda e: e.tensor_copy(out=out_ap, in_=in_ap), reads=reads, writes=writes)

        for t in range(NT):
            st, st_tk = stage.next()
            si = (stage.i - 1) % 2
            P.dma("sp", stage_sem[si], (lambda st=st, t=t: lambda e: e.dma_start(
                out=st[:, 0:1024], in_=x_d[t * 128:(t + 1) * 128, :]))(), writes=[st_tk])
            for cg in range(2):
                ps, ps_tk = PS.next()
                for j in range(4):
                    c = cg * 4 + j
                    P.op("pe", (lambda ps=ps, st=st, c=c, j=j: lambda e: e.transpose(
                        ps[:, j * 128:(j + 1) * 128], st[:, c * 128:(c + 1) * 128], cst["ident"][:]))(),
                        reads=[st_tk, cst_tk], writes=[ps_tk], sig=(j == 3))
                tb = t // 4
                evac_copy(xT[:, cg * 4:(cg + 1) * 4, t * 128:(t + 1) * 128],
                          ps[:].rearrange("p (j n) -> p j n", j=4),
                          reads=[ps_tk], writes=[xT_tk[c][tb] for c in range(cg * 4, cg * 4 + 4)])

        def rms_to_xnT():
            for tb in range(NB):
                blk = slice(tb * 512, (tb + 1) * 512)
                ps, ps_tk = PS.next()
                for c in range(8):
                    sq, sq_tk = tmpf.next()
                    P.op("act", (lambda sq=sq, c=c, blk=blk: lambda e: e.activation(out=sq[:], in_=xT[:, c, blk], func=AF.Square))(),
                         reads=[xT_tk[c][tb]], writes=[sq_tk])
                    P.op("pe", (lambda ps=ps, sq=sq, c=c: lambda e: e.matmul(ps[:], cst["ones"][:], sq[:], start=(c == 0), stop=(c == 7)))(),
                         reads=[sq_tk, cst_tk], writes=[ps_tk], sig=True)
                rs, rs_tk = tmpf.next()
                P.op("act", (lambda rs=rs, ps=ps: lambda e: e.activation(out=rs[:], in_=ps[:], func=AF.Ln, bias=epsb[:, 0:1], scale=1.0 / D))(),
                     reads=[ps_tk, cst_tk], writes=[rs_tk])
                rstd_bc, rstd_tk = rstd_r.next()
                P.op("act", (lambda rs=rs, rstd_bc=rstd_bc: lambda e: e.activation(out=rstd_bc[:], in_=rs[:], func=AF.Exp, scale=-0.5))(),
                     reads=[rs_tk], writes=[rstd_tk])
                for c in range(8):
                    eng = "dve" if c % 2 == 0 else "pool"
                    P.op(eng, (lambda c=c, blk=blk, rstd_bc=rstd_bc: lambda e: e.tensor_tensor(out=xnT[:, c, blk], in0=xT[:, c, blk], in1=rstd_bc[:], op=ALU.mult))(),
                         reads=[xT_tk[c][tb], rstd_tk], writes=[xnT_tk[tb]])

        wsem = [P.new_dma_sem("w%d" % i) for i in range(2)]

        def load_w(dst, dst_tk, src, kc, ncols, gain_idx, gain_c0=0, colgain=None, colgain_tk=None):
            per = max(1, 2048 // ncols)
            c0 = 0
            while c0 < kc:
                n = min(per, kc - c0)
                st, st_tk = stage.next()
                si = (stage.i - 1) % 2
                stv = st[:, 0:n * ncols].rearrange("p (c n) -> p c n", c=n)
                srcv = src[c0 * 128:(c0 + n) * 128, :].rearrange("(c p) n -> p c n", p=128)
                P.dma("sp", stage_sem[si], (lambda stv=stv, srcv=srcv: lambda e: e.dma_start(out=stv, in_=srcv))(), writes=[st_tk])
                dv = dst[:, c0:c0 + n, :]
                if gain_idx is None:
                    P.op("pool", (lambda dv=dv, stv=stv: lambda e: e.tensor_copy(out=dv, in_=stv))(), reads=[st_tk], writes=[dst_tk])
                elif colgain is None:
                    gv = gains[:, gain_idx, gain_c0 + c0:gain_c0 + c0 + n].unsqueeze(2).to_broadcast([128, n, ncols])
                    P.op("pool", (lambda dv=dv, stv=stv, gv=gv: lambda e: e.tensor_tensor(out=dv, in0=stv, in1=gv, op=ALU.mult))(),
                         reads=[st_tk, cst_tk], writes=[dst_tk])
                else:
                    gv = gains[:, gain_idx, gain_c0 + c0:gain_c0 + c0 + n].unsqueeze(2).to_broadcast([128, n, ncols])
                    cv = colgain.unsqueeze(1).to_broadcast([128, n, ncols])
                    P.op("pool", (lambda stv=stv, gv=gv: lambda e: e.tensor_tensor(out=stv, in0=stv, in1=gv, op=ALU.mult))(),
                         reads=[st_tk, cst_tk], writes=[st_tk])
                    P.op("pool", (lambda dv=dv, stv=stv, cv=cv: lambda e: e.tensor_tensor(out=dv, in0=stv, in1=cv, op=ALU.mult))(),
                         reads=[st_tk, colgain_tk], writes=[dst_tk])
                c0 += n

        def mlp(layer, gain_idx):
            with ExitStack() as ph:
                mlp_body(layer, gain_idx, ph)

        def mlp_body(layer, gain_idx, ph):
            FG = 512
            psb = lambda name, shape, dt=F32: ph.enter_context(nc.sbuf_tensor(name, list(shape), dt))
            w1r = Ring([psb("w1g%d_%d" % (layer, i), [128, 8, FG], BF16) for i in range(2)])
            w2r = Ring([psb("w2g%d_%d" % (layer, i), [128, FG // 128, D], BF16) for i in range(2)])
            hr = Ring([psb("hT%d_%d" % (layer, i), [128, FG // 128, 512], BF16) for i in range(2)])
            w1_d = dr["mlp_w1"][layer * D:(layer + 1) * D, :]
            w2_d = dr["mlp_w2"][layer * DFF:(layer + 1) * DFF, :]
            for g in range(DFF // FG):
                w1, w1_tk = w1r.next()
                w2, w2_tk = w2r.next()
                load_w(w1, w1_tk, w1_d[:, g * FG:(g + 1) * FG], 8, FG, gain_idx)
                load_w(w2, w2_tk, w2_d[g * FG:(g + 1) * FG, :], FG // 128, D, None)
                for tb in range(NB):
                    blk = slice(tb * 512, (tb + 1) * 512)
                    hT, hT_tk = hr.next()
                    for fc in range(FG // 128):
                        ps, ps_tk = PS.next()
                        for c in range(8):
                            P.op("pe", (lambda ps=ps, w1=w1, c=c, fc=fc, blk=blk: lambda e: e.matmul(
                                ps[:], w1[:, c, fc * 128:(fc + 1) * 128], xnT[:, c, blk], start=(c == 0), stop=(c == 7)))(),
                                reads=[w1_tk, xnT_tk[tb]], writes=[ps_tk], sig=(c == 7))
                        h1, h1_tk = tmpf.next()
                        P.op("act", (lambda h1=h1, ps=ps: lambda e: e.activation(out=h1[:], in_=ps[:], func=AF.Relu))(),
                             reads=[ps_tk], writes=[h1_tk])
                        P.op("pool", (lambda h1=h1, hT=hT, fc=fc: lambda e: e.tensor_tensor(out=hT[:, fc, :], in0=h1[:], in1=h1[:], op=ALU.mult))(),
                             reads=[h1_tk], writes=[hT_tk])
                    for n in range(8):
                        ps, ps_tk = PS.next()
                        for fc in range(FG // 128):
                            P.op("pe", (lambda ps=ps, w2=w2, n=n, fc=fc, hT=hT: lambda e: e.matmul(
                                ps[:], w2[:, fc, n * 128:(n + 1) * 128], hT[:, fc, :], start=(fc == 0), stop=(fc == FG // 128 - 1)))(),
                                reads=[w2_tk, hT_tk], writes=[ps_tk], sig=(fc == FG // 128 - 1))
                        P.op("dve", (lambda ps=ps, n=n, blk=blk: lambda e: e.tensor_tensor(out=xT[:, n, blk], in0=ps[:], in1=xT[:, n, blk], op=ALU.add))(),
                             reads=[ps_tk], writes=[xT_tk[n][tb]])
            retire(w1r.tks + w2r.tks + hr.tks)


        def gdn():
            with ExitStack() as ph:
                gdn_body(ph)

        def gdn_body(ph):
            psb = lambda name, shape, dt=F32: ph.enter_context(nc.sbuf_tensor(name, list(shape), dt))
            mytks = []

            def tk():
                t = Tk(); mytks.append(t); return t
            PSr = Ring(banks, PS.tks)
            ident = cst["ident"]; onesf = cst["ones"]
            ident_bf = psb("g_identbf", [128, 128], BF16); cb_tk = tk()
            P.op("dve", lambda e: e.tensor_copy(out=ident_bf[:], in_=ident[:]), reads=[cst_tk], writes=[cb_tk])
            one_b = psb("g_oneb", [128, 1]); P.op("pool", lambda e: e.memset(one_b[:], 1.0), writes=[cb_tk])
            s1 = P.new_dma_sem("g_p1"); s2 = P.new_dma_sem("g_p2"); s3 = P.new_dma_sem("g_p3"); s4 = P.new_dma_sem("g_p4")
            alog = psb("g_alog", [128, 8]); dtb = psb("g_dtb", [128, 8]); onw = psb("g_onw", [128, 128]); cw = psb("g_cw", [128, 24, 4])
            alog_tk = tk(); dtb_tk = tk(); onw_tk = tk(); cw_tk = tk()
            P.dma("sp", s1, lambda e: e.dma_start(out=alog[:], in_=dr["a_a_log"][0, :].partition_broadcast(128)), writes=[alog_tk])
            P.dma("sp", s2, lambda e: e.dma_start(out=dtb[:], in_=dr["a_dt_bias"][0, :].partition_broadcast(128)), writes=[dtb_tk])
            P.dma("sp", s3, lambda e: e.dma_start(out=onw[:], in_=dr["a_out_norm"][0, :].partition_broadcast(128)), writes=[onw_tk])
            for j in range(4):
                P.dma("sp", s4, (lambda j=j: lambda e: e.dma_start(out=cw[:, :, j], in_=dr["a_conv_w"][j, :].rearrange("(c p) -> p c", p=128),
                      allow_slow_non_contiguous=True))(), writes=[cw_tk])
            P.op("act", lambda e: e.activation(out=alog[:], in_=alog[:], func=AF.Exp), reads=[alog_tk], writes=[alog_tk])
            P.op("dve", lambda e: e.tensor_scalar(out=alog[:], in0=alog[:], scalar1=-1.0, scalar2=None, op0=ALU.mult), reads=[alog_tk], writes=[alog_tk])
            wbg = psb("g_wbg", [128, 8, 16], BF16); wbg_tk = tk()
            load_w(wbg, wbg_tk, dr["a_w_in"][:, 4096:4112], 8, 16, 0)
            bg = psb("g_bg", [128, NT, 16]); bg_tk = tk()
            ps, ps_tk = PSr.next()
            for t in range(NT):
                for c in range(8):
                    P.op("pe", (lambda ps=ps, t=t, c=c: lambda e: e.matmul(ps[:, t * 16:(t + 1) * 16], xnT[:, c, t * 128:(t + 1) * 128], wbg[:, c, :],
                         start=(c == 0), stop=(c == 7)))(), reads=[xnT_tk[t // 4], wbg_tk], writes=[ps_tk], sig=(t == NT - 1 and c == 7))
            P.op("dve", (lambda ps=ps: lambda e: e.tensor_copy(out=bg[:].rearrange("p t k -> p (t k)"), in_=ps[:, 0:256]))(), reads=[ps_tk], writes=[bg_tk])
            beta = psb("g_beta", [128, NT, 8]); gg = psb("g_g", [128, NT, 8]); par_tk = tk()
            P.op("act", lambda e: e.activation(out=beta[:], in_=bg[:, :, 0:8], func=AF.Sigmoid), reads=[bg_tk], writes=[par_tk])
            P.op("dve", lambda e: e.tensor_tensor(out=gg[:], in0=bg[:, :, 8:16], in1=dtb[:].unsqueeze(1).to_broadcast([128, NT, 8]), op=ALU.add),
                 reads=[bg_tk, dtb_tk, par_tk], writes=[par_tk])
            P.op("act", lambda e: e.activation(out=gg[:], in_=gg[:], func=AF.Exp), reads=[par_tk], writes=[par_tk])
            P.op("act", lambda e: e.activation(out=gg[:], in_=gg[:], func=AF.Ln, bias=one_b[:, 0:1], scale=1.0), reads=[par_tk, cb_tk], writes=[par_tk])
            P.op("dve", lambda e: e.tensor_tensor(out=gg[:], in0=gg[:], in1=alog[:].unsqueeze(1).to_broadcast([128, NT, 8]), op=ALU.mult),
                 reads=[par_tk, alog_tk], writes=[par_tk])
            gc = psb("g_gc", [128, NT, 8]); ngc = psb("g_ngc", [128, NT, 8]); bgam = psb("g_bgam", [128, NT, 8])
            kds = psb("g_kds", [128, NT, 8]); cd = psb("g_cd", [128, NT, 8])
            ps, ps_tk = PSr.next()
            P.op("pe", (lambda ps=ps: lambda e: e.matmul(ps[:, 0:128], cst["trimask"][:], gg[:].rearrange("p t k -> p (t k)"), start=True, stop=True))(),
                 reads=[par_tk, cst_tk], writes=[ps_tk], sig=False)
            P.op("pe", (lambda ps=ps: lambda e: e.matmul(ps[:, 128:256], onesf[:], gg[:].rearrange("p t k -> p (t k)"), start=True, stop=True))(),
                 reads=[par_tk, cst_tk], writes=[ps_tk])
            fl = lambda a: a[:].rearrange("p t k -> p (t k)")
            P.op("dve", (lambda ps=ps: lambda e: e.tensor_copy(out=fl(gc), in_=ps[:, 0:128]))(), reads=[ps_tk], writes=[par_tk])
            P.op("dve", (lambda ps=ps: lambda e: e.tensor_scalar(out=fl(ngc), in0=ps[:, 0:128], scalar1=-1.0, scalar2=None, op0=ALU.mult))(), reads=[ps_tk, par_tk], writes=[par_tk])
            P.op("act", (lambda ps=ps: lambda e: e.activation(out=fl(cd), in_=ps[:, 128:256], func=AF.Exp))(), reads=[ps_tk, par_tk], writes=[par_tk])
            P.op("dve", (lambda ps=ps: lambda e: e.tensor_tensor(out=fl(kds), in0=ps[:, 128:256], in1=fl(gc), op=ALU.subtract))(), reads=[ps_tk, par_tk], writes=[par_tk])
            P.op("act", lambda e: e.activation(out=fl(kds), in_=fl(kds), func=AF.Exp), reads=[par_tk], writes=[par_tk])
            P.op("act", lambda e: e.activation(out=fl(bgam), in_=fl(gc), func=AF.Exp), reads=[par_tk], writes=[par_tk])
            P.op("dve", lambda e: e.tensor_tensor(out=fl(bgam), in0=fl(bgam), in1=fl(beta), op=ALU.mult), reads=[par_tk], writes=[par_tk])
            w_r = [Ring([psb("g_w%d_%d" % (k, i), [128, 8, 128], BF16) for i in range(1)]) for k in range(4)]
            wo_r = Ring([psb("g_wo%d" % i, [128, 1, D], BF16) for i in range(1)])
            prj = psb("g_prj", [128, 3 + S]); prj_tk = tk()
            P.op("pool", lambda e: e.memset(prj[:, 0:3], 0.0), writes=[prj_tk])
            cacc = psb("g_cacc", [128, S]); cacc_tk = tk()
            qk_f = cacc; qkf_tk = cacc_tk
            qhT = psb("g_qhT", [128, S], BF16); khT = psb("g_khT", [128, S], BF16); vT = cacc
            qhT_tk = tk(); khT_tk = tk(); vT_tk = cacc_tk
            zs = psb("g_zs", [128, NT, 128], BF16); zs_tk = tk()
            kbg = psb("g_kbg", [128, NT, 128], BF16); kdec = psb("g_kdec", [128, NT, 128], BF16); bv = psb("g_bv", [128, NT, 128], BF16)
            tok_tk = [tk() for _ in range(NT)]
            u_sb = prj[:, 3:3 + S].rearrange("p (t n) -> p t n", t=NT); wT = psb("g_wT", [128, NT, 128], BF16); qdT = psb("g_qdT", [128, NT, 128], BF16); inT = psb("g_inT", [128, NT, 128], BF16)
            prep_tk = [tk() for _ in range(NT)]
            goT = khT; goT_tk = [khT_tk for _ in range(NB)]
            S_f = psb("g_Sf", [128, 128]); S_b = psb("g_Sb", [128, 128], BF16); S_tk = tk()
            GT = 2
            Pb = [[psb("g_P%d_%d" % (a, i), [128, 128]) for i in range(GT)] for a in range(2)]
            PTb = [[psb("g_PT%d_%d" % (a, i), [128, 128]) for i in range(GT)] for a in range(2)]
            Rb = [psb("g_R%d" % i, [128, 128]) for i in range(GT)]
            Rbf = [psb("g_Rbf%d" % i, [128, 128], BF16) for i in range(GT)]
            P_tk = [[tk() for i in range(GT)] for a in range(2)]
            PT_tk = [[tk() for i in range(GT)] for a in range(2)]
            R_tk = [tk() for i in range(GT)]
            sm = Ring([psb("g_sm%d" % i, [128, 256]) for i in range(4)])
            smb = Ring([psb("g_smb%d" % i, [128, 128], BF16) for i in range(3)])
            st1 = Ring([psb("g_st%d" % i, [128, 4]) for i in range(3)])

            for h in range(8):
                ws = []
                for k in range(4):
                    w, w_tk = w_r[k].next()
                    load_w(w, w_tk, dr["a_w_in"][:, k * 1024 + h * 128:k * 1024 + (h + 1) * 128], 8, 128, 0)
                    ws.append((w, w_tk))
                wo, wo_tk = wo_r.next()
                load_w(wo, wo_tk, dr["a_w_out"][h * 128:(h + 1) * 128, :], 1, D, None)
                for sec in range(3):
                    w, w_tk = ws[sec]
                    ch = sec * 8 + h
                    for tb in range(NB):
                        blk = slice(tb * 512, (tb + 1) * 512)
                        ps, ps_tk = PSr.next()
                        for c in range(8):
                            P.op("pe", (lambda ps=ps, w=w, c=c, blk=blk: lambda e: e.matmul(ps[:], w[:, c, :], xnT[:, c, blk], start=(c == 0), stop=(c == 7)))(),
                                 reads=[w_tk, xnT_tk[tb]], writes=[ps_tk], sig=(c == 7))
                        evac_copy(prj[:, 3 + tb * 512:3 + (tb + 1) * 512], ps[:], reads=[ps_tk], writes=[prj_tk])
                    P.op("dve", (lambda ch=ch: lambda e: e.tensor_scalar(out=cacc[:], in0=prj[:, 0:S], scalar1=cw[:, ch, 0:1], scalar2=None, op0=ALU.mult))(),
                         reads=[prj_tk, cw_tk], writes=[cacc_tk])
                    for j in range(1, 4):
                        P.op("dve", (lambda ch=ch, j=j: lambda e: e.scalar_tensor_tensor(out=cacc[:], in0=prj[:, j:j + S], scalar=cw[:, ch, j:j + 1], in1=cacc[:], op0=ALU.mult, op1=ALU.add))(),
                             reads=[prj_tk, cw_tk, cacc_tk], writes=[cacc_tk])
                    if sec == 2:
                        P.op("act", lambda e: e.activation(out=cacc[:], in_=cacc[:], func=AF.Silu), reads=[cacc_tk], writes=[cacc_tk])
                    else:
                        dst, dst_tk = (qhT, qhT_tk) if sec == 0 else (khT, khT_tk)
                        P.op("act", lambda e: e.activation(out=cacc[:], in_=cacc[:], func=AF.Silu), reads=[cacc_tk], writes=[cacc_tk])
                        for tb in range(NB):
                            blk = slice(tb * 512, (tb + 1) * 512)
                            sq, sq_tk = tmpf.next()
                            P.op("act", (lambda sq=sq, blk=blk: lambda e: e.activation(out=sq[:], in_=qk_f[:, blk], func=AF.Square))(), reads=[qkf_tk], writes=[sq_tk])
                            ps2, ps2_tk = PSr.next()
                            P.op("pe", (lambda ps2=ps2, sq=sq: lambda e: e.matmul(ps2[:], onesf[:], sq[:], start=True, stop=True))(), reads=[sq_tk, cst_tk], writes=[ps2_tk])
                            rs, rs_tk = tmpf.next()
                            P.op("act", (lambda rs=rs, ps2=ps2: lambda e: e.activation(out=rs[:], in_=ps2[:], func=AF.Ln, bias=epsb[:, 0:1], scale=1.0))(),
                                 reads=[ps2_tk, cst_tk], writes=[rs_tk])
                            rs0, rs0_tk = rs, rs_tk
                            rs, rs_tk = tmpf.next()
                            P.op("act", (lambda rs=rs, rs0=rs0: lambda e: e.activation(out=rs[:], in_=rs0[:], func=AF.Exp, scale=-0.5))(), reads=[rs0_tk], writes=[rs_tk])
                            sc = float(128 ** -0.5) if sec == 0 else 1.0
                            P.op("dve", (lambda dst=dst, rs=rs, blk=blk, sc=sc: lambda e: e.scalar_tensor_tensor(out=dst[:, blk], in0=qk_f[:, blk], scalar=sc, in1=rs[:], op0=ALU.mult, op1=ALU.mult))(),
                                 reads=[qkf_tk, rs_tk], writes=[dst_tk])
                w, w_tk = ws[3]
                for tg in range(4):
                    ps, ps_tk = PSr.next()
                    for j in range(4):
                        t = tg * 4 + j
                        for c in range(8):
                            P.op("pe", (lambda ps=ps, j=j, t=t, c=c, w=w: lambda e: e.matmul(ps[:, j * 128:(j + 1) * 128], xnT[:, c, t * 128:(t + 1) * 128], w[:, c, :],
                                 start=(c == 0), stop=(c == 7)))(), reads=[xnT_tk[tg], w_tk], writes=[ps_tk], sig=(j == 3 and c == 7))
                    zf, zf_tk = tmpf.next()
                    P.op("act", (lambda zf=zf, ps=ps: lambda e: e.activation(out=zf[:], in_=ps[:], func=AF.Silu))(), reads=[ps_tk], writes=[zf_tk])
                    P.op("pool", (lambda zf=zf, tg=tg: lambda e: e.tensor_tensor(out=zs[:, tg * 4:(tg + 1) * 4, :], in0=zf[:].rearrange("p (j n) -> p j n", j=4),
                         in1=onw[:].unsqueeze(1).to_broadcast([128, 4, 128]), op=ALU.mult))(), reads=[zf_tk, onw_tk], writes=[zs_tk])
                for t in range(NT):
                    ts_ = slice(t * 128, (t + 1) * 128)
                    ps, ps_tk = PSr.next()
                    P.op("pe", (lambda ps=ps, ts_=ts_: lambda e: e.matmul(ps[:, 0:128], khT[:, ts_], ident_bf[:], start=True, stop=True))(), reads=[khT_tk, cb_tk], writes=[ps_tk], sig=False)
                    P.op("pe", (lambda ps=ps, ts_=ts_: lambda e: e.matmul(ps[:, 128:256], vT[:, ts_], ident[:], start=True, stop=True))(), reads=[vT_tk, cst_tk], writes=[ps_tk])
                    P.op("act", (lambda ps=ps, t=t, h=h: lambda e: e.activation(out=kbg[:, t, :], in_=ps[:, 0:128], func=AF.Copy, scale=bgam[:, t, h:h + 1]))(), reads=[ps_tk, par_tk], writes=[tok_tk[t]])
                    P.op("dve", (lambda ps=ps, t=t, h=h: lambda e: e.tensor_scalar(out=kdec[:, t, :], in0=ps[:, 0:128], scalar1=kds[:, t, h:h + 1], scalar2=None, op0=ALU.mult))(), reads=[ps_tk, par_tk], writes=[tok_tk[t]])
                    P.op("dve", (lambda ps=ps, t=t, h=h: lambda e: e.tensor_scalar(out=bv[:, t, :], in0=ps[:, 128:256], scalar1=beta[:, t, h:h + 1], scalar2=None, op0=ALU.mult))(), reads=[ps_tk, par_tk], writes=[tok_tk[t]])
                for g0 in range(0, NT, GT):
                    tiles = list(range(g0, g0 + GT))
                    for i, t in enumerate(tiles):
                        ts_ = slice(t * 128, (t + 1) * 128)
                        z, z_tk = sm.next()
                        P.op("pool", (lambda z=z, t=t, h=h: lambda e: e.tensor_scalar(out=z[:, 0:128], in0=ident[:], scalar1=gc[:, t, h:h + 1], scalar2=None, op0=ALU.mult))(), reads=[cst_tk, par_tk], writes=[z_tk])
                        P.op("pool", (lambda z=z, t=t, h=h: lambda e: e.tensor_scalar(out=z[:, 128:256], in0=ident[:], scalar1=beta[:, t, h:h + 1], scalar2=None, op0=ALU.mult))(), reads=[cst_tk, par_tk, z_tk], writes=[z_tk])
                        bc, bc_tk = PSr.next()
                        P.op("pe", (lambda bc=bc, z=z: lambda e: e.matmul(bc[:, 0:256], onesf[:], z[:], start=True, stop=True))(), reads=[z_tk, cst_tk], writes=[bc_tk])
                        d1, d1_tk = sm.next()
                        P.op("dve", (lambda d1=d1, bc=bc: lambda e: e.scalar_tensor_tensor(out=d1[:, 0:128], in0=bc[:, 0:128], scalar=-1.0, in1=cst["mnegL"][:], op0=ALU.mult, op1=ALU.add))(), reads=[bc_tk, cst_tk], writes=[d1_tk])
                        P.op("dve", (lambda d1=d1, bc=bc: lambda e: e.tensor_tensor(out=d1[:, 128:256], in0=bc[:, 0:128], in1=cst["mnegU"][:], op=ALU.add))(), reads=[bc_tk, cst_tk, d1_tk], writes=[d1_tk])
                        P.op("act", (lambda d1=d1, t=t, h=h: lambda e: e.activation(out=d1[:, 0:128], in_=d1[:, 0:128], func=AF.Exp, bias=gc[:, t, h:h + 1], scale=1.0))(), reads=[d1_tk, par_tk], writes=[d1_tk])
                        P.op("act", (lambda d1=d1, t=t, h=h: lambda e: e.activation(out=d1[:, 128:256], in_=d1[:, 128:256], func=AF.Exp, bias=ngc[:, t, h:h + 1], scale=1.0))(), reads=[d1_tk, par_tk], writes=[d1_tk])
                        d2, d2_tk = sm.next()
                        P.op("act", (lambda d2=d2, bc=bc: lambda e: e.activation(out=d2[:, 0:128], in_=bc[:, 0:128], func=AF.Exp))(), reads=[bc_tk], writes=[d2_tk])
                        P.op("act", (lambda d2=d2, bc=bc: lambda e: e.activation(out=d2[:, 128:256], in_=bc[:, 128:256], func=AF.Copy))(), reads=[bc_tk, d2_tk], writes=[d2_tk])
                        gq, gq_tk = PSr.next()
                        P.op("pe", (lambda gq=gq, ts_=ts_: lambda e: e.matmul(gq[:, 0:128], khT[:, ts_], khT[:, ts_], start=True, stop=True))(), reads=[khT_tk], writes=[gq_tk], sig=False)
                        P.op("pe", (lambda gq=gq, ts_=ts_: lambda e: e.matmul(gq[:, 128:256], khT[:, ts_], qhT[:, ts_], start=True, stop=True))(), reads=[khT_tk, qhT_tk], writes=[gq_tk])
                        P.op("dve", (lambda gq=gq, d1=d1, t=t: lambda e: e.tensor_tensor(out=inT[:, t, :], in0=gq[:, 128:256], in1=d1[:, 128:256], op=ALU.mult))(), reads=[gq_tk, d1_tk], writes=[prep_tk[t]])
                        P.op("pool", (lambda d2=d2, t=t, ts_=ts_: lambda e: e.tensor_tensor(out=qdT[:, t, :], in0=qhT[:, ts_], in1=d2[:, 0:128], op=ALU.mult))(), reads=[qhT_tk, d2_tk], writes=[prep_tk[t]])
                        P.op("pool", (lambda d1=d1: lambda e: e.tensor_tensor(out=d1[:, 0:128], in0=d1[:, 0:128], in1=cst["strictL"][:], op=ALU.mult))(), reads=[d1_tk, cst_tk, prep_tk[t]], writes=[d1_tk])
                        P.op("pool", (lambda d1=d1: lambda e: e.tensor_tensor(out=d1[:, 128:256], in0=d1[:, 128:256], in1=cst["strictU"][:], op=ALU.mult))(), reads=[d1_tk, cst_tk], writes=[d1_tk])
                        P.op("dve", (lambda gq=gq, d1=d1, i=i, t=t, h=h: lambda e: e.scalar_tensor_tensor(out=Pb[0][i][:], in0=gq[:, 0:128], scalar=beta[:, t, h:h + 1], in1=d1[:, 0:128], op0=ALU.mult, op1=ALU.mult))(),
                             reads=[gq_tk, d1_tk, par_tk], writes=[P_tk[0][i]])
                        t1, t1_tk = sm.next()
                        P.op("dve", (lambda gq=gq, d2=d2, t1=t1: lambda e: e.tensor_tensor(out=t1[:, 0:128], in0=gq[:, 0:128], in1=d2[:, 128:256], op=ALU.mult))(), reads=[gq_tk, d2_tk], writes=[t1_tk])
                        P.op("pool", (lambda t1=t1, d1=d1, i=i: lambda e: e.tensor_tensor(out=PTb[0][i][:], in0=t1[:, 0:128], in1=d1[:, 128:256], op=ALU.mult))(), reads=[t1_tk, d1_tk], writes=[PT_tk[0][i]])
                        P.op("pool", (lambda i=i: lambda e: e.tensor_tensor(out=Rb[i][:], in0=ident[:], in1=PTb[0][i][:], op=ALU.subtract))(), reads=[cst_tk, PT_tk[0][i]], writes=[R_tk[i]])
                    cur = 0
                    for lvl in range(1, 7):
                        nxt = 1 - cur
                        for i, t in enumerate(tiles):
                            pp, pp_tk = PSr.next()
                            P.op("pe", (lambda pp=pp, i=i, cur=cur: lambda e: e.matmul(pp[:, 0:128], PTb[cur][i][:], Pb[cur][i][:], start=True, stop=True))(),
                                 reads=[PT_tk[cur][i], P_tk[cur][i]], writes=[pp_tk], sig=(lvl == 6))
                            if lvl < 6:
                                P.op("pe", (lambda pp=pp, i=i, cur=cur: lambda e: e.matmul(pp[:, 128:256], Pb[cur][i][:], PTb[cur][i][:], start=True, stop=True))(),
                                     reads=[PT_tk[cur][i], P_tk[cur][i]], writes=[pp_tk])
                            P.op("act", (lambda pp=pp, i=i, nxt=nxt: lambda e: e.activation(out=Pb[nxt][i][:], in_=pp[:, 0:128], func=AF.Copy))(), reads=[pp_tk], writes=[P_tk[nxt][i]])
                            if lvl < 6:
                                P.op("dve", (lambda pp=pp, i=i, nxt=nxt: lambda e: e.tensor_copy(out=PTb[nxt][i][:], in_=pp[:, 128:256]))(), reads=[pp_tk], writes=[PT_tk[nxt][i]])
                        for i, t in enumerate(tiles):
                            ru, ru_tk = PSr.next()
                            P.op("pe", (lambda ru=ru, i=i, nxt=nxt: lambda e: e.matmul(ru[:, 0:128], Pb[nxt][i][:], Rb[i][:], start=True, stop=True))(),
                                 reads=[P_tk[nxt][i], R_tk[i]], writes=[ru_tk])
                            P.op("dve", (lambda ru=ru, i=i: lambda e: e.tensor_tensor(out=Rb[i][:], in0=ru[:, 0:128], in1=Rb[i][:], op=ALU.add))(), reads=[ru_tk, R_tk[i]], writes=[R_tk[i]])
                        cur = nxt
                    for i, t in enumerate(tiles):
                        P.op("act", (lambda i=i: lambda e: e.activation(out=Rbf[i][:], in_=Rb[i][:], func=AF.Copy))(), reads=[R_tk[i]], writes=[R_tk[i]])
                        uw, uw_tk = PSr.next()
                        P.op("pe", (lambda uw=uw, i=i, t=t: lambda e: e.matmul(uw[:, 0:128], Rbf[i][:], bv[:, t, :], start=True, stop=True))(), reads=[R_tk[i], tok_tk[t]], writes=[uw_tk], sig=False)
                        P.op("pe", (lambda uw=uw, i=i, t=t: lambda e: e.matmul(uw[:, 128:256], kbg[:, t, :], Rbf[i][:], start=True, stop=True))(), reads=[R_tk[i], tok_tk[t]], writes=[uw_tk])
                        P.op("act", (lambda uw=uw, t=t: lambda e: e.activation(out=u_sb[:, t, :], in_=uw[:, 0:128], func=AF.Copy))(), reads=[uw_tk], writes=[prep_tk[t], prj_tk])
                        P.op("dve", (lambda uw=uw, t=t: lambda e: e.tensor_copy(out=wT[:, t, :], in_=uw[:, 128:256]))(), reads=[uw_tk], writes=[prep_tk[t]])
                P.op("pool", lambda e: e.memset(S_f[:], 0.0), writes=[S_tk])
                P.op("pool", lambda e: e.memset(S_b[:], 0.0), reads=[S_tk], writes=[S_tk])
                for t in range(NT):
                    ts_ = slice(t * 128, (t + 1) * 128)
                    wsp, wsp_tk = PSr.next()
                    P.op("pe", (lambda wsp=wsp, t=t: lambda e: e.matmul(wsp[:, 0:128], wT[:, t, :], S_b[:], start=True, stop=True))(), reads=[prep_tk[t], S_tk], writes=[wsp_tk])
                    vn, vn_tk = smb.next()
                    P.op("dve", (lambda vn=vn, wsp=wsp, t=t: lambda e: e.tensor_tensor(out=vn[:], in0=u_sb[:, t, :], in1=wsp[:, 0:128], op=ALU.subtract))(), reads=[prep_tk[t], wsp_tk, prj_tk], writes=[vn_tk])
                    op_, op_tk = PSr.next()
                    P.op("pe", (lambda op_=op_, t=t: lambda e: e.matmul(op_[:, 0:128], qdT[:, t, :], S_b[:], start=True, stop=False))(), reads=[prep_tk[t], S_tk], writes=[op_tk], sig=False)
                    P.op("pe", (lambda op_=op_, t=t, vn=vn: lambda e: e.matmul(op_[:, 0:128], inT[:, t, :], vn[:], start=False, stop=True))(), reads=[prep_tk[t], vn_tk], writes=[op_tk], sig=False)
                    P.op("pe", (lambda op_=op_, t=t, vn=vn: lambda e: e.matmul(op_[:, 128:256], kdec[:, t, :], vn[:], start=True, stop=True))(), reads=[tok_tk[t], vn_tk], writes=[op_tk])
                    P.op("dve", (lambda op_=op_, t=t, h=h: lambda e: e.scalar_tensor_tensor(out=S_f[:], in0=S_f[:], scalar=cd[:, t, h:h + 1], in1=op_[:, 128:256], op0=ALU.mult, op1=ALU.add))(),
                         reads=[op_tk, par_tk, S_tk], writes=[S_tk])
                    P.op("act", lambda e: e.activation(out=S_b[:], in_=S_f[:], func=AF.Copy), reads=[S_tk], writes=[S_tk])
                    o_sb, o_tk = sm.next()
                    sst, sst_tk = st1.next()
                    P.op("act", (lambda o_sb=o_sb, op_=op_, sst=sst: lambda e: e.activation(out=o_sb[:, 128:256], in_=op_[:, 0:128], func=AF.Square, accum_out=sst[:, 0:1]))(), reads=[op_tk], writes=[o_tk, sst_tk])
                    P.op("dve", (lambda o_sb=o_sb, op_=op_: lambda e: e.tensor_copy(out=o_sb[:, 0:128], in_=op_[:, 0:128]))(), reads=[op_tk, o_tk], writes=[o_tk])
                    P.op("act", (lambda sst=sst: lambda e: e.activation(out=sst[:, 1:2], in_=sst[:, 0:1], func=AF.Ln, bias=epsb[:, 0:1], scale=1.0 / 128))(), reads=[sst_tk, cst_tk], writes=[sst_tk])
                    P.op("act", (lambda sst=sst: lambda e: e.activation(out=sst[:, 2:3], in_=sst[:, 1:2], func=AF.Exp, scale=-0.5))(), reads=[sst_tk], writes=[sst_tk])
                    go, go_tk = smb.next()
                    P.op("dve", (lambda go=go, o_sb=o_sb, sst=sst, t=t: lambda e: e.scalar_tensor_tensor(out=go[:], in0=o_sb[:, 0:128], scalar=sst[:, 2:3], in1=zs[:, t, :], op0=ALU.mult, op1=ALU.mult))(),
                         reads=[o_tk, sst_tk, zs_tk], writes=[go_tk])
                    gp, gp_tk = PSr.next()
                    P.op("pe", (lambda gp=gp, go=go: lambda e: e.matmul(gp[:, 0:128], go[:], ident_bf[:], start=True, stop=True))(), reads=[go_tk, cb_tk], writes=[gp_tk])
                    evac_copy(goT[:, ts_], gp[:, 0:128], reads=[gp_tk], writes=[goT_tk[t // 4]])
                for tb in range(NB):
                    blk = slice(tb * 512, (tb + 1) * 512)
                    for n_ in range(8):
                        ps, ps_tk = PSr.next()
                        P.op("pe", (lambda ps=ps, wo=wo, n_=n_, blk=blk: lambda e: e.matmul(ps[:], wo[:, 0, n_ * 128:(n_ + 1) * 128], goT[:, blk], start=True, stop=True))(),
                             reads=[wo_tk, goT_tk[tb]], writes=[ps_tk])
                        P.op("dve", (lambda ps=ps, n_=n_, blk=blk: lambda e: e.tensor_tensor(out=xT[:, n_, blk], in0=ps[:], in1=xT[:, n_, blk], op=ALU.add))(),
                             reads=[ps_tk], writes=[xT_tk[n_][tb]])
            for r in w_r + [wo_r, sm, smb, st1]:
                mytks.extend(r.tks)
            retire(mytks)

        def attention():
            with ExitStack() as ph:
                attention_body(ph)

        def attention_body(ph):
            import math
            from collections import deque
            lam_init = 0.8 - 0.6 * math.exp(-0.3 * 1)
            psb = lambda name, shape, dt=F32: ph.enter_context(nc.sbuf_tensor(name, list(shape), dt))
            mytks = []

            def tk():
                t = Tk(); mytks.append(t); return t
            PSa = Ring(banks[0:2], PS.tks[0:2])
            PSb = Ring(banks[2:4], PS.tks[2:4])
            acc_b, acc_tk = banks[4:8], PS.tks[4:8]
            pending = deque()

            def drain(n):
                while n > 0 and pending:
                    it_ = pending.popleft()
                    if it_ is not None:
                        it_()
                    n -= 1
            psem = P.new_dma_sem("at_prm"); lsem = P.new_dma_sem("at_lamb"); possem = P.new_dma_sem("at_pos"); gsem = P.new_dma_sem("at_g")
            prm = psb("at_prm", [128, 8]); prm_tk = tk()
            for half in range(2):
                P.dma("sp", psem, (lambda half=half: lambda e: e.dma_start(out=prm[half * 64:(half + 1) * 64, 0:1],
                      in_=dr["b_q_norm"][0, :].rearrange("(p o) -> p o", o=1), allow_slow_non_contiguous=True))(), writes=[prm_tk])
                P.dma("sp", psem, (lambda half=half: lambda e: e.dma_start(out=prm[half * 64:(half + 1) * 64, 1:2],
                      in_=dr["k_norm"][0, :].rearrange("(p o) -> p o", o=1), allow_slow_non_contiguous=True))(), writes=[prm_tk])
            P.dma("sp", psem, lambda e: e.dma_start(out=prm[:, 2:3], in_=dr["b_sub_norm"][0, :].rearrange("(p o) -> p o", o=1),
                  allow_slow_non_contiguous=True), writes=[prm_tk])
            lamb = psb("at_lamb", [128, 4, 64]); lamb_tk = tk()
            P.dma("sp", lsem, lambda e: e.dma_start(out=lamb[:].rearrange("p a b -> p (a b)"),
                  in_=dr["b_lambda"].rearrange("a b -> (a b)").partition_broadcast(128)), writes=[lamb_tk])
            gcol = psb("at_gcol", [128, 2, 128]); gcol_tk = tk()
            for half in range(2):
                P.dma("sp", gsem, (lambda half=half: lambda e: e.dma_start(out=gcol[:, 0, half * 64:(half + 1) * 64], in_=dr["b_q_norm"][0, :].partition_broadcast(128)))(), writes=[gcol_tk])
                P.dma("sp", gsem, (lambda half=half: lambda e: e.dma_start(out=gcol[:, 1, half * 64:(half + 1) * 64], in_=dr["k_norm"][0, :].partition_broadcast(128)))(), writes=[gcol_tk])
            P.op("dve", lambda e: e.tensor_scalar(out=gcol[:, 0, :], in0=gcol[:, 0, :], scalar1=0.125, scalar2=None, op0=ALU.mult), reads=[gcol_tk], writes=[gcol_tk])
            P.op("dve", lambda e: e.tensor_scalar(out=prm[:, 0:1], in0=prm[:, 0:1], scalar1=0.125, scalar2=None, op0=ALU.mult), reads=[prm_tk], writes=[prm_tk])
            P.op("dve", lambda e: e.tensor_scalar(out=prm[:, 2:3], in0=prm[:, 2:3], scalar1=float(1.0 - lam_init), scalar2=None, op0=ALU.mult), reads=[prm_tk], writes=[prm_tk])
            bo = psb("at_bo", [128, 2, 128]); bo_tk = tk()
            ig = psb("at_ig", [128, 2])
            P.op("dve", lambda e: e.tensor_tensor(out=ig[:], in0=prm[:, 0:2], in1=prm[:, 0:2], op=ALU.mult), reads=[prm_tk], writes=[bo_tk])
            P.op("dve", lambda e: e.reciprocal(out=prm[:, 6:8], in_=ig[:]), reads=[bo_tk, prm_tk], writes=[prm_tk])
            for i_ in range(2):
                P.op("dve", (lambda i_=i_: lambda e: e.tensor_scalar(out=bo[:, i_, :], in0=cst["blockones"][:], scalar1=prm[:, 6 + i_:7 + i_], scalar2=None, op0=ALU.mult))(),
                     reads=[prm_tk, cst_tk, bo_tk], writes=[bo_tk])
            lp = psb("at_lp", [128, 2, 64]); lp_tk = tk()
            P.op("dve", lambda e: e.tensor_tensor(out=lp[:, 0, :], in0=lamb[:, 0, :], in1=lamb[:, 1, :], op=ALU.mult), reads=[lamb_tk], writes=[lp_tk])
            P.op("dve", lambda e: e.tensor_tensor(out=lp[:, 1, :], in0=lamb[:, 2, :], in1=lamb[:, 3, :], op=ALU.mult), reads=[lamb_tk, lp_tk], writes=[lp_tk])
            P.op("dve", lambda e: e.tensor_reduce(out=prm[:, 4:6], in_=lp[:], axis=AX.X, op=ALU.add), reads=[lp_tk, prm_tk], writes=[prm_tk])
            P.op("act", lambda e: e.activation(out=prm[:, 4:6], in_=prm[:, 4:6], func=AF.Exp), reads=[prm_tk], writes=[prm_tk])
            P.op("dve", lambda e: e.tensor_tensor(out=prm[:, 3:4], in0=prm[:, 5:6], in1=prm[:, 4:5], op=ALU.subtract), reads=[prm_tk], writes=[prm_tk])
            P.op("dve", lambda e: e.tensor_scalar(out=prm[:, 3:4], in0=prm[:, 3:4], scalar1=float(-lam_init), scalar2=None, op0=ALU.add), reads=[prm_tk], writes=[prm_tk])
            ones_bf = psb("at_ones_bf", [128, 128], BF16); tri_bf = psb("at_tri_bf", [128, 128], BF16)
            cb_tk = tk()
            P.op("dve", lambda e: e.tensor_copy(out=ones_bf[:], in_=cst["ones"][:]), reads=[cst_tk], writes=[cb_tk])
            P.op("dve", lambda e: e.tensor_copy(out=tri_bf[:], in_=cst["trimask"][:]), reads=[cst_tk, cb_tk], writes=[cb_tk])
            cosT = psb("at_cosT", [128, S], BF16); sinT = psb("at_sinT", [128, S], BF16); cs_tk = tk()
            with ExitStack() as ph2:
                posi = ph2.enter_context(nc.sbuf_tensor("at_posi", [128, S], I32))
                ang = ph2.enter_context(nc.sbuf_tensor("at_ang", [128, S], F32))
                kf = ph2.enter_context(nc.sbuf_tensor("at_kf", [128, S], F32))
                t2 = [Tk() for _ in range(3)]
                P.dma("sp", possem, lambda e: e.dma_start(out=posi[:], in_=pos_d[0, :].partition_broadcast(128)), writes=[t2[0]])
                for which, dst in ((0, sinT), (1, cosT)):
                    P.op("dve", lambda e: e.tensor_copy(out=ang[:], in_=posi[:]), reads=[t2[0]], writes=[t2[1]])
                    P.op("dve", (lambda which=which: lambda e: e.tensor_scalar(out=ang[:], in0=ang[:], scalar1=cst["freq"][:, 0:1], scalar2=float(which * np.pi / 2), op0=ALU.mult, op1=ALU.add))(),
                         reads=[t2[1], cst_tk], writes=[t2[1]])
                    P.op("dve", lambda e: e.tensor_scalar(out=kf[:], in0=ang[:], scalar1=float(1.0 / (2 * np.pi)), scalar2=None, op0=ALU.mult), reads=[t2[1]], writes=[t2[2]])
                    P.op("dve", lambda e: e.tensor_copy(out=kf[:].bitcast(I32), in_=kf[:]), reads=[t2[2]], writes=[t2[2]])
                    P.op("dve", lambda e: e.tensor_copy(out=kf[:], in_=kf[:].bitcast(I32)), reads=[t2[2]], writes=[t2[2]])
                    P.op("dve", lambda e: e.scalar_tensor_tensor(out=ang[:], in0=kf[:], scalar=-6.28125, in1=ang[:], op0=ALU.mult, op1=ALU.add), reads=[t2[1], t2[2]], writes=[t2[1]])
                    P.op("dve", lambda e: e.scalar_tensor_tensor(out=ang[:], in0=kf[:], scalar=-float(2 * np.pi - 6.28125), in1=ang[:], op0=ALU.mult, op1=ALU.add), reads=[t2[1], t2[2]], writes=[t2[1]])
                    P.op("dve", lambda e: e.tensor_scalar(out=ang[:], in0=ang[:], scalar1=3.14159, scalar2=-3.14159, op0=ALU.min, op1=ALU.max), reads=[t2[1]], writes=[t2[1]])
                    P.op("act", (lambda dst=dst: lambda e: e.activation(out=dst[:], in_=ang[:], func=AF.Sin))(), reads=[t2[1]], writes=[cs_tk])
                retire(t2)
            wq = psb("at_wq", [128, 8, 128], BF16); wqr = psb("at_wqr", [128, 8, 128], BF16)
            wk = psb("at_wk", [128, 8, 128], BF16); wkr = psb("at_wkr", [128, 8, 128], BF16)
            wv = psb("at_wv", [128, 8, 128], BF16)
            wq_tk = tk(); wqr_tk = tk(); wk_tk = tk(); wkr_tk = tk(); wv_tk = tk()
            wo2 = [psb("at_wo%d" % i, [128, 2, D], BF16) for i in range(2)]; wo2_tk = [tk() for _ in range(2)]
            qT2 = [psb("at_qT%d" % i, [128, S], BF16) for i in range(2)]; qT2_tk = [tk() for _ in range(2)]
            kT2 = [psb("at_kT%d" % i, [128, S], BF16) for i in range(2)]; kT2_tk = [tk() for _ in range(2)]
            vt2 = [psb("at_vt%d" % i, [128, NT, 128], BF16) for i in range(2)]; vt2_tk = [tk() for _ in range(2)]
            aoT = psb("at_aoT", [128, 2, S], BF16); aoT_tk = tk()
            pT_r = Ring([psb("at_pT%d" % i, [128, 512], BF16) for i in range(4)])
            om_r = Ring([psb("at_om%d" % i, [128, 512]) for i in range(6)])
            rings = [pT_r, om_r]

            def rot_weights(w, w_tk, wr, wr_tk):
                wv4 = w[:].rearrange("p c (g r d) -> p c g r d", g=2, r=2)
                wr4 = wr[:].rearrange("p c (g r d) -> p c g r d", g=2, r=2)
                for g in range(2):
                    P.op("pool", (lambda g=g: lambda e: e.tensor_scalar(out=wr4[:, :, g, 0, :], in0=wv4[:, :, g, 1, :], scalar1=-1.0, scalar2=None, op0=ALU.mult))(),
                         reads=[w_tk], writes=[wr_tk])
                    P.op("pool", (lambda g=g: lambda e: e.tensor_copy(out=wr4[:, :, g, 1, :], in_=wv4[:, :, g, 0, :]))(), reads=[w_tk, wr_tk], writes=[wr_tk])

            def prologue_items(h):
                par = h % 2
                hs = slice(h * 128, (h + 1) * 128)
                qT, qT_tk, kT, kT_tk, vt, vt_tk = qT2[par], qT2_tk[par], kT2[par], kT2_tk[par], vt2[par], vt2_tk[par]
                items = []

                def it_w():
                    load_w(wq, wq_tk, dr["b_w_q"][:, hs], 8, 128, 3, colgain=gcol[:, 0, :], colgain_tk=gcol_tk)
                    rot_weights(wq, wq_tk, wqr, wqr_tk)
                    load_w(wk, wk_tk, dr["w_kv"][:, hs], 8, 128, 2, colgain=gcol[:, 1, :], colgain_tk=gcol_tk)
                    rot_weights(wk, wk_tk, wkr, wkr_tk)
                    load_w(wv, wv_tk, dr["w_kv"][:, 1024 + h * 128:1024 + (h + 1) * 128], 8, 128, 2)
                    if h % 2 == 0:
                        load_w(wo2[(h // 2) % 2], wo2_tk[(h // 2) % 2], dr["b_w_out"][h * 128:(h + 2) * 128, :], 2, D, None)
                items += [it_w, None, None, None, None, None]
                for tg in range(4):
                    st = {}

                    def v1(tg=tg, st=st):
                        ps, ps_tk = PSb.next()
                        for j in range(4):
                            t = tg * 4 + j
                            for c in range(8):
                                P.op("pe", (lambda ps=ps, j=j, t=t, c=c: lambda e: e.matmul(
                                    ps[:, j * 128:(j + 1) * 128], xnT[:, c, t * 128:(t + 1) * 128], wv[:, c, :], start=(c == 0), stop=(c == 7)))(),
                                    reads=[xnT_tk[tg], wv_tk], writes=[ps_tk], sig=(j == 3 and c == 7))
                        st.update(ps=ps, ps_tk=ps_tk)

                    def v2(tg=tg, st=st):
                        ps, ps_tk = st["ps"], st["ps_tk"]
                        P.op("dve", (lambda ps=ps: lambda e: e.tensor_copy(out=vt[:, tg * 4:(tg + 1) * 4, :], in_=ps[:].rearrange("p (j n) -> p j n", j=4)))(),
                             reads=[ps_tk], writes=[vt_tk])
                    items += [v1, None, v2]
                for (w, w_tk, wr, wr_tk, dst, dst_tk, gi) in ((wq, wq_tk, wqr, wqr_tk, qT, qT_tk, 0), (wk, wk_tk, wkr, wkr_tk, kT, kT_tk, 1)):
                    for tb in range(NB):
                        st = {}

                        def s1(w=w, w_tk=w_tk, wr=wr, wr_tk=wr_tk, tb=tb, st=st):
                            blk = slice(tb * 512, (tb + 1) * 512)
                            ps, ps_tk = PSb.next()
                            for c in range(8):
                                P.op("pe", (lambda ps=ps, c=c: lambda e: e.matmul(ps[:], w[:, c, :], xnT[:, c, blk], start=(c == 0), stop=(c == 7)))(),
                                     reads=[w_tk, xnT_tk[tb]], writes=[ps_tk], sig=(c == 7))
                            psr, psr_tk = PSb.next()
                            for c in range(8):
                                P.op("pe", (lambda psr=psr, c=c: lambda e: e.matmul(psr[:], wr[:, c, :], xnT[:, c, blk], start=(c == 0), stop=(c == 7)))(),
                                     reads=[wr_tk, xnT_tk[tb]], writes=[psr_tk], sig=(c == 7))
                            st.update(ps=ps, ps_tk=ps_tk, psr=psr, psr_tk=psr_tk)

                        def s2(tb=tb, st=st):
                            blk = slice(tb * 512, (tb + 1) * 512)
                            ps, ps_tk, psr, psr_tk = st["ps"], st["ps_tk"], st["psr"], st["psr_tk"]
                            sq, sq_tk = tmpf.next()
                            P.op("act", (lambda: lambda e: e.activation(out=sq[:], in_=ps[:], func=AF.Square))(), reads=[ps_tk], writes=[sq_tk])
                            t2_, t2_tk = om_r.next()
                            P.op("dve", (lambda: lambda e: e.tensor_tensor(out=t2_[:], in0=psr[:], in1=sinT[:, blk], op=ALU.mult))(), reads=[psr_tk, cs_tk], writes=[t2_tk])
                            t1, t1_tk = om_r.next()
                            P.op("dve", (lambda: lambda e: e.tensor_tensor(out=t1[:], in0=ps[:], in1=cosT[:, blk], op=ALU.mult))(), reads=[ps_tk, cs_tk], writes=[t1_tk])
                            st.update(sq=sq, sq_tk=sq_tk, t1=t1, t1_tk=t1_tk, t2=t2_, t2_tk=t2_tk)

                        def s3(st=st, gi=gi):
                            sq, sq_tk = st["sq"], st["sq_tk"]
                            ps2, ps2_tk = PSb.next()
                            P.op("pe", (lambda: lambda e: e.matmul(ps2[:], bo[:, gi, :], sq[:], start=True, stop=True))(), reads=[sq_tk, bo_tk], writes=[ps2_tk])
                            st.update(ps2=ps2, ps2_tk=ps2_tk)

                        def s4(st=st):
                            ps2, ps2_tk = st["ps2"], st["ps2_tk"]
                            rr, rr_tk = tmpf.next()
                            P.op("act", (lambda: lambda e: e.activation(out=rr[:], in_=ps2[:], func=AF.Ln, bias=epsb[:, 0:1], scale=1.0 / 64))(),
                                 reads=[ps2_tk, cst_tk], writes=[rr_tk])
                            P.op("act", (lambda: lambda e: e.activation(out=rr[:], in_=rr[:], func=AF.Exp, scale=-0.5))(), reads=[rr_tk], writes=[rr_tk])
                            st.update(rr=rr, rr_tk=rr_tk)

                        def s5(tb=tb, st=st, dst=dst, dst_tk=dst_tk):
                            blk = slice(tb * 512, (tb + 1) * 512)
                            t1, t1_tk, t2_, t2_tk, rr, rr_tk = st["t1"], st["t1_tk"], st["t2"], st["t2_tk"], st["rr"], st["rr_tk"]
                            P.op("pool", (lambda: lambda e: e.tensor_tensor(out=t1[:], in0=t1[:], in1=t2_[:], op=ALU.add))(), reads=[t1_tk, t2_tk], writes=[t1_tk])
                            P.op("pool", (lambda: lambda e: e.tensor_tensor(out=dst[:, blk], in0=t1[:], in1=rr[:], op=ALU.mult))(), reads=[t1_tk, rr_tk], writes=[dst_tk])
                        items += [s1, None, s2, s3, None, s4, s5]
                return items

            def outproj_items(hp):
                items = []
                wo, wo_tk = wo2[hp % 2], wo2_tk[hp % 2]
                for tb in range(NB):
                    for n0 in range(0, 8, 2):
                        def it(tb=tb, n0=n0):
                            blk = slice(tb * 512, (tb + 1) * 512)
                            for n_ in range(n0, n0 + 2):
                                ps, ps_tk = PSb.next()
                                for hh in range(2):
                                    P.op("pe", (lambda ps=ps, hh=hh, n_=n_: lambda e: e.matmul(
                                        ps[:], wo[:, hh, n_ * 128:(n_ + 1) * 128], aoT[:, hh, blk], start=(hh == 0), stop=(hh == 1)))(),
                                        reads=[wo_tk, aoT_tk], writes=[ps_tk], sig=(hh == 1))
                                P.op("dve", (lambda ps=ps, n_=n_: lambda e: e.tensor_tensor(out=xT[:, n_, blk], in0=ps[:], in1=xT[:, n_, blk], op=ALU.add))(),
                                     reads=[ps_tk], writes=[xT_tk[n_][tb]])
                        it.is_outproj = True
                        items.append(it)
                return items

            for it in prologue_items(0):
                if it is not None:
                    it()
            for h in range(8):
                par = h % 2
                qT, qT_tk, kT, kT_tk, vt, vt_tk = qT2[par], qT2_tk[par], kT2[par], kT2_tk[par], vt2[par], vt2_tk[par]
                drain(10 ** 9)
                if h + 1 < 8:
                    pending.extend(prologue_items(h + 1))
                tiles = []
                for qb in range(NB):
                    for kt in range(4 * qb + 4):
                        for m in range(2):
                            tiles.append((qb, kt, m))

                def emit_score(i):
                    qb, kt, m = tiles[i]
                    j = kt - 4 * qb
                    q0 = 0 if j < 0 else 128 * j
                    n = 512 - q0
                    ms = slice(m * 64, (m + 1) * 64)
                    ps, ps_tk = PSa.next()
                    P.op("pe", (lambda ps=ps, kT=kT, qT=qT: lambda e: e.matmul(ps[:, 0:n], kT[ms, kt * 128:(kt + 1) * 128], qT[ms, qb * 512 + q0:(qb + 1) * 512], start=True, stop=True))(),
                         reads=[kT_tk, qT_tk], writes=[ps_tk])
                    return ps, ps_tk
                nxt = emit_score(0)
                for i, (qb, kt, m) in enumerate(tiles):
                    ps, ps_tk = nxt
                    if i + 1 < len(tiles):
                        nxt = emit_score(i + 1)
                    j = kt - 4 * qb
                    q0 = 0 if j < 0 else 128 * j
                    n = 512 - q0
                    nkt = 4 * qb + 4
                    pT, pT_tk = pT_r.next()
                    P.op("act", (lambda pT=pT, ps=ps, n=n: lambda e: e.activation(out=pT[:, 0:n], in_=ps[:, 0:n], func=AF.Exp))(), reads=[ps_tk], writes=[pT_tk])
                    if j >= 0:
                        P.op("dve", (lambda pT=pT: lambda e: e.tensor_tensor(out=pT[:, 0:128], in0=pT[:, 0:128], in1=tri_bf[:], op=ALU.mult))(),
                             reads=[pT_tk, cb_tk], writes=[pT_tk])
                    P.op("pe", (lambda pT=pT, m=m, kt=kt, q0=q0, n=n, nkt=nkt, vt=vt: lambda e: e.matmul(
                        acc_b[2 * m][:, q0:512], vt[:, kt, :], pT[:, 0:n], start=(kt == 0), stop=(kt == nkt - 1)))(),
                        reads=[pT_tk, vt_tk], writes=[acc_tk[2 * m]], sig=False)
                    P.op("pe", (lambda pT=pT, m=m, kt=kt, q0=q0, n=n, nkt=nkt: lambda e: e.matmul(
                        acc_b[2 * m + 1][:, q0:512], ones_bf[:], pT[:, 0:n], start=(kt == 0), stop=(kt == nkt - 1)))(),
                        reads=[pT_tk, cb_tk], writes=[acc_tk[2 * m + 1]], sig=True)
                    drain(1)
                    if kt == nkt - 1 and m == 1:
                        blk = slice(qb * 512, (qb + 1) * 512)
                        while any(getattr(it_, "is_epi", False) for it_ in pending):
                            drain(1)
                        o_s = []; rd_s = []
                        for mm in range(2):
                            o_, o_tk = om_r.next()
                            P.op("act", (lambda o_=o_, mm=mm: lambda e: e.activation(out=o_[:], in_=acc_b[2 * mm][:], func=AF.Copy))(), reads=[acc_tk[2 * mm]], writes=[o_tk])
                            rd, rd_tk = om_r.next()
                            P.op("dve", (lambda rd=rd, mm=mm: lambda e: e.tensor_copy(out=rd[:], in_=acc_b[2 * mm + 1][:]))(), reads=[acc_tk[2 * mm + 1]], writes=[rd_tk])
                            o_s.append((o_, o_tk)); rd_s.append((rd, rd_tk))
                        st = {}

                        def e0(rd_s=rd_s):
                            for mm in range(2):
                                rd, rd_tk = rd_s[mm]
                                P.op("act", (lambda rd=rd: lambda e: e.activation(out=rd[:], in_=rd[:], func=AF.Ln))(), reads=[rd_tk], writes=[rd_tk])
                                P.op("act", (lambda rd=rd: lambda e: e.activation(out=rd[:], in_=rd[:], func=AF.Exp, scale=-1.0))(), reads=[rd_tk], writes=[rd_tk])

                        def e1(o_s=o_s, rd_s=rd_s, st=st):
                            for mm in range(2):
                                P.op("pool", (lambda mm=mm: lambda e: e.tensor_tensor(out=o_s[mm][0][:], in0=o_s[mm][0][:], in1=rd_s[mm][0][:], op=ALU.mult))(),
                                     reads=[o_s[mm][1], rd_s[mm][1]], writes=[o_s[mm][1]])
                            df, df_tk = rd_s[0]
                            P.op("dve", (lambda df=df: lambda e: e.scalar_tensor_tensor(out=df[:], in0=o_s[1][0][:], scalar=prm[:, 3:4], in1=o_s[0][0][:], op0=ALU.mult, op1=ALU.add))(),
                                 reads=[o_s[0][1], o_s[1][1], prm_tk], writes=[df_tk])
                            st.update(df=df, df_tk=df_tk)

                        def e1b(rd_s=rd_s, st=st):
                            df, df_tk = st["df"], st["df_tk"]
                            sq, sq_tk = rd_s[1]
                            P.op("act", (lambda: lambda e: e.activation(out=sq[:], in_=df[:], func=AF.Square))(), reads=[df_tk], writes=[sq_tk])
                            st.update(sq=sq, sq_tk=sq_tk)

                        def e1c(st=st):
                            sq, sq_tk = st["sq"], st["sq_tk"]
                            ps2, ps2_tk = PSb.next()
                            P.op("pe", (lambda: lambda e: e.matmul(ps2[:], cst["ones"][:], sq[:], start=True, stop=True))(), reads=[sq_tk, cst_tk], writes=[ps2_tk])
                            st.update(ps2=ps2, ps2_tk=ps2_tk)

                        def e2(st=st, h=h, blk=blk):
                            df, df_tk, ps2, ps2_tk = st["df"], st["df_tk"], st["ps2"], st["ps2_tk"]
                            rr, rr_tk = tmpf.next()
                            P.op("act", (lambda: lambda e: e.activation(out=rr[:], in_=ps2[:], func=AF.Ln, bias=epsb[:, 0:1], scale=1.0 / 128))(),
                                 reads=[ps2_tk, cst_tk], writes=[rr_tk])
                            P.op("act", (lambda: lambda e: e.activation(out=rr[:], in_=rr[:], func=AF.Exp, scale=-0.5))(), reads=[rr_tk], writes=[rr_tk])
                            st.update(rr=rr, rr_tk=rr_tk)

                        def e3(st=st, h=h, blk=blk):
                            df, df_tk, rr, rr_tk = st["df"], st["df_tk"], st["rr"], st["rr_tk"]
                            P.op("dve", (lambda: lambda e: e.scalar_tensor_tensor(
                                out=aoT[:, h % 2, blk], in0=df[:], scalar=prm[:, 2:3], in1=rr[:], op0=ALU.mult, op1=ALU.mult))(),
                                reads=[df_tk, rr_tk, prm_tk], writes=[aoT_tk])
                        epi = [e0, e1, None, e1b, e1c, None, e2, e3]
                        for it_ in epi:
                            if it_ is not None:
                                it_.is_epi = True
                        lst = list(pending)
                        pos_ = 0
                        for ii, it_ in enumerate(lst):
                            if getattr(it_, "is_outproj", False):
                                pos_ = ii + 1
                        pending.clear()
                        pending.extend(lst[:pos_] + epi + lst[pos_:])
                        if qb == NB - 1 and h % 2 == 1:
                            lst = list(pending)
                            pos_ = lst.index(e3) + 1
                            pending.clear()
                            pending.extend(lst[:pos_] + outproj_items(h // 2) + lst[pos_:])
            drain(10 ** 9)
            for r in rings:
                mytks.extend(r.tks)
            retire(mytks)

        if "gdn" in stages:
            rms_to_xnT()
            gdn()
        if "mlp0" in stages:
            rms_to_xnT()
            mlp(0, 1)
        if "attn" in stages:
            rms_to_xnT()
            attention()
        if "mlp1" in stages:
            rms_to_xnT()
            mlp(1, 4)

        osem = [P.new_dma_sem("out%d" % i) for i in range(2)]
        for t in range(NT):
            tb = t // 4
            o, o_tk = stage.next()
            osi = (stage.i - 1) % 2
            for cg in range(2):
                ps, ps_tk = PS.next()
                for j in range(4):
                    c = cg * 4 + j
                    P.op("pe", (lambda ps=ps, c=c, j=j, t=t: lambda e: e.transpose(
                        ps[:, j * 128:(j + 1) * 128], xT[:, c, t * 128:(t + 1) * 128], cst["ident"][:]))(),
                        reads=[xT_tk[c][tb], cst_tk], writes=[ps_tk], sig=(j == 3))
                evac_copy(o[:, cg * 512:(cg + 1) * 512], ps[:], reads=[ps_tk], writes=[o_tk])
            P.dma("sp", osem[osi], (lambda o=o, t=t: lambda e: e.dma_start(out=out_d[t * 128:(t + 1) * 128, :], in_=o[:, 0:1024]))(), reads=[o_tk])
        P.final_wait("sp", osem[0])
        P.final_wait("sp", osem[1])
        P.emit()
        nops = {e: len(P.ops[e]) for e in P.ENG}
        nops["splits"] = P.split_log
    return nc, nops


_CACHE = {}


def kernel(**inputs):
    B = inputs["x"].shape[0]
    consts = host_consts()
    if "nc" not in _CACHE:
        _CACHE["nc"] = build_program()
    nc, _ = _CACHE["nc"]
    shared = {}
    f32 = lambda a: np.ascontiguousarray(np.asarray(a, dtype=np.float32))
    shared["a_norm"] = f32(inputs["a_norm"]).reshape(1, D)
    shared["a_w_in"] = f32(inputs["a_w_in"]).reshape(D, 4112)
    shared["a_conv_w"] = f32(inputs["a_conv_w"]).reshape(4, 3072)
    shared["a_a_log"] = f32(inputs["a_a_log"]).reshape(1, 8)
    shared["a_dt_bias"] = f32(inputs["a_dt_bias"]).reshape(1, 8)
    shared["a_out_norm"] = f32(inputs["a_out_norm"]).reshape(1, 128)
    shared["a_w_out"] = f32(inputs["a_w_out"]).reshape(D, D)
    shared["kv_norm"] = f32(inputs["kv_norm"]).reshape(1, D)
    shared["w_kv"] = f32(inputs["w_kv"]).reshape(D, 2048)
    shared["k_norm"] = f32(inputs["k_norm"]).reshape(1, 64)
    shared["b_norm"] = f32(inputs["b_norm"]).reshape(1, D)
    shared["b_w_q"] = f32(inputs["b_w_q"]).reshape(D, D)
    shared["b_q_norm"] = f32(inputs["b_q_norm"]).reshape(1, 64)
    shared["b_lambda"] = f32(inputs["b_lambda"]).reshape(4, 64)
    shared["b_sub_norm"] = f32(inputs["b_sub_norm"]).reshape(1, 128)
    shared["b_w_out"] = f32(inputs["b_w_out"]).reshape(D, D)
    shared["mlp_norm"] = f32(inputs["mlp_norm"]).reshape(2, D)
    shared["mlp_w1"] = f32(inputs["mlp_w1"]).reshape(2 * D, DFF)
    shared["mlp_w2"] = f32(inputs["mlp_w2"]).reshape(2 * DFF, D)
    for n in CONST_NAMES:
        shared["c_" + n] = consts[n]
    x = f32(inputs["x"])
    pos = np.ascontiguousarray(np.asarray(inputs["positions"], dtype=np.int32))
    in_maps = []
    for b in range(B):
        m = dict(shared)
        m["x"] = x[b]
        m["positions"] = pos[b].reshape(1, S)
        in_maps.append(m)
    res = run_bass_kernel_spmd(nc, in_maps, core_ids=list(range(B)))
    return np.stack([np.asarray(r["out"], dtype=np.float32) for r in res.results], axis=0)
```

```python
import numpy as np
import concourse.bass as bass
import concourse.mybir as mybir
from concourse.bass_utils import run_bass_kernel_spmd
from contextlib import ExitStack

F32 = mybir.dt.float32
BF16 = mybir.dt.bfloat16
I32 = mybir.dt.int32
AF = mybir.ActivationFunctionType
ALU = mybir.AluOpType
AX = mybir.AxisListType

S = 2048
D = 1024
NT = 16
NB = 4
DFF = 4096
EPS = 1e-6


INHERIT = {}


class Tk:
    __slots__ = ("name", "w", "r", "excl")

    def __init__(self, name="", excl=False):
        self.name = name
        self.w = None
        self.r = dict(INHERIT)
        self.excl = excl


def retire(tks):
    for t in tks:
        if t.w is not None:
            k, v = t.w
            if INHERIT.get(k, 0) < v:
                INHERIT[k] = v
        for k, v in t.r.items():
            if INHERIT.get(k, 0) < v:
                INHERIT[k] = v


class Prog:
    ENG = ("pe", "act", "dve", "pool", "sp")

    def __init__(self, nc, es):
        self.nc = nc
        self.es = es
        self.ops = {e: [] for e in self.ENG}
        self.sems = {}
        self.cnt = {}
        for e in self.ENG:
            self.sems[e] = es.enter_context(nc.semaphore("s_" + e))
            self.cnt[e] = 0
        self.waited = {e: {} for e in self.ENG}
        self.pending_nosig = {e: False for e in self.ENG}

    def new_dma_sem(self, name):
        key = "dma_" + name
        self.sems[key] = self.es.enter_context(self.nc.semaphore(key))
        self.cnt[key] = 0
        return key

    def _deps(self, eng, reads, writes):
        deps = {}

        def add(d):
            if d is None:
                return
            k, v = d
            if deps.get(k, 0) < v:
                deps[k] = v
        for t in reads:
            add(t.w)
            if t.excl:
                for k, v in t.r.items():
                    if k != eng:
                        add((k, v))
        for t in writes:
            add(t.w)
            for k, v in t.r.items():
                add((k, v))
        waits = []
        wd = self.waited[eng]
        for k, v in deps.items():
            if k == "pe" and eng == "pe":
                continue
            if wd.get(k, 0) < v:
                wd[k] = v
                waits.append((k, v))
        return waits

    def op(self, eng, fn, reads=(), writes=(), sig=True):
        assert sig or eng == "pe"
        waits = self._deps(eng, reads, writes)
        if sig:
            self.cnt[eng] += 1
            val = self.cnt[eng]
        else:
            val = self.cnt[eng] + 1
        self.pending_nosig[eng] = not sig
        self.ops[eng].append((waits, fn, (eng, 1) if sig else None))
        for t in reads:
            if t.r.get(eng, 0) < val:
                t.r[eng] = val
        for t in writes:
            t.w = (eng, val)
            t.r = {}

    def dma(self, q, semkey, fn, reads=(), writes=()):
        assert q == "sp"
        waits = self._deps(q, reads, writes)
        self.cnt[semkey] += 1
        val = self.cnt[semkey]
        self.ops[q].append((waits, fn, (semkey, 16)))
        for t in reads:
            if t.r.get(semkey, 0) < val:
                t.r[semkey] = val
        for t in writes:
            t.w = (semkey, val)
            t.r = {}

    def final_wait(self, eng, semkey):
        if self.cnt[semkey] > 0:
            self.ops[eng].append(([(semkey, self.cnt[semkey])], None, None))

    def emit(self):
        nc = self.nc
        for e in self.ENG:
            assert not self.pending_nosig[e], e
        actual = {k: [0] for k in self.sems if k.startswith("dma_")}
        self.split_log = []
        with nc.Block() as block:
            def run(engname):
                def body(eng):
                    for waits, fn, inc in self.ops[engname]:
                        for k, v in waits:
                            eng.wait_ge(self.sems[k], actual[k][v] if k in actual else v)
                        if fn is not None:
                            n0 = nc.n_instructions()
                            ins = fn(eng)
                            if inc is not None:
                                ins.then_inc(self.sems[inc[0]], inc[1])
                                if inc[0] in actual:
                                    k_ = max(1, nc.n_instructions() - n0)
                                    if k_ != 1:
                                        self.split_log.append((inc[0], len(actual[inc[0]]), k_))
                                    actual[inc[0]].append(actual[inc[0]][-1] + 16 * k_)
                return body
            block.sync(run("sp"))
            block.tensor(run("pe"))
            block.scalar(run("act"))
            block.vector(run("dve"))
            block.gpsimd(run("pool"))


class Ring:
    def __init__(self, bufs, tks=None):
        self.bufs = bufs
        self.tks = tks if tks is not None else [Tk() for _ in bufs]
        self.i = 0

    def next(self):
        b, t = self.bufs[self.i], self.tks[self.i]
        self.i = (self.i + 1) % len(self.bufs)
        return b, t


def host_consts():
    c = {}
    i = np.arange(128)
    c["ident"] = np.eye(128, dtype=np.float32)
    c["ones"] = np.ones((128, 128), dtype=np.float32)
    bo = np.zeros((128, 128), np.float32); bo[:64, :64] = 1; bo[64:, 64:] = 1
    c["blockones"] = bo
    rm = np.zeros((128, 128), np.float32)
    for p in range(128):
        if p % 64 < 32:
            rm[p + 32, p] = -1.0
        else:
            rm[p - 32, p] = 1.0
    c["rot"] = rm
    fr = np.zeros((128, 128), np.float32)
    fr[:, 0] = (10000.0 ** (-(np.arange(128) % 32).astype(np.float32) / 32.0)).astype(np.float32)
    c["freq"] = fr
    c["trimask"] = (i[:, None] <= i[None, :]).astype(np.float32)
    c["mnegL"] = np.where(i[:, None] >= i[None, :], 0.0, -30000.0).astype(np.float32)
    c["mnegU"] = np.ascontiguousarray(c["mnegL"].T)
    c["mnegLs"] = np.where(i[:, None] > i[None, :], 0.0, -30000.0).astype(np.float32)
    c["strictL"] = (i[:, None] > i[None, :]).astype(np.float32)
    c["strictU"] = np.ascontiguousarray(c["strictL"].T)
    return c


CONST_NAMES = ["ident", "ones", "blockones", "freq", "trimask", "mnegLs", "mnegU", "strictU"]


def build_program(stages=("gdn", "mlp0", "attn", "mlp1"), debug=False):
    INHERIT.clear()
    nc = bass.Bass("TRN2", target_bir_lowering=False)
    dr = {}

    def din(name, shape, dt=F32):
        dr[name] = nc.dram_tensor(name, list(shape), dt, kind="ExternalInput").ap()
        return dr[name]
    x_d = din("x", [S, D])
    pos_d = din("positions", [1, S], I32)
    din("a_norm", [1, D]); din("a_w_in", [D, 4112]); din("a_conv_w", [4, 3072]); din("a_a_log", [1, 8])
    din("a_dt_bias", [1, 8]); din("a_out_norm", [1, 128]); din("a_w_out", [D, D])
    din("kv_norm", [1, D]); din("w_kv", [D, 2048]); din("k_norm", [1, 64])
    din("b_norm", [1, D]); din("b_w_q", [D, D]); din("b_q_norm", [1, 64]); din("b_lambda", [4, 64])
    din("b_sub_norm", [1, 128]); din("b_w_out", [D, D])
    din("mlp_norm", [2, D]); din("mlp_w1", [2 * D, DFF]); din("mlp_w2", [2 * DFF, D])
    for n in CONST_NAMES:
        din("c_" + n, [128, 128])
    out_d = nc.dram_tensor("out", [S, D], F32, kind="ExternalOutput").ap()

    with ExitStack() as es:
        P = Prog(nc, es)

        def sb(name, shape, dt=F32):
            return es.enter_context(nc.sbuf_tensor(name, list(shape), dt))

        xT = sb("xT", [128, 8, S])
        xT_tk = [[Tk() for _ in range(NB)] for _ in range(8)]
        xnT = sb("xnT", [128, 8, S], BF16)
        xnT_tk = [Tk() for _ in range(NB)]
        banks = [es.enter_context(nc.psum_tensor("bank%d" % i, [128, 512], F32)) for i in range(8)]
        PS = Ring(banks, [Tk("bank%d" % i, excl=True) for i in range(8)])
        stage_sem = [P.new_dma_sem("stage%d" % i) for i in range(2)]
        stage_box = {"gen": 0}

        def open_stage(ph_, width):
            g_ = stage_box["gen"]
            stage_box["gen"] = g_ + 1
            bufs = [ph_.enter_context(nc.sbuf_tensor("stage_%d_%d" % (g_, i), [128, width], F32)) for i in range(2)]
            stage_box["ring"] = Ring(bufs)
            stage_box["width"] = width

        def close_stage():
            retire(stage_box["ring"].tks)
        tmpf = Ring([sb("tmpf%d" % i, [128, 512]) for i in range(3)])
        rstd_r = Ring([sb("rstd%d" % i, [128, 512]) for i in range(1)])
        cst = {}
        cst_tk = Tk()
        csem = P.new_dma_sem("const")
        for n in CONST_NAMES:
            cst[n] = sb("cs_" + n, [128, 128])
            P.dma("sp", csem, (lambda n=n: lambda e: e.dma_start(out=cst[n][:], in_=dr["c_" + n]))(), writes=[cst_tk])
        gains = sb("gains", [128, 5, 8])
        gain_src = [dr["a_norm"][0, :], dr["mlp_norm"][0, :], dr["kv_norm"][0, :], dr["b_norm"][0, :], dr["mlp_norm"][1, :]]
        for gi, src in enumerate(gain_src):
            P.dma("sp", csem, (lambda gi=gi, src=src: lambda e: e.dma_start(
                out=gains[:, gi, :], in_=src.rearrange("(c p) -> p c", p=128), allow_slow_non_contiguous=True))(), writes=[cst_tk])
        epsb = sb("epsb", [128, 1])
        P.op("pool", lambda e: e.memset(epsb[:], EPS), writes=[cst_tk])

        evac_flip = [0]

        def evac_copy(out_ap, in_ap, reads, writes):
            evac_flip[0] ^= 1
            if evac_flip[0]:
                P.op("act", lambda e: e.activation(out=out_ap, in_=in_ap, func=AF.Copy), reads=reads, writes=writes)
            else:
                P.op("dve", lambda e: e.tensor_copy(out=out_ap, in_=in_ap), reads=reads, writes=writes)

        ph0 = ExitStack()
        open_stage(ph0, 1024)
        stage = stage_box["ring"]
        for t in range(NT):
            st, st_tk = stage.next()
            si = (stage.i - 1) % 2
            P.dma("sp", stage_sem[si], (lambda st=st, t=t: lambda e: e.dma_start(
                out=st[:, 0:1024], in_=x_d[t * 128:(t + 1) * 128, :]))(), writes=[st_tk])
            for cg in range(2):
                ps, ps_tk = PS.next()
                for j in range(4):
                    c = cg * 4 + j
                    P.op("pe", (lambda ps=ps, st=st, c=c, j=j: lambda e: e.transpose(
                        ps[:, j * 128:(j + 1) * 128], st[:, c * 128:(c + 1) * 128], cst["ident"][:]))(),
                        reads=[st_tk, cst_tk], writes=[ps_tk], sig=(j == 3))
                tb = t // 4
                evac_copy(xT[:, cg * 4:(cg + 1) * 4, t * 128:(t + 1) * 128],
                          ps[:].rearrange("p (j n) -> p j n", j=4),
                          reads=[ps_tk], writes=[xT_tk[c][tb] for c in range(cg * 4, cg * 4 + 4)])
        close_stage()
        ph0.close()

        def rms_to_xnT():
            for tb in range(NB):
                blk = slice(tb * 512, (tb + 1) * 512)
                ps, ps_tk = PS.next()
                for c in range(8):
                    sq, sq_tk = tmpf.next()
                    P.op("act", (lambda sq=sq, c=c, blk=blk: lambda e: e.activation(out=sq[:], in_=xT[:, c, blk], func=AF.Square))(),
                         reads=[xT_tk[c][tb]], writes=[sq_tk])
                    P.op("pe", (lambda ps=ps, sq=sq, c=c: lambda e: e.matmul(ps[:], cst["ones"][:], sq[:], start=(c == 0), stop=(c == 7)))(),
                         reads=[sq_tk, cst_tk], writes=[ps_tk], sig=True)
                rs, rs_tk = tmpf.next()
                P.op("act", (lambda rs=rs, ps=ps: lambda e: e.activation(out=rs[:], in_=ps[:], func=AF.Ln, bias=epsb[:, 0:1], scale=1.0 / D))(),
                     reads=[ps_tk, cst_tk], writes=[rs_tk])
                rstd_bc, rstd_tk = rstd_r.next()
                P.op("act", (lambda rs=rs, rstd_bc=rstd_bc: lambda e: e.activation(out=rstd_bc[:], in_=rs[:], func=AF.Exp, scale=-0.5))(),
                     reads=[rs_tk], writes=[rstd_tk])
                for c in range(8):
                    eng = "dve" if c % 2 == 0 else "pool"
                    P.op(eng, (lambda c=c, blk=blk, rstd_bc=rstd_bc: lambda e: e.tensor_tensor(out=xnT[:, c, blk], in0=xT[:, c, blk], in1=rstd_bc[:], op=ALU.mult))(),
                         reads=[xT_tk[c][tb], rstd_tk], writes=[xnT_tk[tb]])

        wsem = [P.new_dma_sem("w%d" % i) for i in range(2)]

        def load_w(dst, dst_tk, src, kc, ncols, gain_idx, gain_c0=0, colgain=None, colgain_tk=None):
            stage = stage_box["ring"]
            per = max(1, stage_box["width"] // ncols)
            c0 = 0
            while c0 < kc:
                n = min(per, kc - c0)
                st, st_tk = stage.next()
                si = (stage.i - 1) % 2
                stv = st[:, 0:n * ncols].rearrange("p (c n) -> p c n", c=n)
                srcv = src[c0 * 128:(c0 + n) * 128, :].rearrange("(c p) n -> p c n", p=128)
                P.dma("sp", stage_sem[si], (lambda stv=stv, srcv=srcv: lambda e: e.dma_start(out=stv, in_=srcv))(), writes=[st_tk])
                dv = dst[:, c0:c0 + n, :]
                if gain_idx is None:
                    P.op("pool", (lambda dv=dv, stv=stv: lambda e: e.tensor_copy(out=dv, in_=stv))(), reads=[st_tk], writes=[dst_tk])
                elif colgain is None:
                    gv = gains[:, gain_idx, gain_c0 + c0:gain_c0 + c0 + n].unsqueeze(2).to_broadcast([128, n, ncols])
                    P.op("pool", (lambda dv=dv, stv=stv, gv=gv: lambda e: e.tensor_tensor(out=dv, in0=stv, in1=gv, op=ALU.mult))(),
                         reads=[st_tk, cst_tk], writes=[dst_tk])
                else:
                    gv = gains[:, gain_idx, gain_c0 + c0:gain_c0 + c0 + n].unsqueeze(2).to_broadcast([128, n, ncols])
                    cv = colgain.unsqueeze(1).to_broadcast([128, n, ncols])
                    P.op("pool", (lambda stv=stv, gv=gv: lambda e: e.tensor_tensor(out=stv, in0=stv, in1=gv, op=ALU.mult))(),
                         reads=[st_tk, cst_tk], writes=[st_tk])
                    P.op("pool", (lambda dv=dv, stv=stv, cv=cv: lambda e: e.tensor_tensor(out=dv, in0=stv, in1=cv, op=ALU.mult))(),
                         reads=[st_tk, colgain_tk], writes=[dst_tk])
                c0 += n

        def mlp(layer, gain_idx):
            with ExitStack() as ph:
                mlp_body(layer, gain_idx, ph)

        def mlp_body(layer, gain_idx, ph):
            FG = 512
            open_stage(ph, 2048)
            psb = lambda name, shape, dt=F32: ph.enter_context(nc.sbuf_tensor(name, list(shape), dt))
            w1r = Ring([psb("w1g%d_%d" % (layer, i), [128, 8, FG], BF16) for i in range(2)])
            w2r = Ring([psb("w2g%d_%d" % (layer, i), [128, FG // 128, D], BF16) for i in range(2)])
            hr = Ring([psb("hT%d_%d" % (layer, i), [128, FG // 128, 512], BF16) for i in range(2)])
            w1_d = dr["mlp_w1"][layer * D:(layer + 1) * D, :]
            w2_d = dr["mlp_w2"][layer * DFF:(layer + 1) * DFF, :]
            for g in range(DFF // FG):
                w1, w1_tk = w1r.next()
                w2, w2_tk = w2r.next()
                load_w(w1, w1_tk, w1_d[:, g * FG:(g + 1) * FG], 8, FG, gain_idx)
                load_w(w2, w2_tk, w2_d[g * FG:(g + 1) * FG, :], FG // 128, D, None)
                for tb in range(NB):
                    blk = slice(tb * 512, (tb + 1) * 512)
                    hT, hT_tk = hr.next()
                    for fc in range(FG // 128):
                        ps, ps_tk = PS.next()
                        for c in range(8):
                            P.op("pe", (lambda ps=ps, w1=w1, c=c, fc=fc, blk=blk: lambda e: e.matmul(
                                ps[:], w1[:, c, fc * 128:(fc + 1) * 128], xnT[:, c, blk], start=(c == 0), stop=(c == 7)))(),
                                reads=[w1_tk, xnT_tk[tb]], writes=[ps_tk], sig=(c == 7))
                        h1, h1_tk = tmpf.next()
                        P.op("act", (lambda h1=h1, ps=ps: lambda e: e.activation(out=h1[:], in_=ps[:], func=AF.Relu))(),
                             reads=[ps_tk], writes=[h1_tk])
                        P.op("pool", (lambda h1=h1, hT=hT, fc=fc: lambda e: e.tensor_tensor(out=hT[:, fc, :], in0=h1[:], in1=h1[:], op=ALU.mult))(),
                             reads=[h1_tk], writes=[hT_tk])
                    for n in range(8):
                        ps, ps_tk = PS.next()
                        for fc in range(FG // 128):
                            P.op("pe", (lambda ps=ps, w2=w2, n=n, fc=fc, hT=hT: lambda e: e.matmul(
                                ps[:], w2[:, fc, n * 128:(n + 1) * 128], hT[:, fc, :], start=(fc == 0), stop=(fc == FG // 128 - 1)))(),
                                reads=[w2_tk, hT_tk], writes=[ps_tk], sig=(fc == FG // 128 - 1))
                        P.op("dve", (lambda ps=ps, n=n, blk=blk: lambda e: e.tensor_tensor(out=xT[:, n, blk], in0=ps[:], in1=xT[:, n, blk], op=ALU.add))(),
                             reads=[ps_tk], writes=[xT_tk[n][tb]])
            retire(w1r.tks + w2r.tks + hr.tks)
            close_stage()


        def gdn():
            with ExitStack() as ph:
                gdn_body(ph)

        def gdn_body(ph):
            import os
            open_stage(ph, 1024)
            psb = lambda name, shape, dt=F32: ph.enter_context(nc.sbuf_tensor(name, list(shape), dt))
            mytks = []

            def tk():
                t = Tk(); mytks.append(t); return t
            pb = banks[0:4]; pb_tk = PS.tks[0:4]
            PSs = Ring(banks[4:6], PS.tks[4:6])
            OBk = banks[6:8]; OB_tk = PS.tks[6:8]
            ident = cst["ident"]; onesf = cst["ones"]
            bc4 = lambda ap: ap.unsqueeze(1).to_broadcast([128, 4, 128])
            v3 = lambda ap: ap.rearrange("p (i e) -> p i e", i=4)
            ident_bf = psb("g_identbf", [128, 128], BF16); negones = psb("g_negones", [128, 128]); cb_tk = tk()
            P.op("dve", lambda e: e.tensor_copy(out=ident_bf[:], in_=ident[:]), reads=[cst_tk], writes=[cb_tk])
            P.op("dve", lambda e: e.tensor_scalar(out=negones[:], in0=onesf[:], scalar1=-1.0, scalar2=None, op0=ALU.mult), reads=[cst_tk, cb_tk], writes=[cb_tk])
            one_b = psb("g_oneb", [128, 1]); P.op("pool", lambda e: e.memset(one_b[:], 1.0), reads=[cb_tk], writes=[cb_tk])
            s1 = P.new_dma_sem("g_p1"); s2 = P.new_dma_sem("g_p2"); s3 = P.new_dma_sem("g_p3"); s4 = P.new_dma_sem("g_p4")
            alog = psb("g_alog", [128, 8]); dtb = psb("g_dtb", [128, 8]); onw = psb("g_onw", [128, 128]); cw = psb("g_cw", [128, 24, 4])
            alog_tk = tk(); dtb_tk = tk(); onw_tk = tk(); cw_tk = tk()
            P.dma("sp", s1, lambda e: e.dma_start(out=alog[:], in_=dr["a_a_log"][0, :].partition_broadcast(128)), writes=[alog_tk])
            P.dma("sp", s2, lambda e: e.dma_start(out=dtb[:], in_=dr["a_dt_bias"][0, :].partition_broadcast(128)), writes=[dtb_tk])
            P.dma("sp", s3, lambda e: e.dma_start(out=onw[:], in_=dr["a_out_norm"][0, :].partition_broadcast(128)), writes=[onw_tk])
            for j in range(4):
                P.dma("sp", s4, (lambda j=j: lambda e: e.dma_start(out=cw[:, :, j], in_=dr["a_conv_w"][j, :].rearrange("(c p) -> p c", p=128),
                      allow_slow_non_contiguous=True))(), writes=[cw_tk])
            P.op("act", lambda e: e.activation(out=alog[:], in_=alog[:], func=AF.Exp), reads=[alog_tk], writes=[alog_tk])
            P.op("dve", lambda e: e.tensor_scalar(out=alog[:], in0=alog[:], scalar1=-1.0, scalar2=None, op0=ALU.mult), reads=[alog_tk], writes=[alog_tk])
            wbg = psb("g_wbg", [128, 8, 16], BF16); wbg_tk = tk()
            load_w(wbg, wbg_tk, dr["a_w_in"][:, 4096:4112], 8, 16, 0)
            bg = psb("g_bg", [128, NT, 16]); bg_tk = tk()
            ps, ps_tk = pb[0], pb_tk[0]
            for t in range(NT):
                for c in range(8):
                    P.op("pe", (lambda t=t, c=c: lambda e: e.matmul(ps[:, t * 16:(t + 1) * 16], xnT[:, c, t * 128:(t + 1) * 128], wbg[:, c, :],
                         start=(c == 0), stop=(c == 7)))(), reads=[xnT_tk[t // 4], wbg_tk], writes=[ps_tk], sig=(t == NT - 1 and c == 7))
            P.op("dve", lambda e: e.tensor_copy(out=bg[:].rearrange("p t k -> p (t k)"), in_=ps[:, 0:256]), reads=[ps_tk], writes=[bg_tk])
            beta = psb("g_beta", [128, NT, 8]); gg = psb("g_g", [128, NT, 8]); par_tk = tk()
            P.op("act", lambda e: e.activation(out=beta[:], in_=bg[:, :, 0:8], func=AF.Exp, scale=-1.0), reads=[bg_tk], writes=[par_tk])
            P.op("act", lambda e: e.activation(out=beta[:], in_=beta[:], func=AF.Ln, bias=one_b[:, 0:1], scale=1.0), reads=[par_tk, cb_tk], writes=[par_tk])
            P.op("act", lambda e: e.activation(out=beta[:], in_=beta[:], func=AF.Exp, scale=-1.0), reads=[par_tk], writes=[par_tk])
            P.op("dve", lambda e: e.tensor_tensor(out=gg[:], in0=bg[:, :, 8:16], in1=dtb[:].unsqueeze(1).to_broadcast([128, NT, 8]), op=ALU.add),
                 reads=[bg_tk, dtb_tk, par_tk], writes=[par_tk])
            P.op("act", lambda e: e.activation(out=gg[:], in_=gg[:], func=AF.Exp), reads=[par_tk], writes=[par_tk])
            P.op("act", lambda e: e.activation(out=gg[:], in_=gg[:], func=AF.Ln, bias=one_b[:, 0:1], scale=1.0), reads=[par_tk, cb_tk], writes=[par_tk])
            P.op("dve", lambda e: e.tensor_tensor(out=gg[:], in0=gg[:], in1=alog[:].unsqueeze(1).to_broadcast([128, NT, 8]), op=ALU.mult),
                 reads=[par_tk, alog_tk], writes=[par_tk])
            gc = psb("g_gc", [128, NT, 8]); bgam = psb("g_bgam", [128, NT, 8]); kds = psb("g_kds", [128, NT, 8]); cd = psb("g_cd", [128, NT, 8])
            ps, ps_tk = pb[1], pb_tk[1]
            fl = lambda a_: a_[:].rearrange("p t k -> p (t k)")
            P.op("pe", lambda e: e.matmul(ps[:, 0:128], cst["trimask"][:], fl(gg), start=True, stop=True), reads=[par_tk, cst_tk], writes=[ps_tk], sig=False)
            P.op("pe", lambda e: e.matmul(ps[:, 128:256], onesf[:], fl(gg), start=True, stop=True), reads=[par_tk, cst_tk], writes=[ps_tk])
            P.op("dve", lambda e: e.tensor_copy(out=fl(gc), in_=ps[:, 0:128]), reads=[ps_tk], writes=[par_tk])
            P.op("dve", lambda e: e.tensor_tensor(out=fl(kds), in0=ps[:, 128:256], in1=fl(gc), op=ALU.subtract), reads=[ps_tk, par_tk], writes=[par_tk])
            P.op("act", lambda e: e.activation(out=fl(cd), in_=ps[:, 128:256], func=AF.Exp), reads=[ps_tk, par_tk], writes=[par_tk])
            P.op("act", lambda e: e.activation(out=fl(kds), in_=fl(kds), func=AF.Exp), reads=[par_tk], writes=[par_tk])
            P.op("act", lambda e: e.activation(out=fl(bgam), in_=fl(gc), func=AF.Exp), reads=[par_tk], writes=[par_tk])
            P.op("dve", lambda e: e.tensor_tensor(out=fl(bgam), in0=fl(bgam), in1=fl(beta), op=ALU.mult), reads=[par_tk], writes=[par_tk])
            wts = [psb("g_w%d" % k, [128, 8, 128], BF16) for k in range(4)]; wts_tk = [tk() for _ in range(4)]
            wo2 = [psb("g_wo%d" % i, [128, 1, D], BF16) for i in range(2)]; wo2_tk = [tk() for _ in range(2)]
            cdiag = psb("g_cdiag", [128, 12, 128], BF16); cdiag_tk = tk()
            prj = psb("g_prj", [128, 3 + S], BF16); prj_tk = tk()
            P.op("pool", lambda e: e.memset(prj[:, 0:3], 0.0), writes=[prj_tk])
            qhT = psb("g_qhT", [128, S], BF16); khT = psb("g_khT", [128, S], BF16); vT = psb("g_vT", [128, S], BF16)
            qhT_tk = tk(); khT_tk = tk(); vT_tk = tk()
            kbg2 = [psb("g_kbg%d" % i, [128, 512], BF16) for i in range(2)]; kdec2 = [psb("g_kdec%d" % i, [128, 512], BF16) for i in range(2)]
            bv2 = [psb("g_bv%d" % i, [128, 512], BF16) for i in range(2)]; zs2 = [psb("g_zs%d" % i, [128, 512], BF16) for i in range(2)]
            u2 = [psb("g_u%d" % i, [128, 512]) for i in range(2)]; wT2 = [psb("g_wT%d" % i, [128, 512], BF16) for i in range(2)]
            qdT2 = [psb("g_qdT%d" % i, [128, 512], BF16) for i in range(2)]; inT2 = [psb("g_inT%d" % i, [128, 512], BF16) for i in range(2)]
            goT2 = [psb("g_goT%d" % i, [128, 512], BF16) for i in range(2)]
            grp_tk = [[tk() for _ in range(10)] for _ in range(2)]
            KBG, KDEC, BV, ZS, U_, WT, QDT, INT, GOT = range(9)
            NF32 = int(os.environ.get("GDN_NF32", "2"))
            Pf = [psb("g_Pf%d" % a_, [128, 512]) for a_ in range(2)]; PTf = [psb("g_PTf%d" % a_, [128, 512]) for a_ in range(2)]
            Ph = [psb("g_Ph%d" % a_, [128, 512], BF16) for a_ in range(2)]; PTh = [psb("g_PTh%d" % a_, [128, 512], BF16) for a_ in range(2)]
            Rb = psb("g_R", [128, 512]); Rbf = psb("g_Rbf", [128, 512], BF16)
            Pb = Pf; PTb = PTf
            Ph_tk = [tk(), tk()]; PTh_tk = [tk(), tk()]; Rh_tk = tk()
            P_tk = [tk(), tk()]; PT_tk = [tk(), tk()]; R_tk = tk()
            gA = psb("g_gA", [128, 512]); gB = psb("g_gB", [128, 512]); gC = psb("g_gC", [128, 512]); gA_tk = tk(); gB_tk = tk(); gC_tk = tk()
            zA = psb("g_zA", [128, 512]); zA_tk = tk()
            eA = psb("g_eA", [128, 512]); eA_tk = tk()
            go_b = psb("g_go", [128, 512], BF16); go_tk = tk()
            S_f = psb("g_Sf", [128, 128]); S_b = psb("g_Sb", [128, 128], BF16); S_tk = tk()
            vn_r = Ring([psb("g_vn%d" % i, [128, 128], BF16) for i in range(2)])
            st4 = psb("g_st4", [128, 8]); st4_tk = tk()

            def pstage_items(h):
                items = []

                def it_w():
                    for k in range(4):
                        load_w(wts[k], wts_tk[k], dr["a_w_in"][:, k * 1024 + h * 128:k * 1024 + (h + 1) * 128], 8, 128, 0)
                    load_w(wo2[h % 2], wo2_tk[h % 2], dr["a_w_out"][h * 128:(h + 1) * 128, :], 1, D, None)
                    for sec in range(3):
                        for j in range(4):
                            P.op("dve", (lambda sec=sec, j=j: lambda e: e.tensor_scalar(out=cdiag[:, sec * 4 + j, :], in0=ident[:], scalar1=cw[:, sec * 8 + h, j:j + 1], scalar2=None, op0=ALU.mult))(),
                                 reads=[cst_tk, cw_tk], writes=[cdiag_tk])
                items.append(it_w)
                blocks = [(sec, tb) for sec in range(3) for tb in range(NB)]
                stt = [dict() for _ in blocks]

                def proj(bi):
                    sec, tb = blocks[bi]
                    blk = slice(tb * 512, (tb + 1) * 512)
                    bk = (2 * bi) % 4
                    for c in range(8):
                        P.op("pe", (lambda c=c: lambda e: e.matmul(pb[bk][:], wts[sec][:, c, :], xnT[:, c, blk], start=(c == 0), stop=(c == 7)))(),
                             reads=[wts_tk[sec], xnT_tk[tb]], writes=[pb_tk[bk]], sig=(c == 7))
                    P.op("act", (lambda: lambda e: e.activation(out=prj[:, 3 + tb * 512:3 + (tb + 1) * 512], in_=pb[bk][:], func=AF.Copy))(), reads=[pb_tk[bk]], writes=[prj_tk])

                def conv(bi):
                    sec, tb = blocks[bi]
                    bk = (2 * bi + 1) % 4
                    for j in range(4):
                        P.op("pe", (lambda j=j: lambda e: e.matmul(pb[bk][:], cdiag[:, sec * 4 + j, :], prj[:, tb * 512 + j:tb * 512 + j + 512], start=(j == 0), stop=(j == 3)))(),
                             reads=[cdiag_tk, prj_tk], writes=[pb_tk[bk]], sig=(j == 3))
                    e_, e_tk = tmpf.next()
                    P.op("act", (lambda: lambda e: e.activation(out=e_[:], in_=pb[bk][:], func=AF.Exp, scale=-1.0))(), reads=[pb_tk[bk]], writes=[e_tk])
                    P.op("act", (lambda: lambda e: e.activation(out=e_[:], in_=e_[:], func=AF.Ln, bias=one_b[:, 0:1], scale=1.0))(), reads=[e_tk, cb_tk], writes=[e_tk])
                    P.op("act", (lambda: lambda e: e.activation(out=e_[:], in_=e_[:], func=AF.Exp, scale=-1.0))(), reads=[e_tk], writes=[e_tk])
                    stt[bi].update(e=e_, e_tk=e_tk, bk=bk)

                def fin1(bi):
                    sec, tb = blocks[bi]
                    blk = slice(tb * 512, (tb + 1) * 512)
                    e_, e_tk, bk = stt[bi]["e"], stt[bi]["e_tk"], stt[bi]["bk"]
                    if sec == 2:
                        P.op("dve", (lambda: lambda e: e.tensor_tensor(out=vT[:, blk], in0=pb[bk][:], in1=e_[:], op=ALU.mult))(), reads=[pb_tk[bk], e_tk], writes=[vT_tk])
                        return
                    P.op("dve", (lambda: lambda e: e.tensor_tensor(out=e_[:], in0=pb[bk][:], in1=e_[:], op=ALU.mult))(), reads=[pb_tk[bk], e_tk], writes=[e_tk])
                    sq, sq_tk = tmpf.next()
                    P.op("act", (lambda: lambda e: e.activation(out=sq[:], in_=e_[:], func=AF.Square))(), reads=[e_tk], writes=[sq_tk])
                    P.op("pe", (lambda: lambda e: e.matmul(pb[bk][:], onesf[:], sq[:], start=True, stop=True))(), reads=[sq_tk, cst_tk], writes=[pb_tk[bk]])
                    stt[bi].update(sq=sq, sq_tk=sq_tk)

                def fin2(bi):
                    sec, tb = blocks[bi]
                    if sec == 2:
                        return
                    blk = slice(tb * 512, (tb + 1) * 512)
                    e_, e_tk, bk, sq, sq_tk = stt[bi]["e"], stt[bi]["e_tk"], stt[bi]["bk"], stt[bi]["sq"], stt[bi]["sq_tk"]
                    P.op("act", (lambda: lambda e: e.activation(out=sq[:], in_=pb[bk][:], func=AF.Ln, bias=epsb[:, 0:1], scale=1.0))(), reads=[pb_tk[bk], cst_tk], writes=[sq_tk])
                    P.op("act", (lambda: lambda e: e.activation(out=sq[:], in_=sq[:], func=AF.Exp, scale=-0.5))(), reads=[sq_tk], writes=[sq_tk])
                    dst, dst_tk = (qhT, qhT_tk) if sec == 0 else (khT, khT_tk)
                    sc = float(128 ** -0.5) if sec == 0 else 1.0
                    P.op("dve", (lambda: lambda e: e.scalar_tensor_tensor(out=dst[:, blk], in0=e_[:], scalar=sc, in1=sq[:], op0=ALU.mult, op1=ALU.mult))(),
                         reads=[e_tk, sq_tk], writes=[dst_tk])
                nb_ = len(blocks)
                items.append(lambda: proj(0))
                items.append(lambda: conv(0))
                for bi in range(nb_):
                    if bi + 1 < nb_:
                        items.append((lambda bi=bi: lambda: proj(bi + 1))())
                    items.append((lambda bi=bi: lambda: fin1(bi))())
                    if bi + 1 < nb_:
                        items.append((lambda bi=bi: lambda: conv(bi + 1))())
                    items.append((lambda bi=bi: lambda: fin2(bi))())
                return items

            def prep_items(h, g):
                par = g % 2
                t0 = 4 * g
                blk = slice(g * 512, (g + 1) * 512)
                T_ = grp_tk[par]
                kbg, kdec, bv, zs, u_, wT, qdT, inT = kbg2[par], kdec2[par], bv2[par], zs2[par], u2[par], wT2[par], qdT2[par], inT2[par]
                scb = lambda X: X[:, t0:t0 + 4, h].unsqueeze(2).to_broadcast([128, 4, 128])
                tsl = lambda i: slice(i * 128, (i + 1) * 128)
                tok = lambda i: slice((t0 + i) * 128, (t0 + i + 1) * 128)
                items = []

                def p1():
                    P.op("pool", lambda e: e.tensor_copy(out=v3(gA[:]), in_=scb(gc)), reads=[par_tk], writes=[gA_tk])
                    P.op("pool", lambda e: e.tensor_tensor(out=v3(gB[:]), in0=bc4(ident[:]), in1=scb(gc), op=ALU.mult), reads=[par_tk, cst_tk], writes=[gB_tk])
                    P.op("pool", lambda e: e.tensor_tensor(out=v3(gC[:]), in0=bc4(ident[:]), in1=scb(beta), op=ALU.mult), reads=[par_tk, cst_tk], writes=[gC_tk])

                def p2():
                    P.op("pe", lambda e: e.matmul(pb[0][:], ident[:], gA[:], start=True, stop=False), reads=[gA_tk, cst_tk], writes=[pb_tk[0]], sig=False)
                    P.op("pe", lambda e: e.matmul(pb[0][:], negones[:], gB[:], start=False, stop=True), reads=[gB_tk, cb_tk], writes=[pb_tk[0]])
                    P.op("pe", lambda e: e.matmul(pb[1][:], onesf[:], gB[:], start=True, stop=True), reads=[gB_tk, cst_tk], writes=[pb_tk[1]])
                    P.op("pe", lambda e: e.matmul(pb[2][:], onesf[:], gC[:], start=True, stop=True), reads=[gC_tk, cst_tk], writes=[pb_tk[2]])

                def p3():
                    P.op("dve", lambda e: e.tensor_tensor(out=v3(gA[:]), in0=v3(pb[0][:]), in1=bc4(cst["mnegLs"][:]), op=ALU.add), reads=[pb_tk[0], cst_tk], writes=[gA_tk])
                    P.op("dve", lambda e: e.scalar_tensor_tensor(out=v3(gB[:]), in0=v3(pb[0][:]), scalar=-1.0, in1=bc4(cst["mnegU"][:]), op0=ALU.mult, op1=ALU.add),
                         reads=[pb_tk[0], cst_tk], writes=[gB_tk])

                def p4():
                    P.op("act", lambda e: e.activation(out=gA[:], in_=gA[:], func=AF.Exp), reads=[gA_tk], writes=[gA_tk])
                    P.op("act", lambda e: e.activation(out=gB[:], in_=gB[:], func=AF.Exp), reads=[gB_tk], writes=[gB_tk])
                    P.op("act", lambda e: e.activation(out=gC[:], in_=pb[1][:], func=AF.Exp), reads=[pb_tk[1]], writes=[gC_tk])

                def p5a():
                    for i in range(4):
                        P.op("pe", (lambda i=i: lambda e: e.matmul(pb[3][:, tsl(i)], khT[:, tok(i)], khT[:, tok(i)], start=True, stop=True))(), reads=[khT_tk], writes=[pb_tk[3]], sig=(i == 3))
                    for i in range(4):
                        P.op("pe", (lambda i=i: lambda e: e.matmul(pb[0][:, tsl(i)], khT[:, tok(i)], qhT[:, tok(i)], start=True, stop=True))(), reads=[khT_tk, qhT_tk], writes=[pb_tk[0]], sig=(i == 3))

                def p6():
                    P.op("dve", lambda e: e.tensor_tensor(out=inT[:], in0=pb[0][:], in1=gB[:], op=ALU.mult), reads=[pb_tk[0], gB_tk], writes=[T_[INT]])
                    P.op("pool", lambda e: e.tensor_tensor(out=qdT[:], in0=qhT[:, blk], in1=gC[:], op=ALU.mult), reads=[qhT_tk, gC_tk], writes=[T_[QDT]])
                    P.op("pool", lambda e: e.tensor_tensor(out=v3(gB[:]), in0=v3(gB[:]), in1=bc4(cst["strictU"][:]), op=ALU.mult), reads=[gB_tk, cst_tk, T_[INT]], writes=[gB_tk])

                def p7():
                    P.op("pool", lambda e: e.tensor_tensor(out=v3(gA[:]), in0=v3(gA[:]), in1=scb(beta), op=ALU.mult), reads=[gA_tk, par_tk], writes=[gA_tk])
                    P.op("dve", lambda e: e.tensor_tensor(out=gC[:], in0=pb[2][:], in1=gB[:], op=ALU.mult), reads=[pb_tk[2], gB_tk, T_[QDT]], writes=[gC_tk])

                def p8():
                    P.op("dve", lambda e: e.tensor_tensor(out=Pb[0][:], in0=pb[3][:], in1=gA[:], op=ALU.mult), reads=[pb_tk[3], gA_tk], writes=[P_tk[0]])
                    P.op("dve", lambda e: e.tensor_tensor(out=PTb[0][:], in0=pb[3][:], in1=gC[:], op=ALU.mult), reads=[pb_tk[3], gC_tk], writes=[PT_tk[0]])
                    P.op("pool", lambda e: e.tensor_tensor(out=v3(Rb[:]), in0=bc4(ident[:]), in1=v3(PTb[0][:]), op=ALU.subtract), reads=[cst_tk, PT_tk[0]], writes=[R_tk])

                def p5b():
                    for i in range(4):
                        P.op("pe", (lambda i=i: lambda e: e.matmul(pb[1][:, tsl(i)], khT[:, tok(i)], ident_bf[:], start=True, stop=True))(), reads=[khT_tk, cb_tk], writes=[pb_tk[1]], sig=(i == 3))
                    for i in range(4):
                        P.op("pe", (lambda i=i: lambda e: e.matmul(pb[2][:, tsl(i)], vT[:, tok(i)], ident_bf[:], start=True, stop=True))(), reads=[vT_tk, cb_tk], writes=[pb_tk[2]], sig=(i == 3))

                def p5b2():
                    P.op("dve", lambda e: e.tensor_tensor(out=v3(kbg[:]), in0=v3(pb[1][:]), in1=scb(bgam), op=ALU.mult), reads=[pb_tk[1], par_tk], writes=[T_[KBG]])
                    P.op("dve", lambda e: e.tensor_tensor(out=v3(kdec[:]), in0=v3(pb[1][:]), in1=scb(kds), op=ALU.mult), reads=[pb_tk[1], par_tk], writes=[T_[KDEC]])
                    P.op("dve", lambda e: e.tensor_tensor(out=v3(bv[:]), in0=v3(pb[2][:]), in1=scb(beta), op=ALU.mult), reads=[pb_tk[2], par_tk], writes=[T_[BV]])

                def p5c():
                    for i in range(4):
                        for c in range(8):
                            P.op("pe", (lambda i=i, c=c: lambda e: e.matmul(pb[0][:, tsl(i)], xnT[:, c, tok(i)], wts[3][:, c, :], start=(c == 0), stop=(c == 7)))(),
                                 reads=[xnT_tk[g], wts_tk[3]], writes=[pb_tk[0]], sig=(i == 3 and c == 7))

                def p5c2():
                    P.op("act", lambda e: e.activation(out=zA[:], in_=pb[0][:], func=AF.Exp, scale=-1.0), reads=[pb_tk[0]], writes=[zA_tk])
                    P.op("act", lambda e: e.activation(out=zA[:], in_=zA[:], func=AF.Ln, bias=one_b[:, 0:1], scale=1.0), reads=[zA_tk, cb_tk], writes=[zA_tk])
                    P.op("act", lambda e: e.activation(out=zA[:], in_=zA[:], func=AF.Exp, scale=-1.0), reads=[zA_tk], writes=[zA_tk])

                def p5c3():
                    P.op("dve", lambda e: e.tensor_tensor(out=zA[:], in0=pb[0][:], in1=zA[:], op=ALU.mult), reads=[pb_tk[0], zA_tk], writes=[zA_tk])
                    P.op("pool", lambda e: e.tensor_tensor(out=v3(zs[:]), in0=v3(zA[:]), in1=bc4(onw[:]), op=ALU.mult), reads=[zA_tk, onw_tk], writes=[T_[ZS]])
                items += [p1, p2, p3, p4, p5a, p6, p7, p8, p5b, p5b2, p5c, p5c2, p5c3]
                if NF32 == 0:
                    def c0():
                        P.op("act", lambda e: e.activation(out=Ph[0][:], in_=Pf[0][:], func=AF.Copy), reads=[P_tk[0]], writes=[Ph_tk[0]])
                        P.op("act", lambda e: e.activation(out=PTh[0][:], in_=PTf[0][:], func=AF.Copy), reads=[PT_tk[0]], writes=[PTh_tk[0]])
                        P.op("act", lambda e: e.activation(out=Rbf[:], in_=Rb[:], func=AF.Copy), reads=[R_tk], writes=[Rh_tk])
                    items.append(c0)
                for lvl in range(1, 7):
                    cur = (lvl - 1) % 2
                    nxt = 1 - cur
                    f32lvl = lvl <= NF32

                    def d1(lvl=lvl, cur=cur, nxt=nxt, f32lvl=f32lvl):
                        A_, AT_, A_tk, AT_tk = (Pf, PTf, P_tk, PT_tk) if f32lvl else (Ph, PTh, Ph_tk, PTh_tk)
                        for i in range(4):
                            P.op("pe", (lambda i=i: lambda e: e.matmul(pb[1][:, tsl(i)], AT_[cur][:, tsl(i)], A_[cur][:, tsl(i)], start=True, stop=True))(),
                                 reads=[AT_tk[cur], A_tk[cur]], writes=[pb_tk[1]], sig=(i == 3))
                        if lvl < 6:
                            for i in range(4):
                                P.op("pe", (lambda i=i: lambda e: e.matmul(pb[2][:, tsl(i)], A_[cur][:, tsl(i)], AT_[cur][:, tsl(i)], start=True, stop=True))(),
                                     reads=[AT_tk[cur], A_tk[cur]], writes=[pb_tk[2]], sig=(i == 3))

                    def d2(lvl=lvl, cur=cur, nxt=nxt, f32lvl=f32lvl):
                        if f32lvl:
                            P.op("act", lambda e: e.activation(out=Pf[nxt][:], in_=pb[1][:], func=AF.Copy), reads=[pb_tk[1]], writes=[P_tk[nxt]])
                        if lvl >= NF32 and lvl < 6 or not f32lvl:
                            P.op("act", lambda e: e.activation(out=Ph[nxt][:], in_=pb[1][:], func=AF.Copy), reads=[pb_tk[1]], writes=[Ph_tk[nxt]])
                        if lvl < 6:
                            if lvl + 1 <= NF32:
                                P.op("dve", lambda e: e.tensor_copy(out=PTf[nxt][:], in_=pb[2][:]), reads=[pb_tk[2]], writes=[PT_tk[nxt]])
                            else:
                                P.op("dve", lambda e: e.tensor_copy(out=PTh[nxt][:], in_=pb[2][:]), reads=[pb_tk[2]], writes=[PTh_tk[nxt]])

                    def d3(lvl=lvl, cur=cur, nxt=nxt, f32lvl=f32lvl):
                        for i in range(4):
                            if f32lvl:
                                P.op("pe", (lambda i=i: lambda e: e.matmul(pb[3][:, tsl(i)], Pf[nxt][:, tsl(i)], Rb[:, tsl(i)], start=True, stop=True))(),
                                     reads=[P_tk[nxt], R_tk], writes=[pb_tk[3]], sig=(i == 3))
                            else:
                                P.op("pe", (lambda i=i: lambda e: e.matmul(pb[3][:, tsl(i)], Ph[nxt][:, tsl(i)], Rbf[:, tsl(i)], start=True, stop=True))(),
                                     reads=[Ph_tk[nxt], Rh_tk], writes=[pb_tk[3]], sig=(i == 3))

                    def d4(lvl=lvl, f32lvl=f32lvl):
                        if f32lvl:
                            P.op("dve", lambda e: e.tensor_tensor(out=Rb[:], in0=pb[3][:], in1=Rb[:], op=ALU.add), reads=[pb_tk[3], R_tk], writes=[R_tk])
                            if lvl == NF32:
                                P.op("act", lambda e: e.activation(out=Rbf[:], in_=Rb[:], func=AF.Copy), reads=[R_tk], writes=[Rh_tk])
                        else:
                            P.op("dve", lambda e: e.tensor_tensor(out=Rbf[:], in0=pb[3][:], in1=Rbf[:], op=ALU.add), reads=[pb_tk[3], Rh_tk], writes=[Rh_tk])
                    items += [d1, d2, d3, d4]

                def f1():
                    if NF32 >= 6:
                        P.op("act", lambda e: e.activation(out=Rbf[:], in_=Rb[:], func=AF.Copy), reads=[R_tk], writes=[Rh_tk])

                def f2():
                    for i in range(4):
                        P.op("pe", (lambda i=i: lambda e: e.matmul(pb[1][:, tsl(i)], Rbf[:, tsl(i)], bv[:, tsl(i)], start=True, stop=True))(), reads=[Rh_tk, T_[BV]], writes=[pb_tk[1]], sig=(i == 3))
                    for i in range(4):
                        P.op("pe", (lambda i=i: lambda e: e.matmul(pb[2][:, tsl(i)], kbg[:, tsl(i)], Rbf[:, tsl(i)], start=True, stop=True))(), reads=[Rh_tk, T_[KBG]], writes=[pb_tk[2]], sig=(i == 3))

                def f3():
                    P.op("act", lambda e: e.activation(out=u_[:], in_=pb[1][:], func=AF.Copy), reads=[pb_tk[1]], writes=[T_[U_]])
                    P.op("dve", lambda e: e.tensor_copy(out=wT[:], in_=pb[2][:]), reads=[pb_tk[2]], writes=[T_[WT]])
                items += [f1, f2, f3]
                return items

            def scan_items(h, g):
                par = g % 2
                t0 = 4 * g
                blk = slice(g * 512, (g + 1) * 512)
                T_ = grp_tk[par]
                kdec, zs, u_, wT, qdT, inT, goT = kdec2[par], zs2[par], u2[par], wT2[par], qdT2[par], inT2[par], goT2[par]
                OB, OBt = OBk[par], OB_tk[par]
                tsl = lambda i: slice(i * 128, (i + 1) * 128)
                items = []
                if g == 0:
                    def s0():
                        P.op("pool", lambda e: e.memset(S_f[:], 0.0), reads=[S_tk], writes=[S_tk])
                        P.op("pool", lambda e: e.memset(S_b[:], 0.0), reads=[S_tk], writes=[S_tk])
                    items.append(s0)
                for i in range(4):
                    t = t0 + i
                    sd = {}

                    def sa(i=i, sd=sd):
                        wsp, wsp_tk = PSs.next()
                        P.op("pe", (lambda: lambda e: e.matmul(wsp[:, 0:128], wT[:, tsl(i)], S_b[:], start=True, stop=True))(), reads=[T_[WT], S_tk], writes=[wsp_tk])
                        sd.update(wsp=wsp, wsp_tk=wsp_tk)

                    def sb_(i=i, sd=sd):
                        wsp, wsp_tk = sd["wsp"], sd["wsp_tk"]
                        vn, vn_tk = vn_r.next()
                        P.op("dve", (lambda: lambda e: e.tensor_tensor(out=vn[:], in0=u_[:, tsl(i)], in1=wsp[:, 0:128], op=ALU.subtract))(), reads=[T_[U_], wsp_tk], writes=[vn_tk])
                        sd.update(vn=vn, vn_tk=vn_tk)

                    def sc_(i=i, sd=sd):
                        wsp, wsp_tk, vn, vn_tk = sd["wsp"], sd["wsp_tk"], sd["vn"], sd["vn_tk"]
                        P.op("pe", (lambda: lambda e: e.matmul(OB[:, tsl(i)], qdT[:, tsl(i)], S_b[:], start=True, stop=False))(), reads=[T_[QDT], S_tk], writes=[OBt], sig=False)
                        P.op("pe", (lambda: lambda e: e.matmul(OB[:, tsl(i)], inT[:, tsl(i)], vn[:], start=False, stop=True))(), reads=[T_[INT], vn_tk], writes=[OBt], sig=False)
                        P.op("pe", (lambda: lambda e: e.matmul(wsp[:, 128:256], kdec[:, tsl(i)], vn[:], start=True, stop=True))(), reads=[T_[KDEC], vn_tk], writes=[wsp_tk])

                    def sd_(i=i, sd=sd, t=t):
                        wsp, wsp_tk = sd["wsp"], sd["wsp_tk"]
                        P.op("dve", (lambda: lambda e: e.scalar_tensor_tensor(out=S_f[:], in0=S_f[:], scalar=cd[:, t, h:h + 1], in1=wsp[:, 128:256], op0=ALU.mult, op1=ALU.add))(),
                             reads=[wsp_tk, par_tk, S_tk], writes=[S_tk])
                        P.op("act", lambda e: e.activation(out=S_b[:], in_=S_f[:], func=AF.Copy), reads=[S_tk], writes=[S_tk])
                    items += [sa, sb_, sc_, sd_]

                def e1():
                    P.op("act", lambda e: e.activation(out=eA[:], in_=OB[:], func=AF.Square), reads=[OBt], writes=[eA_tk])

                def e2():
                    P.op("dve", lambda e: e.tensor_reduce(out=st4[:, 0:4], in_=v3(eA[:]), axis=AX.X, op=ALU.add), reads=[eA_tk, st4_tk], writes=[st4_tk])
                    P.op("dve", lambda e: e.tensor_tensor(out=eA[:], in0=OB[:], in1=zs[:], op=ALU.mult), reads=[OBt, T_[ZS], st4_tk], writes=[eA_tk])

                def e3():
                    P.op("act", lambda e: e.activation(out=st4[:, 4:8], in_=st4[:, 0:4], func=AF.Ln, bias=epsb[:, 0:1], scale=1.0 / 128), reads=[st4_tk, cst_tk], writes=[st4_tk])
                    P.op("act", lambda e: e.activation(out=st4[:, 4:8], in_=st4[:, 4:8], func=AF.Exp, scale=-0.5), reads=[st4_tk], writes=[st4_tk])

                def e4():
                    P.op("pool", lambda e: e.tensor_tensor(out=v3(go_b[:]), in0=v3(eA[:]), in1=st4[:, 4:8].unsqueeze(2).to_broadcast([128, 4, 128]), op=ALU.mult),
                         reads=[eA_tk, st4_tk], writes=[go_tk])

                def e5():
                    gp, gp_tk = PSs.next()
                    for i in range(4):
                        P.op("pe", (lambda i=i: lambda e: e.matmul(gp[:, tsl(i)], go_b[:, tsl(i)], ident_bf[:], start=True, stop=True))(), reads=[go_tk, cb_tk], writes=[gp_tk], sig=(i == 3))
                    P.op("act", (lambda: lambda e: e.activation(out=goT[:], in_=gp[:], func=AF.Copy))(), reads=[gp_tk], writes=[T_[GOT]])
                items += [e1, e2, e3, e4, e5]
                wo, wo_tk = wo2[h % 2], wo2_tk[h % 2]
                for n0 in range(0, 8, 2):
                    def op_(n0=n0):
                        for n_ in range(n0, n0 + 2):
                            ps_, ps_tk_ = PSs.next()
                            P.op("pe", (lambda ps_=ps_, n_=n_: lambda e: e.matmul(ps_[:], wo[:, 0, n_ * 128:(n_ + 1) * 128], goT[:], start=True, stop=True))(),
                                 reads=[wo_tk, T_[GOT]], writes=[ps_tk_])
                            P.op("dve", (lambda ps_=ps_, n_=n_: lambda e: e.tensor_tensor(out=xT[:, n_, blk], in0=ps_[:], in1=xT[:, n_, blk], op=ALU.add))(),
                                 reads=[ps_tk_], writes=[xT_tk[n_][g]])
                    items.append(op_)
                return items

            def merge(a_, b_):
                if not a_:
                    return list(b_)
                if not b_:
                    return list(a_)
                if len(a_) < len(b_):
                    a_, b_ = b_, a_
                out = []
                ratio = len(a_) / float(len(b_))
                bi = 0
                for i_, it_ in enumerate(a_):
                    out.append(it_)
                    while bi < len(b_) and (bi + 1) * ratio <= i_ + 1:
                        out.append(b_[bi]); bi += 1
                out.extend(b_[bi:])
                return out

            for it_ in pstage_items(0):
                it_()
            prev_scan = []
            for h in range(8):
                for g in range(4):
                    for it_ in merge(prep_items(h, g), prev_scan):
                        it_()
                    prev_scan = scan_items(h, g)
                if h + 1 < 8:
                    for it_ in merge(pstage_items(h + 1), prev_scan):
                        it_()
                    prev_scan = []
            for it_ in prev_scan:
                it_()
            for tl in grp_tk:
                pass
            mytks.extend(vn_r.tks)
            retire(mytks)
            close_stage()

        def attention():
            with ExitStack() as ph:
                attention_body(ph)

        def attention_body(ph):
            import math
            from collections import deque
            lam_init = 0.8 - 0.6 * math.exp(-0.3 * 1)
            open_stage(ph, 2048)
            psb = lambda name, shape, dt=F32: ph.enter_context(nc.sbuf_tensor(name, list(shape), dt))
            mytks = []

            def tk():
                t = Tk(); mytks.append(t); return t
            PSa = Ring(banks[0:2], PS.tks[0:2])
            PSb = Ring(banks[2:4], PS.tks[2:4])
            acc_b, acc_tk = banks[4:8], PS.tks[4:8]
            pending = deque()

            def drain(n):
                while n > 0 and pending:
                    it_ = pending.popleft()
                    if it_ is not None:
                        it_()
                    n -= 1
            psem = P.new_dma_sem("at_prm"); lsem = P.new_dma_sem("at_lamb"); possem = P.new_dma_sem("at_pos"); gsem = P.new_dma_sem("at_g")
            prm = psb("at_prm", [128, 8]); prm_tk = tk()
            for half in range(2):
                P.dma("sp", psem, (lambda half=half: lambda e: e.dma_start(out=prm[half * 64:(half + 1) * 64, 0:1],
                      in_=dr["b_q_norm"][0, :].rearrange("(p o) -> p o", o=1), allow_slow_non_contiguous=True))(), writes=[prm_tk])
                P.dma("sp", psem, (lambda half=half: lambda e: e.dma_start(out=prm[half * 64:(half + 1) * 64, 1:2],
                      in_=dr["k_norm"][0, :].rearrange("(p o) -> p o", o=1), allow_slow_non_contiguous=True))(), writes=[prm_tk])
            P.dma("sp", psem, lambda e: e.dma_start(out=prm[:, 2:3], in_=dr["b_sub_norm"][0, :].rearrange("(p o) -> p o", o=1),
                  allow_slow_non_contiguous=True), writes=[prm_tk])
            lamb = psb("at_lamb", [128, 4, 64]); lamb_tk = tk()
            P.dma("sp", lsem, lambda e: e.dma_start(out=lamb[:].rearrange("p a b -> p (a b)"),
                  in_=dr["b_lambda"].rearrange("a b -> (a b)").partition_broadcast(128)), writes=[lamb_tk])
            gcol = psb("at_gcol", [128, 2, 128]); gcol_tk = tk()
            for half in range(2):
                P.dma("sp", gsem, (lambda half=half: lambda e: e.dma_start(out=gcol[:, 0, half * 64:(half + 1) * 64], in_=dr["b_q_norm"][0, :].partition_broadcast(128)))(), writes=[gcol_tk])
                P.dma("sp", gsem, (lambda half=half: lambda e: e.dma_start(out=gcol[:, 1, half * 64:(half + 1) * 64], in_=dr["k_norm"][0, :].partition_broadcast(128)))(), writes=[gcol_tk])
            P.op("dve", lambda e: e.tensor_scalar(out=gcol[:, 0, :], in0=gcol[:, 0, :], scalar1=0.125, scalar2=None, op0=ALU.mult), reads=[gcol_tk], writes=[gcol_tk])
            P.op("dve", lambda e: e.tensor_scalar(out=prm[:, 0:1], in0=prm[:, 0:1], scalar1=0.125, scalar2=None, op0=ALU.mult), reads=[prm_tk], writes=[prm_tk])
            P.op("dve", lambda e: e.tensor_scalar(out=prm[:, 2:3], in0=prm[:, 2:3], scalar1=float(1.0 - lam_init), scalar2=None, op0=ALU.mult), reads=[prm_tk], writes=[prm_tk])
            bo = psb("at_bo", [128, 2, 128]); bo_tk = tk()
            ig = psb("at_ig", [128, 2])
            P.op("dve", lambda e: e.tensor_tensor(out=ig[:], in0=prm[:, 0:2], in1=prm[:, 0:2], op=ALU.mult), reads=[prm_tk], writes=[bo_tk])
            P.op("dve", lambda e: e.reciprocal(out=prm[:, 6:8], in_=ig[:]), reads=[bo_tk, prm_tk], writes=[prm_tk])
            for i_ in range(2):
                P.op("dve", (lambda i_=i_: lambda e: e.tensor_scalar(out=bo[:, i_, :], in0=cst["blockones"][:], scalar1=prm[:, 6 + i_:7 + i_], scalar2=None, op0=ALU.mult))(),
                     reads=[prm_tk, cst_tk, bo_tk], writes=[bo_tk])
            lp = psb("at_lp", [128, 2, 64]); lp_tk = tk()
            P.op("dve", lambda e: e.tensor_tensor(out=lp[:, 0, :], in0=lamb[:, 0, :], in1=lamb[:, 1, :], op=ALU.mult), reads=[lamb_tk], writes=[lp_tk])
            P.op("dve", lambda e: e.tensor_tensor(out=lp[:, 1, :], in0=lamb[:, 2, :], in1=lamb[:, 3, :], op=ALU.mult), reads=[lamb_tk, lp_tk], writes=[lp_tk])
            P.op("dve", lambda e: e.tensor_reduce(out=prm[:, 4:6], in_=lp[:], axis=AX.X, op=ALU.add), reads=[lp_tk, prm_tk], writes=[prm_tk])
            P.op("act", lambda e: e.activation(out=prm[:, 4:6], in_=prm[:, 4:6], func=AF.Exp), reads=[prm_tk], writes=[prm_tk])
            P.op("dve", lambda e: e.tensor_tensor(out=prm[:, 3:4], in0=prm[:, 5:6], in1=prm[:, 4:5], op=ALU.subtract), reads=[prm_tk], writes=[prm_tk])
            P.op("dve", lambda e: e.tensor_scalar(out=prm[:, 3:4], in0=prm[:, 3:4], scalar1=float(-lam_init), scalar2=None, op0=ALU.add), reads=[prm_tk], writes=[prm_tk])
            ones_bf = psb("at_ones_bf", [128, 128], BF16); tri_bf = psb("at_tri_bf", [128, 128], BF16)
            cb_tk = tk()
            P.op("dve", lambda e: e.tensor_copy(out=ones_bf[:], in_=cst["ones"][:]), reads=[cst_tk], writes=[cb_tk])
            P.op("dve", lambda e: e.tensor_copy(out=tri_bf[:], in_=cst["trimask"][:]), reads=[cst_tk, cb_tk], writes=[cb_tk])
            cosT = psb("at_cosT", [128, S], BF16); sinT = psb("at_sinT", [128, S], BF16); cs_tk = tk()
            with ExitStack() as ph2:
                posi = ph2.enter_context(nc.sbuf_tensor("at_posi", [128, S], I32))
                ang = ph2.enter_context(nc.sbuf_tensor("at_ang", [128, S], F32))
                kf = ph2.enter_context(nc.sbuf_tensor("at_kf", [128, S], F32))
                t2 = [Tk() for _ in range(3)]
                P.dma("sp", possem, lambda e: e.dma_start(out=posi[:], in_=pos_d[0, :].partition_broadcast(128)), writes=[t2[0]])
                for which, dst in ((0, sinT), (1, cosT)):
                    P.op("dve", lambda e: e.tensor_copy(out=ang[:], in_=posi[:]), reads=[t2[0]], writes=[t2[1]])
                    P.op("dve", (lambda which=which: lambda e: e.tensor_scalar(out=ang[:], in0=ang[:], scalar1=cst["freq"][:, 0:1], scalar2=float(which * np.pi / 2), op0=ALU.mult, op1=ALU.add))(),
                         reads=[t2[1], cst_tk], writes=[t2[1]])
                    P.op("dve", lambda e: e.tensor_scalar(out=kf[:], in0=ang[:], scalar1=float(1.0 / (2 * np.pi)), scalar2=None, op0=ALU.mult), reads=[t2[1]], writes=[t2[2]])
                    P.op("dve", lambda e: e.tensor_copy(out=kf[:].bitcast(I32), in_=kf[:]), reads=[t2[2]], writes=[t2[2]])
                    P.op("dve", lambda e: e.tensor_copy(out=kf[:], in_=kf[:].bitcast(I32)), reads=[t2[2]], writes=[t2[2]])
                    P.op("dve", lambda e: e.scalar_tensor_tensor(out=ang[:], in0=kf[:], scalar=-6.28125, in1=ang[:], op0=ALU.mult, op1=ALU.add), reads=[t2[1], t2[2]], writes=[t2[1]])
                    P.op("dve", lambda e: e.scalar_tensor_tensor(out=ang[:], in0=kf[:], scalar=-float(2 * np.pi - 6.28125), in1=ang[:], op0=ALU.mult, op1=ALU.add), reads=[t2[1], t2[2]], writes=[t2[1]])
                    P.op("dve", lambda e: e.tensor_scalar(out=ang[:], in0=ang[:], scalar1=3.14159, scalar2=-3.14159, op0=ALU.min, op1=ALU.max), reads=[t2[1]], writes=[t2[1]])
                    P.op("act", (lambda dst=dst: lambda e: e.activation(out=dst[:], in_=ang[:], func=AF.Sin))(), reads=[t2[1]], writes=[cs_tk])
                retire(t2)
            wq = psb("at_wq", [128, 8, 128], BF16); wqr = psb("at_wqr", [128, 8, 128], BF16)
            wk = psb("at_wk", [128, 8, 128], BF16); wkr = psb("at_wkr", [128, 8, 128], BF16)
            wv = psb("at_wv", [128, 8, 128], BF16)
            wq_tk = tk(); wqr_tk = tk(); wk_tk = tk(); wkr_tk = tk(); wv_tk = tk()
            wo2 = [psb("at_wo%d" % i, [128, 2, D], BF16) for i in range(2)]; wo2_tk = [tk() for _ in range(2)]
            qT2 = [psb("at_qT%d" % i, [128, S], BF16) for i in range(2)]; qT2_tk = [tk() for _ in range(2)]
            kT2 = [psb("at_kT%d" % i, [128, S], BF16) for i in range(2)]; kT2_tk = [tk() for _ in range(2)]
            vt2 = [psb("at_vt%d" % i, [128, NT, 128], BF16) for i in range(2)]; vt2_tk = [tk() for _ in range(2)]
            aoT = psb("at_aoT", [128, 2, S], BF16); aoT_tk = tk()
            pT_r = Ring([psb("at_pT%d" % i, [128, 512], BF16) for i in range(4)])
            om_r = Ring([psb("at_om%d" % i, [128, 512]) for i in range(6)])
            rings = [pT_r, om_r]

            def rot_weights(w, w_tk, wr, wr_tk):
                wv4 = w[:].rearrange("p c (g r d) -> p c g r d", g=2, r=2)
                wr4 = wr[:].rearrange("p c (g r d) -> p c g r d", g=2, r=2)
                for g in range(2):
                    P.op("pool", (lambda g=g: lambda e: e.tensor_scalar(out=wr4[:, :, g, 0, :], in0=wv4[:, :, g, 1, :], scalar1=-1.0, scalar2=None, op0=ALU.mult))(),
                         reads=[w_tk], writes=[wr_tk])
                    P.op("pool", (lambda g=g: lambda e: e.tensor_copy(out=wr4[:, :, g, 1, :], in_=wv4[:, :, g, 0, :]))(), reads=[w_tk, wr_tk], writes=[wr_tk])

            def prologue_items(h):
                par = h % 2
                hs = slice(h * 128, (h + 1) * 128)
                qT, qT_tk, kT, kT_tk, vt, vt_tk = qT2[par], qT2_tk[par], kT2[par], kT2_tk[par], vt2[par], vt2_tk[par]
                items = []

                def it_w():
                    load_w(wq, wq_tk, dr["b_w_q"][:, hs], 8, 128, 3, colgain=gcol[:, 0, :], colgain_tk=gcol_tk)
                    rot_weights(wq, wq_tk, wqr, wqr_tk)
                    load_w(wk, wk_tk, dr["w_kv"][:, hs], 8, 128, 2, colgain=gcol[:, 1, :], colgain_tk=gcol_tk)
                    rot_weights(wk, wk_tk, wkr, wkr_tk)
                    load_w(wv, wv_tk, dr["w_kv"][:, 1024 + h * 128:1024 + (h + 1) * 128], 8, 128, 2)
                    if h % 2 == 0:
                        load_w(wo2[(h // 2) % 2], wo2_tk[(h // 2) % 2], dr["b_w_out"][h * 128:(h + 2) * 128, :], 2, D, None)
                items += [it_w, None, None, None, None, None]
                for tg in range(4):
                    st = {}

                    def v1(tg=tg, st=st):
                        ps, ps_tk = PSb.next()
                        for j in range(4):
                            t = tg * 4 + j
                            for c in range(8):
                                P.op("pe", (lambda ps=ps, j=j, t=t, c=c: lambda e: e.matmul(
                                    ps[:, j * 128:(j + 1) * 128], xnT[:, c, t * 128:(t + 1) * 128], wv[:, c, :], start=(c == 0), stop=(c == 7)))(),
                                    reads=[xnT_tk[tg], wv_tk], writes=[ps_tk], sig=(j == 3 and c == 7))
                        st.update(ps=ps, ps_tk=ps_tk)

                    def v2(tg=tg, st=st):
                        ps, ps_tk = st["ps"], st["ps_tk"]
                        P.op("dve", (lambda ps=ps: lambda e: e.tensor_copy(out=vt[:, tg * 4:(tg + 1) * 4, :], in_=ps[:].rearrange("p (j n) -> p j n", j=4)))(),
                             reads=[ps_tk], writes=[vt_tk])
                    items += [v1, None, v2]
                for (w, w_tk, wr, wr_tk, dst, dst_tk, gi) in ((wq, wq_tk, wqr, wqr_tk, qT, qT_tk, 0), (wk, wk_tk, wkr, wkr_tk, kT, kT_tk, 1)):
                    for tb in range(NB):
                        st = {}

                        def s1(w=w, w_tk=w_tk, wr=wr, wr_tk=wr_tk, tb=tb, st=st):
                            blk = slice(tb * 512, (tb + 1) * 512)
                            ps, ps_tk = PSb.next()
                            for c in range(8):
                                P.op("pe", (lambda ps=ps, c=c: lambda e: e.matmul(ps[:], w[:, c, :], xnT[:, c, blk], start=(c == 0), stop=(c == 7)))(),
                                     reads=[w_tk, xnT_tk[tb]], writes=[ps_tk], sig=(c == 7))
                            psr, psr_tk = PSb.next()
                            for c in range(8):
                                P.op("pe", (lambda psr=psr, c=c: lambda e: e.matmul(psr[:], wr[:, c, :], xnT[:, c, blk], start=(c == 0), stop=(c == 7)))(),
                                     reads=[wr_tk, xnT_tk[tb]], writes=[psr_tk], sig=(c == 7))
                            st.update(ps=ps, ps_tk=ps_tk, psr=psr, psr_tk=psr_tk)

                        def s2(tb=tb, st=st):
                            blk = slice(tb * 512, (tb + 1) * 512)
                            ps, ps_tk, psr, psr_tk = st["ps"], st["ps_tk"], st["psr"], st["psr_tk"]
                            sq, sq_tk = tmpf.next()
                            P.op("act", (lambda: lambda e: e.activation(out=sq[:], in_=ps[:], func=AF.Square))(), reads=[ps_tk], writes=[sq_tk])
                            t2_, t2_tk = om_r.next()
                            P.op("dve", (lambda: lambda e: e.tensor_tensor(out=t2_[:], in0=psr[:], in1=sinT[:, blk], op=ALU.mult))(), reads=[psr_tk, cs_tk], writes=[t2_tk])
                            t1, t1_tk = om_r.next()
                            P.op("dve", (lambda: lambda e: e.tensor_tensor(out=t1[:], in0=ps[:], in1=cosT[:, blk], op=ALU.mult))(), reads=[ps_tk, cs_tk], writes=[t1_tk])
                            st.update(sq=sq, sq_tk=sq_tk, t1=t1, t1_tk=t1_tk, t2=t2_, t2_tk=t2_tk)

                        def s3(st=st, gi=gi):
                            sq, sq_tk = st["sq"], st["sq_tk"]
                            ps2, ps2_tk = PSb.next()
                            P.op("pe", (lambda: lambda e: e.matmul(ps2[:], bo[:, gi, :], sq[:], start=True, stop=True))(), reads=[sq_tk, bo_tk], writes=[ps2_tk])
                            st.update(ps2=ps2, ps2_tk=ps2_tk)

                        def s4(st=st):
                            ps2, ps2_tk = st["ps2"], st["ps2_tk"]
                            rr, rr_tk = tmpf.next()
                            P.op("act", (lambda: lambda e: e.activation(out=rr[:], in_=ps2[:], func=AF.Ln, bias=epsb[:, 0:1], scale=1.0 / 64))(),
                                 reads=[ps2_tk, cst_tk], writes=[rr_tk])
                            P.op("act", (lambda: lambda e: e.activation(out=rr[:], in_=rr[:], func=AF.Exp, scale=-0.5))(), reads=[rr_tk], writes=[rr_tk])
                            st.update(rr=rr, rr_tk=rr_tk)

                        def s5(tb=tb, st=st, dst=dst, dst_tk=dst_tk):
                            blk = slice(tb * 512, (tb + 1) * 512)
                            t1, t1_tk, t2_, t2_tk, rr, rr_tk = st["t1"], st["t1_tk"], st["t2"], st["t2_tk"], st["rr"], st["rr_tk"]
                            P.op("pool", (lambda: lambda e: e.tensor_tensor(out=t1[:], in0=t1[:], in1=t2_[:], op=ALU.add))(), reads=[t1_tk, t2_tk], writes=[t1_tk])
                            P.op("pool", (lambda: lambda e: e.tensor_tensor(out=dst[:, blk], in0=t1[:], in1=rr[:], op=ALU.mult))(), reads=[t1_tk, rr_tk], writes=[dst_tk])
                        items += [s1, None, s2, s3, None, s4, s5]
                return items

            def outproj_items(hp):
                items = []
                wo, wo_tk = wo2[hp % 2], wo2_tk[hp % 2]
                for tb in range(NB):
                    for n0 in range(0, 8, 2):
                        def it(tb=tb, n0=n0):
                            blk = slice(tb * 512, (tb + 1) * 512)
                            for n_ in range(n0, n0 + 2):
                                ps, ps_tk = PSb.next()
                                for hh in range(2):
                                    P.op("pe", (lambda ps=ps, hh=hh, n_=n_: lambda e: e.matmul(
                                        ps[:], wo[:, hh, n_ * 128:(n_ + 1) * 128], aoT[:, hh, blk], start=(hh == 0), stop=(hh == 1)))(),
                                        reads=[wo_tk, aoT_tk], writes=[ps_tk], sig=(hh == 1))
                                P.op("dve", (lambda ps=ps, n_=n_: lambda e: e.tensor_tensor(out=xT[:, n_, blk], in0=ps[:], in1=xT[:, n_, blk], op=ALU.add))(),
                                     reads=[ps_tk], writes=[xT_tk[n_][tb]])
                        it.is_outproj = True
                        items.append(it)
                return items

            for it in prologue_items(0):
                if it is not None:
                    it()
            for h in range(8):
                par = h % 2
                qT, qT_tk, kT, kT_tk, vt, vt_tk = qT2[par], qT2_tk[par], kT2[par], kT2_tk[par], vt2[par], vt2_tk[par]
                drain(10 ** 9)
                if h + 1 < 8:
                    pending.extend(prologue_items(h + 1))
                tiles = []
                for qb in range(NB):
                    for kt in range(4 * qb + 4):
                        for m in range(2):
                            tiles.append((qb, kt, m))

                def emit_score(i):
                    qb, kt, m = tiles[i]
                    j = kt - 4 * qb
                    q0 = 0 if j < 0 else 128 * j
                    n = 512 - q0
                    ms = slice(m * 64, (m + 1) * 64)
                    ps, ps_tk = PSa.next()
                    P.op("pe", (lambda ps=ps, kT=kT, qT=qT: lambda e: e.matmul(ps[:, 0:n], kT[ms, kt * 128:(kt + 1) * 128], qT[ms, qb * 512 + q0:(qb + 1) * 512], start=True, stop=True))(),
                         reads=[kT_tk, qT_tk], writes=[ps_tk])
                    return ps, ps_tk
                nxt = emit_score(0)
                for i, (qb, kt, m) in enumerate(tiles):
                    ps, ps_tk = nxt
                    if i + 1 < len(tiles):
                        nxt = emit_score(i + 1)
                    j = kt - 4 * qb
                    q0 = 0 if j < 0 else 128 * j
                    n = 512 - q0
                    nkt = 4 * qb + 4
                    pT, pT_tk = pT_r.next()
                    P.op("act", (lambda pT=pT, ps=ps, n=n: lambda e: e.activation(out=pT[:, 0:n], in_=ps[:, 0:n], func=AF.Exp))(), reads=[ps_tk], writes=[pT_tk])
                    if j >= 0:
                        P.op("dve", (lambda pT=pT: lambda e: e.tensor_tensor(out=pT[:, 0:128], in0=pT[:, 0:128], in1=tri_bf[:], op=ALU.mult))(),
                             reads=[pT_tk, cb_tk], writes=[pT_tk])
                    P.op("pe", (lambda pT=pT, m=m, kt=kt, q0=q0, n=n, nkt=nkt, vt=vt: lambda e: e.matmul(
                        acc_b[2 * m][:, q0:512], vt[:, kt, :], pT[:, 0:n], start=(kt == 0), stop=(kt == nkt - 1)))(),
                        reads=[pT_tk, vt_tk], writes=[acc_tk[2 * m]], sig=False)
                    P.op("pe", (lambda pT=pT, m=m, kt=kt, q0=q0, n=n, nkt=nkt: lambda e: e.matmul(
                        acc_b[2 * m + 1][:, q0:512], ones_bf[:], pT[:, 0:n], start=(kt == 0), stop=(kt == nkt - 1)))(),
                        reads=[pT_tk, cb_tk], writes=[acc_tk[2 * m + 1]], sig=True)
                    drain(1)
                    if kt == nkt - 1 and m == 1:
                        blk = slice(qb * 512, (qb + 1) * 512)
                        while any(getattr(it_, "is_epi", False) for it_ in pending):
                            drain(1)
                        o_s = []; rd_s = []
                        for mm in range(2):
                            o_, o_tk = om_r.next()
                            P.op("act", (lambda o_=o_, mm=mm: lambda e: e.activation(out=o_[:], in_=acc_b[2 * mm][:], func=AF.Copy))(), reads=[acc_tk[2 * mm]], writes=[o_tk])
                            rd, rd_tk = om_r.next()
                            P.op("dve", (lambda rd=rd, mm=mm: lambda e: e.tensor_copy(out=rd[:], in_=acc_b[2 * mm + 1][:]))(), reads=[acc_tk[2 * mm + 1]], writes=[rd_tk])
                            o_s.append((o_, o_tk)); rd_s.append((rd, rd_tk))
                        st = {}

                        def e0(rd_s=rd_s):
                            for mm in range(2):
                                rd, rd_tk = rd_s[mm]
                                P.op("act", (lambda rd=rd: lambda e: e.activation(out=rd[:], in_=rd[:], func=AF.Ln))(), reads=[rd_tk], writes=[rd_tk])
                                P.op("act", (lambda rd=rd: lambda e: e.activation(out=rd[:], in_=rd[:], func=AF.Exp, scale=-1.0))(), reads=[rd_tk], writes=[rd_tk])

                        def e1(o_s=o_s, rd_s=rd_s, st=st):
                            for mm in range(2):
                                P.op("pool", (lambda mm=mm: lambda e: e.tensor_tensor(out=o_s[mm][0][:], in0=o_s[mm][0][:], in1=rd_s[mm][0][:], op=ALU.mult))(),
                                     reads=[o_s[mm][1], rd_s[mm][1]], writes=[o_s[mm][1]])
                            df, df_tk = rd_s[0]
                            P.op("dve", (lambda df=df: lambda e: e.scalar_tensor_tensor(out=df[:], in0=o_s[1][0][:], scalar=prm[:, 3:4], in1=o_s[0][0][:], op0=ALU.mult, op1=ALU.add))(),
                                 reads=[o_s[0][1], o_s[1][1], prm_tk], writes=[df_tk])
                            st.update(df=df, df_tk=df_tk)

                        def e1b(rd_s=rd_s, st=st):
                            df, df_tk = st["df"], st["df_tk"]
                            sq, sq_tk = rd_s[1]
                            P.op("act", (lambda: lambda e: e.activation(out=sq[:], in_=df[:], func=AF.Square))(), reads=[df_tk], writes=[sq_tk])
                            st.update(sq=sq, sq_tk=sq_tk)

                        def e1c(st=st):
                            sq, sq_tk = st["sq"], st["sq_tk"]
                            ps2, ps2_tk = PSb.next()
                            P.op("pe", (lambda: lambda e: e.matmul(ps2[:], cst["ones"][:], sq[:], start=True, stop=True))(), reads=[sq_tk, cst_tk], writes=[ps2_tk])
                            st.update(ps2=ps2, ps2_tk=ps2_tk)

                        def e2(st=st, h=h, blk=blk):
                            df, df_tk, ps2, ps2_tk = st["df"], st["df_tk"], st["ps2"], st["ps2_tk"]
                            rr, rr_tk = tmpf.next()
                            P.op("act", (lambda: lambda e: e.activation(out=rr[:], in_=ps2[:], func=AF.Ln, bias=epsb[:, 0:1], scale=1.0 / 128))(),
                                 reads=[ps2_tk, cst_tk], writes=[rr_tk])
                            P.op("act", (lambda: lambda e: e.activation(out=rr[:], in_=rr[:], func=AF.Exp, scale=-0.5))(), reads=[rr_tk], writes=[rr_tk])
                            st.update(rr=rr, rr_tk=rr_tk)

                        def e3(st=st, h=h, blk=blk):
                            df, df_tk, rr, rr_tk = st["df"], st["df_tk"], st["rr"], st["rr_tk"]
                            P.op("dve", (lambda: lambda e: e.scalar_tensor_tensor(
                                out=aoT[:, h % 2, blk], in0=df[:], scalar=prm[:, 2:3], in1=rr[:], op0=ALU.mult, op1=ALU.mult))(),
                                reads=[df_tk, rr_tk, prm_tk], writes=[aoT_tk])
                        epi = [e0, e1, None, e1b, e1c, None, e2, e3]
                        for it_ in epi:
                            if it_ is not None:
                                it_.is_epi = True
                        lst = list(pending)
                        pos_ = 0
                        for ii, it_ in enumerate(lst):
                            if getattr(it_, "is_outproj", False):
                                pos_ = ii + 1
                        pending.clear()
                        pending.extend(lst[:pos_] + epi + lst[pos_:])
                        if qb == NB - 1 and h % 2 == 1:
                            lst = list(pending)
                            pos_ = lst.index(e3) + 1
                            pending.clear()
                            pending.extend(lst[:pos_] + outproj_items(h // 2) + lst[pos_:])
            drain(10 ** 9)
            for r in rings:
                mytks.extend(r.tks)
            retire(mytks)
            close_stage()

        if "gdn" in stages:
            rms_to_xnT()
            gdn()
        if "mlp0" in stages:
            rms_to_xnT()
            mlp(0, 1)
        if "attn" in stages:
            rms_to_xnT()
            attention()
        if "mlp1" in stages:
            rms_to_xnT()
            mlp(1, 4)

        osem = [P.new_dma_sem("out%d" % i) for i in range(2)]
        phf = ExitStack()
        open_stage(phf, 1024)
        stage = stage_box["ring"]
        for t in range(NT):
            tb = t // 4
            o, o_tk = stage.next()
            osi = (stage.i - 1) % 2
            for cg in range(2):
                ps, ps_tk = PS.next()
                for j in range(4):
                    c = cg * 4 + j
                    P.op("pe", (lambda ps=ps, c=c, j=j, t=t: lambda e: e.transpose(
                        ps[:, j * 128:(j + 1) * 128], xT[:, c, t * 128:(t + 1) * 128], cst["ident"][:]))(),
                        reads=[xT_tk[c][tb], cst_tk], writes=[ps_tk], sig=(j == 3))
                evac_copy(o[:, cg * 512:(cg + 1) * 512], ps[:], reads=[ps_tk], writes=[o_tk])
            P.dma("sp", osem[osi], (lambda o=o, t=t: lambda e: e.dma_start(out=out_d[t * 128:(t + 1) * 128, :], in_=o[:, 0:1024]))(), reads=[o_tk])
        P.final_wait("sp", osem[0])
        P.final_wait("sp", osem[1])
        phf.close()
        P.emit()
        nops = {e: len(P.ops[e]) for e in P.ENG}
        nops["splits"] = P.split_log
    return nc, nops


_CACHE = {}


def kernel(**inputs):
    B = inputs["x"].shape[0]
    consts = host_consts()
    if "nc" not in _CACHE:
        _CACHE["nc"] = build_program()
    nc, _ = _CACHE["nc"]
    shared = {}
    f32 = lambda a: np.ascontiguousarray(np.asarray(a, dtype=np.float32))
    shared["a_norm"] = f32(inputs["a_norm"]).reshape(1, D)
    shared["a_w_in"] = f32(inputs["a_w_in"]).reshape(D, 4112)
    shared["a_conv_w"] = f32(inputs["a_conv_w"]).reshape(4, 3072)
    shared["a_a_log"] = f32(inputs["a_a_log"]).reshape(1, 8)
    shared["a_dt_bias"] = f32(inputs["a_dt_bias"]).reshape(1, 8)
    shared["a_out_norm"] = f32(inputs["a_out_norm"]).reshape(1, 128)
    shared["a_w_out"] = f32(inputs["a_w_out"]).reshape(D, D)
    shared["kv_norm"] = f32(inputs["kv_norm"]).reshape(1, D)
    shared["w_kv"] = f32(inputs["w_kv"]).reshape(D, 2048)
    shared["k_norm"] = f32(inputs["k_norm"]).reshape(1, 64)
    shared["b_norm"] = f32(inputs["b_norm"]).reshape(1, D)
    shared["b_w_q"] = f32(inputs["b_w_q"]).reshape(D, D)
    shared["b_q_norm"] = f32(inputs["b_q_norm"]).reshape(1, 64)
    shared["b_lambda"] = f32(inputs["b_lambda"]).reshape(4, 64)
    shared["b_sub_norm"] = f32(inputs["b_sub_norm"]).reshape(1, 128)
    shared["b_w_out"] = f32(inputs["b_w_out"]).reshape(D, D)
    shared["mlp_norm"] = f32(inputs["mlp_norm"]).reshape(2, D)
    shared["mlp_w1"] = f32(inputs["mlp_w1"]).reshape(2 * D, DFF)
    shared["mlp_w2"] = f32(inputs["mlp_w2"]).reshape(2 * DFF, D)
    for n in CONST_NAMES:
        shared["c_" + n] = consts[n]
    x = f32(inputs["x"])
    pos = np.ascontiguousarray(np.asarray(inputs["positions"], dtype=np.int32))
    in_maps = []
    for b in range(B):
        m = dict(shared)
        m["x"] = x[b]
        m["positions"] = pos[b].reshape(1, S)
        in_maps.append(m)
    res = run_bass_kernel_spmd(nc, in_maps, core_ids=list(range(B)))
    return np.stack([np.asarray(r["out"], dtype=np.float32) for r in res.results], axis=0)
```

```python
import numpy as np
import concourse.bass as bass
import concourse.mybir as mybir
from concourse.bass_utils import run_bass_kernel_spmd
from contextlib import ExitStack

F32 = mybir.dt.float32
BF16 = mybir.dt.bfloat16
I32 = mybir.dt.int32
AF = mybir.ActivationFunctionType
ALU = mybir.AluOpType
AX = mybir.AxisListType

S = 2048
D = 1024
NT = 16
NB = 4
DFF = 4096
EPS = 1e-6


INHERIT = {}


class Tk:
    __slots__ = ("name", "w", "r", "excl")

    def __init__(self, name="", excl=False):
        self.name = name
        self.w = None
        self.r = dict(INHERIT)
        self.excl = excl


def retire(tks):
    for t in tks:
        if t.w is not None:
            k, v = t.w
            if INHERIT.get(k, 0) < v:
                INHERIT[k] = v
        for k, v in t.r.items():
            if INHERIT.get(k, 0) < v:
                INHERIT[k] = v


class Prog:
    ENG = ("pe", "act", "dve", "pool", "sp")

    def __init__(self, nc, es):
        self.nc = nc
        self.es = es
        self.ops = {e: [] for e in self.ENG}
        self.sems = {}
        self.cnt = {}
        for e in self.ENG:
            self.sems[e] = es.enter_context(nc.semaphore("s_" + e))
            self.cnt[e] = 0
        self.waited = {e: {} for e in self.ENG}
        self.pending_nosig = {e: False for e in self.ENG}

    def new_dma_sem(self, name):
        key = "dma_" + name
        self.sems[key] = self.es.enter_context(self.nc.semaphore(key))
        self.cnt[key] = 0
        return key

    def _deps(self, eng, reads, writes):
        deps = {}

        def add(d):
            if d is None:
                return
            k, v = d
            if deps.get(k, 0) < v:
                deps[k] = v
        for t in reads:
            add(t.w)
            if t.excl:
                for k, v in t.r.items():
                    if k != eng:
                        add((k, v))
        for t in writes:
            add(t.w)
            for k, v in t.r.items():
                add((k, v))
        waits = []
        wd = self.waited[eng]
        for k, v in deps.items():
            if k == "pe" and eng == "pe":
                continue
            if wd.get(k, 0) < v:
                wd[k] = v
                waits.append((k, v))
        return waits

    def op(self, eng, fn, reads=(), writes=(), sig=True):
        assert sig or eng == "pe"
        waits = self._deps(eng, reads, writes)
        if sig:
            self.cnt[eng] += 1
            val = self.cnt[eng]
        else:
            val = self.cnt[eng] + 1
        self.pending_nosig[eng] = not sig
        self.ops[eng].append((waits, fn, (eng, 1) if sig else None))
        for t in reads:
            if t.r.get(eng, 0) < val:
                t.r[eng] = val
        for t in writes:
            t.w = (eng, val)
            t.r = {}

    def dma(self, q, semkey, fn, reads=(), writes=()):
        assert q == "sp"
        waits = self._deps(q, reads, writes)
        self.cnt[semkey] += 1
        val = self.cnt[semkey]
        self.ops[q].append((waits, fn, (semkey, 16)))
        for t in reads:
            if t.r.get(semkey, 0) < val:
                t.r[semkey] = val
        for t in writes:
            t.w = (semkey, val)
            t.r = {}

    def final_wait(self, eng, semkey):
        if self.cnt[semkey] > 0:
            self.ops[eng].append(([(semkey, self.cnt[semkey])], None, None))

    def emit(self):
        nc = self.nc
        for e in self.ENG:
            assert not self.pending_nosig[e], e
        actual = {k: [0] for k in self.sems if k.startswith("dma_")}
        self.split_log = []
        with nc.Block() as block:
            def run(engname):
                def body(eng):
                    for waits, fn, inc in self.ops[engname]:
                        for k, v in waits:
                            eng.wait_ge(self.sems[k], actual[k][v] if k in actual else v)
                        if fn is not None:
                            n0 = nc.n_instructions()
                            ins = fn(eng)
                            if inc is not None:
                                ins.then_inc(self.sems[inc[0]], inc[1])
                                if inc[0] in actual:
                                    k_ = max(1, nc.n_instructions() - n0)
                                    if k_ != 1:
                                        self.split_log.append((inc[0], len(actual[inc[0]]), k_))
                                    actual[inc[0]].append(actual[inc[0]][-1] + 16 * k_)
                return body
            block.sync(run("sp"))
            block.tensor(run("pe"))
            block.scalar(run("act"))
            block.vector(run("dve"))
            block.gpsimd(run("pool"))


class Ring:
    def __init__(self, bufs, tks=None):
        self.bufs = bufs
        self.tks = tks if tks is not None else [Tk() for _ in bufs]
        self.i = 0

    def next(self):
        b, t = self.bufs[self.i], self.tks[self.i]
        self.i = (self.i + 1) % len(self.bufs)
        return b, t


def host_consts():
    c = {}
    i = np.arange(128)
    c["ident"] = np.eye(128, dtype=np.float32)
    c["ones"] = np.ones((128, 128), dtype=np.float32)
    bo = np.zeros((128, 128), np.float32); bo[:64, :64] = 1; bo[64:, 64:] = 1
    c["blockones"] = bo
    rm = np.zeros((128, 128), np.float32)
    for p in range(128):
        if p % 64 < 32:
            rm[p + 32, p] = -1.0
        else:
            rm[p - 32, p] = 1.0
    c["rot"] = rm
    fr = np.zeros((128, 128), np.float32)
    fr[:, 0] = (10000.0 ** (-(np.arange(128) % 32).astype(np.float32) / 32.0)).astype(np.float32)
    c["freq"] = fr
    c["trimask"] = (i[:, None] <= i[None, :]).astype(np.float32)
    c["mnegL"] = np.where(i[:, None] >= i[None, :], 0.0, -30000.0).astype(np.float32)
    c["mnegU"] = np.ascontiguousarray(c["mnegL"].T)
    c["mnegLs"] = np.where(i[:, None] > i[None, :], 0.0, -30000.0).astype(np.float32)
    c["strictL"] = (i[:, None] > i[None, :]).astype(np.float32)
    c["strictU"] = np.ascontiguousarray(c["strictL"].T)
    return c


CONST_NAMES = ["ident", "ones", "blockones", "freq", "trimask", "mnegLs", "mnegU", "strictU"]


def build_program(stages=("gdn", "mlp0", "attn", "mlp1"), debug=False):
    INHERIT.clear()
    nc = bass.Bass("TRN2", target_bir_lowering=False)
    dr = {}

    def din(name, shape, dt=F32):
        dr[name] = nc.dram_tensor(name, list(shape), dt, kind="ExternalInput").ap()
        return dr[name]
    x_d = din("x", [S, D])
    pos_d = din("positions", [1, S], I32)
    din("a_norm", [1, D]); din("a_w_in", [D, 4112]); din("a_conv_w", [4, 3072]); din("a_a_log", [1, 8])
    din("a_dt_bias", [1, 8]); din("a_out_norm", [1, 128]); din("a_w_out", [D, D])
    din("kv_norm", [1, D]); din("w_kv", [D, 2048]); din("k_norm", [1, 64])
    din("b_norm", [1, D]); din("b_w_q", [D, D]); din("b_q_norm", [1, 64]); din("b_lambda", [4, 64])
    din("b_sub_norm", [1, 128]); din("b_w_out", [D, D])
    din("mlp_norm", [2, D]); din("mlp_w1", [2 * D, DFF]); din("mlp_w2", [2 * DFF, D])
    for n in CONST_NAMES:
        din("c_" + n, [128, 128])
    out_d = nc.dram_tensor("out", [S, D], F32, kind="ExternalOutput").ap()

    with ExitStack() as es:
        P = Prog(nc, es)

        def sb(name, shape, dt=F32):
            return es.enter_context(nc.sbuf_tensor(name, list(shape), dt))

        xT = sb("xT", [128, 8, S])
        xT_tk = [[Tk() for _ in range(NB)] for _ in range(8)]
        xnT = sb("xnT", [128, 8, S], BF16)
        xnT_tk = [Tk() for _ in range(NB)]
        banks = [es.enter_context(nc.psum_tensor("bank%d" % i, [128, 512], F32)) for i in range(8)]
        PS = Ring(banks, [Tk("bank%d" % i, excl=True) for i in range(8)])
        stage_sem = [P.new_dma_sem("stage%d" % i) for i in range(2)]
        stage_box = {"gen": 0}

        def open_stage(ph_, width):
            g_ = stage_box["gen"]
            stage_box["gen"] = g_ + 1
            bufs = [ph_.enter_context(nc.sbuf_tensor("stage_%d_%d" % (g_, i), [128, width], F32)) for i in range(2)]
            stage_box["ring"] = Ring(bufs)
            stage_box["width"] = width

        def close_stage():
            retire(stage_box["ring"].tks)
        tmpf = Ring([sb("tmpf%d" % i, [128, 512]) for i in range(3)])
        rstd_r = Ring([sb("rstd%d" % i, [128, 512]) for i in range(1)])
        cst = {}
        cst_tk = Tk()
        csem = P.new_dma_sem("const")
        for n in CONST_NAMES:
            cst[n] = sb("cs_" + n, [128, 128])
            P.dma("sp", csem, (lambda n=n: lambda e: e.dma_start(out=cst[n][:], in_=dr["c_" + n]))(), writes=[cst_tk])
        gains = sb("gains", [128, 5, 8])
        gain_src = [dr["a_norm"][0, :], dr["mlp_norm"][0, :], dr["kv_norm"][0, :], dr["b_norm"][0, :], dr["mlp_norm"][1, :]]
        for gi, src in enumerate(gain_src):
            P.dma("sp", csem, (lambda gi=gi, src=src: lambda e: e.dma_start(
                out=gains[:, gi, :], in_=src.rearrange("(c p) -> p c", p=128), allow_slow_non_contiguous=True))(), writes=[cst_tk])
        epsb = sb("epsb", [128, 1])
        P.op("pool", lambda e: e.memset(epsb[:], EPS), writes=[cst_tk])

        evac_flip = [0]

        def evac_copy(out_ap, in_ap, reads, writes):
            evac_flip[0] ^= 1
            if evac_flip[0]:
                P.op("act", lambda e: e.activation(out=out_ap, in_=in_ap, func=AF.Copy), reads=reads, writes=writes)
            else:
                P.op("dve", lambda e: e.tensor_copy(out=out_ap, in_=in_ap), reads=reads, writes=writes)

        ph0 = ExitStack()
        open_stage(ph0, 1024)
        stage = stage_box["ring"]
        for t in range(NT):
            st, st_tk = stage.next()
            si = (stage.i - 1) % 2
            P.dma("sp", stage_sem[si], (lambda st=st, t=t: lambda e: e.dma_start(
                out=st[:, 0:1024], in_=x_d[t * 128:(t + 1) * 128, :]))(), writes=[st_tk])
            for cg in range(2):
                ps, ps_tk = PS.next()
                for j in range(4):
                    c = cg * 4 + j
                    P.op("pe", (lambda ps=ps, st=st, c=c, j=j: lambda e: e.transpose(
                        ps[:, j * 128:(j + 1) * 128], st[:, c * 128:(c + 1) * 128], cst["ident"][:]))(),
                        reads=[st_tk, cst_tk], writes=[ps_tk], sig=(j == 3))
                tb = t // 4
                evac_copy(xT[:, cg * 4:(cg + 1) * 4, t * 128:(t + 1) * 128],
                          ps[:].rearrange("p (j n) -> p j n", j=4),
                          reads=[ps_tk], writes=[xT_tk[c][tb] for c in range(cg * 4, cg * 4 + 4)])
        close_stage()
        ph0.close()

        def rms_to_xnT():
            for tb in range(NB):
                blk = slice(tb * 512, (tb + 1) * 512)
                ps, ps_tk = PS.next()
                for c in range(8):
                    sq, sq_tk = tmpf.next()
                    P.op("act", (lambda sq=sq, c=c, blk=blk: lambda e: e.activation(out=sq[:], in_=xT[:, c, blk], func=AF.Square))(),
                         reads=[xT_tk[c][tb]], writes=[sq_tk])
                    P.op("pe", (lambda ps=ps, sq=sq, c=c: lambda e: e.matmul(ps[:], cst["ones"][:], sq[:], start=(c == 0), stop=(c == 7)))(),
                         reads=[sq_tk, cst_tk], writes=[ps_tk], sig=True)
                rs, rs_tk = tmpf.next()
                P.op("act", (lambda rs=rs, ps=ps: lambda e: e.activation(out=rs[:], in_=ps[:], func=AF.Ln, bias=epsb[:, 0:1], scale=1.0 / D))(),
                     reads=[ps_tk, cst_tk], writes=[rs_tk])
                rstd_bc, rstd_tk = rstd_r.next()
                P.op("act", (lambda rs=rs, rstd_bc=rstd_bc: lambda e: e.activation(out=rstd_bc[:], in_=rs[:], func=AF.Exp, scale=-0.5))(),
                     reads=[rs_tk], writes=[rstd_tk])
                for c in range(8):
                    eng = "dve" if c % 2 == 0 else "pool"
                    P.op(eng, (lambda c=c, blk=blk, rstd_bc=rstd_bc: lambda e: e.tensor_tensor(out=xnT[:, c, blk], in0=xT[:, c, blk], in1=rstd_bc[:], op=ALU.mult))(),
                         reads=[xT_tk[c][tb], rstd_tk], writes=[xnT_tk[tb]])

        wsem = [P.new_dma_sem("w%d" % i) for i in range(2)]

        def load_w(dst, dst_tk, src, kc, ncols, gain_idx, gain_c0=0, colgain=None, colgain_tk=None):
            stage = stage_box["ring"]
            per = max(1, stage_box["width"] // ncols)
            c0 = 0
            while c0 < kc:
                n = min(per, kc - c0)
                st, st_tk = stage.next()
                si = (stage.i - 1) % 2
                stv = st[:, 0:n * ncols].rearrange("p (c n) -> p c n", c=n)
                srcv = src[c0 * 128:(c0 + n) * 128, :].rearrange("(c p) n -> p c n", p=128)
                P.dma("sp", stage_sem[si], (lambda stv=stv, srcv=srcv: lambda e: e.dma_start(out=stv, in_=srcv))(), writes=[st_tk])
                dv = dst[:, c0:c0 + n, :]
                if gain_idx is None:
                    P.op("pool", (lambda dv=dv, stv=stv: lambda e: e.tensor_copy(out=dv, in_=stv))(), reads=[st_tk], writes=[dst_tk])
                elif colgain is None:
                    gv = gains[:, gain_idx, gain_c0 + c0:gain_c0 + c0 + n].unsqueeze(2).to_broadcast([128, n, ncols])
                    P.op("pool", (lambda dv=dv, stv=stv, gv=gv: lambda e: e.tensor_tensor(out=dv, in0=stv, in1=gv, op=ALU.mult))(),
                         reads=[st_tk, cst_tk], writes=[dst_tk])
                else:
                    gv = gains[:, gain_idx, gain_c0 + c0:gain_c0 + c0 + n].unsqueeze(2).to_broadcast([128, n, ncols])
                    cv = colgain.unsqueeze(1).to_broadcast([128, n, ncols])
                    P.op("pool", (lambda stv=stv, gv=gv: lambda e: e.tensor_tensor(out=stv, in0=stv, in1=gv, op=ALU.mult))(),
                         reads=[st_tk, cst_tk], writes=[st_tk])
                    P.op("pool", (lambda dv=dv, stv=stv, cv=cv: lambda e: e.tensor_tensor(out=dv, in0=stv, in1=cv, op=ALU.mult))(),
                         reads=[st_tk, colgain_tk], writes=[dst_tk])
                c0 += n

        def mlp(layer, gain_idx):
            with ExitStack() as ph:
                mlp_body(layer, gain_idx, ph)

        def mlp_body(layer, gain_idx, ph):
            FG = 512
            open_stage(ph, 2048)
            psb = lambda name, shape, dt=F32: ph.enter_context(nc.sbuf_tensor(name, list(shape), dt))
            w1r = Ring([psb("w1g%d_%d" % (layer, i), [128, 8, FG], BF16) for i in range(2)])
            w2r = Ring([psb("w2g%d_%d" % (layer, i), [128, FG // 128, D], BF16) for i in range(2)])
            hr = Ring([psb("hT%d_%d" % (layer, i), [128, FG // 128, 512], BF16) for i in range(2)])
            w1_d = dr["mlp_w1"][layer * D:(layer + 1) * D, :]
            w2_d = dr["mlp_w2"][layer * DFF:(layer + 1) * DFF, :]
            NG = DFF // FG
            wts_ = {}

            def load_group(g):
                w1, w1_tk = w1r.next()
                w2, w2_tk = w2r.next()
                load_w(w1, w1_tk, w1_d[:, g * FG:(g + 1) * FG], 8, FG, gain_idx)
                load_w(w2, w2_tk, w2_d[g * FG:(g + 1) * FG, :], FG // 128, D, None)
                wts_[g] = (w1, w1_tk, w2, w2_tk)

            def emit_h(g, tb):
                w1, w1_tk, _, _ = wts_[g]
                blk = slice(tb * 512, (tb + 1) * 512)
                hT, hT_tk = hr.next()
                for fc in range(FG // 128):
                    ps, ps_tk = PS.next()
                    for c in range(8):
                        P.op("pe", (lambda ps=ps, c=c, fc=fc: lambda e: e.matmul(
                            ps[:], w1[:, c, fc * 128:(fc + 1) * 128], xnT[:, c, blk], start=(c == 0), stop=(c == 7)))(),
                            reads=[w1_tk, xnT_tk[tb]], writes=[ps_tk], sig=(c == 7))
                    h1, h1_tk = tmpf.next()
                    P.op("act", (lambda h1=h1, ps=ps: lambda e: e.activation(out=h1[:], in_=ps[:], func=AF.Relu))(), reads=[ps_tk], writes=[h1_tk])
                    P.op("act", (lambda h1=h1, fc=fc: lambda e: e.activation(out=hT[:, fc, :], in_=h1[:], func=AF.Square))(), reads=[h1_tk], writes=[hT_tk])
                return hT, hT_tk

            def emit_o(g, tb, hT, hT_tk):
                _, _, w2, w2_tk = wts_[g]
                blk = slice(tb * 512, (tb + 1) * 512)
                for n in range(8):
                    ps, ps_tk = PS.next()
                    for fc in range(FG // 128):
                        P.op("pe", (lambda ps=ps, n=n, fc=fc: lambda e: e.matmul(
                            ps[:], w2[:, fc, n * 128:(n + 1) * 128], hT[:, fc, :], start=(fc == 0), stop=(fc == FG // 128 - 1)))(),
                            reads=[w2_tk, hT_tk], writes=[ps_tk], sig=(fc == FG // 128 - 1))
                    P.op("dve", (lambda ps=ps, n=n: lambda e: e.tensor_tensor(out=xT[:, n, blk], in0=ps[:], in1=xT[:, n, blk], op=ALU.add))(),
                         reads=[ps_tk], writes=[xT_tk[n][tb]])
            seq = [(g, tb) for g in range(NG) for tb in range(NB)]
            load_group(0)
            cur = emit_h(*seq[0])
            for i, (g, tb) in enumerate(seq):
                if tb == 0 and g + 1 < NG:
                    load_group(g + 1)
                nxt = emit_h(*seq[i + 1]) if i + 1 < len(seq) else None
                emit_o(g, tb, *cur)
                cur = nxt
            retire(w1r.tks + w2r.tks + hr.tks)
            close_stage()


        def gdn():
            with ExitStack() as ph:
                gdn_body(ph)

        def gdn_body(ph):
            import os
            open_stage(ph, 1024)
            psb = lambda name, shape, dt=F32: ph.enter_context(nc.sbuf_tensor(name, list(shape), dt))
            mytks = []

            def tk():
                t = Tk(); mytks.append(t); return t
            pb = banks[0:4]; pb_tk = PS.tks[0:4]
            qb_ = [banks[4], banks[6]]; qb_tk = [PS.tks[4], PS.tks[6]]
            PSs = Ring([banks[5]], [PS.tks[5]])
            OBk = [banks[7], banks[7]]; OB_tk = [PS.tks[7], PS.tks[7]]
            ident = cst["ident"]; onesf = cst["ones"]
            bc4 = lambda ap: ap.unsqueeze(1).to_broadcast([128, 4, 128])
            v3 = lambda ap: ap.rearrange("p (i e) -> p i e", i=4)
            ident_bf = psb("g_identbf", [128, 128], BF16); negones = psb("g_negones", [128, 128]); cb_tk = tk()
            P.op("dve", lambda e: e.tensor_copy(out=ident_bf[:], in_=ident[:]), reads=[cst_tk], writes=[cb_tk])
            P.op("dve", lambda e: e.tensor_scalar(out=negones[:], in0=onesf[:], scalar1=-1.0, scalar2=None, op0=ALU.mult), reads=[cst_tk, cb_tk], writes=[cb_tk])
            one_b = psb("g_oneb", [128, 1]); P.op("pool", lambda e: e.memset(one_b[:], 1.0), reads=[cb_tk], writes=[cb_tk])
            s1 = P.new_dma_sem("g_p1"); s2 = P.new_dma_sem("g_p2"); s3 = P.new_dma_sem("g_p3"); s4 = P.new_dma_sem("g_p4")
            alog = psb("g_alog", [128, 8]); dtb = psb("g_dtb", [128, 8]); onw = psb("g_onw", [128, 128]); cw = psb("g_cw", [128, 24, 4])
            alog_tk = tk(); dtb_tk = tk(); onw_tk = tk(); cw_tk = tk()
            P.dma("sp", s1, lambda e: e.dma_start(out=alog[:], in_=dr["a_a_log"][0, :].partition_broadcast(128)), writes=[alog_tk])
            P.dma("sp", s2, lambda e: e.dma_start(out=dtb[:], in_=dr["a_dt_bias"][0, :].partition_broadcast(128)), writes=[dtb_tk])
            P.dma("sp", s3, lambda e: e.dma_start(out=onw[:], in_=dr["a_out_norm"][0, :].partition_broadcast(128)), writes=[onw_tk])
            for j in range(4):
                P.dma("sp", s4, (lambda j=j: lambda e: e.dma_start(out=cw[:, :, j], in_=dr["a_conv_w"][j, :].rearrange("(c p) -> p c", p=128),
                      allow_slow_non_contiguous=True))(), writes=[cw_tk])
            P.op("act", lambda e: e.activation(out=alog[:], in_=alog[:], func=AF.Exp), reads=[alog_tk], writes=[alog_tk])
            P.op("dve", lambda e: e.tensor_scalar(out=alog[:], in0=alog[:], scalar1=-1.0, scalar2=None, op0=ALU.mult), reads=[alog_tk], writes=[alog_tk])
            wbg = psb("g_wbg", [128, 8, 16], BF16); wbg_tk = tk()
            load_w(wbg, wbg_tk, dr["a_w_in"][:, 4096:4112], 8, 16, 0)
            bg = psb("g_bg", [128, NT, 16]); bg_tk = tk()
            ps, ps_tk = pb[0], pb_tk[0]
            for t in range(NT):
                for c in range(8):
                    P.op("pe", (lambda t=t, c=c: lambda e: e.matmul(ps[:, t * 16:(t + 1) * 16], xnT[:, c, t * 128:(t + 1) * 128], wbg[:, c, :],
                         start=(c == 0), stop=(c == 7)))(), reads=[xnT_tk[t // 4], wbg_tk], writes=[ps_tk], sig=(t == NT - 1 and c == 7))
            P.op("dve", lambda e: e.tensor_copy(out=bg[:].rearrange("p t k -> p (t k)"), in_=ps[:, 0:256]), reads=[ps_tk], writes=[bg_tk])
            beta = psb("g_beta", [128, NT, 8]); gg = psb("g_g", [128, NT, 8]); par_tk = tk()
            P.op("act", lambda e: e.activation(out=beta[:], in_=bg[:, :, 0:8], func=AF.Exp, scale=-1.0), reads=[bg_tk], writes=[par_tk])
            P.op("act", lambda e: e.activation(out=beta[:], in_=beta[:], func=AF.Ln, bias=one_b[:, 0:1], scale=1.0), reads=[par_tk, cb_tk], writes=[par_tk])
            P.op("act", lambda e: e.activation(out=beta[:], in_=beta[:], func=AF.Exp, scale=-1.0), reads=[par_tk], writes=[par_tk])
            P.op("dve", lambda e: e.tensor_tensor(out=gg[:], in0=bg[:, :, 8:16], in1=dtb[:].unsqueeze(1).to_broadcast([128, NT, 8]), op=ALU.add),
                 reads=[bg_tk, dtb_tk, par_tk], writes=[par_tk])
            P.op("act", lambda e: e.activation(out=gg[:], in_=gg[:], func=AF.Exp), reads=[par_tk], writes=[par_tk])
            P.op("act", lambda e: e.activation(out=gg[:], in_=gg[:], func=AF.Ln, bias=one_b[:, 0:1], scale=1.0), reads=[par_tk, cb_tk], writes=[par_tk])
            P.op("dve", lambda e: e.tensor_tensor(out=gg[:], in0=gg[:], in1=alog[:].unsqueeze(1).to_broadcast([128, NT, 8]), op=ALU.mult),
                 reads=[par_tk, alog_tk], writes=[par_tk])
            gc = psb("g_gc", [128, NT, 8]); bgam = psb("g_bgam", [128, NT, 8]); kds = psb("g_kds", [128, NT, 8]); cd = psb("g_cd", [128, NT, 8])
            ps, ps_tk = pb[1], pb_tk[1]
            fl = lambda a_: a_[:].rearrange("p t k -> p (t k)")
            P.op("pe", lambda e: e.matmul(ps[:, 0:128], cst["trimask"][:], fl(gg), start=True, stop=True), reads=[par_tk, cst_tk], writes=[ps_tk], sig=False)
            P.op("pe", lambda e: e.matmul(ps[:, 128:256], onesf[:], fl(gg), start=True, stop=True), reads=[par_tk, cst_tk], writes=[ps_tk])
            P.op("dve", lambda e: e.tensor_copy(out=fl(gc), in_=ps[:, 0:128]), reads=[ps_tk], writes=[par_tk])
            P.op("dve", lambda e: e.tensor_tensor(out=fl(kds), in0=ps[:, 128:256], in1=fl(gc), op=ALU.subtract), reads=[ps_tk, par_tk], writes=[par_tk])
            P.op("act", lambda e: e.activation(out=fl(cd), in_=ps[:, 128:256], func=AF.Exp), reads=[ps_tk, par_tk], writes=[par_tk])
            P.op("act", lambda e: e.activation(out=fl(kds), in_=fl(kds), func=AF.Exp), reads=[par_tk], writes=[par_tk])
            P.op("act", lambda e: e.activation(out=fl(bgam), in_=fl(gc), func=AF.Exp), reads=[par_tk], writes=[par_tk])
            P.op("dve", lambda e: e.tensor_tensor(out=fl(bgam), in0=fl(bgam), in1=fl(beta), op=ALU.mult), reads=[par_tk], writes=[par_tk])
            wts = [psb("g_w%d" % k, [128, 8, 128], BF16) for k in range(3)]; wts_tk = [tk() for _ in range(3)]
            wz2 = [psb("g_wz%d" % k, [128, 8, 128], BF16) for k in range(2)]; wz2_tk = [tk() for _ in range(2)]
            wo2 = [psb("g_wo%d" % i, [128, 1, D], BF16) for i in range(2)]; wo2_tk = [tk() for _ in range(2)]
            cdiag = psb("g_cdiag", [128, 12, 128], BF16); cdiag_tk = tk()
            prjb = [psb("g_prjb%d" % k, [128, 516], BF16) for k in range(2)]; prjb_tk = [tk() for _ in range(2)]
            tails = psb("g_tails", [128, 3, 4], BF16); tails_tk = tk()
            qhT = psb("g_qhT", [128, S], BF16); khT = psb("g_khT", [128, S], BF16); vT = psb("g_vT", [128, S], BF16)
            qhT_tk = [tk() for _ in range(NB)]; khT_tk = [tk() for _ in range(NB)]; vT_tk = [tk() for _ in range(NB)]
            kbg2 = [psb("g_kbg%d" % i, [128, 512], BF16) for i in range(2)]; kdec2 = [psb("g_kdec%d" % i, [128, 512], BF16) for i in range(2)]
            bv2 = [psb("g_bv%d" % i, [128, 512], BF16) for i in range(2)]; zs2 = [psb("g_zs%d" % i, [128, 512], BF16) for i in range(2)]
            u2 = [psb("g_u%d" % i, [128, 512]) for i in range(2)]; wT2 = [psb("g_wT%d" % i, [128, 512], BF16) for i in range(2)]
            qdT2 = [psb("g_qdT%d" % i, [128, 512], BF16) for i in range(2)]; inT2 = [psb("g_inT%d" % i, [128, 512], BF16) for i in range(2)]
            goT2 = [psb("g_goT%d" % i, [128, 512], BF16) for i in range(2)]
            grp_tk = [[tk() for _ in range(10)] for _ in range(2)]
            KBG, KDEC, BV, ZS, U_, WT, QDT, INT, GOT = range(9)
            NF32 = int(os.environ.get("GDN_NF32", "6"))
            Pf = [psb("g_Pf%d" % a_, [128, 512]) for a_ in range(2)]; PTf = [psb("g_PTf%d" % a_, [128, 512]) for a_ in range(2)]
            Ph = [psb("g_Ph%d" % a_, [128, 512], BF16) for a_ in range(2)]; PTh = [psb("g_PTh%d" % a_, [128, 512], BF16) for a_ in range(2)]
            Rb = psb("g_R", [128, 512]); Rbf = psb("g_Rbf", [128, 512], BF16)
            Pb = Pf; PTb = PTf
            Ph_tk = [tk(), tk()]; PTh_tk = [tk(), tk()]; Rh_tk = tk()
            P_tk = [tk(), tk()]; PT_tk = [tk(), tk()]; R_tk = tk()
            gA = psb("g_gA", [128, 512]); gB = psb("g_gB", [128, 512]); gC = psb("g_gC", [128, 512]); gA_tk = tk(); gB_tk = tk(); gC_tk = tk()
            zA = psb("g_zA", [128, 512]); zA_tk = tk()
            eA = psb("g_eA", [128, 512]); eA_tk = tk()
            go_b = psb("g_go", [128, 512], BF16); go_tk = tk()
            S_f = psb("g_Sf", [128, 128]); S_b = psb("g_Sb", [128, 128], BF16); S_tk = tk()
            vn_r = Ring([psb("g_vn%d" % i, [128, 128], BF16) for i in range(2)])
            st4 = psb("g_st4", [128, 8]); st4_tk = tk()
            if os.environ.get("SBUF_DBG"):
                print("GDN sbuf bytes remaining per partition:", nc.sbuf_bytes_remaining)

            def pstage_items(h, tb):
                items = []
                if tb == 0:
                    def it_w():
                        for k in range(3):
                            load_w(wts[k], wts_tk[k], dr["a_w_in"][:, k * 1024 + h * 128:k * 1024 + (h + 1) * 128], 8, 128, 0)
                        load_w(wz2[h % 2], wz2_tk[h % 2], dr["a_w_in"][:, 3 * 1024 + h * 128:3 * 1024 + (h + 1) * 128], 8, 128, 0)
                        load_w(wo2[h % 2], wo2_tk[h % 2], dr["a_w_out"][h * 128:(h + 1) * 128, :], 1, D, None)
                        for sec in range(3):
                            for j in range(4):
                                P.op("dve", (lambda sec=sec, j=j: lambda e: e.tensor_scalar(out=cdiag[:, sec * 4 + j, :], in0=ident[:], scalar1=cw[:, sec * 8 + h, j:j + 1], scalar2=None, op0=ALU.mult))(),
                                     reads=[cst_tk, cw_tk], writes=[cdiag_tk])
                    items.append(it_w)
                blk = slice(tb * 512, (tb + 1) * 512)
                stt = [dict() for _ in range(3)]

                def proj(sec):
                    k_ = sec % 2
                    pj, pj_tk = prjb[k_], prjb_tk[k_]
                    bk, bk_tk = qb_[0], qb_tk[0]
                    for c in range(8):
                        P.op("pe", (lambda c=c: lambda e: e.matmul(bk[:], wts[sec][:, c, :], xnT[:, c, blk], start=(c == 0), stop=(c == 7)))(),
                             reads=[wts_tk[sec], xnT_tk[tb]], writes=[bk_tk], sig=(c == 7))
                    P.op("act", (lambda: lambda e: e.activation(out=pj[:, 3:515], in_=bk[:], func=AF.Copy))(), reads=[bk_tk], writes=[pj_tk])
                    if tb == 0:
                        P.op("pool", (lambda: lambda e: e.memset(pj[:, 0:3], 0.0))(), reads=[pj_tk], writes=[pj_tk])
                    else:
                        P.op("pool", (lambda: lambda e: e.tensor_copy(out=pj[:, 0:3], in_=tails[:, sec, 0:3]))(), reads=[tails_tk, pj_tk], writes=[pj_tk])

                def conv(sec):
                    k_ = sec % 2
                    pj, pj_tk = prjb[k_], prjb_tk[k_]
                    bk, bk_tk = qb_[1], qb_tk[1]
                    for j in range(4):
                        P.op("pe", (lambda j=j: lambda e: e.matmul(bk[:], cdiag[:, sec * 4 + j, :], pj[:, j:j + 512], start=(j == 0), stop=(j == 3)))(),
                             reads=[cdiag_tk, pj_tk], writes=[bk_tk], sig=(j == 3))
                    P.op("pool", (lambda: lambda e: e.tensor_copy(out=tails[:, sec, 0:3], in_=pj[:, 512:515]))(), reads=[pj_tk, tails_tk], writes=[tails_tk])
                    e_, e_tk = tmpf.next()
                    P.op("act", (lambda: lambda e: e.activation(out=e_[:], in_=bk[:], func=AF.Exp, scale=-1.0))(), reads=[bk_tk], writes=[e_tk])
                    P.op("act", (lambda: lambda e: e.activation(out=e_[:], in_=e_[:], func=AF.Ln, bias=one_b[:, 0:1], scale=1.0))(), reads=[e_tk, cb_tk], writes=[e_tk])
                    P.op("act", (lambda: lambda e: e.activation(out=e_[:], in_=e_[:], func=AF.Exp, scale=-1.0))(), reads=[e_tk], writes=[e_tk])
                    stt[sec].update(e=e_, e_tk=e_tk)

                def fin1(sec):
                    bk, bk_tk = qb_[1], qb_tk[1]
                    e_, e_tk = stt[sec]["e"], stt[sec]["e_tk"]
                    if sec == 2:
                        P.op("dve", (lambda: lambda e: e.tensor_tensor(out=vT[:, blk], in0=bk[:], in1=e_[:], op=ALU.mult))(), reads=[bk_tk, e_tk], writes=[vT_tk[tb]])
                        return
                    P.op("dve", (lambda: lambda e: e.tensor_tensor(out=e_[:], in0=bk[:], in1=e_[:], op=ALU.mult))(), reads=[bk_tk, e_tk], writes=[e_tk])
                    sq, sq_tk = tmpf.next()
                    P.op("act", (lambda: lambda e: e.activation(out=sq[:], in_=e_[:], func=AF.Square))(), reads=[e_tk], writes=[sq_tk])
                    P.op("pe", (lambda: lambda e: e.matmul(bk[:], onesf[:], sq[:], start=True, stop=True))(), reads=[sq_tk, cst_tk], writes=[bk_tk])
                    stt[sec].update(sq=sq, sq_tk=sq_tk)

                def fin2(sec):
                    if sec == 2:
                        return
                    bk, bk_tk = qb_[1], qb_tk[1]
                    e_, e_tk, sq, sq_tk = stt[sec]["e"], stt[sec]["e_tk"], stt[sec]["sq"], stt[sec]["sq_tk"]
                    P.op("act", (lambda: lambda e: e.activation(out=sq[:], in_=bk[:], func=AF.Ln, bias=epsb[:, 0:1], scale=1.0))(), reads=[bk_tk, cst_tk], writes=[sq_tk])
                    P.op("act", (lambda: lambda e: e.activation(out=sq[:], in_=sq[:], func=AF.Exp, scale=-0.5))(), reads=[sq_tk], writes=[sq_tk])
                    dst, dst_tk = (qhT, qhT_tk[tb]) if sec == 0 else (khT, khT_tk[tb])
                    sc = float(128 ** -0.5) if sec == 0 else 1.0
                    P.op("dve", (lambda: lambda e: e.scalar_tensor_tensor(out=dst[:, blk], in0=e_[:], scalar=sc, in1=sq[:], op0=ALU.mult, op1=ALU.mult))(),
                         reads=[e_tk, sq_tk], writes=[dst_tk])
                items.append(lambda: proj(0))
                for sec in range(3):
                    items.append((lambda sec=sec: lambda: conv(sec))())
                    if sec + 1 < 3:
                        items.append((lambda sec=sec: lambda: proj(sec + 1))())
                    items.append((lambda sec=sec: lambda: fin1(sec))())
                    items.append((lambda sec=sec: lambda: fin2(sec))())
                return items

            def prep_items(h, g):
                par = g % 2
                t0 = 4 * g
                blk = slice(g * 512, (g + 1) * 512)
                T_ = grp_tk[par]
                kbg, kdec, bv, zs, u_, wT, qdT, inT = kbg2[par], kdec2[par], bv2[par], zs2[par], u2[par], wT2[par], qdT2[par], inT2[par]
                scb = lambda X: X[:, t0:t0 + 4, h].unsqueeze(2).to_broadcast([128, 4, 128])
                tsl = lambda i: slice(i * 128, (i + 1) * 128)
                tok = lambda i: slice((t0 + i) * 128, (t0 + i + 1) * 128)
                items = []

                def p1():
                    P.op("pool", lambda e: e.tensor_copy(out=v3(gA[:]), in_=scb(gc)), reads=[par_tk], writes=[gA_tk])
                    P.op("pool", lambda e: e.tensor_tensor(out=v3(gB[:]), in0=bc4(ident[:]), in1=scb(gc), op=ALU.mult), reads=[par_tk, cst_tk], writes=[gB_tk])
                    P.op("pool", lambda e: e.tensor_tensor(out=v3(gC[:]), in0=bc4(ident[:]), in1=scb(beta), op=ALU.mult), reads=[par_tk, cst_tk], writes=[gC_tk])

                def p2():
                    P.op("pe", lambda e: e.matmul(pb[0][:], ident[:], gA[:], start=True, stop=False), reads=[gA_tk, cst_tk], writes=[pb_tk[0]], sig=False)
                    P.op("pe", lambda e: e.matmul(pb[0][:], negones[:], gB[:], start=False, stop=True), reads=[gB_tk, cb_tk], writes=[pb_tk[0]])
                    P.op("pe", lambda e: e.matmul(pb[1][:], onesf[:], gB[:], start=True, stop=True), reads=[gB_tk, cst_tk], writes=[pb_tk[1]])
                    P.op("pe", lambda e: e.matmul(pb[2][:], onesf[:], gC[:], start=True, stop=True), reads=[gC_tk, cst_tk], writes=[pb_tk[2]])

                def p3():
                    P.op("dve", lambda e: e.tensor_tensor(out=v3(gA[:]), in0=v3(pb[0][:]), in1=bc4(cst["mnegLs"][:]), op=ALU.add), reads=[pb_tk[0], cst_tk], writes=[gA_tk])
                    P.op("dve", lambda e: e.scalar_tensor_tensor(out=v3(gB[:]), in0=v3(pb[0][:]), scalar=-1.0, in1=bc4(cst["mnegU"][:]), op0=ALU.mult, op1=ALU.add),
                         reads=[pb_tk[0], cst_tk], writes=[gB_tk])

                def p4():
                    P.op("act", lambda e: e.activation(out=gA[:], in_=gA[:], func=AF.Exp), reads=[gA_tk], writes=[gA_tk])
                    P.op("act", lambda e: e.activation(out=gB[:], in_=gB[:], func=AF.Exp), reads=[gB_tk], writes=[gB_tk])
                    P.op("act", lambda e: e.activation(out=gC[:], in_=pb[1][:], func=AF.Exp), reads=[pb_tk[1]], writes=[gC_tk])

                def p5a():
                    for i in range(4):
                        P.op("pe", (lambda i=i: lambda e: e.matmul(pb[3][:, tsl(i)], khT[:, tok(i)], khT[:, tok(i)], start=True, stop=True))(), reads=[khT_tk[g]], writes=[pb_tk[3]], sig=(i == 3))
                    for i in range(4):
                        P.op("pe", (lambda i=i: lambda e: e.matmul(pb[0][:, tsl(i)], khT[:, tok(i)], qhT[:, tok(i)], start=True, stop=True))(), reads=[khT_tk[g], qhT_tk[g]], writes=[pb_tk[0]], sig=(i == 3))

                def p6():
                    P.op("dve", lambda e: e.tensor_tensor(out=inT[:], in0=pb[0][:], in1=gB[:], op=ALU.mult), reads=[pb_tk[0], gB_tk], writes=[T_[INT]])
                    P.op("pool", lambda e: e.tensor_tensor(out=qdT[:], in0=qhT[:, blk], in1=gC[:], op=ALU.mult), reads=[qhT_tk[g], gC_tk], writes=[T_[QDT]])
                    P.op("pool", lambda e: e.tensor_tensor(out=v3(gB[:]), in0=v3(gB[:]), in1=bc4(cst["strictU"][:]), op=ALU.mult), reads=[gB_tk, cst_tk, T_[INT]], writes=[gB_tk])

                def p7():
                    P.op("pool", lambda e: e.tensor_tensor(out=v3(gA[:]), in0=v3(gA[:]), in1=scb(beta), op=ALU.mult), reads=[gA_tk, par_tk], writes=[gA_tk])
                    P.op("dve", lambda e: e.tensor_tensor(out=gC[:], in0=pb[2][:], in1=gB[:], op=ALU.mult), reads=[pb_tk[2], gB_tk, T_[QDT]], writes=[gC_tk])

                def p8():
                    P.op("dve", lambda e: e.tensor_tensor(out=Pb[0][:], in0=pb[3][:], in1=gA[:], op=ALU.mult), reads=[pb_tk[3], gA_tk], writes=[P_tk[0]])
                    P.op("dve", lambda e: e.tensor_tensor(out=PTb[0][:], in0=pb[3][:], in1=gC[:], op=ALU.mult), reads=[pb_tk[3], gC_tk], writes=[PT_tk[0]])
                    P.op("pool", lambda e: e.tensor_tensor(out=v3(Rb[:]), in0=bc4(ident[:]), in1=v3(PTb[0][:]), op=ALU.subtract), reads=[cst_tk, PT_tk[0]], writes=[R_tk])

                def p5b():
                    for i in range(4):
                        P.op("pe", (lambda i=i: lambda e: e.matmul(pb[1][:, tsl(i)], khT[:, tok(i)], ident_bf[:], start=True, stop=True))(), reads=[khT_tk[g], cb_tk], writes=[pb_tk[1]], sig=(i == 3))
                    for i in range(4):
                        P.op("pe", (lambda i=i: lambda e: e.matmul(pb[2][:, tsl(i)], vT[:, tok(i)], ident_bf[:], start=True, stop=True))(), reads=[vT_tk[g], cb_tk], writes=[pb_tk[2]], sig=(i == 3))

                def p5b2():
                    P.op("dve", lambda e: e.tensor_tensor(out=v3(kbg[:]), in0=v3(pb[1][:]), in1=scb(bgam), op=ALU.mult), reads=[pb_tk[1], par_tk], writes=[T_[KBG]])
                    P.op("dve", lambda e: e.tensor_tensor(out=v3(kdec[:]), in0=v3(pb[1][:]), in1=scb(kds), op=ALU.mult), reads=[pb_tk[1], par_tk], writes=[T_[KDEC]])
                    P.op("dve", lambda e: e.tensor_tensor(out=v3(bv[:]), in0=v3(pb[2][:]), in1=scb(beta), op=ALU.mult), reads=[pb_tk[2], par_tk], writes=[T_[BV]])

                def p5c():
                    for i in range(4):
                        for c in range(8):
                            P.op("pe", (lambda i=i, c=c: lambda e: e.matmul(pb[0][:, tsl(i)], xnT[:, c, tok(i)], wz2[h % 2][:, c, :], start=(c == 0), stop=(c == 7)))(),
                                 reads=[xnT_tk[g], wz2_tk[h % 2]], writes=[pb_tk[0]], sig=(i == 3 and c == 7))

                def p5c2():
                    P.op("act", lambda e: e.activation(out=zA[:], in_=pb[0][:], func=AF.Exp, scale=-1.0), reads=[pb_tk[0]], writes=[zA_tk])
                    P.op("act", lambda e: e.activation(out=zA[:], in_=zA[:], func=AF.Ln, bias=one_b[:, 0:1], scale=1.0), reads=[zA_tk, cb_tk], writes=[zA_tk])
                    P.op("act", lambda e: e.activation(out=zA[:], in_=zA[:], func=AF.Exp, scale=-1.0), reads=[zA_tk], writes=[zA_tk])

                def p5c3():
                    P.op("dve", lambda e: e.tensor_tensor(out=zA[:], in0=pb[0][:], in1=zA[:], op=ALU.mult), reads=[pb_tk[0], zA_tk], writes=[zA_tk])
                    P.op("pool", lambda e: e.tensor_tensor(out=v3(zs[:]), in0=v3(zA[:]), in1=bc4(onw[:]), op=ALU.mult), reads=[zA_tk, onw_tk], writes=[T_[ZS]])
                items += [p1, p2, p3, p4, p5a, p6, p7, p8, p5b, p5b2, p5c, p5c2, p5c3]
                if NF32 == 0:
                    def c0():
                        P.op("act", lambda e: e.activation(out=Ph[0][:], in_=Pf[0][:], func=AF.Copy), reads=[P_tk[0]], writes=[Ph_tk[0]])
                        P.op("act", lambda e: e.activation(out=PTh[0][:], in_=PTf[0][:], func=AF.Copy), reads=[PT_tk[0]], writes=[PTh_tk[0]])
                        P.op("act", lambda e: e.activation(out=Rbf[:], in_=Rb[:], func=AF.Copy), reads=[R_tk], writes=[Rh_tk])
                    items.append(c0)
                dbl_stages = []
                for lvl in range(1, 7):
                    cur = (lvl - 1) % 2
                    nxt = 1 - cur
                    f32lvl = lvl <= NF32

                    def d1(lvl=lvl, cur=cur, nxt=nxt, f32lvl=f32lvl):
                        A_, AT_, A_tk, AT_tk = (Pf, PTf, P_tk, PT_tk) if f32lvl else (Ph, PTh, Ph_tk, PTh_tk)
                        for i in range(4):
                            P.op("pe", (lambda i=i: lambda e: e.matmul(pb[1][:, tsl(i)], AT_[cur][:, tsl(i)], A_[cur][:, tsl(i)], start=True, stop=True))(),
                                 reads=[AT_tk[cur], A_tk[cur]], writes=[pb_tk[1]], sig=(i == 3))
                        if lvl < 6:
                            for i in range(4):
                                P.op("pe", (lambda i=i: lambda e: e.matmul(pb[2][:, tsl(i)], A_[cur][:, tsl(i)], AT_[cur][:, tsl(i)], start=True, stop=True))(),
                                     reads=[AT_tk[cur], A_tk[cur]], writes=[pb_tk[2]], sig=(i == 3))

                    def d2(lvl=lvl, cur=cur, nxt=nxt, f32lvl=f32lvl):
                        if f32lvl:
                            P.op("act", lambda e: e.activation(out=Pf[nxt][:], in_=pb[1][:], func=AF.Copy), reads=[pb_tk[1]], writes=[P_tk[nxt]])
                        if lvl >= NF32 and lvl < 6 or not f32lvl:
                            P.op("act", lambda e: e.activation(out=Ph[nxt][:], in_=pb[1][:], func=AF.Copy), reads=[pb_tk[1]], writes=[Ph_tk[nxt]])
                        if lvl < 6:
                            if lvl + 1 <= NF32:
                                P.op("dve", lambda e: e.tensor_copy(out=PTf[nxt][:], in_=pb[2][:]), reads=[pb_tk[2]], writes=[PT_tk[nxt]])
                            else:
                                P.op("dve", lambda e: e.tensor_copy(out=PTh[nxt][:], in_=pb[2][:]), reads=[pb_tk[2]], writes=[PTh_tk[nxt]])

                    def d3(lvl=lvl, cur=cur, nxt=nxt, f32lvl=f32lvl):
                        for i in range(4):
                            if f32lvl:
                                P.op("pe", (lambda i=i: lambda e: e.matmul(pb[3][:, tsl(i)], Pf[nxt][:, tsl(i)], Rb[:, tsl(i)], start=True, stop=True))(),
                                     reads=[P_tk[nxt], R_tk], writes=[pb_tk[3]], sig=(i == 3))
                            else:
                                P.op("pe", (lambda i=i: lambda e: e.matmul(pb[3][:, tsl(i)], Ph[nxt][:, tsl(i)], Rbf[:, tsl(i)], start=True, stop=True))(),
                                     reads=[Ph_tk[nxt], Rh_tk], writes=[pb_tk[3]], sig=(i == 3))

                    def d4(lvl=lvl, f32lvl=f32lvl):
                        if f32lvl:
                            P.op("dve", lambda e: e.tensor_tensor(out=Rb[:], in0=pb[3][:], in1=Rb[:], op=ALU.add), reads=[pb_tk[3], R_tk], writes=[R_tk])
                            if lvl == NF32:
                                P.op("act", lambda e: e.activation(out=Rbf[:], in_=Rb[:], func=AF.Copy), reads=[R_tk], writes=[Rh_tk])
                        else:
                            P.op("dve", lambda e: e.tensor_tensor(out=Rbf[:], in0=pb[3][:], in1=Rbf[:], op=ALU.add), reads=[pb_tk[3], Rh_tk], writes=[Rh_tk])
                    dbl_stages.append((d1, d2, d3, d4))
                for li in range(6):
                    d1_, d2_, d3_, d4_ = dbl_stages[li]
                    if li == 0:
                        items += [d1_, d2_]
                    if li + 1 < 6:
                        n1, n2, _, _ = dbl_stages[li + 1]
                        items += [n1, d3_, n2, d4_]
                    else:
                        items += [d3_, d4_]

                def f1():
                    if NF32 >= 6:
                        P.op("act", lambda e: e.activation(out=Rbf[:], in_=Rb[:], func=AF.Copy), reads=[R_tk], writes=[Rh_tk])

                def f2():
                    for i in range(4):
                        P.op("pe", (lambda i=i: lambda e: e.matmul(pb[1][:, tsl(i)], Rbf[:, tsl(i)], bv[:, tsl(i)], start=True, stop=True))(), reads=[Rh_tk, T_[BV]], writes=[pb_tk[1]], sig=(i == 3))
                    for i in range(4):
                        P.op("pe", (lambda i=i: lambda e: e.matmul(pb[2][:, tsl(i)], kbg[:, tsl(i)], Rbf[:, tsl(i)], start=True, stop=True))(), reads=[Rh_tk, T_[KBG]], writes=[pb_tk[2]], sig=(i == 3))

                def f3():
                    P.op("act", lambda e: e.activation(out=u_[:], in_=pb[1][:], func=AF.Copy), reads=[pb_tk[1]], writes=[T_[U_]])
                    P.op("dve", lambda e: e.tensor_copy(out=wT[:], in_=pb[2][:]), reads=[pb_tk[2]], writes=[T_[WT]])
                items += [f1, f2, f3]
                return items

            def scan_items(h, g):
                par = g % 2
                t0 = 4 * g
                blk = slice(g * 512, (g + 1) * 512)
                T_ = grp_tk[par]
                kdec, zs, u_, wT, qdT, inT, goT = kdec2[par], zs2[par], u2[par], wT2[par], qdT2[par], inT2[par], goT2[par]
                OB, OBt = OBk[par], OB_tk[par]
                tsl = lambda i: slice(i * 128, (i + 1) * 128)
                items = []
                if g == 0:
                    def s0():
                        P.op("pool", lambda e: e.memset(S_f[:], 0.0), reads=[S_tk], writes=[S_tk])
                        P.op("pool", lambda e: e.memset(S_b[:], 0.0), reads=[S_tk], writes=[S_tk])
                    items.append(s0)
                for i in range(4):
                    t = t0 + i
                    sd = {}

                    def sa(i=i, sd=sd):
                        wsp, wsp_tk = PSs.next()
                        P.op("pe", (lambda: lambda e: e.matmul(wsp[:, 0:128], wT[:, tsl(i)], S_b[:], start=True, stop=True))(), reads=[T_[WT], S_tk], writes=[wsp_tk])
                        sd.update(wsp=wsp, wsp_tk=wsp_tk)

                    def sb_(i=i, sd=sd):
                        wsp, wsp_tk = sd["wsp"], sd["wsp_tk"]
                        vn, vn_tk = vn_r.next()
                        P.op("dve", (lambda: lambda e: e.tensor_tensor(out=vn[:], in0=u_[:, tsl(i)], in1=wsp[:, 0:128], op=ALU.subtract))(), reads=[T_[U_], wsp_tk], writes=[vn_tk])
                        sd.update(vn=vn, vn_tk=vn_tk)

                    def sc_(i=i, sd=sd):
                        wsp, wsp_tk, vn, vn_tk = sd["wsp"], sd["wsp_tk"], sd["vn"], sd["vn_tk"]
                        P.op("pe", (lambda: lambda e: e.matmul(OB[:, tsl(i)], qdT[:, tsl(i)], S_b[:], start=True, stop=False))(), reads=[T_[QDT], S_tk], writes=[OBt], sig=False)
                        P.op("pe", (lambda: lambda e: e.matmul(OB[:, tsl(i)], inT[:, tsl(i)], vn[:], start=False, stop=True))(), reads=[T_[INT], vn_tk], writes=[OBt], sig=False)
                        P.op("pe", (lambda: lambda e: e.matmul(wsp[:, 128:256], kdec[:, tsl(i)], vn[:], start=True, stop=True))(), reads=[T_[KDEC], vn_tk], writes=[wsp_tk])

                    def sd_(i=i, sd=sd, t=t):
                        wsp, wsp_tk = sd["wsp"], sd["wsp_tk"]
                        P.op("dve", (lambda: lambda e: e.scalar_tensor_tensor(out=S_f[:], in0=S_f[:], scalar=cd[:, t, h:h + 1], in1=wsp[:, 128:256], op0=ALU.mult, op1=ALU.add))(),
                             reads=[wsp_tk, par_tk, S_tk], writes=[S_tk])
                        P.op("act", lambda e: e.activation(out=S_b[:], in_=S_f[:], func=AF.Copy), reads=[S_tk], writes=[S_tk])
                    items += [sa, sb_, sc_, sd_]

                def e1():
                    P.op("act", lambda e: e.activation(out=eA[:], in_=OB[:], func=AF.Square), reads=[OBt], writes=[eA_tk])

                def e2():
                    P.op("dve", lambda e: e.tensor_reduce(out=st4[:, 0:4], in_=v3(eA[:]), axis=AX.X, op=ALU.add), reads=[eA_tk, st4_tk], writes=[st4_tk])
                    P.op("dve", lambda e: e.tensor_tensor(out=eA[:], in0=OB[:], in1=zs[:], op=ALU.mult), reads=[OBt, T_[ZS], st4_tk], writes=[eA_tk])

                def e3():
                    P.op("act", lambda e: e.activation(out=st4[:, 4:8], in_=st4[:, 0:4], func=AF.Ln, bias=epsb[:, 0:1], scale=1.0 / 128), reads=[st4_tk, cst_tk], writes=[st4_tk])
                    P.op("act", lambda e: e.activation(out=st4[:, 4:8], in_=st4[:, 4:8], func=AF.Exp, scale=-0.5), reads=[st4_tk], writes=[st4_tk])

                def e4():
                    P.op("pool", lambda e: e.tensor_tensor(out=v3(go_b[:]), in0=v3(eA[:]), in1=st4[:, 4:8].unsqueeze(2).to_broadcast([128, 4, 128]), op=ALU.mult),
                         reads=[eA_tk, st4_tk], writes=[go_tk])

                def e5():
                    gp, gp_tk = PSs.next()
                    for i in range(4):
                        P.op("pe", (lambda i=i: lambda e: e.matmul(gp[:, tsl(i)], go_b[:, tsl(i)], ident_bf[:], start=True, stop=True))(), reads=[go_tk, cb_tk], writes=[gp_tk], sig=(i == 3))
                    P.op("act", (lambda: lambda e: e.activation(out=goT[:], in_=gp[:], func=AF.Copy))(), reads=[gp_tk], writes=[T_[GOT]])
                items += [e1, e2, e3, e4, e5]
                wo, wo_tk = wo2[h % 2], wo2_tk[h % 2]
                for n0 in range(0, 8, 2):
                    def op_(n0=n0):
                        for n_ in range(n0, n0 + 2):
                            ps_, ps_tk_ = PSs.next()
                            P.op("pe", (lambda ps_=ps_, n_=n_: lambda e: e.matmul(ps_[:], wo[:, 0, n_ * 128:(n_ + 1) * 128], goT[:], start=True, stop=True))(),
                                 reads=[wo_tk, T_[GOT]], writes=[ps_tk_])
                            P.op("dve", (lambda ps_=ps_, n_=n_: lambda e: e.tensor_tensor(out=xT[:, n_, blk], in0=ps_[:], in1=xT[:, n_, blk], op=ALU.add))(),
                                 reads=[ps_tk_], writes=[xT_tk[n_][g]])
                    items.append(op_)
                return items

            def merge(a_, b_):
                if not a_:
                    return list(b_)
                if not b_:
                    return list(a_)
                if len(a_) < len(b_):
                    a_, b_ = b_, a_
                out = []
                ratio = len(a_) / float(len(b_))
                bi = 0
                for i_, it_ in enumerate(a_):
                    out.append(it_)
                    while bi < len(b_) and (bi + 1) * ratio <= i_ + 1:
                        out.append(b_[bi]); bi += 1
                out.extend(b_[bi:])
                return out

            for tb_ in range(NB):
                for it_ in pstage_items(0, tb_):
                    it_()
            prev_scan = []
            for h in range(8):
                for g in range(4):
                    if g >= 1 and h + 1 < 8:
                        pit = pstage_items(h + 1, g - 1)
                    elif g == 0 and h >= 1:
                        pit = pstage_items(h, 3)
                    else:
                        pit = []
                    for it_ in merge(merge(prep_items(h, g), prev_scan), pit):
                        it_()
                    prev_scan = scan_items(h, g)
            for it_ in prev_scan:
                it_()
            for tl in grp_tk:
                pass
            mytks.extend(vn_r.tks)
            retire(mytks)
            close_stage()

        def attention():
            with ExitStack() as ph:
                attention_body(ph)

        def attention_body(ph):
            import math
            from collections import deque
            lam_init = 0.8 - 0.6 * math.exp(-0.3 * 1)
            open_stage(ph, 2048)
            psb = lambda name, shape, dt=F32: ph.enter_context(nc.sbuf_tensor(name, list(shape), dt))
            mytks = []

            def tk():
                t = Tk(); mytks.append(t); return t
            PSa = Ring(banks[0:2], PS.tks[0:2])
            PSb = Ring(banks[2:4], PS.tks[2:4])
            acc_b, acc_tk = banks[4:8], PS.tks[4:8]
            pending = deque()

            def drain(n):
                while n > 0 and pending:
                    it_ = pending.popleft()
                    if it_ is not None:
                        it_()
                    n -= 1
            psem = P.new_dma_sem("at_prm"); lsem = P.new_dma_sem("at_lamb"); possem = P.new_dma_sem("at_pos"); gsem = P.new_dma_sem("at_g")
            prm = psb("at_prm", [128, 8]); prm_tk = tk()
            for half in range(2):
                P.dma("sp", psem, (lambda half=half: lambda e: e.dma_start(out=prm[half * 64:(half + 1) * 64, 0:1],
                      in_=dr["b_q_norm"][0, :].rearrange("(p o) -> p o", o=1), allow_slow_non_contiguous=True))(), writes=[prm_tk])
                P.dma("sp", psem, (lambda half=half: lambda e: e.dma_start(out=prm[half * 64:(half + 1) * 64, 1:2],
                      in_=dr["k_norm"][0, :].rearrange("(p o) -> p o", o=1), allow_slow_non_contiguous=True))(), writes=[prm_tk])
            P.dma("sp", psem, lambda e: e.dma_start(out=prm[:, 2:3], in_=dr["b_sub_norm"][0, :].rearrange("(p o) -> p o", o=1),
                  allow_slow_non_contiguous=True), writes=[prm_tk])
            lamb = psb("at_lamb", [128, 4, 64]); lamb_tk = tk()
            P.dma("sp", lsem, lambda e: e.dma_start(out=lamb[:].rearrange("p a b -> p (a b)"),
                  in_=dr["b_lambda"].rearrange("a b -> (a b)").partition_broadcast(128)), writes=[lamb_tk])
            gcol = psb("at_gcol", [128, 2, 128]); gcol_tk = tk()
            for half in range(2):
                P.dma("sp", gsem, (lambda half=half: lambda e: e.dma_start(out=gcol[:, 0, half * 64:(half + 1) * 64], in_=dr["b_q_norm"][0, :].partition_broadcast(128)))(), writes=[gcol_tk])
                P.dma("sp", gsem, (lambda half=half: lambda e: e.dma_start(out=gcol[:, 1, half * 64:(half + 1) * 64], in_=dr["k_norm"][0, :].partition_broadcast(128)))(), writes=[gcol_tk])
            P.op("dve", lambda e: e.tensor_scalar(out=gcol[:, 0, :], in0=gcol[:, 0, :], scalar1=0.125, scalar2=None, op0=ALU.mult), reads=[gcol_tk], writes=[gcol_tk])
            P.op("dve", lambda e: e.tensor_scalar(out=prm[:, 0:1], in0=prm[:, 0:1], scalar1=0.125, scalar2=None, op0=ALU.mult), reads=[prm_tk], writes=[prm_tk])
            P.op("dve", lambda e: e.tensor_scalar(out=prm[:, 2:3], in0=prm[:, 2:3], scalar1=float(1.0 - lam_init), scalar2=None, op0=ALU.mult), reads=[prm_tk], writes=[prm_tk])
            bo = psb("at_bo", [128, 2, 128]); bo_tk = tk()
            ig = psb("at_ig", [128, 2])
            P.op("dve", lambda e: e.tensor_tensor(out=ig[:], in0=prm[:, 0:2], in1=prm[:, 0:2], op=ALU.mult), reads=[prm_tk], writes=[bo_tk])
            P.op("dve", lambda e: e.reciprocal(out=prm[:, 6:8], in_=ig[:]), reads=[bo_tk, prm_tk], writes=[prm_tk])
            for i_ in range(2):
                P.op("dve", (lambda i_=i_: lambda e: e.tensor_scalar(out=bo[:, i_, :], in0=cst["blockones"][:], scalar1=prm[:, 6 + i_:7 + i_], scalar2=None, op0=ALU.mult))(),
                     reads=[prm_tk, cst_tk, bo_tk], writes=[bo_tk])
            lp = psb("at_lp", [128, 2, 64]); lp_tk = tk()
            P.op("dve", lambda e: e.tensor_tensor(out=lp[:, 0, :], in0=lamb[:, 0, :], in1=lamb[:, 1, :], op=ALU.mult), reads=[lamb_tk], writes=[lp_tk])
            P.op("dve", lambda e: e.tensor_tensor(out=lp[:, 1, :], in0=lamb[:, 2, :], in1=lamb[:, 3, :], op=ALU.mult), reads=[lamb_tk, lp_tk], writes=[lp_tk])
            P.op("dve", lambda e: e.tensor_reduce(out=prm[:, 4:6], in_=lp[:], axis=AX.X, op=ALU.add), reads=[lp_tk, prm_tk], writes=[prm_tk])
            P.op("act", lambda e: e.activation(out=prm[:, 4:6], in_=prm[:, 4:6], func=AF.Exp), reads=[prm_tk], writes=[prm_tk])
            P.op("dve", lambda e: e.tensor_tensor(out=prm[:, 3:4], in0=prm[:, 5:6], in1=prm[:, 4:5], op=ALU.subtract), reads=[prm_tk], writes=[prm_tk])
            P.op("dve", lambda e: e.tensor_scalar(out=prm[:, 3:4], in0=prm[:, 3:4], scalar1=float(-lam_init), scalar2=None, op0=ALU.add), reads=[prm_tk], writes=[prm_tk])
            ones_bf = psb("at_ones_bf", [128, 128], BF16); tri_bf = psb("at_tri_bf", [128, 128], BF16)
            cb_tk = tk()
            P.op("dve", lambda e: e.tensor_copy(out=ones_bf[:], in_=cst["ones"][:]), reads=[cst_tk], writes=[cb_tk])
            P.op("dve", lambda e: e.tensor_copy(out=tri_bf[:], in_=cst["trimask"][:]), reads=[cst_tk, cb_tk], writes=[cb_tk])
            cosT = psb("at_cosT", [128, S], BF16); sinT = psb("at_sinT", [128, S], BF16); cs_tk = tk()
            with ExitStack() as ph2:
                posi = ph2.enter_context(nc.sbuf_tensor("at_posi", [128, S], I32))
                ang = ph2.enter_context(nc.sbuf_tensor("at_ang", [128, S], F32))
                kf = ph2.enter_context(nc.sbuf_tensor("at_kf", [128, S], F32))
                t2 = [Tk() for _ in range(3)]
                P.dma("sp", possem, lambda e: e.dma_start(out=posi[:], in_=pos_d[0, :].partition_broadcast(128)), writes=[t2[0]])
                for which, dst in ((0, sinT), (1, cosT)):
                    P.op("dve", lambda e: e.tensor_copy(out=ang[:], in_=posi[:]), reads=[t2[0]], writes=[t2[1]])
                    P.op("dve", (lambda which=which: lambda e: e.tensor_scalar(out=ang[:], in0=ang[:], scalar1=cst["freq"][:, 0:1], scalar2=float(which * np.pi / 2), op0=ALU.mult, op1=ALU.add))(),
                         reads=[t2[1], cst_tk], writes=[t2[1]])
                    P.op("dve", lambda e: e.tensor_scalar(out=kf[:], in0=ang[:], scalar1=float(1.0 / (2 * np.pi)), scalar2=None, op0=ALU.mult), reads=[t2[1]], writes=[t2[2]])
                    P.op("dve", lambda e: e.tensor_copy(out=kf[:].bitcast(I32), in_=kf[:]), reads=[t2[2]], writes=[t2[2]])
                    P.op("dve", lambda e: e.tensor_copy(out=kf[:], in_=kf[:].bitcast(I32)), reads=[t2[2]], writes=[t2[2]])
                    P.op("dve", lambda e: e.scalar_tensor_tensor(out=ang[:], in0=kf[:], scalar=-6.28125, in1=ang[:], op0=ALU.mult, op1=ALU.add), reads=[t2[1], t2[2]], writes=[t2[1]])
                    P.op("dve", lambda e: e.scalar_tensor_tensor(out=ang[:], in0=kf[:], scalar=-float(2 * np.pi - 6.28125), in1=ang[:], op0=ALU.mult, op1=ALU.add), reads=[t2[1], t2[2]], writes=[t2[1]])
                    P.op("dve", lambda e: e.tensor_scalar(out=ang[:], in0=ang[:], scalar1=3.14159, scalar2=-3.14159, op0=ALU.min, op1=ALU.max), reads=[t2[1]], writes=[t2[1]])
                    P.op("act", (lambda dst=dst: lambda e: e.activation(out=dst[:], in_=ang[:], func=AF.Sin))(), reads=[t2[1]], writes=[cs_tk])
                retire(t2)
            wq = psb("at_wq", [128, 8, 128], BF16); wqr = psb("at_wqr", [128, 8, 128], BF16)
            wk = psb("at_wk", [128, 8, 128], BF16); wkr = psb("at_wkr", [128, 8, 128], BF16)
            wv = psb("at_wv", [128, 8, 128], BF16)
            wq_tk = tk(); wqr_tk = tk(); wk_tk = tk(); wkr_tk = tk(); wv_tk = tk()
            wo2 = [psb("at_wo%d" % i, [128, 2, D], BF16) for i in range(2)]; wo2_tk = [tk() for _ in range(2)]
            qT2 = [psb("at_qT%d" % i, [128, S], BF16) for i in range(2)]; qT2_tk = [tk() for _ in range(2)]
            kT2 = [psb("at_kT%d" % i, [128, S], BF16) for i in range(2)]; kT2_tk = [tk() for _ in range(2)]
            vt2 = [psb("at_vt%d" % i, [128, NT, 128], BF16) for i in range(2)]; vt2_tk = [tk() for _ in range(2)]
            aoT = psb("at_aoT", [128, 2, S], BF16); aoT_tk = tk()
            pT_r = Ring([psb("at_pT%d" % i, [128, 512], BF16) for i in range(4)])
            om_r = Ring([psb("at_om%d" % i, [128, 512]) for i in range(6)])
            rings = [pT_r, om_r]

            def rot_weights(w, w_tk, wr, wr_tk):
                wv4 = w[:].rearrange("p c (g r d) -> p c g r d", g=2, r=2)
                wr4 = wr[:].rearrange("p c (g r d) -> p c g r d", g=2, r=2)
                for g in range(2):
                    P.op("pool", (lambda g=g: lambda e: e.tensor_scalar(out=wr4[:, :, g, 0, :], in0=wv4[:, :, g, 1, :], scalar1=-1.0, scalar2=None, op0=ALU.mult))(),
                         reads=[w_tk], writes=[wr_tk])
                    P.op("pool", (lambda g=g: lambda e: e.tensor_copy(out=wr4[:, :, g, 1, :], in_=wv4[:, :, g, 0, :]))(), reads=[w_tk, wr_tk], writes=[wr_tk])

            def prologue_items(h):
                par = h % 2
                hs = slice(h * 128, (h + 1) * 128)
                qT, qT_tk, kT, kT_tk, vt, vt_tk = qT2[par], qT2_tk[par], kT2[par], kT2_tk[par], vt2[par], vt2_tk[par]
                items = []

                def it_w():
                    load_w(wq, wq_tk, dr["b_w_q"][:, hs], 8, 128, 3, colgain=gcol[:, 0, :], colgain_tk=gcol_tk)
                    rot_weights(wq, wq_tk, wqr, wqr_tk)
                    load_w(wk, wk_tk, dr["w_kv"][:, hs], 8, 128, 2, colgain=gcol[:, 1, :], colgain_tk=gcol_tk)
                    rot_weights(wk, wk_tk, wkr, wkr_tk)
                    load_w(wv, wv_tk, dr["w_kv"][:, 1024 + h * 128:1024 + (h + 1) * 128], 8, 128, 2)
                    if h % 2 == 0:
                        load_w(wo2[(h // 2) % 2], wo2_tk[(h // 2) % 2], dr["b_w_out"][h * 128:(h + 2) * 128, :], 2, D, None)
                items += [it_w, None, None, None, None, None]
                for tg in range(4):
                    st = {}

                    def v1(tg=tg, st=st):
                        ps, ps_tk = PSb.next()
                        for j in range(4):
                            t = tg * 4 + j
                            for c in range(8):
                                P.op("pe", (lambda ps=ps, j=j, t=t, c=c: lambda e: e.matmul(
                                    ps[:, j * 128:(j + 1) * 128], xnT[:, c, t * 128:(t + 1) * 128], wv[:, c, :], start=(c == 0), stop=(c == 7)))(),
                                    reads=[xnT_tk[tg], wv_tk], writes=[ps_tk], sig=(j == 3 and c == 7))
                        st.update(ps=ps, ps_tk=ps_tk)

                    def v2(tg=tg, st=st):
                        ps, ps_tk = st["ps"], st["ps_tk"]
                        P.op("dve", (lambda ps=ps: lambda e: e.tensor_copy(out=vt[:, tg * 4:(tg + 1) * 4, :], in_=ps[:].rearrange("p (j n) -> p j n", j=4)))(),
                             reads=[ps_tk], writes=[vt_tk])
                    items += [v1, None, v2]
                for (w, w_tk, wr, wr_tk, dst, dst_tk, gi) in ((wq, wq_tk, wqr, wqr_tk, qT, qT_tk, 0), (wk, wk_tk, wkr, wkr_tk, kT, kT_tk, 1)):
                    for tb in range(NB):
                        st = {}

                        def s1(w=w, w_tk=w_tk, wr=wr, wr_tk=wr_tk, tb=tb, st=st):
                            blk = slice(tb * 512, (tb + 1) * 512)
                            ps, ps_tk = PSb.next()
                            for c in range(8):
                                P.op("pe", (lambda ps=ps, c=c: lambda e: e.matmul(ps[:], w[:, c, :], xnT[:, c, blk], start=(c == 0), stop=(c == 7)))(),
                                     reads=[w_tk, xnT_tk[tb]], writes=[ps_tk], sig=(c == 7))
                            psr, psr_tk = PSb.next()
                            for c in range(8):
                                P.op("pe", (lambda psr=psr, c=c: lambda e: e.matmul(psr[:], wr[:, c, :], xnT[:, c, blk], start=(c == 0), stop=(c == 7)))(),
                                     reads=[wr_tk, xnT_tk[tb]], writes=[psr_tk], sig=(c == 7))
                            st.update(ps=ps, ps_tk=ps_tk, psr=psr, psr_tk=psr_tk)

                        def s2(tb=tb, st=st):
                            blk = slice(tb * 512, (tb + 1) * 512)
                            ps, ps_tk, psr, psr_tk = st["ps"], st["ps_tk"], st["psr"], st["psr_tk"]
                            sq, sq_tk = tmpf.next()
                            P.op("act", (lambda: lambda e: e.activation(out=sq[:], in_=ps[:], func=AF.Square))(), reads=[ps_tk], writes=[sq_tk])
                            t2_, t2_tk = om_r.next()
                            P.op("dve", (lambda: lambda e: e.tensor_tensor(out=t2_[:], in0=psr[:], in1=sinT[:, blk], op=ALU.mult))(), reads=[psr_tk, cs_tk], writes=[t2_tk])
                            t1, t1_tk = om_r.next()
                            P.op("dve", (lambda: lambda e: e.tensor_tensor(out=t1[:], in0=ps[:], in1=cosT[:, blk], op=ALU.mult))(), reads=[ps_tk, cs_tk], writes=[t1_tk])
                            st.update(sq=sq, sq_tk=sq_tk, t1=t1, t1_tk=t1_tk, t2=t2_, t2_tk=t2_tk)

                        def s3(st=st, gi=gi):
                            sq, sq_tk = st["sq"], st["sq_tk"]
                            ps2, ps2_tk = PSb.next()
                            P.op("pe", (lambda: lambda e: e.matmul(ps2[:], bo[:, gi, :], sq[:], start=True, stop=True))(), reads=[sq_tk, bo_tk], writes=[ps2_tk])
                            st.update(ps2=ps2, ps2_tk=ps2_tk)

                        def s4(st=st):
                            ps2, ps2_tk = st["ps2"], st["ps2_tk"]
                            rr, rr_tk = tmpf.next()
                            P.op("act", (lambda: lambda e: e.activation(out=rr[:], in_=ps2[:], func=AF.Ln, bias=epsb[:, 0:1], scale=1.0 / 64))(),
                                 reads=[ps2_tk, cst_tk], writes=[rr_tk])
                            P.op("act", (lambda: lambda e: e.activation(out=rr[:], in_=rr[:], func=AF.Exp, scale=-0.5))(), reads=[rr_tk], writes=[rr_tk])
                            st.update(rr=rr, rr_tk=rr_tk)

                        def s5(tb=tb, st=st, dst=dst, dst_tk=dst_tk):
                            blk = slice(tb * 512, (tb + 1) * 512)
                            t1, t1_tk, t2_, t2_tk, rr, rr_tk = st["t1"], st["t1_tk"], st["t2"], st["t2_tk"], st["rr"], st["rr_tk"]
                            P.op("pool", (lambda: lambda e: e.tensor_tensor(out=t1[:], in0=t1[:], in1=t2_[:], op=ALU.add))(), reads=[t1_tk, t2_tk], writes=[t1_tk])
                            P.op("pool", (lambda: lambda e: e.tensor_tensor(out=dst[:, blk], in0=t1[:], in1=rr[:], op=ALU.mult))(), reads=[t1_tk, rr_tk], writes=[dst_tk])
                        items += [s1, None, s2, s3, None, s4, s5]
                return items

            def outproj_items(hp):
                items = []
                wo, wo_tk = wo2[hp % 2], wo2_tk[hp % 2]
                for tb in range(NB):
                    for n0 in range(0, 8, 2):
                        def it(tb=tb, n0=n0):
                            blk = slice(tb * 512, (tb + 1) * 512)
                            for n_ in range(n0, n0 + 2):
                                ps, ps_tk = PSb.next()
                                for hh in range(2):
                                    P.op("pe", (lambda ps=ps, hh=hh, n_=n_: lambda e: e.matmul(
                                        ps[:], wo[:, hh, n_ * 128:(n_ + 1) * 128], aoT[:, hh, blk], start=(hh == 0), stop=(hh == 1)))(),
                                        reads=[wo_tk, aoT_tk], writes=[ps_tk], sig=(hh == 1))
                                P.op("dve", (lambda ps=ps, n_=n_: lambda e: e.tensor_tensor(out=xT[:, n_, blk], in0=ps[:], in1=xT[:, n_, blk], op=ALU.add))(),
                                     reads=[ps_tk], writes=[xT_tk[n_][tb]])
                        it.is_outproj = True
                        items.append(it)
                return items

            for it in prologue_items(0):
                if it is not None:
                    it()
            for h in range(8):
                par = h % 2
                qT, qT_tk, kT, kT_tk, vt, vt_tk = qT2[par], qT2_tk[par], kT2[par], kT2_tk[par], vt2[par], vt2_tk[par]
                drain(10 ** 9)
                if h + 1 < 8:
                    pending.extend(prologue_items(h + 1))
                tiles = []
                for qb in range(NB):
                    for kt in range(4 * qb + 4):
                        for m in range(2):
                            tiles.append((qb, kt, m))

                def emit_score(i):
                    qb, kt, m = tiles[i]
                    j = kt - 4 * qb
                    q0 = 0 if j < 0 else 128 * j
                    n = 512 - q0
                    ms = slice(m * 64, (m + 1) * 64)
                    ps, ps_tk = PSa.next()
                    P.op("pe", (lambda ps=ps, kT=kT, qT=qT: lambda e: e.matmul(ps[:, 0:n], kT[ms, kt * 128:(kt + 1) * 128], qT[ms, qb * 512 + q0:(qb + 1) * 512], start=True, stop=True))(),
                         reads=[kT_tk, qT_tk], writes=[ps_tk])
                    return ps, ps_tk
                nxt = emit_score(0)
                for i, (qb, kt, m) in enumerate(tiles):
                    ps, ps_tk = nxt
                    if i + 1 < len(tiles):
                        nxt = emit_score(i + 1)
                    j = kt - 4 * qb
                    q0 = 0 if j < 0 else 128 * j
                    n = 512 - q0
                    nkt = 4 * qb + 4
                    pT, pT_tk = pT_r.next()
                    P.op("act", (lambda pT=pT, ps=ps, n=n: lambda e: e.activation(out=pT[:, 0:n], in_=ps[:, 0:n], func=AF.Exp))(), reads=[ps_tk], writes=[pT_tk])
                    if j >= 0:
                        P.op("dve", (lambda pT=pT: lambda e: e.tensor_tensor(out=pT[:, 0:128], in0=pT[:, 0:128], in1=tri_bf[:], op=ALU.mult))(),
                             reads=[pT_tk, cb_tk], writes=[pT_tk])
                    P.op("pe", (lambda pT=pT, m=m, kt=kt, q0=q0, n=n, nkt=nkt, vt=vt: lambda e: e.matmul(
                        acc_b[2 * m][:, q0:512], vt[:, kt, :], pT[:, 0:n], start=(kt == 0), stop=(kt == nkt - 1)))(),
                        reads=[pT_tk, vt_tk], writes=[acc_tk[2 * m]], sig=False)
                    P.op("pe", (lambda pT=pT, m=m, kt=kt, q0=q0, n=n, nkt=nkt: lambda e: e.matmul(
                        acc_b[2 * m + 1][:, q0:512], ones_bf[:], pT[:, 0:n], start=(kt == 0), stop=(kt == nkt - 1)))(),
                        reads=[pT_tk, cb_tk], writes=[acc_tk[2 * m + 1]], sig=True)
                    drain(1)
                    if kt == nkt - 1 and m == 1:
                        blk = slice(qb * 512, (qb + 1) * 512)
                        while any(getattr(it_, "is_epi", False) for it_ in pending):
                            drain(1)
                        o_s = []; rd_s = []
                        for mm in range(2):
                            o_, o_tk = om_r.next()
                            P.op("act", (lambda o_=o_, mm=mm: lambda e: e.activation(out=o_[:], in_=acc_b[2 * mm][:], func=AF.Copy))(), reads=[acc_tk[2 * mm]], writes=[o_tk])
                            rd, rd_tk = om_r.next()
                            P.op("dve", (lambda rd=rd, mm=mm: lambda e: e.tensor_copy(out=rd[:], in_=acc_b[2 * mm + 1][:]))(), reads=[acc_tk[2 * mm + 1]], writes=[rd_tk])
                            o_s.append((o_, o_tk)); rd_s.append((rd, rd_tk))
                        st = {}

                        def e0(rd_s=rd_s):
                            for mm in range(2):
                                rd, rd_tk = rd_s[mm]
                                P.op("act", (lambda rd=rd: lambda e: e.activation(out=rd[:], in_=rd[:], func=AF.Ln))(), reads=[rd_tk], writes=[rd_tk])
                                P.op("act", (lambda rd=rd: lambda e: e.activation(out=rd[:], in_=rd[:], func=AF.Exp, scale=-1.0))(), reads=[rd_tk], writes=[rd_tk])

                        def e1(o_s=o_s, rd_s=rd_s, st=st):
                            for mm in range(2):
                                P.op("pool", (lambda mm=mm: lambda e: e.tensor_tensor(out=o_s[mm][0][:], in0=o_s[mm][0][:], in1=rd_s[mm][0][:], op=ALU.mult))(),
                                     reads=[o_s[mm][1], rd_s[mm][1]], writes=[o_s[mm][1]])
                            df, df_tk = rd_s[0]
                            P.op("dve", (lambda df=df: lambda e: e.scalar_tensor_tensor(out=df[:], in0=o_s[1][0][:], scalar=prm[:, 3:4], in1=o_s[0][0][:], op0=ALU.mult, op1=ALU.add))(),
                                 reads=[o_s[0][1], o_s[1][1], prm_tk], writes=[df_tk])
                            st.update(df=df, df_tk=df_tk)

                        def e1b(rd_s=rd_s, st=st):
                            df, df_tk = st["df"], st["df_tk"]
                            sq, sq_tk = rd_s[1]
                            P.op("act", (lambda: lambda e: e.activation(out=sq[:], in_=df[:], func=AF.Square))(), reads=[df_tk], writes=[sq_tk])
                            st.update(sq=sq, sq_tk=sq_tk)

                        def e1c(st=st):
                            sq, sq_tk = st["sq"], st["sq_tk"]
                            ps2, ps2_tk = PSb.next()
                            P.op("pe", (lambda: lambda e: e.matmul(ps2[:], cst["ones"][:], sq[:], start=True, stop=True))(), reads=[sq_tk, cst_tk], writes=[ps2_tk])
                            st.update(ps2=ps2, ps2_tk=ps2_tk)

                        def e2(st=st, h=h, blk=blk):
                            df, df_tk, ps2, ps2_tk = st["df"], st["df_tk"], st["ps2"], st["ps2_tk"]
                            rr, rr_tk = tmpf.next()
                            P.op("act", (lambda: lambda e: e.activation(out=rr[:], in_=ps2[:], func=AF.Ln, bias=epsb[:, 0:1], scale=1.0 / 128))(),
                                 reads=[ps2_tk, cst_tk], writes=[rr_tk])
                            P.op("act", (lambda: lambda e: e.activation(out=rr[:], in_=rr[:], func=AF.Exp, scale=-0.5))(), reads=[rr_tk], writes=[rr_tk])
                            st.update(rr=rr, rr_tk=rr_tk)

                        def e3(st=st, h=h, blk=blk):
                            df, df_tk, rr, rr_tk = st["df"], st["df_tk"], st["rr"], st["rr_tk"]
                            P.op("dve", (lambda: lambda e: e.scalar_tensor_tensor(
                                out=aoT[:, h % 2, blk], in0=df[:], scalar=prm[:, 2:3], in1=rr[:], op0=ALU.mult, op1=ALU.mult))(),
                                reads=[df_tk, rr_tk, prm_tk], writes=[aoT_tk])
                        epi = [e0, e1, None, e1b, e1c, None, e2, e3]
                        for it_ in epi:
                            if it_ is not None:
                                it_.is_epi = True
                        lst = list(pending)
                        pos_ = 0
                        for ii, it_ in enumerate(lst):
                            if getattr(it_, "is_outproj", False):
                                pos_ = ii + 1
                        pending.clear()
                        pending.extend(lst[:pos_] + epi + lst[pos_:])
                        if qb == NB - 1 and h % 2 == 1:
                            lst = list(pending)
                            pos_ = lst.index(e3) + 1
                            pending.clear()
                            pending.extend(lst[:pos_] + outproj_items(h // 2) + lst[pos_:])
            drain(10 ** 9)
            for r in rings:
                mytks.extend(r.tks)
            retire(mytks)
            close_stage()

        if "gdn" in stages:
            rms_to_xnT()
            gdn()
        if "mlp0" in stages:
            rms_to_xnT()
            mlp(0, 1)
        if "attn" in stages:
            rms_to_xnT()
            attention()
        if "mlp1" in stages:
            rms_to_xnT()
            mlp(1, 4)

        osem = [P.new_dma_sem("out%d" % i) for i in range(2)]
        phf = ExitStack()
        open_stage(phf, 1024)
        stage = stage_box["ring"]
        for t in range(NT):
            tb = t // 4
            o, o_tk = stage.next()
            osi = (stage.i - 1) % 2
            for cg in range(2):
                ps, ps_tk = PS.next()
                for j in range(4):
                    c = cg * 4 + j
                    P.op("pe", (lambda ps=ps, c=c, j=j, t=t: lambda e: e.transpose(
                        ps[:, j * 128:(j + 1) * 128], xT[:, c, t * 128:(t + 1) * 128], cst["ident"][:]))(),
                        reads=[xT_tk[c][tb], cst_tk], writes=[ps_tk], sig=(j == 3))
                evac_copy(o[:, cg * 512:(cg + 1) * 512], ps[:], reads=[ps_tk], writes=[o_tk])
            P.dma("sp", osem[osi], (lambda o=o, t=t: lambda e: e.dma_start(out=out_d[t * 128:(t + 1) * 128, :], in_=o[:, 0:1024]))(), reads=[o_tk])
        P.final_wait("sp", osem[0])
        P.final_wait("sp", osem[1])
        phf.close()
        P.emit()
        nops = {e: len(P.ops[e]) for e in P.ENG}
        nops["splits"] = P.split_log
    return nc, nops


_CACHE = {}


def kernel(**inputs):
    B = inputs["x"].shape[0]
    consts = host_consts()
    if "nc" not in _CACHE:
        _CACHE["nc"] = build_program()
    nc, _ = _CACHE["nc"]
    shared = {}
    f32 = lambda a: np.ascontiguousarray(np.asarray(a, dtype=np.float32))
    shared["a_norm"] = f32(inputs["a_norm"]).reshape(1, D)
    shared["a_w_in"] = f32(inputs["a_w_in"]).reshape(D, 4112)
    shared["a_conv_w"] = f32(inputs["a_conv_w"]).reshape(4, 3072)
    shared["a_a_log"] = f32(inputs["a_a_log"]).reshape(1, 8)
    shared["a_dt_bias"] = f32(inputs["a_dt_bias"]).reshape(1, 8)
    shared["a_out_norm"] = f32(inputs["a_out_norm"]).reshape(1, 128)
    shared["a_w_out"] = f32(inputs["a_w_out"]).reshape(D, D)
    shared["kv_norm"] = f32(inputs["kv_norm"]).reshape(1, D)
    shared["w_kv"] = f32(inputs["w_kv"]).reshape(D, 2048)
    shared["k_norm"] = f32(inputs["k_norm"]).reshape(1, 64)
    shared["b_norm"] = f32(inputs["b_norm"]).reshape(1, D)
    shared["b_w_q"] = f32(inputs["b_w_q"]).reshape(D, D)
    shared["b_q_norm"] = f32(inputs["b_q_norm"]).reshape(1, 64)
    shared["b_lambda"] = f32(inputs["b_lambda"]).reshape(4, 64)
    shared["b_sub_norm"] = f32(inputs["b_sub_norm"]).reshape(1, 128)
    shared["b_w_out"] = f32(inputs["b_w_out"]).reshape(D, D)
    shared["mlp_norm"] = f32(inputs["mlp_norm"]).reshape(2, D)
    shared["mlp_w1"] = f32(inputs["mlp_w1"]).reshape(2 * D, DFF)
    shared["mlp_w2"] = f32(inputs["mlp_w2"]).reshape(2 * DFF, D)
    for n in CONST_NAMES:
        shared["c_" + n] = consts[n]
    x = f32(inputs["x"])
    pos = np.ascontiguousarray(np.asarray(inputs["positions"], dtype=np.int32))
    in_maps = []
    for b in range(B):
        m = dict(shared)
        m["x"] = x[b]
        m["positions"] = pos[b].reshape(1, S)
        in_maps.append(m)
    res = run_bass_kernel_spmd(nc, in_maps, core_ids=list(range(B)))
    return np.stack([np.asarray(r["out"], dtype=np.float32) for r in res.results], axis=0)
```

```python
import numpy as np
import concourse.bass as bass
import concourse.mybir as mybir
from concourse.bass_utils import run_bass_kernel_spmd
from contextlib import ExitStack

F32 = mybir.dt.float32
BF16 = mybir.dt.bfloat16
I32 = mybir.dt.int32
AF = mybir.ActivationFunctionType
ALU = mybir.AluOpType
AX = mybir.AxisListType

S = 2048
D = 1024
NT = 16
NB = 4
DFF = 4096
EPS = 1e-6


INHERIT = {}


class Tk:
    __slots__ = ("name", "w", "r", "excl")

    def __init__(self, name="", excl=False):
        self.name = name
        self.w = None
        self.r = dict(INHERIT)
        self.excl = excl


def retire(tks):
    for t in tks:
        if t.w is not None:
            k, v = t.w
            if INHERIT.get(k, 0) < v:
                INHERIT[k] = v
        for k, v in t.r.items():
            if INHERIT.get(k, 0) < v:
                INHERIT[k] = v


class Prog:
    ENG = ("pe", "act", "dve", "pool", "sp")

    def __init__(self, nc, es):
        self.nc = nc
        self.es = es
        self.ops = {e: [] for e in self.ENG}
        self.sems = {}
        self.cnt = {}
        for e in self.ENG:
            self.sems[e] = es.enter_context(nc.semaphore("s_" + e))
            self.cnt[e] = 0
        self.waited = {e: {} for e in self.ENG}
        self.pending_nosig = {e: False for e in self.ENG}

    def new_dma_sem(self, name):
        key = "dma_" + name
        self.sems[key] = self.es.enter_context(self.nc.semaphore(key))
        self.cnt[key] = 0
        return key

    def _deps(self, eng, reads, writes):
        deps = {}

        def add(d):
            if d is None:
                return
            k, v = d
            if deps.get(k, 0) < v:
                deps[k] = v
        for t in reads:
            add(t.w)
            if t.excl:
                for k, v in t.r.items():
                    if k != eng:
                        add((k, v))
        for t in writes:
            add(t.w)
            for k, v in t.r.items():
                add((k, v))
        waits = []
        wd = self.waited[eng]
        for k, v in deps.items():
            if k == "pe" and eng == "pe":
                continue
            if wd.get(k, 0) < v:
                wd[k] = v
                waits.append((k, v))
        return waits

    def op(self, eng, fn, reads=(), writes=(), sig=True):
        assert sig or eng == "pe"
        waits = self._deps(eng, reads, writes)
        if sig:
            self.cnt[eng] += 1
            val = self.cnt[eng]
        else:
            val = self.cnt[eng] + 1
        self.pending_nosig[eng] = not sig
        self.ops[eng].append((waits, fn, (eng, 1) if sig else None))
        for t in reads:
            if t.r.get(eng, 0) < val:
                t.r[eng] = val
        for t in writes:
            t.w = (eng, val)
            t.r = {}

    def dma(self, q, semkey, fn, reads=(), writes=()):
        assert q == "sp"
        waits = self._deps(q, reads, writes)
        self.cnt[semkey] += 1
        val = self.cnt[semkey]
        self.ops[q].append((waits, fn, (semkey, 16)))
        for t in reads:
            if t.r.get(semkey, 0) < val:
                t.r[semkey] = val
        for t in writes:
            t.w = (semkey, val)
            t.r = {}

    def final_wait(self, eng, semkey):
        if self.cnt[semkey] > 0:
            self.ops[eng].append(([(semkey, self.cnt[semkey])], None, None))

    def emit(self):
        nc = self.nc
        for e in self.ENG:
            assert not self.pending_nosig[e], e
        actual = {k: [0] for k in self.sems if k.startswith("dma_")}
        self.split_log = []
        with nc.Block() as block:
            def run(engname):
                def body(eng):
                    for waits, fn, inc in self.ops[engname]:
                        for k, v in waits:
                            eng.wait_ge(self.sems[k], actual[k][v] if k in actual else v)
                        if fn is not None:
                            n0 = nc.n_instructions()
                            ins = fn(eng)
                            if inc is not None:
                                ins.then_inc(self.sems[inc[0]], inc[1])
                                if inc[0] in actual:
                                    k_ = max(1, nc.n_instructions() - n0)
                                    if k_ != 1:
                                        self.split_log.append((inc[0], len(actual[inc[0]]), k_))
                                    actual[inc[0]].append(actual[inc[0]][-1] + 16 * k_)
                return body
            block.sync(run("sp"))
            block.tensor(run("pe"))
            block.scalar(run("act"))
            block.vector(run("dve"))
            block.gpsimd(run("pool"))


class Ring:
    def __init__(self, bufs, tks=None):
        self.bufs = bufs
        self.tks = tks if tks is not None else [Tk() for _ in bufs]
        self.i = 0

    def next(self):
        b, t = self.bufs[self.i], self.tks[self.i]
        self.i = (self.i + 1) % len(self.bufs)
        return b, t


def host_consts():
    c = {}
    i = np.arange(128)
    c["ident"] = np.eye(128, dtype=np.float32)
    c["ones"] = np.ones((128, 128), dtype=np.float32)
    bo = np.zeros((128, 128), np.float32); bo[:64, :64] = 1; bo[64:, 64:] = 1
    c["blockones"] = bo
    rm = np.zeros((128, 128), np.float32)
    for p in range(128):
        if p % 64 < 32:
            rm[p + 32, p] = -1.0
        else:
            rm[p - 32, p] = 1.0
    c["rot"] = rm
    fr = np.zeros((128, 128), np.float32)
    fr[:, 0] = (10000.0 ** (-(np.arange(128) % 32).astype(np.float32) / 32.0)).astype(np.float32)
    c["freq"] = fr
    c["trimask"] = (i[:, None] <= i[None, :]).astype(np.float32)
    c["mnegL"] = np.where(i[:, None] >= i[None, :], 0.0, -30000.0).astype(np.float32)
    c["mnegU"] = np.ascontiguousarray(c["mnegL"].T)
    c["mnegLs"] = np.where(i[:, None] > i[None, :], 0.0, -30000.0).astype(np.float32)
    c["strictL"] = (i[:, None] > i[None, :]).astype(np.float32)
    c["strictU"] = np.ascontiguousarray(c["strictL"].T)
    return c


CONST_NAMES = ["ident", "ones", "blockones", "freq", "trimask", "mnegLs", "mnegU", "strictU"]


def build_program(stages=("gdn", "mlp0", "attn", "mlp1"), debug=False):
    INHERIT.clear()
    nc = bass.Bass("TRN2", target_bir_lowering=False)
    dr = {}

    def din(name, shape, dt=F32):
        dr[name] = nc.dram_tensor(name, list(shape), dt, kind="ExternalInput").ap()
        return dr[name]
    x_d = din("x", [S, D])
    pos_d = din("positions", [1, S], I32)
    din("a_norm", [1, D]); din("a_w_in", [D, 4112]); din("a_conv_w", [4, 3072]); din("a_a_log", [1, 8])
    din("a_dt_bias", [1, 8]); din("a_out_norm", [1, 128]); din("a_w_out", [D, D])
    din("kv_norm", [1, D]); din("w_kv", [D, 2048]); din("k_norm", [1, 64])
    din("b_norm", [1, D]); din("b_w_q", [D, D]); din("b_q_norm", [1, 64]); din("b_lambda", [4, 64])
    din("b_sub_norm", [1, 128]); din("b_w_out", [D, D])
    din("mlp_norm", [2, D]); din("mlp_w1", [2 * D, DFF]); din("mlp_w2", [2 * DFF, D])
    for n in CONST_NAMES:
        din("c_" + n, [128, 128])
    out_d = nc.dram_tensor("out", [S, D], F32, kind="ExternalOutput").ap()

    with ExitStack() as es:
        P = Prog(nc, es)

        def sb(name, shape, dt=F32):
            return es.enter_context(nc.sbuf_tensor(name, list(shape), dt))

        xT = sb("xT", [128, 8, S])
        xT_tk = [[Tk() for _ in range(NB)] for _ in range(8)]
        xnT = sb("xnT", [128, 8, S], BF16)
        xnT_tk = [Tk() for _ in range(NB)]
        banks = [es.enter_context(nc.psum_tensor("bank%d" % i, [128, 512], F32)) for i in range(8)]
        PS = Ring(banks, [Tk("bank%d" % i, excl=True) for i in range(8)])
        stage_sem = [P.new_dma_sem("stage%d" % i) for i in range(2)]
        stage_box = {"gen": 0}

        def open_stage(ph_, width):
            g_ = stage_box["gen"]
            stage_box["gen"] = g_ + 1
            bufs = [ph_.enter_context(nc.sbuf_tensor("stage_%d_%d" % (g_, i), [128, width], F32)) for i in range(2)]
            stage_box["ring"] = Ring(bufs)
            stage_box["width"] = width

        def close_stage():
            retire(stage_box["ring"].tks)
        tmpf = Ring([sb("tmpf%d" % i, [128, 512]) for i in range(3)])
        rstd_r = Ring([sb("rstd%d" % i, [128, 512]) for i in range(1)])
        cst = {}
        cst_tk = Tk()
        csem = P.new_dma_sem("const")
        for n in CONST_NAMES:
            cst[n] = sb("cs_" + n, [128, 128])
            P.dma("sp", csem, (lambda n=n: lambda e: e.dma_start(out=cst[n][:], in_=dr["c_" + n]))(), writes=[cst_tk])
        gains = sb("gains", [128, 5, 8])
        gain_src = [dr["a_norm"][0, :], dr["mlp_norm"][0, :], dr["kv_norm"][0, :], dr["b_norm"][0, :], dr["mlp_norm"][1, :]]
        for gi, src in enumerate(gain_src):
            P.dma("sp", csem, (lambda gi=gi, src=src: lambda e: e.dma_start(
                out=gains[:, gi, :], in_=src.rearrange("(c p) -> p c", p=128), allow_slow_non_contiguous=True))(), writes=[cst_tk])
        epsb = sb("epsb", [128, 1])
        P.op("pool", lambda e: e.memset(epsb[:], EPS), writes=[cst_tk])
        ones_bf_p = sb("ones_bf_p", [128, 128], BF16)
        onesb_tk = Tk()
        P.op("dve", lambda e: e.tensor_copy(out=ones_bf_p[:], in_=cst["ones"][:]), reads=[cst_tk], writes=[onesb_tk])
        bfv = lambda t_: t_[:].bitcast(BF16)[:, 0:512]

        evac_flip = [0]

        def evac_copy(out_ap, in_ap, reads, writes):
            evac_flip[0] ^= 1
            if evac_flip[0]:
                P.op("act", lambda e: e.activation(out=out_ap, in_=in_ap, func=AF.Copy), reads=reads, writes=writes)
            else:
                P.op("dve", lambda e: e.tensor_copy(out=out_ap, in_=in_ap), reads=reads, writes=writes)

        ph0 = ExitStack()
        open_stage(ph0, 1024)
        stage = stage_box["ring"]
        for t in range(NT):
            st, st_tk = stage.next()
            si = (stage.i - 1) % 2
            P.dma("sp", stage_sem[si], (lambda st=st, t=t: lambda e: e.dma_start(
                out=st[:, 0:1024], in_=x_d[t * 128:(t + 1) * 128, :]))(), writes=[st_tk])
            for cg in range(2):
                ps, ps_tk = PS.next()
                for j in range(4):
                    c = cg * 4 + j
                    P.op("pe", (lambda ps=ps, st=st, c=c, j=j: lambda e: e.transpose(
                        ps[:, j * 128:(j + 1) * 128], st[:, c * 128:(c + 1) * 128], cst["ident"][:]))(),
                        reads=[st_tk, cst_tk], writes=[ps_tk], sig=(j == 3))
                tb = t // 4
                evac_copy(xT[:, cg * 4:(cg + 1) * 4, t * 128:(t + 1) * 128],
                          ps[:].rearrange("p (j n) -> p j n", j=4),
                          reads=[ps_tk], writes=[xT_tk[c][tb] for c in range(cg * 4, cg * 4 + 4)])
        close_stage()
        ph0.close()

        def rms_to_xnT():
            for tb in range(NB):
                blk = slice(tb * 512, (tb + 1) * 512)
                ps, ps_tk = PS.next()
                for c in range(8):
                    sq, sq_tk = tmpf.next()
                    P.op("act", (lambda sq=sq, c=c, blk=blk: lambda e: e.activation(out=bfv(sq), in_=xT[:, c, blk], func=AF.Square))(),
                         reads=[xT_tk[c][tb]], writes=[sq_tk])
                    P.op("pe", (lambda ps=ps, sq=sq, c=c: lambda e: e.matmul(ps[:], ones_bf_p[:], bfv(sq), start=(c == 0), stop=(c == 7)))(),
                         reads=[sq_tk, onesb_tk], writes=[ps_tk], sig=True)
                rs, rs_tk = tmpf.next()
                P.op("act", (lambda rs=rs, ps=ps: lambda e: e.activation(out=rs[:], in_=ps[:], func=AF.Ln, bias=epsb[:, 0:1], scale=1.0 / D))(),
                     reads=[ps_tk, cst_tk], writes=[rs_tk])
                rstd_bc, rstd_tk = rstd_r.next()
                P.op("act", (lambda rs=rs, rstd_bc=rstd_bc: lambda e: e.activation(out=rstd_bc[:], in_=rs[:], func=AF.Exp, scale=-0.5))(),
                     reads=[rs_tk], writes=[rstd_tk])
                for c in range(8):
                    eng = "dve" if c % 2 == 0 else "pool"
                    P.op(eng, (lambda c=c, blk=blk, rstd_bc=rstd_bc: lambda e: e.tensor_tensor(out=xnT[:, c, blk], in0=xT[:, c, blk], in1=rstd_bc[:], op=ALU.mult))(),
                         reads=[xT_tk[c][tb], rstd_tk], writes=[xnT_tk[tb]])

        wsem = [P.new_dma_sem("w%d" % i) for i in range(2)]

        def load_w(dst, dst_tk, src, kc, ncols, gain_idx, gain_c0=0, colgain=None, colgain_tk=None):
            stage = stage_box["ring"]
            per = max(1, stage_box["width"] // ncols)
            c0 = 0
            while c0 < kc:
                n = min(per, kc - c0)
                st, st_tk = stage.next()
                si = (stage.i - 1) % 2
                stv = st[:, 0:n * ncols].rearrange("p (c n) -> p c n", c=n)
                srcv = src[c0 * 128:(c0 + n) * 128, :].rearrange("(c p) n -> p c n", p=128)
                P.dma("sp", stage_sem[si], (lambda stv=stv, srcv=srcv: lambda e: e.dma_start(out=stv, in_=srcv))(), writes=[st_tk])
                dv = dst[:, c0:c0 + n, :]
                if gain_idx is None:
                    P.op("pool", (lambda dv=dv, stv=stv: lambda e: e.tensor_copy(out=dv, in_=stv))(), reads=[st_tk], writes=[dst_tk])
                elif colgain is None:
                    gv = gains[:, gain_idx, gain_c0 + c0:gain_c0 + c0 + n].unsqueeze(2).to_broadcast([128, n, ncols])
                    P.op("pool", (lambda dv=dv, stv=stv, gv=gv: lambda e: e.tensor_tensor(out=dv, in0=stv, in1=gv, op=ALU.mult))(),
                         reads=[st_tk, cst_tk], writes=[dst_tk])
                else:
                    gv = gains[:, gain_idx, gain_c0 + c0:gain_c0 + c0 + n].unsqueeze(2).to_broadcast([128, n, ncols])
                    cv = colgain.unsqueeze(1).to_broadcast([128, n, ncols])
                    P.op("pool", (lambda stv=stv, gv=gv: lambda e: e.tensor_tensor(out=stv, in0=stv, in1=gv, op=ALU.mult))(),
                         reads=[st_tk, cst_tk], writes=[st_tk])
                    P.op("pool", (lambda dv=dv, stv=stv, cv=cv: lambda e: e.tensor_tensor(out=dv, in0=stv, in1=cv, op=ALU.mult))(),
                         reads=[st_tk, colgain_tk], writes=[dst_tk])
                c0 += n

        def mlp(layer, gain_idx):
            with ExitStack() as ph:
                mlp_body(layer, gain_idx, ph)

        def mlp_body(layer, gain_idx, ph):
            FG = 512
            open_stage(ph, 2048)
            psb = lambda name, shape, dt=F32: ph.enter_context(nc.sbuf_tensor(name, list(shape), dt))
            w1r = Ring([psb("w1g%d_%d" % (layer, i), [128, 8, FG], BF16) for i in range(2)])
            w2r = Ring([psb("w2g%d_%d" % (layer, i), [128, FG // 128, D], BF16) for i in range(2)])
            hr = Ring([psb("hT%d_%d" % (layer, i), [128, FG // 128, 512], BF16) for i in range(2)])
            w1_d = dr["mlp_w1"][layer * D:(layer + 1) * D, :]
            w2_d = dr["mlp_w2"][layer * DFF:(layer + 1) * DFF, :]
            NG = DFF // FG
            wts_ = {}

            def load_group(g):
                w1, w1_tk = w1r.next()
                w2, w2_tk = w2r.next()
                load_w(w1, w1_tk, w1_d[:, g * FG:(g + 1) * FG], 8, FG, gain_idx)
                load_w(w2, w2_tk, w2_d[g * FG:(g + 1) * FG, :], FG // 128, D, None)
                wts_[g] = (w1, w1_tk, w2, w2_tk)

            def emit_h(g, tb):
                w1, w1_tk, _, _ = wts_[g]
                blk = slice(tb * 512, (tb + 1) * 512)
                hT, hT_tk = hr.next()
                for fc in range(FG // 128):
                    ps, ps_tk = PS.next()
                    for c in range(8):
                        P.op("pe", (lambda ps=ps, c=c, fc=fc: lambda e: e.matmul(
                            ps[:], w1[:, c, fc * 128:(fc + 1) * 128], xnT[:, c, blk], start=(c == 0), stop=(c == 7)))(),
                            reads=[w1_tk, xnT_tk[tb]], writes=[ps_tk], sig=(c == 7))
                    h1, h1_tk = tmpf.next()
                    P.op("act", (lambda h1=h1, ps=ps: lambda e: e.activation(out=h1[:], in_=ps[:], func=AF.Relu))(), reads=[ps_tk], writes=[h1_tk])
                    P.op("act", (lambda h1=h1, fc=fc: lambda e: e.activation(out=hT[:, fc, :], in_=h1[:], func=AF.Square))(), reads=[h1_tk], writes=[hT_tk])
                return hT, hT_tk

            def emit_o(g, tb, hT, hT_tk):
                _, _, w2, w2_tk = wts_[g]
                blk = slice(tb * 512, (tb + 1) * 512)
                for n in range(8):
                    ps, ps_tk = PS.next()
                    for fc in range(FG // 128):
                        P.op("pe", (lambda ps=ps, n=n, fc=fc: lambda e: e.matmul(
                            ps[:], w2[:, fc, n * 128:(n + 1) * 128], hT[:, fc, :], start=(fc == 0), stop=(fc == FG // 128 - 1)))(),
                            reads=[w2_tk, hT_tk], writes=[ps_tk], sig=(fc == FG // 128 - 1))
                    P.op("dve", (lambda ps=ps, n=n: lambda e: e.tensor_tensor(out=xT[:, n, blk], in0=ps[:], in1=xT[:, n, blk], op=ALU.add))(),
                         reads=[ps_tk], writes=[xT_tk[n][tb]])
            seq = [(g, tb) for g in range(NG) for tb in range(NB)]
            load_group(0)
            cur = emit_h(*seq[0])
            for i, (g, tb) in enumerate(seq):
                if tb == 0 and g + 1 < NG:
                    load_group(g + 1)
                nxt = emit_h(*seq[i + 1]) if i + 1 < len(seq) else None
                emit_o(g, tb, *cur)
                cur = nxt
            retire(w1r.tks + w2r.tks + hr.tks)
            close_stage()


        def gdn():
            with ExitStack() as ph:
                gdn_body(ph)

        def gdn_body(ph):
            import os
            open_stage(ph, 1024)
            psb = lambda name, shape, dt=F32: ph.enter_context(nc.sbuf_tensor(name, list(shape), dt))
            mytks = []

            def tk():
                t = Tk(); mytks.append(t); return t
            pb = banks[0:4]; pb_tk = PS.tks[0:4]
            qb_ = [banks[4], banks[6]]; qb_tk = [PS.tks[4], PS.tks[6]]
            PSs = Ring([banks[5]], [PS.tks[5]])
            OBk = [banks[7], banks[7]]; OB_tk = [PS.tks[7], PS.tks[7]]
            ident = cst["ident"]; onesf = cst["ones"]
            bc4 = lambda ap: ap.unsqueeze(1).to_broadcast([128, 4, 128])
            v3 = lambda ap: ap.rearrange("p (i e) -> p i e", i=4)
            ident_bf = psb("g_identbf", [128, 128], BF16); negones = psb("g_negones", [128, 128]); cb_tk = tk()
            P.op("dve", lambda e: e.tensor_copy(out=ident_bf[:], in_=ident[:]), reads=[cst_tk], writes=[cb_tk])
            P.op("dve", lambda e: e.tensor_scalar(out=negones[:], in0=onesf[:], scalar1=-1.0, scalar2=None, op0=ALU.mult), reads=[cst_tk, cb_tk], writes=[cb_tk])
            one_b = psb("g_oneb", [128, 1]); P.op("pool", lambda e: e.memset(one_b[:], 1.0), reads=[cb_tk], writes=[cb_tk])
            s1 = P.new_dma_sem("g_p1"); s2 = P.new_dma_sem("g_p2"); s3 = P.new_dma_sem("g_p3"); s4 = P.new_dma_sem("g_p4")
            alog = psb("g_alog", [128, 8]); dtb = psb("g_dtb", [128, 8]); onw = psb("g_onw", [128, 128]); cw = psb("g_cw", [128, 24, 4])
            alog_tk = tk(); dtb_tk = tk(); onw_tk = tk(); cw_tk = tk()
            P.dma("sp", s1, lambda e: e.dma_start(out=alog[:], in_=dr["a_a_log"][0, :].partition_broadcast(128)), writes=[alog_tk])
            P.dma("sp", s2, lambda e: e.dma_start(out=dtb[:], in_=dr["a_dt_bias"][0, :].partition_broadcast(128)), writes=[dtb_tk])
            P.dma("sp", s3, lambda e: e.dma_start(out=onw[:], in_=dr["a_out_norm"][0, :].partition_broadcast(128)), writes=[onw_tk])
            for j in range(4):
                P.dma("sp", s4, (lambda j=j: lambda e: e.dma_start(out=cw[:, :, j], in_=dr["a_conv_w"][j, :].rearrange("(c p) -> p c", p=128),
                      allow_slow_non_contiguous=True))(), writes=[cw_tk])
            P.op("act", lambda e: e.activation(out=alog[:], in_=alog[:], func=AF.Exp), reads=[alog_tk], writes=[alog_tk])
            P.op("dve", lambda e: e.tensor_scalar(out=alog[:], in0=alog[:], scalar1=-1.0, scalar2=None, op0=ALU.mult), reads=[alog_tk], writes=[alog_tk])
            wbg = psb("g_wbg", [128, 8, 16], BF16); wbg_tk = tk()
            load_w(wbg, wbg_tk, dr["a_w_in"][:, 4096:4112], 8, 16, 0)
            bg = psb("g_bg", [128, NT, 16]); bg_tk = tk()
            ps, ps_tk = pb[0], pb_tk[0]
            for t in range(NT):
                for c in range(8):
                    P.op("pe", (lambda t=t, c=c: lambda e: e.matmul(ps[:, t * 16:(t + 1) * 16], xnT[:, c, t * 128:(t + 1) * 128], wbg[:, c, :],
                         start=(c == 0), stop=(c == 7)))(), reads=[xnT_tk[t // 4], wbg_tk], writes=[ps_tk], sig=(t == NT - 1 and c == 7))
            P.op("dve", lambda e: e.tensor_copy(out=bg[:].rearrange("p t k -> p (t k)"), in_=ps[:, 0:256]), reads=[ps_tk], writes=[bg_tk])
            beta = psb("g_beta", [128, NT, 8]); gg = psb("g_g", [128, NT, 8]); par_tk = tk()
            P.op("act", lambda e: e.activation(out=beta[:], in_=bg[:, :, 0:8], func=AF.Exp, scale=-1.0), reads=[bg_tk], writes=[par_tk])
            P.op("act", lambda e: e.activation(out=beta[:], in_=beta[:], func=AF.Ln, bias=one_b[:, 0:1], scale=1.0), reads=[par_tk, cb_tk], writes=[par_tk])
            P.op("act", lambda e: e.activation(out=beta[:], in_=beta[:], func=AF.Exp, scale=-1.0), reads=[par_tk], writes=[par_tk])
            P.op("dve", lambda e: e.tensor_tensor(out=gg[:], in0=bg[:, :, 8:16], in1=dtb[:].unsqueeze(1).to_broadcast([128, NT, 8]), op=ALU.add),
                 reads=[bg_tk, dtb_tk, par_tk], writes=[par_tk])
            P.op("act", lambda e: e.activation(out=gg[:], in_=gg[:], func=AF.Exp), reads=[par_tk], writes=[par_tk])
            P.op("act", lambda e: e.activation(out=gg[:], in_=gg[:], func=AF.Ln, bias=one_b[:, 0:1], scale=1.0), reads=[par_tk, cb_tk], writes=[par_tk])
            P.op("dve", lambda e: e.tensor_tensor(out=gg[:], in0=gg[:], in1=alog[:].unsqueeze(1).to_broadcast([128, NT, 8]), op=ALU.mult),
                 reads=[par_tk, alog_tk], writes=[par_tk])
            gc = psb("g_gc", [128, NT, 8]); bgam = psb("g_bgam", [128, NT, 8]); kds = psb("g_kds", [128, NT, 8]); cd = psb("g_cd", [128, NT, 8])
            ps, ps_tk = pb[1], pb_tk[1]
            fl = lambda a_: a_[:].rearrange("p t k -> p (t k)")
            P.op("pe", lambda e: e.matmul(ps[:, 0:128], cst["trimask"][:], fl(gg), start=True, stop=True), reads=[par_tk, cst_tk], writes=[ps_tk], sig=False)
            P.op("pe", lambda e: e.matmul(ps[:, 128:256], onesf[:], fl(gg), start=True, stop=True), reads=[par_tk, cst_tk], writes=[ps_tk])
            P.op("dve", lambda e: e.tensor_copy(out=fl(gc), in_=ps[:, 0:128]), reads=[ps_tk], writes=[par_tk])
            P.op("dve", lambda e: e.tensor_tensor(out=fl(kds), in0=ps[:, 128:256], in1=fl(gc), op=ALU.subtract), reads=[ps_tk, par_tk], writes=[par_tk])
            P.op("act", lambda e: e.activation(out=fl(cd), in_=ps[:, 128:256], func=AF.Exp), reads=[ps_tk, par_tk], writes=[par_tk])
            P.op("act", lambda e: e.activation(out=fl(kds), in_=fl(kds), func=AF.Exp), reads=[par_tk], writes=[par_tk])
            P.op("act", lambda e: e.activation(out=fl(bgam), in_=fl(gc), func=AF.Exp), reads=[par_tk], writes=[par_tk])
            P.op("dve", lambda e: e.tensor_tensor(out=fl(bgam), in0=fl(bgam), in1=fl(beta), op=ALU.mult), reads=[par_tk], writes=[par_tk])
            wts = [psb("g_w%d" % k, [128, 8, 128], BF16) for k in range(3)]; wts_tk = [tk() for _ in range(3)]
            wz2 = [psb("g_wz%d" % k, [128, 8, 128], BF16) for k in range(2)]; wz2_tk = [tk() for _ in range(2)]
            wo2 = [psb("g_wo%d" % i, [128, 1, D], BF16) for i in range(2)]; wo2_tk = [tk() for _ in range(2)]
            cdiag = psb("g_cdiag", [128, 12, 128], BF16); cdiag_tk = tk()
            prjb = [psb("g_prjb%d" % k, [128, 516], BF16) for k in range(2)]; prjb_tk = [tk() for _ in range(2)]
            tails = psb("g_tails", [128, 3, 4], BF16); tails_tk = tk()
            qhT = psb("g_qhT", [128, S], BF16); khT = psb("g_khT", [128, S], BF16); vT = psb("g_vT", [128, S], BF16)
            qhT_tk = [tk() for _ in range(NB)]; khT_tk = [tk() for _ in range(NB)]; vT_tk = [tk() for _ in range(NB)]
            kbg2 = [psb("g_kbg%d" % i, [128, 512], BF16) for i in range(2)]; kdec2 = [psb("g_kdec%d" % i, [128, 512], BF16) for i in range(2)]
            bv2 = [psb("g_bv%d" % i, [128, 512], BF16) for i in range(2)]; zs2 = [psb("g_zs%d" % i, [128, 512], BF16) for i in range(2)]
            u2 = [psb("g_u%d" % i, [128, 512]) for i in range(2)]; wT2 = [psb("g_wT%d" % i, [128, 512], BF16) for i in range(2)]
            qdT2 = [psb("g_qdT%d" % i, [128, 512], BF16) for i in range(2)]; inT2 = [psb("g_inT%d" % i, [128, 512], BF16) for i in range(2)]
            goT2 = [psb("g_goT%d" % i, [128, 512], BF16) for i in range(2)]
            grp_tk = [[tk() for _ in range(10)] for _ in range(2)]
            KBG, KDEC, BV, ZS, U_, WT, QDT, INT, GOT = range(9)
            NF32 = int(os.environ.get("GDN_NF32", "6"))
            Pf = [psb("g_Pf%d" % a_, [128, 512]) for a_ in range(2)]; PTf = [psb("g_PTf%d" % a_, [128, 512]) for a_ in range(2)]
            Ph = [psb("g_Ph%d" % a_, [128, 512], BF16) for a_ in range(2)]; PTh = [psb("g_PTh%d" % a_, [128, 512], BF16) for a_ in range(2)]
            Rb = psb("g_R", [128, 512]); Rbf = psb("g_Rbf", [128, 512], BF16)
            Pb = Pf; PTb = PTf
            Ph_tk = [tk(), tk()]; PTh_tk = [tk(), tk()]; Rh_tk = tk()
            P_tk = [tk(), tk()]; PT_tk = [tk(), tk()]; R_tk = tk()
            gA = psb("g_gA", [128, 512]); gB = psb("g_gB", [128, 512]); gC = psb("g_gC", [128, 512]); gA_tk = tk(); gB_tk = tk(); gC_tk = tk()
            zA = psb("g_zA", [128, 512]); zA_tk = tk()
            eA = psb("g_eA", [128, 512]); eA_tk = tk()
            go_b = psb("g_go", [128, 512], BF16); go_tk = tk()
            S_f = psb("g_Sf", [128, 128]); S_b = psb("g_Sb", [128, 128], BF16); S_tk = tk()
            vn_r = Ring([psb("g_vn%d" % i, [128, 128], BF16) for i in range(2)])
            st4 = psb("g_st4", [128, 8]); st4_tk = tk()
            if os.environ.get("SBUF_DBG"):
                print("GDN sbuf bytes remaining per partition:", nc.sbuf_bytes_remaining)

            def pstage_items(h, tb):
                items = []
                if tb == 0:
                    def it_w():
                        for k in range(3):
                            load_w(wts[k], wts_tk[k], dr["a_w_in"][:, k * 1024 + h * 128:k * 1024 + (h + 1) * 128], 8, 128, 0)
                        load_w(wz2[h % 2], wz2_tk[h % 2], dr["a_w_in"][:, 3 * 1024 + h * 128:3 * 1024 + (h + 1) * 128], 8, 128, 0)
                        load_w(wo2[h % 2], wo2_tk[h % 2], dr["a_w_out"][h * 128:(h + 1) * 128, :], 1, D, None)
                        for sec in range(3):
                            for j in range(4):
                                P.op("dve", (lambda sec=sec, j=j: lambda e: e.tensor_scalar(out=cdiag[:, sec * 4 + j, :], in0=ident[:], scalar1=cw[:, sec * 8 + h, j:j + 1], scalar2=None, op0=ALU.mult))(),
                                     reads=[cst_tk, cw_tk], writes=[cdiag_tk])
                    items.append(it_w)
                blk = slice(tb * 512, (tb + 1) * 512)
                stt = [dict() for _ in range(3)]

                def proj(sec):
                    k_ = sec % 2
                    pj, pj_tk = prjb[k_], prjb_tk[k_]
                    bk, bk_tk = qb_[0], qb_tk[0]
                    for c in range(8):
                        P.op("pe", (lambda c=c: lambda e: e.matmul(bk[:], wts[sec][:, c, :], xnT[:, c, blk], start=(c == 0), stop=(c == 7)))(),
                             reads=[wts_tk[sec], xnT_tk[tb]], writes=[bk_tk], sig=(c == 7))
                    P.op("act", (lambda: lambda e: e.activation(out=pj[:, 3:515], in_=bk[:], func=AF.Copy))(), reads=[bk_tk], writes=[pj_tk])
                    if tb == 0:
                        P.op("pool", (lambda: lambda e: e.memset(pj[:, 0:3], 0.0))(), reads=[pj_tk], writes=[pj_tk])
                    else:
                        P.op("pool", (lambda: lambda e: e.tensor_copy(out=pj[:, 0:3], in_=tails[:, sec, 0:3]))(), reads=[tails_tk, pj_tk], writes=[pj_tk])

                def conv(sec):
                    k_ = sec % 2
                    pj, pj_tk = prjb[k_], prjb_tk[k_]
                    bk, bk_tk = qb_[1], qb_tk[1]
                    for j in range(4):
                        P.op("pe", (lambda j=j: lambda e: e.matmul(bk[:], cdiag[:, sec * 4 + j, :], pj[:, j:j + 512], start=(j == 0), stop=(j == 3)))(),
                             reads=[cdiag_tk, pj_tk], writes=[bk_tk], sig=(j == 3))
                    P.op("pool", (lambda: lambda e: e.tensor_copy(out=tails[:, sec, 0:3], in_=pj[:, 512:515]))(), reads=[pj_tk, tails_tk], writes=[tails_tk])
                    e_, e_tk = tmpf.next()
                    P.op("act", (lambda: lambda e: e.activation(out=e_[:], in_=bk[:], func=AF.Exp, scale=-1.0))(), reads=[bk_tk], writes=[e_tk])
                    P.op("act", (lambda: lambda e: e.activation(out=e_[:], in_=e_[:], func=AF.Ln, bias=one_b[:, 0:1], scale=1.0))(), reads=[e_tk, cb_tk], writes=[e_tk])
                    P.op("act", (lambda: lambda e: e.activation(out=e_[:], in_=e_[:], func=AF.Exp, scale=-1.0))(), reads=[e_tk], writes=[e_tk])
                    stt[sec].update(e=e_, e_tk=e_tk)

                def fin1(sec):
                    bk, bk_tk = qb_[1], qb_tk[1]
                    e_, e_tk = stt[sec]["e"], stt[sec]["e_tk"]
                    if sec == 2:
                        P.op("dve", (lambda: lambda e: e.tensor_tensor(out=vT[:, blk], in0=bk[:], in1=e_[:], op=ALU.mult))(), reads=[bk_tk, e_tk], writes=[vT_tk[tb]])
                        return
                    P.op("dve", (lambda: lambda e: e.tensor_tensor(out=e_[:], in0=bk[:], in1=e_[:], op=ALU.mult))(), reads=[bk_tk, e_tk], writes=[e_tk])
                    sq, sq_tk = tmpf.next()
                    P.op("act", (lambda: lambda e: e.activation(out=bfv(sq), in_=e_[:], func=AF.Square))(), reads=[e_tk], writes=[sq_tk])
                    P.op("pe", (lambda: lambda e: e.matmul(bk[:], ones_bf_p[:], bfv(sq), start=True, stop=True))(), reads=[sq_tk, onesb_tk], writes=[bk_tk])
                    stt[sec].update(sq=sq, sq_tk=sq_tk)

                def fin2(sec):
                    if sec == 2:
                        return
                    bk, bk_tk = qb_[1], qb_tk[1]
                    e_, e_tk, sq, sq_tk = stt[sec]["e"], stt[sec]["e_tk"], stt[sec]["sq"], stt[sec]["sq_tk"]
                    P.op("act", (lambda: lambda e: e.activation(out=sq[:], in_=bk[:], func=AF.Ln, bias=epsb[:, 0:1], scale=1.0))(), reads=[bk_tk, cst_tk], writes=[sq_tk])
                    P.op("act", (lambda: lambda e: e.activation(out=sq[:], in_=sq[:], func=AF.Exp, scale=-0.5))(), reads=[sq_tk], writes=[sq_tk])
                    dst, dst_tk = (qhT, qhT_tk[tb]) if sec == 0 else (khT, khT_tk[tb])
                    sc = float(128 ** -0.5) if sec == 0 else 1.0
                    P.op("dve", (lambda: lambda e: e.scalar_tensor_tensor(out=dst[:, blk], in0=e_[:], scalar=sc, in1=sq[:], op0=ALU.mult, op1=ALU.mult))(),
                         reads=[e_tk, sq_tk], writes=[dst_tk])
                items.append(lambda: proj(0))
                for sec in range(3):
                    items.append((lambda sec=sec: lambda: conv(sec))())
                    if sec + 1 < 3:
                        items.append((lambda sec=sec: lambda: proj(sec + 1))())
                    items.append((lambda sec=sec: lambda: fin1(sec))())
                    items.append((lambda sec=sec: lambda: fin2(sec))())
                return items

            def prep_items(h, g):
                par = g % 2
                t0 = 4 * g
                blk = slice(g * 512, (g + 1) * 512)
                T_ = grp_tk[par]
                kbg, kdec, bv, zs, u_, wT, qdT, inT = kbg2[par], kdec2[par], bv2[par], zs2[par], u2[par], wT2[par], qdT2[par], inT2[par]
                scb = lambda X: X[:, t0:t0 + 4, h].unsqueeze(2).to_broadcast([128, 4, 128])
                tsl = lambda i: slice(i * 128, (i + 1) * 128)
                tok = lambda i: slice((t0 + i) * 128, (t0 + i + 1) * 128)
                items = []

                def p1():
                    P.op("pool", lambda e: e.tensor_copy(out=v3(gA[:]), in_=scb(gc)), reads=[par_tk], writes=[gA_tk])
                    P.op("pool", lambda e: e.tensor_tensor(out=v3(gB[:]), in0=bc4(ident[:]), in1=scb(gc), op=ALU.mult), reads=[par_tk, cst_tk], writes=[gB_tk])
                    P.op("pool", lambda e: e.tensor_tensor(out=v3(gC[:]), in0=bc4(ident[:]), in1=scb(beta), op=ALU.mult), reads=[par_tk, cst_tk], writes=[gC_tk])

                def p2():
                    P.op("pe", lambda e: e.matmul(pb[0][:], ident[:], gA[:], start=True, stop=False), reads=[gA_tk, cst_tk], writes=[pb_tk[0]], sig=False)
                    P.op("pe", lambda e: e.matmul(pb[0][:], negones[:], gB[:], start=False, stop=True), reads=[gB_tk, cb_tk], writes=[pb_tk[0]])
                    P.op("pe", lambda e: e.matmul(pb[1][:], onesf[:], gB[:], start=True, stop=True), reads=[gB_tk, cst_tk], writes=[pb_tk[1]])
                    P.op("pe", lambda e: e.matmul(pb[2][:], onesf[:], gC[:], start=True, stop=True), reads=[gC_tk, cst_tk], writes=[pb_tk[2]])

                def p3():
                    P.op("dve", lambda e: e.tensor_tensor(out=v3(gA[:]), in0=v3(pb[0][:]), in1=bc4(cst["mnegLs"][:]), op=ALU.add), reads=[pb_tk[0], cst_tk], writes=[gA_tk])
                    P.op("dve", lambda e: e.scalar_tensor_tensor(out=v3(gB[:]), in0=v3(pb[0][:]), scalar=-1.0, in1=bc4(cst["mnegU"][:]), op0=ALU.mult, op1=ALU.add),
                         reads=[pb_tk[0], cst_tk], writes=[gB_tk])

                def p4():
                    P.op("act", lambda e: e.activation(out=gA[:], in_=gA[:], func=AF.Exp), reads=[gA_tk], writes=[gA_tk])
                    P.op("act", lambda e: e.activation(out=gB[:], in_=gB[:], func=AF.Exp), reads=[gB_tk], writes=[gB_tk])
                    P.op("act", lambda e: e.activation(out=gC[:], in_=pb[1][:], func=AF.Exp), reads=[pb_tk[1]], writes=[gC_tk])

                def p5a():
                    for i in range(4):
                        P.op("pe", (lambda i=i: lambda e: e.matmul(pb[3][:, tsl(i)], khT[:, tok(i)], khT[:, tok(i)], start=True, stop=True))(), reads=[khT_tk[g]], writes=[pb_tk[3]], sig=(i == 3))
                    for i in range(4):
                        P.op("pe", (lambda i=i: lambda e: e.matmul(pb[0][:, tsl(i)], khT[:, tok(i)], qhT[:, tok(i)], start=True, stop=True))(), reads=[khT_tk[g], qhT_tk[g]], writes=[pb_tk[0]], sig=(i == 3))

                def p6():
                    P.op("dve", lambda e: e.tensor_tensor(out=inT[:], in0=pb[0][:], in1=gB[:], op=ALU.mult), reads=[pb_tk[0], gB_tk], writes=[T_[INT]])
                    P.op("pool", lambda e: e.tensor_tensor(out=qdT[:], in0=qhT[:, blk], in1=gC[:], op=ALU.mult), reads=[qhT_tk[g], gC_tk], writes=[T_[QDT]])
                    P.op("pool", lambda e: e.tensor_tensor(out=v3(gB[:]), in0=v3(gB[:]), in1=bc4(cst["strictU"][:]), op=ALU.mult), reads=[gB_tk, cst_tk, T_[INT]], writes=[gB_tk])

                def p7():
                    P.op("pool", lambda e: e.tensor_tensor(out=v3(gA[:]), in0=v3(gA[:]), in1=scb(beta), op=ALU.mult), reads=[gA_tk, par_tk], writes=[gA_tk])
                    P.op("dve", lambda e: e.tensor_tensor(out=gC[:], in0=pb[2][:], in1=gB[:], op=ALU.mult), reads=[pb_tk[2], gB_tk, T_[QDT]], writes=[gC_tk])

                def p8():
                    P.op("dve", lambda e: e.tensor_tensor(out=Pb[0][:], in0=pb[3][:], in1=gA[:], op=ALU.mult), reads=[pb_tk[3], gA_tk], writes=[P_tk[0]])
                    P.op("dve", lambda e: e.tensor_tensor(out=PTb[0][:], in0=pb[3][:], in1=gC[:], op=ALU.mult), reads=[pb_tk[3], gC_tk], writes=[PT_tk[0]])
                    P.op("pool", lambda e: e.tensor_tensor(out=v3(Rb[:]), in0=bc4(ident[:]), in1=v3(PTb[0][:]), op=ALU.subtract), reads=[cst_tk, PT_tk[0]], writes=[R_tk])

                def p5b():
                    for i in range(4):
                        P.op("pe", (lambda i=i: lambda e: e.matmul(pb[1][:, tsl(i)], khT[:, tok(i)], ident_bf[:], start=True, stop=True))(), reads=[khT_tk[g], cb_tk], writes=[pb_tk[1]], sig=(i == 3))
                    for i in range(4):
                        P.op("pe", (lambda i=i: lambda e: e.matmul(pb[2][:, tsl(i)], vT[:, tok(i)], ident_bf[:], start=True, stop=True))(), reads=[vT_tk[g], cb_tk], writes=[pb_tk[2]], sig=(i == 3))

                def p5b2():
                    P.op("dve", lambda e: e.tensor_tensor(out=v3(kbg[:]), in0=v3(pb[1][:]), in1=scb(bgam), op=ALU.mult), reads=[pb_tk[1], par_tk], writes=[T_[KBG]])
                    P.op("dve", lambda e: e.tensor_tensor(out=v3(kdec[:]), in0=v3(pb[1][:]), in1=scb(kds), op=ALU.mult), reads=[pb_tk[1], par_tk], writes=[T_[KDEC]])
                    P.op("dve", lambda e: e.tensor_tensor(out=v3(bv[:]), in0=v3(pb[2][:]), in1=scb(beta), op=ALU.mult), reads=[pb_tk[2], par_tk], writes=[T_[BV]])

                def p5c():
                    for i in range(4):
                        for c in range(8):
                            P.op("pe", (lambda i=i, c=c: lambda e: e.matmul(pb[0][:, tsl(i)], xnT[:, c, tok(i)], wz2[h % 2][:, c, :], start=(c == 0), stop=(c == 7)))(),
                                 reads=[xnT_tk[g], wz2_tk[h % 2]], writes=[pb_tk[0]], sig=(i == 3 and c == 7))

                def p5c2():
                    P.op("act", lambda e: e.activation(out=zA[:], in_=pb[0][:], func=AF.Exp, scale=-1.0), reads=[pb_tk[0]], writes=[zA_tk])
                    P.op("act", lambda e: e.activation(out=zA[:], in_=zA[:], func=AF.Ln, bias=one_b[:, 0:1], scale=1.0), reads=[zA_tk, cb_tk], writes=[zA_tk])
                    P.op("act", lambda e: e.activation(out=zA[:], in_=zA[:], func=AF.Exp, scale=-1.0), reads=[zA_tk], writes=[zA_tk])

                def p5c3():
                    P.op("dve", lambda e: e.tensor_tensor(out=zA[:], in0=pb[0][:], in1=zA[:], op=ALU.mult), reads=[pb_tk[0], zA_tk], writes=[zA_tk])
                    P.op("pool", lambda e: e.tensor_tensor(out=v3(zs[:]), in0=v3(zA[:]), in1=bc4(onw[:]), op=ALU.mult), reads=[zA_tk, onw_tk], writes=[T_[ZS]])
                items += [p1, p2, p3, p4, p5a, p6, p7, p8, p5b, p5b2, p5c, p5c2, p5c3]
                if NF32 == 0:
                    def c0():
                        P.op("act", lambda e: e.activation(out=Ph[0][:], in_=Pf[0][:], func=AF.Copy), reads=[P_tk[0]], writes=[Ph_tk[0]])
                        P.op("act", lambda e: e.activation(out=PTh[0][:], in_=PTf[0][:], func=AF.Copy), reads=[PT_tk[0]], writes=[PTh_tk[0]])
                        P.op("act", lambda e: e.activation(out=Rbf[:], in_=Rb[:], func=AF.Copy), reads=[R_tk], writes=[Rh_tk])
                    items.append(c0)
                dbl_stages = []
                for lvl in range(1, 7):
                    cur = (lvl - 1) % 2
                    nxt = 1 - cur
                    f32lvl = lvl <= NF32

                    def d1(lvl=lvl, cur=cur, nxt=nxt, f32lvl=f32lvl):
                        A_, AT_, A_tk, AT_tk = (Pf, PTf, P_tk, PT_tk) if f32lvl else (Ph, PTh, Ph_tk, PTh_tk)
                        for i in range(4):
                            P.op("pe", (lambda i=i: lambda e: e.matmul(pb[1][:, tsl(i)], AT_[cur][:, tsl(i)], A_[cur][:, tsl(i)], start=True, stop=True))(),
                                 reads=[AT_tk[cur], A_tk[cur]], writes=[pb_tk[1]], sig=(i == 3))
                        if lvl < 6:
                            for i in range(4):
                                P.op("pe", (lambda i=i: lambda e: e.matmul(pb[2][:, tsl(i)], A_[cur][:, tsl(i)], AT_[cur][:, tsl(i)], start=True, stop=True))(),
                                     reads=[AT_tk[cur], A_tk[cur]], writes=[pb_tk[2]], sig=(i == 3))

                    def d2(lvl=lvl, cur=cur, nxt=nxt, f32lvl=f32lvl):
                        if f32lvl:
                            P.op("act", lambda e: e.activation(out=Pf[nxt][:], in_=pb[1][:], func=AF.Copy), reads=[pb_tk[1]], writes=[P_tk[nxt]])
                        if lvl >= NF32 and lvl < 6 or not f32lvl:
                            P.op("act", lambda e: e.activation(out=Ph[nxt][:], in_=pb[1][:], func=AF.Copy), reads=[pb_tk[1]], writes=[Ph_tk[nxt]])
                        if lvl < 6:
                            if lvl + 1 <= NF32:
                                P.op("dve", lambda e: e.tensor_copy(out=PTf[nxt][:], in_=pb[2][:]), reads=[pb_tk[2]], writes=[PT_tk[nxt]])
                            else:
                                P.op("dve", lambda e: e.tensor_copy(out=PTh[nxt][:], in_=pb[2][:]), reads=[pb_tk[2]], writes=[PTh_tk[nxt]])

                    def d3(lvl=lvl, cur=cur, nxt=nxt, f32lvl=f32lvl):
                        for i in range(4):
                            if f32lvl:
                                P.op("pe", (lambda i=i: lambda e: e.matmul(pb[3][:, tsl(i)], Pf[nxt][:, tsl(i)], Rb[:, tsl(i)], start=True, stop=True))(),
                                     reads=[P_tk[nxt], R_tk], writes=[pb_tk[3]], sig=(i == 3))
                            else:
                                P.op("pe", (lambda i=i: lambda e: e.matmul(pb[3][:, tsl(i)], Ph[nxt][:, tsl(i)], Rbf[:, tsl(i)], start=True, stop=True))(),
                                     reads=[Ph_tk[nxt], Rh_tk], writes=[pb_tk[3]], sig=(i == 3))

                    def d4(lvl=lvl, f32lvl=f32lvl):
                        if f32lvl:
                            P.op("dve", lambda e: e.tensor_tensor(out=Rb[:], in0=pb[3][:], in1=Rb[:], op=ALU.add), reads=[pb_tk[3], R_tk], writes=[R_tk])
                            if lvl == NF32:
                                P.op("act", lambda e: e.activation(out=Rbf[:], in_=Rb[:], func=AF.Copy), reads=[R_tk], writes=[Rh_tk])
                        else:
                            P.op("dve", lambda e: e.tensor_tensor(out=Rbf[:], in0=pb[3][:], in1=Rbf[:], op=ALU.add), reads=[pb_tk[3], Rh_tk], writes=[Rh_tk])
                    dbl_stages.append((d1, d2, d3, d4))
                for li in range(6):
                    d1_, d2_, d3_, d4_ = dbl_stages[li]
                    if li == 0:
                        items += [d1_, d2_]
                    if li + 1 < 6:
                        n1, n2, _, _ = dbl_stages[li + 1]
                        items += [n1, d3_, n2, d4_]
                    else:
                        items += [d3_, d4_]

                def f1():
                    if NF32 >= 6:
                        P.op("act", lambda e: e.activation(out=Rbf[:], in_=Rb[:], func=AF.Copy), reads=[R_tk], writes=[Rh_tk])

                def f2():
                    for i in range(4):
                        P.op("pe", (lambda i=i: lambda e: e.matmul(pb[1][:, tsl(i)], Rbf[:, tsl(i)], bv[:, tsl(i)], start=True, stop=True))(), reads=[Rh_tk, T_[BV]], writes=[pb_tk[1]], sig=(i == 3))
                    for i in range(4):
                        P.op("pe", (lambda i=i: lambda e: e.matmul(pb[2][:, tsl(i)], kbg[:, tsl(i)], Rbf[:, tsl(i)], start=True, stop=True))(), reads=[Rh_tk, T_[KBG]], writes=[pb_tk[2]], sig=(i == 3))

                def f3():
                    P.op("act", lambda e: e.activation(out=u_[:], in_=pb[1][:], func=AF.Copy), reads=[pb_tk[1]], writes=[T_[U_]])
                    P.op("dve", lambda e: e.tensor_copy(out=wT[:], in_=pb[2][:]), reads=[pb_tk[2]], writes=[T_[WT]])
                items += [f1, f2, f3]
                return items

            def scan_items(h, g):
                par = g % 2
                t0 = 4 * g
                blk = slice(g * 512, (g + 1) * 512)
                T_ = grp_tk[par]
                kdec, zs, u_, wT, qdT, inT, goT = kdec2[par], zs2[par], u2[par], wT2[par], qdT2[par], inT2[par], goT2[par]
                OB, OBt = OBk[par], OB_tk[par]
                tsl = lambda i: slice(i * 128, (i + 1) * 128)
                items = []
                if g == 0:
                    def s0():
                        P.op("pool", lambda e: e.memset(S_f[:], 0.0), reads=[S_tk], writes=[S_tk])
                        P.op("pool", lambda e: e.memset(S_b[:], 0.0), reads=[S_tk], writes=[S_tk])
                    items.append(s0)
                for i in range(4):
                    t = t0 + i
                    sd = {}

                    def sa(i=i, sd=sd):
                        wsp, wsp_tk = PSs.next()
                        P.op("pe", (lambda: lambda e: e.matmul(wsp[:, 0:128], wT[:, tsl(i)], S_b[:], start=True, stop=True))(), reads=[T_[WT], S_tk], writes=[wsp_tk])
                        sd.update(wsp=wsp, wsp_tk=wsp_tk)

                    def sb_(i=i, sd=sd):
                        wsp, wsp_tk = sd["wsp"], sd["wsp_tk"]
                        vn, vn_tk = vn_r.next()
                        P.op("dve", (lambda: lambda e: e.tensor_tensor(out=vn[:], in0=u_[:, tsl(i)], in1=wsp[:, 0:128], op=ALU.subtract))(), reads=[T_[U_], wsp_tk], writes=[vn_tk])
                        sd.update(vn=vn, vn_tk=vn_tk)

                    def sc_(i=i, sd=sd):
                        wsp, wsp_tk, vn, vn_tk = sd["wsp"], sd["wsp_tk"], sd["vn"], sd["vn_tk"]
                        P.op("pe", (lambda: lambda e: e.matmul(OB[:, tsl(i)], qdT[:, tsl(i)], S_b[:], start=True, stop=False))(), reads=[T_[QDT], S_tk], writes=[OBt], sig=False)
                        P.op("pe", (lambda: lambda e: e.matmul(OB[:, tsl(i)], inT[:, tsl(i)], vn[:], start=False, stop=True))(), reads=[T_[INT], vn_tk], writes=[OBt], sig=False)
                        P.op("pe", (lambda: lambda e: e.matmul(wsp[:, 128:256], kdec[:, tsl(i)], vn[:], start=True, stop=True))(), reads=[T_[KDEC], vn_tk], writes=[wsp_tk])

                    def sd_(i=i, sd=sd, t=t):
                        wsp, wsp_tk = sd["wsp"], sd["wsp_tk"]
                        P.op("dve", (lambda: lambda e: e.scalar_tensor_tensor(out=S_f[:], in0=S_f[:], scalar=cd[:, t, h:h + 1], in1=wsp[:, 128:256], op0=ALU.mult, op1=ALU.add))(),
                             reads=[wsp_tk, par_tk, S_tk], writes=[S_tk])
                        P.op("act", lambda e: e.activation(out=S_b[:], in_=S_f[:], func=AF.Copy), reads=[S_tk], writes=[S_tk])
                    items += [sa, sb_, sc_, sd_]

                def e1():
                    P.op("act", lambda e: e.activation(out=eA[:], in_=OB[:], func=AF.Square), reads=[OBt], writes=[eA_tk])

                def e2():
                    P.op("dve", lambda e: e.tensor_reduce(out=st4[:, 0:4], in_=v3(eA[:]), axis=AX.X, op=ALU.add), reads=[eA_tk, st4_tk], writes=[st4_tk])
                    P.op("dve", lambda e: e.tensor_tensor(out=eA[:], in0=OB[:], in1=zs[:], op=ALU.mult), reads=[OBt, T_[ZS], st4_tk], writes=[eA_tk])

                def e3():
                    P.op("act", lambda e: e.activation(out=st4[:, 4:8], in_=st4[:, 0:4], func=AF.Ln, bias=epsb[:, 0:1], scale=1.0 / 128), reads=[st4_tk, cst_tk], writes=[st4_tk])
                    P.op("act", lambda e: e.activation(out=st4[:, 4:8], in_=st4[:, 4:8], func=AF.Exp, scale=-0.5), reads=[st4_tk], writes=[st4_tk])

                def e4():
                    P.op("pool", lambda e: e.tensor_tensor(out=v3(go_b[:]), in0=v3(eA[:]), in1=st4[:, 4:8].unsqueeze(2).to_broadcast([128, 4, 128]), op=ALU.mult),
                         reads=[eA_tk, st4_tk], writes=[go_tk])

                def e5():
                    gp, gp_tk = PSs.next()
                    for i in range(4):
                        P.op("pe", (lambda i=i: lambda e: e.matmul(gp[:, tsl(i)], go_b[:, tsl(i)], ident_bf[:], start=True, stop=True))(), reads=[go_tk, cb_tk], writes=[gp_tk], sig=(i == 3))
                    P.op("act", (lambda: lambda e: e.activation(out=goT[:], in_=gp[:], func=AF.Copy))(), reads=[gp_tk], writes=[T_[GOT]])
                items += [e1, e2, e3, e4, e5]
                wo, wo_tk = wo2[h % 2], wo2_tk[h % 2]
                for n0 in range(0, 8, 2):
                    def op_(n0=n0):
                        for n_ in range(n0, n0 + 2):
                            ps_, ps_tk_ = PSs.next()
                            P.op("pe", (lambda ps_=ps_, n_=n_: lambda e: e.matmul(ps_[:], wo[:, 0, n_ * 128:(n_ + 1) * 128], goT[:], start=True, stop=True))(),
                                 reads=[wo_tk, T_[GOT]], writes=[ps_tk_])
                            P.op("dve", (lambda ps_=ps_, n_=n_: lambda e: e.tensor_tensor(out=xT[:, n_, blk], in0=ps_[:], in1=xT[:, n_, blk], op=ALU.add))(),
                                 reads=[ps_tk_], writes=[xT_tk[n_][g]])
                    items.append(op_)
                return items

            def merge(a_, b_):
                if not a_:
                    return list(b_)
                if not b_:
                    return list(a_)
                if len(a_) < len(b_):
                    a_, b_ = b_, a_
                out = []
                ratio = len(a_) / float(len(b_))
                bi = 0
                for i_, it_ in enumerate(a_):
                    out.append(it_)
                    while bi < len(b_) and (bi + 1) * ratio <= i_ + 1:
                        out.append(b_[bi]); bi += 1
                out.extend(b_[bi:])
                return out

            for tb_ in range(NB):
                for it_ in pstage_items(0, tb_):
                    it_()
            prev_scan = []
            for h in range(8):
                for g in range(4):
                    if g >= 1 and h + 1 < 8:
                        pit = pstage_items(h + 1, g - 1)
                    elif g == 0 and h >= 1:
                        pit = pstage_items(h, 3)
                    else:
                        pit = []
                    for it_ in merge(merge(prep_items(h, g), prev_scan), pit):
                        it_()
                    prev_scan = scan_items(h, g)
            for it_ in prev_scan:
                it_()
            for tl in grp_tk:
                pass
            mytks.extend(vn_r.tks)
            retire(mytks)
            close_stage()

        def attention():
            with ExitStack() as ph:
                attention_body(ph)

        def attention_body(ph):
            import math
            from collections import deque
            lam_init = 0.8 - 0.6 * math.exp(-0.3 * 1)
            open_stage(ph, 2048)
            psb = lambda name, shape, dt=F32: ph.enter_context(nc.sbuf_tensor(name, list(shape), dt))
            mytks = []

            def tk():
                t = Tk(); mytks.append(t); return t
            PSa = Ring(banks[0:2], PS.tks[0:2])
            PSb = Ring(banks[2:4], PS.tks[2:4])
            acc_b, acc_tk = banks[4:8], PS.tks[4:8]
            pending = deque()

            def drain(n):
                while n > 0 and pending:
                    it_ = pending.popleft()
                    if it_ is not None:
                        it_()
                    n -= 1
            psem = P.new_dma_sem("at_prm"); lsem = P.new_dma_sem("at_lamb"); possem = P.new_dma_sem("at_pos"); gsem = P.new_dma_sem("at_g")
            prm = psb("at_prm", [128, 8]); prm_tk = tk()
            for half in range(2):
                P.dma("sp", psem, (lambda half=half: lambda e: e.dma_start(out=prm[half * 64:(half + 1) * 64, 0:1],
                      in_=dr["b_q_norm"][0, :].rearrange("(p o) -> p o", o=1), allow_slow_non_contiguous=True))(), writes=[prm_tk])
                P.dma("sp", psem, (lambda half=half: lambda e: e.dma_start(out=prm[half * 64:(half + 1) * 64, 1:2],
                      in_=dr["k_norm"][0, :].rearrange("(p o) -> p o", o=1), allow_slow_non_contiguous=True))(), writes=[prm_tk])
            P.dma("sp", psem, lambda e: e.dma_start(out=prm[:, 2:3], in_=dr["b_sub_norm"][0, :].rearrange("(p o) -> p o", o=1),
                  allow_slow_non_contiguous=True), writes=[prm_tk])
            lamb = psb("at_lamb", [128, 4, 64]); lamb_tk = tk()
            P.dma("sp", lsem, lambda e: e.dma_start(out=lamb[:].rearrange("p a b -> p (a b)"),
                  in_=dr["b_lambda"].rearrange("a b -> (a b)").partition_broadcast(128)), writes=[lamb_tk])
            gcol = psb("at_gcol", [128, 2, 128]); gcol_tk = tk()
            for half in range(2):
                P.dma("sp", gsem, (lambda half=half: lambda e: e.dma_start(out=gcol[:, 0, half * 64:(half + 1) * 64], in_=dr["b_q_norm"][0, :].partition_broadcast(128)))(), writes=[gcol_tk])
                P.dma("sp", gsem, (lambda half=half: lambda e: e.dma_start(out=gcol[:, 1, half * 64:(half + 1) * 64], in_=dr["k_norm"][0, :].partition_broadcast(128)))(), writes=[gcol_tk])
            P.op("dve", lambda e: e.tensor_scalar(out=gcol[:, 0, :], in0=gcol[:, 0, :], scalar1=0.125, scalar2=None, op0=ALU.mult), reads=[gcol_tk], writes=[gcol_tk])
            P.op("dve", lambda e: e.tensor_scalar(out=prm[:, 0:1], in0=prm[:, 0:1], scalar1=0.125, scalar2=None, op0=ALU.mult), reads=[prm_tk], writes=[prm_tk])
            P.op("dve", lambda e: e.tensor_scalar(out=prm[:, 2:3], in0=prm[:, 2:3], scalar1=float(1.0 - lam_init), scalar2=None, op0=ALU.mult), reads=[prm_tk], writes=[prm_tk])
            bo = psb("at_bo", [128, 2, 128]); bo_tk = tk()
            ig = psb("at_ig", [128, 2])
            P.op("dve", lambda e: e.tensor_tensor(out=ig[:], in0=prm[:, 0:2], in1=prm[:, 0:2], op=ALU.mult), reads=[prm_tk], writes=[bo_tk])
            P.op("dve", lambda e: e.reciprocal(out=prm[:, 6:8], in_=ig[:]), reads=[bo_tk, prm_tk], writes=[prm_tk])
            for i_ in range(2):
                P.op("dve", (lambda i_=i_: lambda e: e.tensor_scalar(out=bo[:, i_, :], in0=cst["blockones"][:], scalar1=prm[:, 6 + i_:7 + i_], scalar2=None, op0=ALU.mult))(),
                     reads=[prm_tk, cst_tk, bo_tk], writes=[bo_tk])
            bo_bf = psb("at_bo_bf", [128, 2, 128], BF16)
            P.op("dve", lambda e: e.tensor_copy(out=bo_bf[:], in_=bo[:]), reads=[bo_tk], writes=[bo_tk])
            lp = psb("at_lp", [128, 2, 64]); lp_tk = tk()
            P.op("dve", lambda e: e.tensor_tensor(out=lp[:, 0, :], in0=lamb[:, 0, :], in1=lamb[:, 1, :], op=ALU.mult), reads=[lamb_tk], writes=[lp_tk])
            P.op("dve", lambda e: e.tensor_tensor(out=lp[:, 1, :], in0=lamb[:, 2, :], in1=lamb[:, 3, :], op=ALU.mult), reads=[lamb_tk, lp_tk], writes=[lp_tk])
            P.op("dve", lambda e: e.tensor_reduce(out=prm[:, 4:6], in_=lp[:], axis=AX.X, op=ALU.add), reads=[lp_tk, prm_tk], writes=[prm_tk])
            P.op("act", lambda e: e.activation(out=prm[:, 4:6], in_=prm[:, 4:6], func=AF.Exp), reads=[prm_tk], writes=[prm_tk])
            P.op("dve", lambda e: e.tensor_tensor(out=prm[:, 3:4], in0=prm[:, 5:6], in1=prm[:, 4:5], op=ALU.subtract), reads=[prm_tk], writes=[prm_tk])
            P.op("dve", lambda e: e.tensor_scalar(out=prm[:, 3:4], in0=prm[:, 3:4], scalar1=float(-lam_init), scalar2=None, op0=ALU.add), reads=[prm_tk], writes=[prm_tk])
            ones_bf = psb("at_ones_bf", [128, 128], BF16); tri_bf = psb("at_tri_bf", [128, 128], BF16)
            cb_tk = tk()
            P.op("dve", lambda e: e.tensor_copy(out=ones_bf[:], in_=cst["ones"][:]), reads=[cst_tk], writes=[cb_tk])
            P.op("dve", lambda e: e.tensor_copy(out=tri_bf[:], in_=cst["trimask"][:]), reads=[cst_tk, cb_tk], writes=[cb_tk])
            cosT = psb("at_cosT", [128, S], BF16); sinT = psb("at_sinT", [128, S], BF16); cs_tk = tk()
            with ExitStack() as ph2:
                posi = ph2.enter_context(nc.sbuf_tensor("at_posi", [128, S], I32))
                ang = ph2.enter_context(nc.sbuf_tensor("at_ang", [128, S], F32))
                kf = ph2.enter_context(nc.sbuf_tensor("at_kf", [128, S], F32))
                t2 = [Tk() for _ in range(3)]
                P.dma("sp", possem, lambda e: e.dma_start(out=posi[:], in_=pos_d[0, :].partition_broadcast(128)), writes=[t2[0]])
                for which, dst in ((0, sinT), (1, cosT)):
                    P.op("dve", lambda e: e.tensor_copy(out=ang[:], in_=posi[:]), reads=[t2[0]], writes=[t2[1]])
                    P.op("dve", (lambda which=which: lambda e: e.tensor_scalar(out=ang[:], in0=ang[:], scalar1=cst["freq"][:, 0:1], scalar2=float(which * np.pi / 2), op0=ALU.mult, op1=ALU.add))(),
                         reads=[t2[1], cst_tk], writes=[t2[1]])
                    P.op("dve", lambda e: e.tensor_scalar(out=kf[:], in0=ang[:], scalar1=float(1.0 / (2 * np.pi)), scalar2=None, op0=ALU.mult), reads=[t2[1]], writes=[t2[2]])
                    P.op("dve", lambda e: e.tensor_copy(out=kf[:].bitcast(I32), in_=kf[:]), reads=[t2[2]], writes=[t2[2]])
                    P.op("dve", lambda e: e.tensor_copy(out=kf[:], in_=kf[:].bitcast(I32)), reads=[t2[2]], writes=[t2[2]])
                    P.op("dve", lambda e: e.scalar_tensor_tensor(out=ang[:], in0=kf[:], scalar=-6.28125, in1=ang[:], op0=ALU.mult, op1=ALU.add), reads=[t2[1], t2[2]], writes=[t2[1]])
                    P.op("dve", lambda e: e.scalar_tensor_tensor(out=ang[:], in0=kf[:], scalar=-float(2 * np.pi - 6.28125), in1=ang[:], op0=ALU.mult, op1=ALU.add), reads=[t2[1], t2[2]], writes=[t2[1]])
                    P.op("dve", lambda e: e.tensor_scalar(out=ang[:], in0=ang[:], scalar1=3.14159, scalar2=-3.14159, op0=ALU.min, op1=ALU.max), reads=[t2[1]], writes=[t2[1]])
                    P.op("act", (lambda dst=dst: lambda e: e.activation(out=dst[:], in_=ang[:], func=AF.Sin))(), reads=[t2[1]], writes=[cs_tk])
                retire(t2)
            wq = psb("at_wq", [128, 8, 128], BF16); wqr = psb("at_wqr", [128, 8, 128], BF16)
            wk = psb("at_wk", [128, 8, 128], BF16); wkr = psb("at_wkr", [128, 8, 128], BF16)
            wv = psb("at_wv", [128, 8, 128], BF16)
            wq_tk = tk(); wqr_tk = tk(); wk_tk = tk(); wkr_tk = tk(); wv_tk = tk()
            wo2 = [psb("at_wo%d" % i, [128, 2, D], BF16) for i in range(2)]; wo2_tk = [tk() for _ in range(2)]
            qT2 = [psb("at_qT%d" % i, [128, S], BF16) for i in range(2)]; qT2_tk = [tk() for _ in range(2)]
            kT2 = [psb("at_kT%d" % i, [128, S], BF16) for i in range(2)]; kT2_tk = [tk() for _ in range(2)]
            vt2 = [psb("at_vt%d" % i, [128, NT, 128], BF16) for i in range(2)]; vt2_tk = [tk() for _ in range(2)]
            aoT = psb("at_aoT", [128, 2, S], BF16); aoT_tk = tk()
            pT_r = Ring([psb("at_pT%d" % i, [128, 512], BF16) for i in range(4)])
            om_r = Ring([psb("at_om%d" % i, [128, 512]) for i in range(6)])
            rings = [pT_r, om_r]

            def rot_weights(w, w_tk, wr, wr_tk):
                wv4 = w[:].rearrange("p c (g r d) -> p c g r d", g=2, r=2)
                wr4 = wr[:].rearrange("p c (g r d) -> p c g r d", g=2, r=2)
                for g in range(2):
                    P.op("pool", (lambda g=g: lambda e: e.tensor_scalar(out=wr4[:, :, g, 0, :], in0=wv4[:, :, g, 1, :], scalar1=-1.0, scalar2=None, op0=ALU.mult))(),
                         reads=[w_tk], writes=[wr_tk])
                    P.op("pool", (lambda g=g: lambda e: e.tensor_copy(out=wr4[:, :, g, 1, :], in_=wv4[:, :, g, 0, :]))(), reads=[w_tk, wr_tk], writes=[wr_tk])

            def prologue_items(h):
                par = h % 2
                hs = slice(h * 128, (h + 1) * 128)
                qT, qT_tk, kT, kT_tk, vt, vt_tk = qT2[par], qT2_tk[par], kT2[par], kT2_tk[par], vt2[par], vt2_tk[par]
                items = []

                def it_w():
                    load_w(wq, wq_tk, dr["b_w_q"][:, hs], 8, 128, 3, colgain=gcol[:, 0, :], colgain_tk=gcol_tk)
                    rot_weights(wq, wq_tk, wqr, wqr_tk)
                    load_w(wk, wk_tk, dr["w_kv"][:, hs], 8, 128, 2, colgain=gcol[:, 1, :], colgain_tk=gcol_tk)
                    rot_weights(wk, wk_tk, wkr, wkr_tk)
                    load_w(wv, wv_tk, dr["w_kv"][:, 1024 + h * 128:1024 + (h + 1) * 128], 8, 128, 2)
                    if h % 2 == 0:
                        load_w(wo2[(h // 2) % 2], wo2_tk[(h // 2) % 2], dr["b_w_out"][h * 128:(h + 2) * 128, :], 2, D, None)
                items += [it_w, None, None, None, None, None]
                for tg in range(4):
                    st = {}

                    def v1(tg=tg, st=st):
                        ps, ps_tk = PSb.next()
                        for j in range(4):
                            t = tg * 4 + j
                            for c in range(8):
                                P.op("pe", (lambda ps=ps, j=j, t=t, c=c: lambda e: e.matmul(
                                    ps[:, j * 128:(j + 1) * 128], xnT[:, c, t * 128:(t + 1) * 128], wv[:, c, :], start=(c == 0), stop=(c == 7)))(),
                                    reads=[xnT_tk[tg], wv_tk], writes=[ps_tk], sig=(j == 3 and c == 7))
                        st.update(ps=ps, ps_tk=ps_tk)

                    def v2(tg=tg, st=st):
                        ps, ps_tk = st["ps"], st["ps_tk"]
                        P.op("dve", (lambda ps=ps: lambda e: e.tensor_copy(out=vt[:, tg * 4:(tg + 1) * 4, :], in_=ps[:].rearrange("p (j n) -> p j n", j=4)))(),
                             reads=[ps_tk], writes=[vt_tk])
                    items += [v1, None, v2]
                for (w, w_tk, wr, wr_tk, dst, dst_tk, gi) in ((wq, wq_tk, wqr, wqr_tk, qT, qT_tk, 0), (wk, wk_tk, wkr, wkr_tk, kT, kT_tk, 1)):
                    for tb in range(NB):
                        st = {}

                        def s1(w=w, w_tk=w_tk, wr=wr, wr_tk=wr_tk, tb=tb, st=st):
                            blk = slice(tb * 512, (tb + 1) * 512)
                            ps, ps_tk = PSb.next()
                            for c in range(8):
                                P.op("pe", (lambda ps=ps, c=c: lambda e: e.matmul(ps[:], w[:, c, :], xnT[:, c, blk], start=(c == 0), stop=(c == 7)))(),
                                     reads=[w_tk, xnT_tk[tb]], writes=[ps_tk], sig=(c == 7))
                            psr, psr_tk = PSb.next()
                            for c in range(8):
                                P.op("pe", (lambda psr=psr, c=c: lambda e: e.matmul(psr[:], wr[:, c, :], xnT[:, c, blk], start=(c == 0), stop=(c == 7)))(),
                                     reads=[wr_tk, xnT_tk[tb]], writes=[psr_tk], sig=(c == 7))
                            st.update(ps=ps, ps_tk=ps_tk, psr=psr, psr_tk=psr_tk)

                        def s2(tb=tb, st=st):
                            blk = slice(tb * 512, (tb + 1) * 512)
                            ps, ps_tk, psr, psr_tk = st["ps"], st["ps_tk"], st["psr"], st["psr_tk"]
                            sq, sq_tk = tmpf.next()
                            P.op("act", (lambda: lambda e: e.activation(out=bfv(sq), in_=ps[:], func=AF.Square))(), reads=[ps_tk], writes=[sq_tk])
                            t2_, t2_tk = om_r.next()
                            P.op("dve", (lambda: lambda e: e.tensor_tensor(out=t2_[:], in0=psr[:], in1=sinT[:, blk], op=ALU.mult))(), reads=[psr_tk, cs_tk], writes=[t2_tk])
                            t1, t1_tk = om_r.next()
                            P.op("dve", (lambda: lambda e: e.tensor_tensor(out=t1[:], in0=ps[:], in1=cosT[:, blk], op=ALU.mult))(), reads=[ps_tk, cs_tk], writes=[t1_tk])
                            st.update(sq=sq, sq_tk=sq_tk, t1=t1, t1_tk=t1_tk, t2=t2_, t2_tk=t2_tk)

                        def s3(st=st, gi=gi):
                            sq, sq_tk = st["sq"], st["sq_tk"]
                            ps2, ps2_tk = PSb.next()
                            P.op("pe", (lambda: lambda e: e.matmul(ps2[:], bo_bf[:, gi, :], bfv(sq), start=True, stop=True))(), reads=[sq_tk, bo_tk], writes=[ps2_tk])
                            st.update(ps2=ps2, ps2_tk=ps2_tk)

                        def s4(st=st):
                            ps2, ps2_tk = st["ps2"], st["ps2_tk"]
                            rr, rr_tk = tmpf.next()
                            P.op("act", (lambda: lambda e: e.activation(out=rr[:], in_=ps2[:], func=AF.Ln, bias=epsb[:, 0:1], scale=1.0 / 64))(),
                                 reads=[ps2_tk, cst_tk], writes=[rr_tk])
                            P.op("act", (lambda: lambda e: e.activation(out=rr[:], in_=rr[:], func=AF.Exp, scale=-0.5))(), reads=[rr_tk], writes=[rr_tk])
                            st.update(rr=rr, rr_tk=rr_tk)

                        def s5(tb=tb, st=st, dst=dst, dst_tk=dst_tk):
                            blk = slice(tb * 512, (tb + 1) * 512)
                            t1, t1_tk, t2_, t2_tk, rr, rr_tk = st["t1"], st["t1_tk"], st["t2"], st["t2_tk"], st["rr"], st["rr_tk"]
                            P.op("pool", (lambda: lambda e: e.tensor_tensor(out=t1[:], in0=t1[:], in1=t2_[:], op=ALU.add))(), reads=[t1_tk, t2_tk], writes=[t1_tk])
                            P.op("pool", (lambda: lambda e: e.tensor_tensor(out=dst[:, blk], in0=t1[:], in1=rr[:], op=ALU.mult))(), reads=[t1_tk, rr_tk], writes=[dst_tk])
                        items += [s1, None, s2, s3, None, s4, s5]
                return items

            def outproj_items(hp):
                items = []
                wo, wo_tk = wo2[hp % 2], wo2_tk[hp % 2]
                for tb in range(NB):
                    for n0 in range(0, 8, 2):
                        def it(tb=tb, n0=n0):
                            blk = slice(tb * 512, (tb + 1) * 512)
                            for n_ in range(n0, n0 + 2):
                                ps, ps_tk = PSb.next()
                                for hh in range(2):
                                    P.op("pe", (lambda ps=ps, hh=hh, n_=n_: lambda e: e.matmul(
                                        ps[:], wo[:, hh, n_ * 128:(n_ + 1) * 128], aoT[:, hh, blk], start=(hh == 0), stop=(hh == 1)))(),
                                        reads=[wo_tk, aoT_tk], writes=[ps_tk], sig=(hh == 1))
                                P.op("dve", (lambda ps=ps, n_=n_: lambda e: e.tensor_tensor(out=xT[:, n_, blk], in0=ps[:], in1=xT[:, n_, blk], op=ALU.add))(),
                                     reads=[ps_tk], writes=[xT_tk[n_][tb]])
                        it.is_outproj = True
                        items.append(it)
                return items

            for it in prologue_items(0):
                if it is not None:
                    it()
            for h in range(8):
                par = h % 2
                qT, qT_tk, kT, kT_tk, vt, vt_tk = qT2[par], qT2_tk[par], kT2[par], kT2_tk[par], vt2[par], vt2_tk[par]
                drain(10 ** 9)
                if h + 1 < 8:
                    pending.extend(prologue_items(h + 1))
                tiles = []
                for qb in range(NB):
                    for kt in range(4 * qb + 4):
                        for m in range(2):
                            tiles.append((qb, kt, m))

                def emit_score(i):
                    qb, kt, m = tiles[i]
                    j = kt - 4 * qb
                    q0 = 0 if j < 0 else 128 * j
                    n = 512 - q0
                    ms = slice(m * 64, (m + 1) * 64)
                    ps, ps_tk = PSa.next()
                    P.op("pe", (lambda ps=ps, kT=kT, qT=qT: lambda e: e.matmul(ps[:, 0:n], kT[ms, kt * 128:(kt + 1) * 128], qT[ms, qb * 512 + q0:(qb + 1) * 512], start=True, stop=True))(),
                         reads=[kT_tk, qT_tk], writes=[ps_tk])
                    return ps, ps_tk
                nxt = emit_score(0)
                for i, (qb, kt, m) in enumerate(tiles):
                    ps, ps_tk = nxt
                    if i + 1 < len(tiles):
                        nxt = emit_score(i + 1)
                    j = kt - 4 * qb
                    q0 = 0 if j < 0 else 128 * j
                    n = 512 - q0
                    nkt = 4 * qb + 4
                    pT, pT_tk = pT_r.next()
                    P.op("act", (lambda pT=pT, ps=ps, n=n: lambda e: e.activation(out=pT[:, 0:n], in_=ps[:, 0:n], func=AF.Exp))(), reads=[ps_tk], writes=[pT_tk])
                    if j >= 0:
                        P.op("dve", (lambda pT=pT: lambda e: e.tensor_tensor(out=pT[:, 0:128], in0=pT[:, 0:128], in1=tri_bf[:], op=ALU.mult))(),
                             reads=[pT_tk, cb_tk], writes=[pT_tk])
                    P.op("pe", (lambda pT=pT, m=m, kt=kt, q0=q0, n=n, nkt=nkt, vt=vt: lambda e: e.matmul(
                        acc_b[2 * m][:, q0:512], vt[:, kt, :], pT[:, 0:n], start=(kt == 0), stop=(kt == nkt - 1)))(),
                        reads=[pT_tk, vt_tk], writes=[acc_tk[2 * m]], sig=False)
                    P.op("pe", (lambda pT=pT, m=m, kt=kt, q0=q0, n=n, nkt=nkt: lambda e: e.matmul(
                        acc_b[2 * m + 1][:, q0:512], ones_bf[:], pT[:, 0:n], start=(kt == 0), stop=(kt == nkt - 1)))(),
                        reads=[pT_tk, cb_tk], writes=[acc_tk[2 * m + 1]], sig=True)
                    drain(1)
                    if kt == nkt - 1 and m == 1:
                        blk = slice(qb * 512, (qb + 1) * 512)
                        while any(getattr(it_, "is_epi", False) for it_ in pending):
                            drain(1)
                        o_s = []; rd_s = []
                        for mm in range(2):
                            o_, o_tk = om_r.next()
                            P.op("act", (lambda o_=o_, mm=mm: lambda e: e.activation(out=o_[:], in_=acc_b[2 * mm][:], func=AF.Copy))(), reads=[acc_tk[2 * mm]], writes=[o_tk])
                            rd, rd_tk = om_r.next()
                            P.op("dve", (lambda rd=rd, mm=mm: lambda e: e.tensor_copy(out=rd[:], in_=acc_b[2 * mm + 1][:]))(), reads=[acc_tk[2 * mm + 1]], writes=[rd_tk])
                            o_s.append((o_, o_tk)); rd_s.append((rd, rd_tk))
                        st = {}

                        def e0(rd_s=rd_s):
                            for mm in range(2):
                                rd, rd_tk = rd_s[mm]
                                P.op("act", (lambda rd=rd: lambda e: e.activation(out=rd[:], in_=rd[:], func=AF.Ln))(), reads=[rd_tk], writes=[rd_tk])
                                P.op("act", (lambda rd=rd: lambda e: e.activation(out=rd[:], in_=rd[:], func=AF.Exp, scale=-1.0))(), reads=[rd_tk], writes=[rd_tk])

                        def e1(o_s=o_s, rd_s=rd_s, st=st):
                            for mm in range(2):
                                P.op("pool", (lambda mm=mm: lambda e: e.tensor_tensor(out=o_s[mm][0][:], in0=o_s[mm][0][:], in1=rd_s[mm][0][:], op=ALU.mult))(),
                                     reads=[o_s[mm][1], rd_s[mm][1]], writes=[o_s[mm][1]])
                            df, df_tk = rd_s[0]
                            P.op("dve", (lambda df=df: lambda e: e.scalar_tensor_tensor(out=df[:], in0=o_s[1][0][:], scalar=prm[:, 3:4], in1=o_s[0][0][:], op0=ALU.mult, op1=ALU.add))(),
                                 reads=[o_s[0][1], o_s[1][1], prm_tk], writes=[df_tk])
                            st.update(df=df, df_tk=df_tk)

                        def e1b(rd_s=rd_s, st=st):
                            df, df_tk = st["df"], st["df_tk"]
                            sq, sq_tk = rd_s[1]
                            P.op("act", (lambda: lambda e: e.activation(out=bfv(sq), in_=df[:], func=AF.Square))(), reads=[df_tk], writes=[sq_tk])
                            st.update(sq=sq, sq_tk=sq_tk)

                        def e1c(st=st):
                            sq, sq_tk = st["sq"], st["sq_tk"]
                            ps2, ps2_tk = PSb.next()
                            P.op("pe", (lambda: lambda e: e.matmul(ps2[:], ones_bf_p[:], bfv(sq), start=True, stop=True))(), reads=[sq_tk, onesb_tk], writes=[ps2_tk])
                            st.update(ps2=ps2, ps2_tk=ps2_tk)

                        def e2(st=st, h=h, blk=blk):
                            df, df_tk, ps2, ps2_tk = st["df"], st["df_tk"], st["ps2"], st["ps2_tk"]
                            rr, rr_tk = tmpf.next()
                            P.op("act", (lambda: lambda e: e.activation(out=rr[:], in_=ps2[:], func=AF.Ln, bias=epsb[:, 0:1], scale=1.0 / 128))(),
                                 reads=[ps2_tk, cst_tk], writes=[rr_tk])
                            P.op("act", (lambda: lambda e: e.activation(out=rr[:], in_=rr[:], func=AF.Exp, scale=-0.5))(), reads=[rr_tk], writes=[rr_tk])
                            st.update(rr=rr, rr_tk=rr_tk)

                        def e3(st=st, h=h, blk=blk):
                            df, df_tk, rr, rr_tk = st["df"], st["df_tk"], st["rr"], st["rr_tk"]
                            P.op("dve", (lambda: lambda e: e.scalar_tensor_tensor(
                                out=aoT[:, h % 2, blk], in0=df[:], scalar=prm[:, 2:3], in1=rr[:], op0=ALU.mult, op1=ALU.mult))(),
                                reads=[df_tk, rr_tk, prm_tk], writes=[aoT_tk])
                        epi = [e0, e1, None, e1b, e1c, None, e2, e3]
                        for it_ in epi:
                            if it_ is not None:
                                it_.is_epi = True
                        lst = list(pending)
                        pos_ = 0
                        for ii, it_ in enumerate(lst):
                            if getattr(it_, "is_outproj", False):
                                pos_ = ii + 1
                        pending.clear()
                        pending.extend(lst[:pos_] + epi + lst[pos_:])
                        if qb == NB - 1 and h % 2 == 1:
                            lst = list(pending)
                            pos_ = lst.index(e3) + 1
                            pending.clear()
                            pending.extend(lst[:pos_] + outproj_items(h // 2) + lst[pos_:])
            drain(10 ** 9)
            for r in rings:
                mytks.extend(r.tks)
            retire(mytks)
            close_stage()

        if "gdn" in stages:
            rms_to_xnT()
            gdn()
        if "mlp0" in stages:
            rms_to_xnT()
            mlp(0, 1)
        if "attn" in stages:
            rms_to_xnT()
            attention()
        if "mlp1" in stages:
            rms_to_xnT()
            mlp(1, 4)

        osem = [P.new_dma_sem("out%d" % i) for i in range(2)]
        phf = ExitStack()
        open_stage(phf, 1024)
        stage = stage_box["ring"]
        for t in range(NT):
            tb = t // 4
            o, o_tk = stage.next()
            osi = (stage.i - 1) % 2
            for cg in range(2):
                ps, ps_tk = PS.next()
                for j in range(4):
                    c = cg * 4 + j
                    P.op("pe", (lambda ps=ps, c=c, j=j, t=t: lambda e: e.transpose(
                        ps[:, j * 128:(j + 1) * 128], xT[:, c, t * 128:(t + 1) * 128], cst["ident"][:]))(),
                        reads=[xT_tk[c][tb], cst_tk], writes=[ps_tk], sig=(j == 3))
                evac_copy(o[:, cg * 512:(cg + 1) * 512], ps[:], reads=[ps_tk], writes=[o_tk])
            P.dma("sp", osem[osi], (lambda o=o, t=t: lambda e: e.dma_start(out=out_d[t * 128:(t + 1) * 128, :], in_=o[:, 0:1024]))(), reads=[o_tk])
        P.final_wait("sp", osem[0])
        P.final_wait("sp", osem[1])
        phf.close()
        P.emit()
        nops = {e: len(P.ops[e]) for e in P.ENG}
        nops["splits"] = P.split_log
    return nc, nops


_CACHE = {}


def kernel(**inputs):
    B = inputs["x"].shape[0]
    consts = host_consts()
    if "nc" not in _CACHE:
        _CACHE["nc"] = build_program()
    nc, _ = _CACHE["nc"]
    shared = {}
    f32 = lambda a: np.ascontiguousarray(np.asarray(a, dtype=np.float32))
    shared["a_norm"] = f32(inputs["a_norm"]).reshape(1, D)
    shared["a_w_in"] = f32(inputs["a_w_in"]).reshape(D, 4112)
    shared["a_conv_w"] = f32(inputs["a_conv_w"]).reshape(4, 3072)
    shared["a_a_log"] = f32(inputs["a_a_log"]).reshape(1, 8)
    shared["a_dt_bias"] = f32(inputs["a_dt_bias"]).reshape(1, 8)
    shared["a_out_norm"] = f32(inputs["a_out_norm"]).reshape(1, 128)
    shared["a_w_out"] = f32(inputs["a_w_out"]).reshape(D, D)
    shared["kv_norm"] = f32(inputs["kv_norm"]).reshape(1, D)
    shared["w_kv"] = f32(inputs["w_kv"]).reshape(D, 2048)
    shared["k_norm"] = f32(inputs["k_norm"]).reshape(1, 64)
    shared["b_norm"] = f32(inputs["b_norm"]).reshape(1, D)
    shared["b_w_q"] = f32(inputs["b_w_q"]).reshape(D, D)
    shared["b_q_norm"] = f32(inputs["b_q_norm"]).reshape(1, 64)
    shared["b_lambda"] = f32(inputs["b_lambda"]).reshape(4, 64)
    shared["b_sub_norm"] = f32(inputs["b_sub_norm"]).reshape(1, 128)
    shared["b_w_out"] = f32(inputs["b_w_out"]).reshape(D, D)
    shared["mlp_norm"] = f32(inputs["mlp_norm"]).reshape(2, D)
    shared["mlp_w1"] = f32(inputs["mlp_w1"]).reshape(2 * D, DFF)
    shared["mlp_w2"] = f32(inputs["mlp_w2"]).reshape(2 * DFF, D)
    for n in CONST_NAMES:
        shared["c_" + n] = consts[n]
    x = f32(inputs["x"])
    pos = np.ascontiguousarray(np.asarray(inputs["positions"], dtype=np.int32))
    in_maps = []
    for b in range(B):
        m = dict(shared)
        m["x"] = x[b]
        m["positions"] = pos[b].reshape(1, S)
        in_maps.append(m)
    res = run_bass_kernel_spmd(nc, in_maps, core_ids=list(range(B)))
    return np.stack([np.asarray(r["out"], dtype=np.float32) for r in res.results], axis=0)
```

```python
import numpy as np
import concourse.bass as bass
import concourse.mybir as mybir
from concourse.bass_utils import run_bass_kernel_spmd
from contextlib import ExitStack

F32 = mybir.dt.float32
BF16 = mybir.dt.bfloat16
I32 = mybir.dt.int32
AF = mybir.ActivationFunctionType
ALU = mybir.AluOpType
AX = mybir.AxisListType

S = 2048
D = 1024
NT = 16
NB = 4
DFF = 4096
EPS = 1e-6


INHERIT = {}


class Tk:
    __slots__ = ("name", "w", "r", "excl")

    def __init__(self, name="", excl=False):
        self.name = name
        self.w = None
        self.r = dict(INHERIT)
        self.excl = excl


def retire(tks):
    for t in tks:
        if t.w is not None:
            k, v = t.w
            if INHERIT.get(k, 0) < v:
                INHERIT[k] = v
        for k, v in t.r.items():
            if INHERIT.get(k, 0) < v:
                INHERIT[k] = v


class Prog:
    ENG = ("pe", "act", "dve", "pool", "sp")

    def __init__(self, nc, es):
        self.nc = nc
        self.es = es
        self.ops = {e: [] for e in self.ENG}
        self.sems = {}
        self.cnt = {}
        for e in self.ENG:
            self.sems[e] = es.enter_context(nc.semaphore("s_" + e))
            self.cnt[e] = 0
        self.waited = {e: {} for e in self.ENG}
        self.pending_nosig = {e: False for e in self.ENG}

    def new_dma_sem(self, name):
        key = "dma_" + name
        self.sems[key] = self.es.enter_context(self.nc.semaphore(key))
        self.cnt[key] = 0
        return key

    def _deps(self, eng, reads, writes):
        deps = {}

        def add(d):
            if d is None:
                return
            k, v = d
            if deps.get(k, 0) < v:
                deps[k] = v
        for t in reads:
            add(t.w)
            if t.excl:
                for k, v in t.r.items():
                    if k != eng:
                        add((k, v))
        for t in writes:
            add(t.w)
            for k, v in t.r.items():
                add((k, v))
        waits = []
        wd = self.waited[eng]
        for k, v in deps.items():
            if k == "pe" and eng == "pe":
                continue
            if wd.get(k, 0) < v:
                wd[k] = v
                waits.append((k, v))
        return waits

    def op(self, eng, fn, reads=(), writes=(), sig=True):
        assert sig or eng == "pe"
        waits = self._deps(eng, reads, writes)
        if sig:
            self.cnt[eng] += 1
            val = self.cnt[eng]
        else:
            val = self.cnt[eng] + 1
        self.pending_nosig[eng] = not sig
        self.ops[eng].append((waits, fn, (eng, 1) if sig else None))
        for t in reads:
            if t.r.get(eng, 0) < val:
                t.r[eng] = val
        for t in writes:
            t.w = (eng, val)
            t.r = {}

    def dma(self, q, semkey, fn, reads=(), writes=()):
        assert q == "sp"
        waits = self._deps(q, reads, writes)
        self.cnt[semkey] += 1
        val = self.cnt[semkey]
        self.ops[q].append((waits, fn, (semkey, 16)))
        for t in reads:
            if t.r.get(semkey, 0) < val:
                t.r[semkey] = val
        for t in writes:
            t.w = (semkey, val)
            t.r = {}

    def final_wait(self, eng, semkey):
        if self.cnt[semkey] > 0:
            self.ops[eng].append(([(semkey, self.cnt[semkey])], None, None))

    def emit(self):
        nc = self.nc
        for e in self.ENG:
            assert not self.pending_nosig[e], e
        actual = {k: [0] for k in self.sems if k.startswith("dma_")}
        self.split_log = []
        with nc.Block() as block:
            def run(engname):
                def body(eng):
                    for waits, fn, inc in self.ops[engname]:
                        for k, v in waits:
                            eng.wait_ge(self.sems[k], actual[k][v] if k in actual else v)
                        if fn is not None:
                            n0 = nc.n_instructions()
                            ins = fn(eng)
                            if inc is not None:
                                ins.then_inc(self.sems[inc[0]], inc[1])
                                if inc[0] in actual:
                                    k_ = max(1, nc.n_instructions() - n0)
                                    if k_ != 1:
                                        self.split_log.append((inc[0], len(actual[inc[0]]), k_))
                                    actual[inc[0]].append(actual[inc[0]][-1] + 16 * k_)
                return body
            block.sync(run("sp"))
            block.tensor(run("pe"))
            block.scalar(run("act"))
            block.vector(run("dve"))
            block.gpsimd(run("pool"))


class Ring:
    def __init__(self, bufs, tks=None):
        self.bufs = bufs
        self.tks = tks if tks is not None else [Tk() for _ in bufs]
        self.i = 0

    def next(self):
        b, t = self.bufs[self.i], self.tks[self.i]
        self.i = (self.i + 1) % len(self.bufs)
        return b, t


def host_consts():
    c = {}
    i = np.arange(128)
    c["ident"] = np.eye(128, dtype=np.float32)
    c["ones"] = np.ones((128, 128), dtype=np.float32)
    bo = np.zeros((128, 128), np.float32); bo[:64, :64] = 1; bo[64:, 64:] = 1
    c["blockones"] = bo
    rm = np.zeros((128, 128), np.float32)
    for p in range(128):
        if p % 64 < 32:
            rm[p + 32, p] = -1.0
        else:
            rm[p - 32, p] = 1.0
    c["rot"] = rm
    fr = np.zeros((128, 128), np.float32)
    fr[:, 0] = (10000.0 ** (-(np.arange(128) % 32).astype(np.float32) / 32.0)).astype(np.float32)
    c["freq"] = fr
    c["trimask"] = (i[:, None] <= i[None, :]).astype(np.float32)
    c["mnegL"] = np.where(i[:, None] >= i[None, :], 0.0, -30000.0).astype(np.float32)
    c["mnegU"] = np.ascontiguousarray(c["mnegL"].T)
    c["mnegLs"] = np.where(i[:, None] > i[None, :], 0.0, -30000.0).astype(np.float32)
    c["strictL"] = (i[:, None] > i[None, :]).astype(np.float32)
    c["strictU"] = np.ascontiguousarray(c["strictL"].T)
    return c


CONST_NAMES = ["ident", "ones", "blockones", "rot", "freq", "trimask", "mnegLs", "mnegU", "strictU"]


def build_program(stages=("gdn", "mlp0", "attn", "mlp1"), debug=False):
    INHERIT.clear()
    nc = bass.Bass("TRN2", target_bir_lowering=False)
    dr = {}

    def din(name, shape, dt=F32):
        dr[name] = nc.dram_tensor(name, list(shape), dt, kind="ExternalInput").ap()
        return dr[name]
    x_d = din("x", [S, D])
    pos_d = din("positions", [1, S], I32)
    din("a_norm", [1, D]); din("a_w_in", [D, 4112]); din("a_conv_w", [4, 3072]); din("a_a_log", [1, 8])
    din("a_dt_bias", [1, 8]); din("a_out_norm", [1, 128]); din("a_w_out", [D, D])
    din("kv_norm", [1, D]); din("w_kv", [D, 2048]); din("k_norm", [1, 64])
    din("b_norm", [1, D]); din("b_w_q", [D, D]); din("b_q_norm", [1, 64]); din("b_lambda", [4, 64])
    din("b_sub_norm", [1, 128]); din("b_w_out", [D, D])
    din("mlp_norm", [2, D]); din("mlp_w1", [2 * D, DFF]); din("mlp_w2", [2 * DFF, D])
    for n in CONST_NAMES:
        din("c_" + n, [128, 128])
    out_d = nc.dram_tensor("out", [S, D], F32, kind="ExternalOutput").ap()

    with ExitStack() as es:
        P = Prog(nc, es)

        def sb(name, shape, dt=F32):
            return es.enter_context(nc.sbuf_tensor(name, list(shape), dt))

        xT = sb("xT", [128, 8, S])
        xT_tk = [[Tk() for _ in range(NB)] for _ in range(8)]
        xnT = sb("xnT", [128, 8, S], BF16)
        xnT_tk = [Tk() for _ in range(NB)]
        banks = [es.enter_context(nc.psum_tensor("bank%d" % i, [128, 512], F32)) for i in range(8)]
        PS = Ring(banks, [Tk("bank%d" % i, excl=True) for i in range(8)])
        stage_sem = [P.new_dma_sem("stage%d" % i) for i in range(2)]
        stage_box = {"gen": 0}

        def open_stage(ph_, width):
            g_ = stage_box["gen"]
            stage_box["gen"] = g_ + 1
            bufs = [ph_.enter_context(nc.sbuf_tensor("stage_%d_%d" % (g_, i), [128, width], F32)) for i in range(2)]
            stage_box["ring"] = Ring(bufs)
            stage_box["width"] = width

        def close_stage():
            retire(stage_box["ring"].tks)
        tmpf = Ring([sb("tmpf%d" % i, [128, 512]) for i in range(3)])
        rstd_r = Ring([sb("rstd%d" % i, [128, 512]) for i in range(1)])
        cst = {}
        cst_tk = Tk()
        csem = P.new_dma_sem("const")
        for n in CONST_NAMES:
            cst[n] = sb("cs_" + n, [128, 128])
            P.dma("sp", csem, (lambda n=n: lambda e: e.dma_start(out=cst[n][:], in_=dr["c_" + n]))(), writes=[cst_tk])
        gains = sb("gains", [128, 5, 8])
        gain_src = [dr["a_norm"][0, :], dr["mlp_norm"][0, :], dr["kv_norm"][0, :], dr["b_norm"][0, :], dr["mlp_norm"][1, :]]
        for gi, src in enumerate(gain_src):
            P.dma("sp", csem, (lambda gi=gi, src=src: lambda e: e.dma_start(
                out=gains[:, gi, :], in_=src.rearrange("(c p) -> p c", p=128), allow_slow_non_contiguous=True))(), writes=[cst_tk])
        epsb = sb("epsb", [128, 1])
        P.op("pool", lambda e: e.memset(epsb[:], EPS), writes=[cst_tk])
        ones_bf_p = sb("ones_bf_p", [128, 128], BF16)
        onesb_tk = Tk()
        P.op("dve", lambda e: e.tensor_copy(out=ones_bf_p[:], in_=cst["ones"][:]), reads=[cst_tk], writes=[onesb_tk])
        bfv = lambda t_: t_[:].bitcast(BF16)[:, 0:512]

        evac_flip = [0]

        def evac_copy(out_ap, in_ap, reads, writes):
            evac_flip[0] ^= 1
            if evac_flip[0]:
                P.op("act", lambda e: e.activation(out=out_ap, in_=in_ap, func=AF.Copy), reads=reads, writes=writes)
            else:
                P.op("dve", lambda e: e.tensor_copy(out=out_ap, in_=in_ap), reads=reads, writes=writes)

        ph0 = ExitStack()
        open_stage(ph0, 1024)
        stage = stage_box["ring"]
        for t in range(NT):
            st, st_tk = stage.next()
            si = (stage.i - 1) % 2
            P.dma("sp", stage_sem[si], (lambda st=st, t=t: lambda e: e.dma_start(
                out=st[:, 0:1024], in_=x_d[t * 128:(t + 1) * 128, :]))(), writes=[st_tk])
            for cg in range(2):
                ps, ps_tk = PS.next()
                for j in range(4):
                    c = cg * 4 + j
                    P.op("pe", (lambda ps=ps, st=st, c=c, j=j: lambda e: e.transpose(
                        ps[:, j * 128:(j + 1) * 128], st[:, c * 128:(c + 1) * 128], cst["ident"][:]))(),
                        reads=[st_tk, cst_tk], writes=[ps_tk], sig=(j == 3))
                tb = t // 4
                evac_copy(xT[:, cg * 4:(cg + 1) * 4, t * 128:(t + 1) * 128],
                          ps[:].rearrange("p (j n) -> p j n", j=4),
                          reads=[ps_tk], writes=[xT_tk[c][tb] for c in range(cg * 4, cg * 4 + 4)])
        close_stage()
        ph0.close()

        def rms_to_xnT():
            for tb in range(NB):
                blk = slice(tb * 512, (tb + 1) * 512)
                ps, ps_tk = PS.next()
                for c in range(8):
                    sq, sq_tk = tmpf.next()
                    P.op("act", (lambda sq=sq, c=c, blk=blk: lambda e: e.activation(out=bfv(sq), in_=xT[:, c, blk], func=AF.Square))(),
                         reads=[xT_tk[c][tb]], writes=[sq_tk])
                    P.op("pe", (lambda ps=ps, sq=sq, c=c: lambda e: e.matmul(ps[:], ones_bf_p[:], bfv(sq), start=(c == 0), stop=(c == 7)))(),
                         reads=[sq_tk, onesb_tk], writes=[ps_tk], sig=True)
                rs, rs_tk = tmpf.next()
                P.op("act", (lambda rs=rs, ps=ps: lambda e: e.activation(out=rs[:], in_=ps[:], func=AF.Ln, bias=epsb[:, 0:1], scale=1.0 / D))(),
                     reads=[ps_tk, cst_tk], writes=[rs_tk])
                rstd_bc, rstd_tk = rstd_r.next()
                P.op("act", (lambda rs=rs, rstd_bc=rstd_bc: lambda e: e.activation(out=rstd_bc[:], in_=rs[:], func=AF.Exp, scale=-0.5))(),
                     reads=[rs_tk], writes=[rstd_tk])
                for c in range(8):
                    eng = "dve" if c % 2 == 0 else "pool"
                    P.op(eng, (lambda c=c, blk=blk, rstd_bc=rstd_bc: lambda e: e.tensor_tensor(out=xnT[:, c, blk], in0=xT[:, c, blk], in1=rstd_bc[:], op=ALU.mult))(),
                         reads=[xT_tk[c][tb], rstd_tk], writes=[xnT_tk[tb]])

        wsem = [P.new_dma_sem("w%d" % i) for i in range(2)]

        def load_w(dst, dst_tk, src, kc, ncols, gain_idx, gain_c0=0, colgain=None, colgain_tk=None):
            stage = stage_box["ring"]
            per = max(1, stage_box["width"] // ncols)
            c0 = 0
            while c0 < kc:
                n = min(per, kc - c0)
                st, st_tk = stage.next()
                si = (stage.i - 1) % 2
                stv = st[:, 0:n * ncols].rearrange("p (c n) -> p c n", c=n)
                srcv = src[c0 * 128:(c0 + n) * 128, :].rearrange("(c p) n -> p c n", p=128)
                P.dma("sp", stage_sem[si], (lambda stv=stv, srcv=srcv: lambda e: e.dma_start(out=stv, in_=srcv))(), writes=[st_tk])
                dv = dst[:, c0:c0 + n, :]
                if gain_idx is None:
                    P.op("pool", (lambda dv=dv, stv=stv: lambda e: e.tensor_copy(out=dv, in_=stv))(), reads=[st_tk], writes=[dst_tk])
                elif colgain is None:
                    gv = gains[:, gain_idx, gain_c0 + c0:gain_c0 + c0 + n].unsqueeze(2).to_broadcast([128, n, ncols])
                    P.op("pool", (lambda dv=dv, stv=stv, gv=gv: lambda e: e.tensor_tensor(out=dv, in0=stv, in1=gv, op=ALU.mult))(),
                         reads=[st_tk, cst_tk], writes=[dst_tk])
                else:
                    gv = gains[:, gain_idx, gain_c0 + c0:gain_c0 + c0 + n].unsqueeze(2).to_broadcast([128, n, ncols])
                    cv = colgain.unsqueeze(1).to_broadcast([128, n, ncols])
                    P.op("pool", (lambda stv=stv, gv=gv: lambda e: e.tensor_tensor(out=stv, in0=stv, in1=gv, op=ALU.mult))(),
                         reads=[st_tk, cst_tk], writes=[st_tk])
                    P.op("pool", (lambda dv=dv, stv=stv, cv=cv: lambda e: e.tensor_tensor(out=dv, in0=stv, in1=cv, op=ALU.mult))(),
                         reads=[st_tk, colgain_tk], writes=[dst_tk])
                c0 += n

        def mlp(layer, gain_idx):
            with ExitStack() as ph:
                mlp_body(layer, gain_idx, ph)

        def mlp_body(layer, gain_idx, ph):
            FG = 512
            open_stage(ph, 2048)
            psb = lambda name, shape, dt=F32: ph.enter_context(nc.sbuf_tensor(name, list(shape), dt))
            w1r = Ring([psb("w1g%d_%d" % (layer, i), [128, 8, FG], BF16) for i in range(2)])
            w2r = Ring([psb("w2g%d_%d" % (layer, i), [128, FG // 128, D], BF16) for i in range(2)])
            hr = Ring([psb("hT%d_%d" % (layer, i), [128, FG // 128, 512], BF16) for i in range(2)])
            w1_d = dr["mlp_w1"][layer * D:(layer + 1) * D, :]
            w2_d = dr["mlp_w2"][layer * DFF:(layer + 1) * DFF, :]
            NG = DFF // FG
            wts_ = {}

            def load_group(g):
                w1, w1_tk = w1r.next()
                w2, w2_tk = w2r.next()
                load_w(w1, w1_tk, w1_d[:, g * FG:(g + 1) * FG], 8, FG, gain_idx)
                load_w(w2, w2_tk, w2_d[g * FG:(g + 1) * FG, :], FG // 128, D, None)
                wts_[g] = (w1, w1_tk, w2, w2_tk)

            def emit_h(g, tb):
                w1, w1_tk, _, _ = wts_[g]
                blk = slice(tb * 512, (tb + 1) * 512)
                hT, hT_tk = hr.next()
                for fc in range(FG // 128):
                    ps, ps_tk = PS.next()
                    for c in range(8):
                        P.op("pe", (lambda ps=ps, c=c, fc=fc: lambda e: e.matmul(
                            ps[:], w1[:, c, fc * 128:(fc + 1) * 128], xnT[:, c, blk], start=(c == 0), stop=(c == 7)))(),
                            reads=[w1_tk, xnT_tk[tb]], writes=[ps_tk], sig=(c == 7))
                    h1, h1_tk = tmpf.next()
                    P.op("act", (lambda h1=h1, ps=ps: lambda e: e.activation(out=h1[:], in_=ps[:], func=AF.Relu))(), reads=[ps_tk], writes=[h1_tk])
                    P.op("act", (lambda h1=h1, fc=fc: lambda e: e.activation(out=hT[:, fc, :], in_=h1[:], func=AF.Square))(), reads=[h1_tk], writes=[hT_tk])
                return hT, hT_tk

            def emit_o(g, tb, hT, hT_tk):
                _, _, w2, w2_tk = wts_[g]
                blk = slice(tb * 512, (tb + 1) * 512)
                for n in range(8):
                    ps, ps_tk = PS.next()
                    for fc in range(FG // 128):
                        P.op("pe", (lambda ps=ps, n=n, fc=fc: lambda e: e.matmul(
                            ps[:], w2[:, fc, n * 128:(n + 1) * 128], hT[:, fc, :], start=(fc == 0), stop=(fc == FG // 128 - 1)))(),
                            reads=[w2_tk, hT_tk], writes=[ps_tk], sig=(fc == FG // 128 - 1))
                    P.op("dve", (lambda ps=ps, n=n: lambda e: e.tensor_tensor(out=xT[:, n, blk], in0=ps[:], in1=xT[:, n, blk], op=ALU.add))(),
                         reads=[ps_tk], writes=[xT_tk[n][tb]])
            seq = [(g, tb) for g in range(NG) for tb in range(NB)]
            load_group(0)
            cur = emit_h(*seq[0])
            for i, (g, tb) in enumerate(seq):
                if tb == 0 and g + 1 < NG:
                    load_group(g + 1)
                nxt = emit_h(*seq[i + 1]) if i + 1 < len(seq) else None
                emit_o(g, tb, *cur)
                cur = nxt
            retire(w1r.tks + w2r.tks + hr.tks)
            close_stage()


        def gdn():
            with ExitStack() as ph:
                gdn_body(ph)

        def gdn_body(ph):
            import os
            open_stage(ph, 1024)
            psb = lambda name, shape, dt=F32: ph.enter_context(nc.sbuf_tensor(name, list(shape), dt))
            mytks = []

            def tk():
                t = Tk(); mytks.append(t); return t
            pb = banks[0:4]; pb_tk = PS.tks[0:4]
            qb_ = [banks[4], banks[6]]; qb_tk = [PS.tks[4], PS.tks[6]]
            PSs = Ring([banks[5]], [PS.tks[5]])
            OBk = [banks[7], banks[7]]; OB_tk = [PS.tks[7], PS.tks[7]]
            ident = cst["ident"]; onesf = cst["ones"]
            bc4 = lambda ap: ap.unsqueeze(1).to_broadcast([128, 4, 128])
            v3 = lambda ap: ap.rearrange("p (i e) -> p i e", i=4)
            ident_bf = psb("g_identbf", [128, 128], BF16); negones = psb("g_negones", [128, 128]); cb_tk = tk()
            P.op("dve", lambda e: e.tensor_copy(out=ident_bf[:], in_=ident[:]), reads=[cst_tk], writes=[cb_tk])
            P.op("dve", lambda e: e.tensor_scalar(out=negones[:], in0=onesf[:], scalar1=-1.0, scalar2=None, op0=ALU.mult), reads=[cst_tk, cb_tk], writes=[cb_tk])
            one_b = psb("g_oneb", [128, 1]); P.op("pool", lambda e: e.memset(one_b[:], 1.0), reads=[cb_tk], writes=[cb_tk])
            s1 = P.new_dma_sem("g_p1"); s2 = P.new_dma_sem("g_p2"); s3 = P.new_dma_sem("g_p3"); s4 = P.new_dma_sem("g_p4")
            alog = psb("g_alog", [128, 8]); dtb = psb("g_dtb", [128, 8]); onw = psb("g_onw", [128, 128]); cw = psb("g_cw", [128, 24, 4])
            alog_tk = tk(); dtb_tk = tk(); onw_tk = tk(); cw_tk = tk()
            P.dma("sp", s1, lambda e: e.dma_start(out=alog[:], in_=dr["a_a_log"][0, :].partition_broadcast(128)), writes=[alog_tk])
            P.dma("sp", s2, lambda e: e.dma_start(out=dtb[:], in_=dr["a_dt_bias"][0, :].partition_broadcast(128)), writes=[dtb_tk])
            P.dma("sp", s3, lambda e: e.dma_start(out=onw[:], in_=dr["a_out_norm"][0, :].partition_broadcast(128)), writes=[onw_tk])
            for j in range(4):
                P.dma("sp", s4, (lambda j=j: lambda e: e.dma_start(out=cw[:, :, j], in_=dr["a_conv_w"][j, :].rearrange("(c p) -> p c", p=128),
                      allow_slow_non_contiguous=True))(), writes=[cw_tk])
            P.op("act", lambda e: e.activation(out=alog[:], in_=alog[:], func=AF.Exp), reads=[alog_tk], writes=[alog_tk])
            P.op("dve", lambda e: e.tensor_scalar(out=alog[:], in0=alog[:], scalar1=-1.0, scalar2=None, op0=ALU.mult), reads=[alog_tk], writes=[alog_tk])
            wbg = psb("g_wbg", [128, 8, 16], BF16); wbg_tk = tk()
            load_w(wbg, wbg_tk, dr["a_w_in"][:, 4096:4112], 8, 16, 0)
            bg = psb("g_bg", [128, NT, 16]); bg_tk = tk()
            ps, ps_tk = pb[0], pb_tk[0]
            for t in range(NT):
                for c in range(8):
                    P.op("pe", (lambda t=t, c=c: lambda e: e.matmul(ps[:, t * 16:(t + 1) * 16], xnT[:, c, t * 128:(t + 1) * 128], wbg[:, c, :],
                         start=(c == 0), stop=(c == 7)))(), reads=[xnT_tk[t // 4], wbg_tk], writes=[ps_tk], sig=(t == NT - 1 and c == 7))
            P.op("dve", lambda e: e.tensor_copy(out=bg[:].rearrange("p t k -> p (t k)"), in_=ps[:, 0:256]), reads=[ps_tk], writes=[bg_tk])
            beta = psb("g_beta", [128, NT, 8]); gg = psb("g_g", [128, NT, 8]); par_tk = tk()
            P.op("act", lambda e: e.activation(out=beta[:], in_=bg[:, :, 0:8], func=AF.Exp, scale=-1.0), reads=[bg_tk], writes=[par_tk])
            P.op("act", lambda e: e.activation(out=beta[:], in_=beta[:], func=AF.Ln, bias=one_b[:, 0:1], scale=1.0), reads=[par_tk, cb_tk], writes=[par_tk])
            P.op("act", lambda e: e.activation(out=beta[:], in_=beta[:], func=AF.Exp, scale=-1.0), reads=[par_tk], writes=[par_tk])
            P.op("dve", lambda e: e.tensor_tensor(out=gg[:], in0=bg[:, :, 8:16], in1=dtb[:].unsqueeze(1).to_broadcast([128, NT, 8]), op=ALU.add),
                 reads=[bg_tk, dtb_tk, par_tk], writes=[par_tk])
            P.op("act", lambda e: e.activation(out=gg[:], in_=gg[:], func=AF.Exp), reads=[par_tk], writes=[par_tk])
            P.op("act", lambda e: e.activation(out=gg[:], in_=gg[:], func=AF.Ln, bias=one_b[:, 0:1], scale=1.0), reads=[par_tk, cb_tk], writes=[par_tk])
            P.op("dve", lambda e: e.tensor_tensor(out=gg[:], in0=gg[:], in1=alog[:].unsqueeze(1).to_broadcast([128, NT, 8]), op=ALU.mult),
                 reads=[par_tk, alog_tk], writes=[par_tk])
            gc = psb("g_gc", [128, NT, 8]); bgam = psb("g_bgam", [128, NT, 8]); kds = psb("g_kds", [128, NT, 8]); cd = psb("g_cd", [128, NT, 8])
            ps, ps_tk = pb[1], pb_tk[1]
            fl = lambda a_: a_[:].rearrange("p t k -> p (t k)")
            P.op("pe", lambda e: e.matmul(ps[:, 0:128], cst["trimask"][:], fl(gg), start=True, stop=True), reads=[par_tk, cst_tk], writes=[ps_tk], sig=False)
            P.op("pe", lambda e: e.matmul(ps[:, 128:256], onesf[:], fl(gg), start=True, stop=True), reads=[par_tk, cst_tk], writes=[ps_tk])
            P.op("dve", lambda e: e.tensor_copy(out=fl(gc), in_=ps[:, 0:128]), reads=[ps_tk], writes=[par_tk])
            P.op("dve", lambda e: e.tensor_tensor(out=fl(kds), in0=ps[:, 128:256], in1=fl(gc), op=ALU.subtract), reads=[ps_tk, par_tk], writes=[par_tk])
            P.op("act", lambda e: e.activation(out=fl(cd), in_=ps[:, 128:256], func=AF.Exp), reads=[ps_tk, par_tk], writes=[par_tk])
            P.op("act", lambda e: e.activation(out=fl(kds), in_=fl(kds), func=AF.Exp), reads=[par_tk], writes=[par_tk])
            P.op("act", lambda e: e.activation(out=fl(bgam), in_=fl(gc), func=AF.Exp), reads=[par_tk], writes=[par_tk])
            P.op("dve", lambda e: e.tensor_tensor(out=fl(bgam), in0=fl(bgam), in1=fl(beta), op=ALU.mult), reads=[par_tk], writes=[par_tk])
            wts = [psb("g_w%d" % k, [128, 8, 128], BF16) for k in range(3)]; wts_tk = [tk() for _ in range(3)]
            wz2 = [psb("g_wz%d" % k, [128, 8, 128], BF16) for k in range(2)]; wz2_tk = [tk() for _ in range(2)]
            wo2 = [psb("g_wo%d" % i, [128, 1, D], BF16) for i in range(2)]; wo2_tk = [tk() for _ in range(2)]
            cdiag = psb("g_cdiag", [128, 12, 128], BF16); cdiag_tk = tk()
            prjb = [psb("g_prjb%d" % k, [128, 516], BF16) for k in range(2)]; prjb_tk = [tk() for _ in range(2)]
            tails = psb("g_tails", [128, 3, 4], BF16); tails_tk = tk()
            qhT = psb("g_qhT", [128, S], BF16); khT = psb("g_khT", [128, S], BF16); vT = psb("g_vT", [128, S], BF16)
            qhT_tk = [tk() for _ in range(NB)]; khT_tk = [tk() for _ in range(NB)]; vT_tk = [tk() for _ in range(NB)]
            kbg2 = [psb("g_kbg%d" % i, [128, 512], BF16) for i in range(2)]; kdec2 = [psb("g_kdec%d" % i, [128, 512], BF16) for i in range(2)]
            bv2 = [psb("g_bv%d" % i, [128, 512], BF16) for i in range(2)]; zs2 = [psb("g_zs%d" % i, [128, 512], BF16) for i in range(2)]
            u2 = [psb("g_u%d" % i, [128, 512]) for i in range(2)]; wT2 = [psb("g_wT%d" % i, [128, 512], BF16) for i in range(2)]
            qdT2 = [psb("g_qdT%d" % i, [128, 512], BF16) for i in range(2)]; inT2 = [psb("g_inT%d" % i, [128, 512], BF16) for i in range(2)]
            goT2 = [psb("g_goT%d" % i, [128, 512], BF16) for i in range(2)]
            grp_tk = [[tk() for _ in range(10)] for _ in range(2)]
            KBG, KDEC, BV, ZS, U_, WT, QDT, INT, GOT = range(9)
            NF32 = int(os.environ.get("GDN_NF32", "6"))
            Pf = [psb("g_Pf%d" % a_, [128, 512]) for a_ in range(2)]; PTf = [psb("g_PTf%d" % a_, [128, 512]) for a_ in range(2)]
            Ph = [psb("g_Ph%d" % a_, [128, 512], BF16) for a_ in range(2)]; PTh = [psb("g_PTh%d" % a_, [128, 512], BF16) for a_ in range(2)]
            Rb = psb("g_R", [128, 512]); Rbf = psb("g_Rbf", [128, 512], BF16)
            Pb = Pf; PTb = PTf
            Ph_tk = [tk(), tk()]; PTh_tk = [tk(), tk()]; Rh_tk = tk()
            P_tk = [tk(), tk()]; PT_tk = [tk(), tk()]; R_tk = tk()
            gA = psb("g_gA", [128, 512]); gB = psb("g_gB", [128, 512]); gC = psb("g_gC", [128, 512]); gA_tk = tk(); gB_tk = tk(); gC_tk = tk()
            zA = psb("g_zA", [128, 512]); zA_tk = tk()
            eA = psb("g_eA", [128, 512]); eA_tk = tk()
            go_b = psb("g_go", [128, 512], BF16); go_tk = tk()
            S_f = psb("g_Sf", [128, 128]); S_b = psb("g_Sb", [128, 128], BF16); S_tk = tk()
            vn_r = Ring([psb("g_vn%d" % i, [128, 128], BF16) for i in range(2)])
            st4 = psb("g_st4", [128, 8]); st4_tk = tk()
            if os.environ.get("SBUF_DBG"):
                print("GDN sbuf bytes remaining per partition:", nc.sbuf_bytes_remaining)

            def pstage_items(h, tb):
                items = []
                if tb == 0:
                    def it_w():
                        for k in range(3):
                            load_w(wts[k], wts_tk[k], dr["a_w_in"][:, k * 1024 + h * 128:k * 1024 + (h + 1) * 128], 8, 128, 0)
                        load_w(wz2[h % 2], wz2_tk[h % 2], dr["a_w_in"][:, 3 * 1024 + h * 128:3 * 1024 + (h + 1) * 128], 8, 128, 0)
                        load_w(wo2[h % 2], wo2_tk[h % 2], dr["a_w_out"][h * 128:(h + 1) * 128, :], 1, D, None)
                        for sec in range(3):
                            for j in range(4):
                                P.op("dve", (lambda sec=sec, j=j: lambda e: e.tensor_scalar(out=cdiag[:, sec * 4 + j, :], in0=ident[:], scalar1=cw[:, sec * 8 + h, j:j + 1], scalar2=None, op0=ALU.mult))(),
                                     reads=[cst_tk, cw_tk], writes=[cdiag_tk])
                    items.append(it_w)
                blk = slice(tb * 512, (tb + 1) * 512)
                stt = [dict() for _ in range(3)]

                def proj(sec):
                    k_ = sec % 2
                    pj, pj_tk = prjb[k_], prjb_tk[k_]
                    bk, bk_tk = qb_[0], qb_tk[0]
                    for c in range(8):
                        P.op("pe", (lambda c=c: lambda e: e.matmul(bk[:], wts[sec][:, c, :], xnT[:, c, blk], start=(c == 0), stop=(c == 7)))(),
                             reads=[wts_tk[sec], xnT_tk[tb]], writes=[bk_tk], sig=(c == 7))
                    P.op("act", (lambda: lambda e: e.activation(out=pj[:, 3:515], in_=bk[:], func=AF.Copy))(), reads=[bk_tk], writes=[pj_tk])
                    if tb == 0:
                        P.op("pool", (lambda: lambda e: e.memset(pj[:, 0:3], 0.0))(), reads=[pj_tk], writes=[pj_tk])
                    else:
                        P.op("pool", (lambda: lambda e: e.tensor_copy(out=pj[:, 0:3], in_=tails[:, sec, 0:3]))(), reads=[tails_tk, pj_tk], writes=[pj_tk])

                def conv(sec):
                    k_ = sec % 2
                    pj, pj_tk = prjb[k_], prjb_tk[k_]
                    bk, bk_tk = qb_[1], qb_tk[1]
                    for j in range(4):
                        P.op("pe", (lambda j=j: lambda e: e.matmul(bk[:], cdiag[:, sec * 4 + j, :], pj[:, j:j + 512], start=(j == 0), stop=(j == 3)))(),
                             reads=[cdiag_tk, pj_tk], writes=[bk_tk], sig=(j == 3))
                    P.op("pool", (lambda: lambda e: e.tensor_copy(out=tails[:, sec, 0:3], in_=pj[:, 512:515]))(), reads=[pj_tk, tails_tk], writes=[tails_tk])
                    e_, e_tk = tmpf.next()
                    P.op("act", (lambda: lambda e: e.activation(out=e_[:], in_=bk[:], func=AF.Exp, scale=-1.0))(), reads=[bk_tk], writes=[e_tk])
                    P.op("act", (lambda: lambda e: e.activation(out=e_[:], in_=e_[:], func=AF.Ln, bias=one_b[:, 0:1], scale=1.0))(), reads=[e_tk, cb_tk], writes=[e_tk])
                    P.op("act", (lambda: lambda e: e.activation(out=e_[:], in_=e_[:], func=AF.Exp, scale=-1.0))(), reads=[e_tk], writes=[e_tk])
                    stt[sec].update(e=e_, e_tk=e_tk)

                def fin1(sec):
                    bk, bk_tk = qb_[1], qb_tk[1]
                    e_, e_tk = stt[sec]["e"], stt[sec]["e_tk"]
                    if sec == 2:
                        P.op("dve", (lambda: lambda e: e.tensor_tensor(out=vT[:, blk], in0=bk[:], in1=e_[:], op=ALU.mult))(), reads=[bk_tk, e_tk], writes=[vT_tk[tb]])
                        return
                    P.op("dve", (lambda: lambda e: e.tensor_tensor(out=e_[:], in0=bk[:], in1=e_[:], op=ALU.mult))(), reads=[bk_tk, e_tk], writes=[e_tk])
                    sq, sq_tk = tmpf.next()
                    P.op("act", (lambda: lambda e: e.activation(out=bfv(sq), in_=e_[:], func=AF.Square))(), reads=[e_tk], writes=[sq_tk])
                    P.op("pe", (lambda: lambda e: e.matmul(bk[:], ones_bf_p[:], bfv(sq), start=True, stop=True))(), reads=[sq_tk, onesb_tk], writes=[bk_tk])
                    stt[sec].update(sq=sq, sq_tk=sq_tk)

                def fin2(sec):
                    if sec == 2:
                        return
                    bk, bk_tk = qb_[1], qb_tk[1]
                    e_, e_tk, sq, sq_tk = stt[sec]["e"], stt[sec]["e_tk"], stt[sec]["sq"], stt[sec]["sq_tk"]
                    P.op("act", (lambda: lambda e: e.activation(out=sq[:], in_=bk[:], func=AF.Ln, bias=epsb[:, 0:1], scale=1.0))(), reads=[bk_tk, cst_tk], writes=[sq_tk])
                    P.op("act", (lambda: lambda e: e.activation(out=sq[:], in_=sq[:], func=AF.Exp, scale=-0.5))(), reads=[sq_tk], writes=[sq_tk])
                    dst, dst_tk = (qhT, qhT_tk[tb]) if sec == 0 else (khT, khT_tk[tb])
                    sc = float(128 ** -0.5) if sec == 0 else 1.0
                    P.op("dve", (lambda: lambda e: e.scalar_tensor_tensor(out=dst[:, blk], in0=e_[:], scalar=sc, in1=sq[:], op0=ALU.mult, op1=ALU.mult))(),
                         reads=[e_tk, sq_tk], writes=[dst_tk])
                items.append(lambda: proj(0))
                for sec in range(3):
                    items.append((lambda sec=sec: lambda: conv(sec))())
                    if sec + 1 < 3:
                        items.append((lambda sec=sec: lambda: proj(sec + 1))())
                    items.append((lambda sec=sec: lambda: fin1(sec))())
                    items.append((lambda sec=sec: lambda: fin2(sec))())
                return items

            def prep_items(h, g):
                par = g % 2
                t0 = 4 * g
                blk = slice(g * 512, (g + 1) * 512)
                T_ = grp_tk[par]
                kbg, kdec, bv, zs, u_, wT, qdT, inT = kbg2[par], kdec2[par], bv2[par], zs2[par], u2[par], wT2[par], qdT2[par], inT2[par]
                scb = lambda X: X[:, t0:t0 + 4, h].unsqueeze(2).to_broadcast([128, 4, 128])
                tsl = lambda i: slice(i * 128, (i + 1) * 128)
                tok = lambda i: slice((t0 + i) * 128, (t0 + i + 1) * 128)
                items = []

                def p1():
                    P.op("pool", lambda e: e.tensor_copy(out=v3(gA[:]), in_=scb(gc)), reads=[par_tk], writes=[gA_tk])
                    P.op("pool", lambda e: e.tensor_tensor(out=v3(gB[:]), in0=bc4(ident[:]), in1=scb(gc), op=ALU.mult), reads=[par_tk, cst_tk], writes=[gB_tk])
                    P.op("pool", lambda e: e.tensor_tensor(out=v3(gC[:]), in0=bc4(ident[:]), in1=scb(beta), op=ALU.mult), reads=[par_tk, cst_tk], writes=[gC_tk])

                def p2():
                    P.op("pe", lambda e: e.matmul(pb[0][:], ident[:], gA[:], start=True, stop=False), reads=[gA_tk, cst_tk], writes=[pb_tk[0]], sig=False)
                    P.op("pe", lambda e: e.matmul(pb[0][:], negones[:], gB[:], start=False, stop=True), reads=[gB_tk, cb_tk], writes=[pb_tk[0]])
                    P.op("pe", lambda e: e.matmul(pb[1][:], onesf[:], gB[:], start=True, stop=True), reads=[gB_tk, cst_tk], writes=[pb_tk[1]])
                    P.op("pe", lambda e: e.matmul(pb[2][:], onesf[:], gC[:], start=True, stop=True), reads=[gC_tk, cst_tk], writes=[pb_tk[2]])

                def p3():
                    P.op("dve", lambda e: e.tensor_tensor(out=v3(gA[:]), in0=v3(pb[0][:]), in1=bc4(cst["mnegLs"][:]), op=ALU.add), reads=[pb_tk[0], cst_tk], writes=[gA_tk])
                    P.op("dve", lambda e: e.scalar_tensor_tensor(out=v3(gB[:]), in0=v3(pb[0][:]), scalar=-1.0, in1=bc4(cst["mnegU"][:]), op0=ALU.mult, op1=ALU.add),
                         reads=[pb_tk[0], cst_tk], writes=[gB_tk])

                def p4():
                    P.op("act", lambda e: e.activation(out=gA[:], in_=gA[:], func=AF.Exp), reads=[gA_tk], writes=[gA_tk])
                    P.op("act", lambda e: e.activation(out=gB[:], in_=gB[:], func=AF.Exp), reads=[gB_tk], writes=[gB_tk])
                    P.op("act", lambda e: e.activation(out=gC[:], in_=pb[1][:], func=AF.Exp), reads=[pb_tk[1]], writes=[gC_tk])

                def p5a():
                    for i in range(4):
                        P.op("pe", (lambda i=i: lambda e: e.matmul(pb[3][:, tsl(i)], khT[:, tok(i)], khT[:, tok(i)], start=True, stop=True))(), reads=[khT_tk[g]], writes=[pb_tk[3]], sig=(i == 3))
                    for i in range(4):
                        P.op("pe", (lambda i=i: lambda e: e.matmul(pb[0][:, tsl(i)], khT[:, tok(i)], qhT[:, tok(i)], start=True, stop=True))(), reads=[khT_tk[g], qhT_tk[g]], writes=[pb_tk[0]], sig=(i == 3))

                def p6():
                    P.op("dve", lambda e: e.tensor_tensor(out=inT[:], in0=pb[0][:], in1=gB[:], op=ALU.mult), reads=[pb_tk[0], gB_tk], writes=[T_[INT]])
                    P.op("pool", lambda e: e.tensor_tensor(out=qdT[:], in0=qhT[:, blk], in1=gC[:], op=ALU.mult), reads=[qhT_tk[g], gC_tk], writes=[T_[QDT]])
                    P.op("pool", lambda e: e.tensor_tensor(out=v3(gB[:]), in0=v3(gB[:]), in1=bc4(cst["strictU"][:]), op=ALU.mult), reads=[gB_tk, cst_tk, T_[INT]], writes=[gB_tk])

                def p7():
                    P.op("pool", lambda e: e.tensor_tensor(out=v3(gA[:]), in0=v3(gA[:]), in1=scb(beta), op=ALU.mult), reads=[gA_tk, par_tk], writes=[gA_tk])
                    P.op("dve", lambda e: e.tensor_tensor(out=gC[:], in0=pb[2][:], in1=gB[:], op=ALU.mult), reads=[pb_tk[2], gB_tk, T_[QDT]], writes=[gC_tk])

                def p8():
                    P.op("dve", lambda e: e.tensor_tensor(out=Pb[0][:], in0=pb[3][:], in1=gA[:], op=ALU.mult), reads=[pb_tk[3], gA_tk], writes=[P_tk[0]])
                    P.op("dve", lambda e: e.tensor_tensor(out=PTb[0][:], in0=pb[3][:], in1=gC[:], op=ALU.mult), reads=[pb_tk[3], gC_tk], writes=[PT_tk[0]])
                    P.op("pool", lambda e: e.tensor_tensor(out=v3(Rb[:]), in0=bc4(ident[:]), in1=v3(PTb[0][:]), op=ALU.subtract), reads=[cst_tk, PT_tk[0]], writes=[R_tk])

                def p5b():
                    for i in range(4):
                        P.op("pe", (lambda i=i: lambda e: e.matmul(pb[1][:, tsl(i)], khT[:, tok(i)], ident_bf[:], start=True, stop=True))(), reads=[khT_tk[g], cb_tk], writes=[pb_tk[1]], sig=(i == 3))
                    for i in range(4):
                        P.op("pe", (lambda i=i: lambda e: e.matmul(pb[2][:, tsl(i)], vT[:, tok(i)], ident_bf[:], start=True, stop=True))(), reads=[vT_tk[g], cb_tk], writes=[pb_tk[2]], sig=(i == 3))

                def p5b2():
                    P.op("dve", lambda e: e.tensor_tensor(out=v3(kbg[:]), in0=v3(pb[1][:]), in1=scb(bgam), op=ALU.mult), reads=[pb_tk[1], par_tk], writes=[T_[KBG]])
                    P.op("dve", lambda e: e.tensor_tensor(out=v3(kdec[:]), in0=v3(pb[1][:]), in1=scb(kds), op=ALU.mult), reads=[pb_tk[1], par_tk], writes=[T_[KDEC]])
                    P.op("dve", lambda e: e.tensor_tensor(out=v3(bv[:]), in0=v3(pb[2][:]), in1=scb(beta), op=ALU.mult), reads=[pb_tk[2], par_tk], writes=[T_[BV]])

                def p5c():
                    for i in range(4):
                        for c in range(8):
                            P.op("pe", (lambda i=i, c=c: lambda e: e.matmul(pb[0][:, tsl(i)], xnT[:, c, tok(i)], wz2[h % 2][:, c, :], start=(c == 0), stop=(c == 7)))(),
                                 reads=[xnT_tk[g], wz2_tk[h % 2]], writes=[pb_tk[0]], sig=(i == 3 and c == 7))

                def p5c2():
                    P.op("act", lambda e: e.activation(out=zA[:], in_=pb[0][:], func=AF.Exp, scale=-1.0), reads=[pb_tk[0]], writes=[zA_tk])
                    P.op("act", lambda e: e.activation(out=zA[:], in_=zA[:], func=AF.Ln, bias=one_b[:, 0:1], scale=1.0), reads=[zA_tk, cb_tk], writes=[zA_tk])
                    P.op("act", lambda e: e.activation(out=zA[:], in_=zA[:], func=AF.Exp, scale=-1.0), reads=[zA_tk], writes=[zA_tk])

                def p5c3():
                    P.op("dve", lambda e: e.tensor_tensor(out=zA[:], in0=pb[0][:], in1=zA[:], op=ALU.mult), reads=[pb_tk[0], zA_tk], writes=[zA_tk])
                    P.op("pool", lambda e: e.tensor_tensor(out=v3(zs[:]), in0=v3(zA[:]), in1=bc4(onw[:]), op=ALU.mult), reads=[zA_tk, onw_tk], writes=[T_[ZS]])
                items += [p1, p2, p3, p4, p5a, p6, p7, p8, p5b, p5b2, p5c, p5c2, p5c3]
                if NF32 == 0:
                    def c0():
                        P.op("act", lambda e: e.activation(out=Ph[0][:], in_=Pf[0][:], func=AF.Copy), reads=[P_tk[0]], writes=[Ph_tk[0]])
                        P.op("act", lambda e: e.activation(out=PTh[0][:], in_=PTf[0][:], func=AF.Copy), reads=[PT_tk[0]], writes=[PTh_tk[0]])
                        P.op("act", lambda e: e.activation(out=Rbf[:], in_=Rb[:], func=AF.Copy), reads=[R_tk], writes=[Rh_tk])
                    items.append(c0)
                dbl_stages = []
                for lvl in range(1, 7):
                    cur = (lvl - 1) % 2
                    nxt = 1 - cur
                    f32lvl = lvl <= NF32

                    def d1(lvl=lvl, cur=cur, nxt=nxt, f32lvl=f32lvl):
                        A_, AT_, A_tk, AT_tk = (Pf, PTf, P_tk, PT_tk) if f32lvl else (Ph, PTh, Ph_tk, PTh_tk)
                        for i in range(4):
                            P.op("pe", (lambda i=i: lambda e: e.matmul(pb[1][:, tsl(i)], AT_[cur][:, tsl(i)], A_[cur][:, tsl(i)], start=True, stop=True))(),
                                 reads=[AT_tk[cur], A_tk[cur]], writes=[pb_tk[1]], sig=(i == 3))
                        if lvl < 6:
                            for i in range(4):
                                P.op("pe", (lambda i=i: lambda e: e.matmul(pb[2][:, tsl(i)], A_[cur][:, tsl(i)], AT_[cur][:, tsl(i)], start=True, stop=True))(),
                                     reads=[AT_tk[cur], A_tk[cur]], writes=[pb_tk[2]], sig=(i == 3))

                    def d2(lvl=lvl, cur=cur, nxt=nxt, f32lvl=f32lvl):
                        if f32lvl:
                            P.op("act", lambda e: e.activation(out=Pf[nxt][:], in_=pb[1][:], func=AF.Copy), reads=[pb_tk[1]], writes=[P_tk[nxt]])
                        if lvl >= NF32 and lvl < 6 or not f32lvl:
                            P.op("act", lambda e: e.activation(out=Ph[nxt][:], in_=pb[1][:], func=AF.Copy), reads=[pb_tk[1]], writes=[Ph_tk[nxt]])
                        if lvl < 6:
                            if lvl + 1 <= NF32:
                                P.op("dve", lambda e: e.tensor_copy(out=PTf[nxt][:], in_=pb[2][:]), reads=[pb_tk[2]], writes=[PT_tk[nxt]])
                            else:
                                P.op("dve", lambda e: e.tensor_copy(out=PTh[nxt][:], in_=pb[2][:]), reads=[pb_tk[2]], writes=[PTh_tk[nxt]])

                    def d3(lvl=lvl, cur=cur, nxt=nxt, f32lvl=f32lvl):
                        for i in range(4):
                            if f32lvl:
                                P.op("pe", (lambda i=i: lambda e: e.matmul(pb[3][:, tsl(i)], Pf[nxt][:, tsl(i)], Rb[:, tsl(i)], start=True, stop=True))(),
                                     reads=[P_tk[nxt], R_tk], writes=[pb_tk[3]], sig=(i == 3))
                            else:
                                P.op("pe", (lambda i=i: lambda e: e.matmul(pb[3][:, tsl(i)], Ph[nxt][:, tsl(i)], Rbf[:, tsl(i)], start=True, stop=True))(),
                                     reads=[Ph_tk[nxt], Rh_tk], writes=[pb_tk[3]], sig=(i == 3))

                    def d4(lvl=lvl, f32lvl=f32lvl):
                        if f32lvl:
                            P.op("dve", lambda e: e.tensor_tensor(out=Rb[:], in0=pb[3][:], in1=Rb[:], op=ALU.add), reads=[pb_tk[3], R_tk], writes=[R_tk])
                            if lvl == NF32:
                                P.op("act", lambda e: e.activation(out=Rbf[:], in_=Rb[:], func=AF.Copy), reads=[R_tk], writes=[Rh_tk])
                        else:
                            P.op("dve", lambda e: e.tensor_tensor(out=Rbf[:], in0=pb[3][:], in1=Rbf[:], op=ALU.add), reads=[pb_tk[3], Rh_tk], writes=[Rh_tk])
                    dbl_stages.append((d1, d2, d3, d4))
                for li in range(6):
                    d1_, d2_, d3_, d4_ = dbl_stages[li]
                    if li == 0:
                        items += [d1_, d2_]
                    if li + 1 < 6:
                        n1, n2, _, _ = dbl_stages[li + 1]
                        items += [n1, d3_, n2, d4_]
                    else:
                        items += [d3_, d4_]

                def f1():
                    if NF32 >= 6:
                        P.op("act", lambda e: e.activation(out=Rbf[:], in_=Rb[:], func=AF.Copy), reads=[R_tk], writes=[Rh_tk])

                def f2():
                    for i in range(4):
                        P.op("pe", (lambda i=i: lambda e: e.matmul(pb[1][:, tsl(i)], Rbf[:, tsl(i)], bv[:, tsl(i)], start=True, stop=True))(), reads=[Rh_tk, T_[BV]], writes=[pb_tk[1]], sig=(i == 3))
                    for i in range(4):
                        P.op("pe", (lambda i=i: lambda e: e.matmul(pb[2][:, tsl(i)], kbg[:, tsl(i)], Rbf[:, tsl(i)], start=True, stop=True))(), reads=[Rh_tk, T_[KBG]], writes=[pb_tk[2]], sig=(i == 3))

                def f3():
                    P.op("act", lambda e: e.activation(out=u_[:], in_=pb[1][:], func=AF.Copy), reads=[pb_tk[1]], writes=[T_[U_]])
                    P.op("dve", lambda e: e.tensor_copy(out=wT[:], in_=pb[2][:]), reads=[pb_tk[2]], writes=[T_[WT]])
                items += [f1, f2, f3]
                return items

            def scan_items(h, g):
                par = g % 2
                t0 = 4 * g
                blk = slice(g * 512, (g + 1) * 512)
                T_ = grp_tk[par]
                kdec, zs, u_, wT, qdT, inT, goT = kdec2[par], zs2[par], u2[par], wT2[par], qdT2[par], inT2[par], goT2[par]
                OB, OBt = OBk[par], OB_tk[par]
                tsl = lambda i: slice(i * 128, (i + 1) * 128)
                items = []
                if g == 0:
                    def s0():
                        P.op("pool", lambda e: e.memset(S_f[:], 0.0), reads=[S_tk], writes=[S_tk])
                        P.op("pool", lambda e: e.memset(S_b[:], 0.0), reads=[S_tk], writes=[S_tk])
                    items.append(s0)
                for i in range(4):
                    t = t0 + i
                    sd = {}

                    def sa(i=i, sd=sd):
                        wsp, wsp_tk = PSs.next()
                        P.op("pe", (lambda: lambda e: e.matmul(wsp[:, 0:128], wT[:, tsl(i)], S_b[:], start=True, stop=True))(), reads=[T_[WT], S_tk], writes=[wsp_tk])
                        sd.update(wsp=wsp, wsp_tk=wsp_tk)

                    def sb_(i=i, sd=sd):
                        wsp, wsp_tk = sd["wsp"], sd["wsp_tk"]
                        vn, vn_tk = vn_r.next()
                        P.op("dve", (lambda: lambda e: e.tensor_tensor(out=vn[:], in0=u_[:, tsl(i)], in1=wsp[:, 0:128], op=ALU.subtract))(), reads=[T_[U_], wsp_tk], writes=[vn_tk])
                        sd.update(vn=vn, vn_tk=vn_tk)

                    def sc_(i=i, sd=sd):
                        wsp, wsp_tk, vn, vn_tk = sd["wsp"], sd["wsp_tk"], sd["vn"], sd["vn_tk"]
                        P.op("pe", (lambda: lambda e: e.matmul(OB[:, tsl(i)], qdT[:, tsl(i)], S_b[:], start=True, stop=False))(), reads=[T_[QDT], S_tk], writes=[OBt], sig=False)
                        P.op("pe", (lambda: lambda e: e.matmul(OB[:, tsl(i)], inT[:, tsl(i)], vn[:], start=False, stop=True))(), reads=[T_[INT], vn_tk], writes=[OBt], sig=False)
                        P.op("pe", (lambda: lambda e: e.matmul(wsp[:, 128:256], kdec[:, tsl(i)], vn[:], start=True, stop=True))(), reads=[T_[KDEC], vn_tk], writes=[wsp_tk])

                    def sd_(i=i, sd=sd, t=t):
                        wsp, wsp_tk = sd["wsp"], sd["wsp_tk"]
                        P.op("dve", (lambda: lambda e: e.scalar_tensor_tensor(out=S_f[:], in0=S_f[:], scalar=cd[:, t, h:h + 1], in1=wsp[:, 128:256], op0=ALU.mult, op1=ALU.add))(),
                             reads=[wsp_tk, par_tk, S_tk], writes=[S_tk])
                        P.op("act", lambda e: e.activation(out=S_b[:], in_=S_f[:], func=AF.Copy), reads=[S_tk], writes=[S_tk])
                    items += [sa, sb_, sc_, sd_]

                def e1():
                    P.op("act", lambda e: e.activation(out=eA[:], in_=OB[:], func=AF.Square), reads=[OBt], writes=[eA_tk])

                def e2():
                    P.op("dve", lambda e: e.tensor_reduce(out=st4[:, 0:4], in_=v3(eA[:]), axis=AX.X, op=ALU.add), reads=[eA_tk, st4_tk], writes=[st4_tk])
                    P.op("dve", lambda e: e.tensor_tensor(out=eA[:], in0=OB[:], in1=zs[:], op=ALU.mult), reads=[OBt, T_[ZS], st4_tk], writes=[eA_tk])

                def e3():
                    P.op("act", lambda e: e.activation(out=st4[:, 4:8], in_=st4[:, 0:4], func=AF.Ln, bias=epsb[:, 0:1], scale=1.0 / 128), reads=[st4_tk, cst_tk], writes=[st4_tk])
                    P.op("act", lambda e: e.activation(out=st4[:, 4:8], in_=st4[:, 4:8], func=AF.Exp, scale=-0.5), reads=[st4_tk], writes=[st4_tk])

                def e4():
                    P.op("pool", lambda e: e.tensor_tensor(out=v3(go_b[:]), in0=v3(eA[:]), in1=st4[:, 4:8].unsqueeze(2).to_broadcast([128, 4, 128]), op=ALU.mult),
                         reads=[eA_tk, st4_tk], writes=[go_tk])

                def e5():
                    gp, gp_tk = PSs.next()
                    for i in range(4):
                        P.op("pe", (lambda i=i: lambda e: e.matmul(gp[:, tsl(i)], go_b[:, tsl(i)], ident_bf[:], start=True, stop=True))(), reads=[go_tk, cb_tk], writes=[gp_tk], sig=(i == 3))
                    P.op("act", (lambda: lambda e: e.activation(out=goT[:], in_=gp[:], func=AF.Copy))(), reads=[gp_tk], writes=[T_[GOT]])
                items += [e1, e2, e3, e4, e5]
                wo, wo_tk = wo2[h % 2], wo2_tk[h % 2]
                for n0 in range(0, 8, 2):
                    def op_(n0=n0):
                        for n_ in range(n0, n0 + 2):
                            ps_, ps_tk_ = PSs.next()
                            P.op("pe", (lambda ps_=ps_, n_=n_: lambda e: e.matmul(ps_[:], wo[:, 0, n_ * 128:(n_ + 1) * 128], goT[:], start=True, stop=True))(),
                                 reads=[wo_tk, T_[GOT]], writes=[ps_tk_])
                            P.op("dve", (lambda ps_=ps_, n_=n_: lambda e: e.tensor_tensor(out=xT[:, n_, blk], in0=ps_[:], in1=xT[:, n_, blk], op=ALU.add))(),
                                 reads=[ps_tk_], writes=[xT_tk[n_][g]])
                    items.append(op_)
                return items

            def merge(a_, b_):
                if not a_:
                    return list(b_)
                if not b_:
                    return list(a_)
                if len(a_) < len(b_):
                    a_, b_ = b_, a_
                out = []
                ratio = len(a_) / float(len(b_))
                bi = 0
                for i_, it_ in enumerate(a_):
                    out.append(it_)
                    while bi < len(b_) and (bi + 1) * ratio <= i_ + 1:
                        out.append(b_[bi]); bi += 1
                out.extend(b_[bi:])
                return out

            for tb_ in range(NB):
                for it_ in pstage_items(0, tb_):
                    it_()
            prev_scan = []
            for h in range(8):
                for g in range(4):
                    if g >= 1 and h + 1 < 8:
                        pit = pstage_items(h + 1, g - 1)
                    elif g == 0 and h >= 1:
                        pit = pstage_items(h, 3)
                    else:
                        pit = []
                    for it_ in merge(merge(prep_items(h, g), prev_scan), pit):
                        it_()
                    prev_scan = scan_items(h, g)
            for it_ in prev_scan:
                it_()
            for tl in grp_tk:
                pass
            mytks.extend(vn_r.tks)
            retire(mytks)
            close_stage()

        def attention():
            with ExitStack() as ph:
                attention_body(ph)

        def attention_body(ph):
            import math
            from collections import deque
            lam_init = 0.8 - 0.6 * math.exp(-0.3 * 1)
            open_stage(ph, 2048)
            psb = lambda name, shape, dt=F32: ph.enter_context(nc.sbuf_tensor(name, list(shape), dt))
            mytks = []

            def tk():
                t = Tk(); mytks.append(t); return t
            PSa = Ring(banks[0:2], PS.tks[0:2])
            PSb = Ring(banks[2:4], PS.tks[2:4])
            acc_b, acc_tk = banks[4:8], PS.tks[4:8]
            pending = deque()

            def drain(n):
                while n > 0 and pending:
                    it_ = pending.popleft()
                    if it_ is not None:
                        it_()
                    n -= 1
            psem = P.new_dma_sem("at_prm"); lsem = P.new_dma_sem("at_lamb"); possem = P.new_dma_sem("at_pos"); gsem = P.new_dma_sem("at_g")
            prm = psb("at_prm", [128, 8]); prm_tk = tk()
            for half in range(2):
                P.dma("sp", psem, (lambda half=half: lambda e: e.dma_start(out=prm[half * 64:(half + 1) * 64, 0:1],
                      in_=dr["b_q_norm"][0, :].rearrange("(p o) -> p o", o=1), allow_slow_non_contiguous=True))(), writes=[prm_tk])
                P.dma("sp", psem, (lambda half=half: lambda e: e.dma_start(out=prm[half * 64:(half + 1) * 64, 1:2],
                      in_=dr["k_norm"][0, :].rearrange("(p o) -> p o", o=1), allow_slow_non_contiguous=True))(), writes=[prm_tk])
            P.dma("sp", psem, lambda e: e.dma_start(out=prm[:, 2:3], in_=dr["b_sub_norm"][0, :].rearrange("(p o) -> p o", o=1),
                  allow_slow_non_contiguous=True), writes=[prm_tk])
            lamb = psb("at_lamb", [128, 4, 64]); lamb_tk = tk()
            P.dma("sp", lsem, lambda e: e.dma_start(out=lamb[:].rearrange("p a b -> p (a b)"),
                  in_=dr["b_lambda"].rearrange("a b -> (a b)").partition_broadcast(128)), writes=[lamb_tk])
            gcol = psb("at_gcol", [128, 2, 128]); gcol_tk = tk()
            for half in range(2):
                P.dma("sp", gsem, (lambda half=half: lambda e: e.dma_start(out=gcol[:, 0, half * 64:(half + 1) * 64], in_=dr["b_q_norm"][0, :].partition_broadcast(128)))(), writes=[gcol_tk])
                P.dma("sp", gsem, (lambda half=half: lambda e: e.dma_start(out=gcol[:, 1, half * 64:(half + 1) * 64], in_=dr["k_norm"][0, :].partition_broadcast(128)))(), writes=[gcol_tk])
            P.op("dve", lambda e: e.tensor_scalar(out=gcol[:, 0, :], in0=gcol[:, 0, :], scalar1=0.125, scalar2=None, op0=ALU.mult), reads=[gcol_tk], writes=[gcol_tk])
            P.op("dve", lambda e: e.tensor_scalar(out=prm[:, 0:1], in0=prm[:, 0:1], scalar1=0.125, scalar2=None, op0=ALU.mult), reads=[prm_tk], writes=[prm_tk])
            P.op("dve", lambda e: e.tensor_scalar(out=prm[:, 2:3], in0=prm[:, 2:3], scalar1=float(1.0 - lam_init), scalar2=None, op0=ALU.mult), reads=[prm_tk], writes=[prm_tk])
            bo = psb("at_bo", [128, 2, 128]); bo_tk = tk()
            ig = psb("at_ig", [128, 2])
            P.op("dve", lambda e: e.tensor_tensor(out=ig[:], in0=prm[:, 0:2], in1=prm[:, 0:2], op=ALU.mult), reads=[prm_tk], writes=[bo_tk])
            P.op("dve", lambda e: e.reciprocal(out=prm[:, 6:8], in_=ig[:]), reads=[bo_tk, prm_tk], writes=[prm_tk])
            for i_ in range(2):
                P.op("dve", (lambda i_=i_: lambda e: e.tensor_scalar(out=bo[:, i_, :], in0=cst["blockones"][:], scalar1=prm[:, 6 + i_:7 + i_], scalar2=None, op0=ALU.mult))(),
                     reads=[prm_tk, cst_tk, bo_tk], writes=[bo_tk])
            bo_bf = psb("at_bo_bf", [128, 2, 128], BF16)
            P.op("dve", lambda e: e.tensor_copy(out=bo_bf[:], in_=bo[:]), reads=[bo_tk], writes=[bo_tk])
            lp = psb("at_lp", [128, 2, 64]); lp_tk = tk()
            P.op("dve", lambda e: e.tensor_tensor(out=lp[:, 0, :], in0=lamb[:, 0, :], in1=lamb[:, 1, :], op=ALU.mult), reads=[lamb_tk], writes=[lp_tk])
            P.op("dve", lambda e: e.tensor_tensor(out=lp[:, 1, :], in0=lamb[:, 2, :], in1=lamb[:, 3, :], op=ALU.mult), reads=[lamb_tk, lp_tk], writes=[lp_tk])
            P.op("dve", lambda e: e.tensor_reduce(out=prm[:, 4:6], in_=lp[:], axis=AX.X, op=ALU.add), reads=[lp_tk, prm_tk], writes=[prm_tk])
            P.op("act", lambda e: e.activation(out=prm[:, 4:6], in_=prm[:, 4:6], func=AF.Exp), reads=[prm_tk], writes=[prm_tk])
            P.op("dve", lambda e: e.tensor_tensor(out=prm[:, 3:4], in0=prm[:, 5:6], in1=prm[:, 4:5], op=ALU.subtract), reads=[prm_tk], writes=[prm_tk])
            P.op("dve", lambda e: e.tensor_scalar(out=prm[:, 3:4], in0=prm[:, 3:4], scalar1=float(-lam_init), scalar2=None, op0=ALU.add), reads=[prm_tk], writes=[prm_tk])
            ones_bf = psb("at_ones_bf", [128, 128], BF16); tri_bf = psb("at_tri_bf", [128, 128], BF16)
            cb_tk = tk()
            P.op("dve", lambda e: e.tensor_copy(out=ones_bf[:], in_=cst["ones"][:]), reads=[cst_tk], writes=[cb_tk])
            P.op("dve", lambda e: e.tensor_copy(out=tri_bf[:], in_=cst["trimask"][:]), reads=[cst_tk, cb_tk], writes=[cb_tk])
            cosT = psb("at_cosT", [128, S], BF16); sinT = psb("at_sinT", [128, S], BF16); cs_tk = tk()
            with ExitStack() as ph2:
                posi = ph2.enter_context(nc.sbuf_tensor("at_posi", [128, S], I32))
                ang = ph2.enter_context(nc.sbuf_tensor("at_ang", [128, S], F32))
                kf = ph2.enter_context(nc.sbuf_tensor("at_kf", [128, S], F32))
                t2 = [Tk() for _ in range(3)]
                P.dma("sp", possem, lambda e: e.dma_start(out=posi[:], in_=pos_d[0, :].partition_broadcast(128)), writes=[t2[0]])
                for which, dst in ((0, sinT), (1, cosT)):
                    P.op("dve", lambda e: e.tensor_copy(out=ang[:], in_=posi[:]), reads=[t2[0]], writes=[t2[1]])
                    P.op("dve", (lambda which=which: lambda e: e.tensor_scalar(out=ang[:], in0=ang[:], scalar1=cst["freq"][:, 0:1], scalar2=float(which * np.pi / 2), op0=ALU.mult, op1=ALU.add))(),
                         reads=[t2[1], cst_tk], writes=[t2[1]])
                    P.op("dve", lambda e: e.tensor_scalar(out=kf[:], in0=ang[:], scalar1=float(1.0 / (2 * np.pi)), scalar2=None, op0=ALU.mult), reads=[t2[1]], writes=[t2[2]])
                    P.op("dve", lambda e: e.tensor_copy(out=kf[:].bitcast(I32), in_=kf[:]), reads=[t2[2]], writes=[t2[2]])
                    P.op("dve", lambda e: e.tensor_copy(out=kf[:], in_=kf[:].bitcast(I32)), reads=[t2[2]], writes=[t2[2]])
                    P.op("dve", lambda e: e.scalar_tensor_tensor(out=ang[:], in0=kf[:], scalar=-6.28125, in1=ang[:], op0=ALU.mult, op1=ALU.add), reads=[t2[1], t2[2]], writes=[t2[1]])
                    P.op("dve", lambda e: e.scalar_tensor_tensor(out=ang[:], in0=kf[:], scalar=-float(2 * np.pi - 6.28125), in1=ang[:], op0=ALU.mult, op1=ALU.add), reads=[t2[1], t2[2]], writes=[t2[1]])
                    P.op("dve", lambda e: e.tensor_scalar(out=ang[:], in0=ang[:], scalar1=3.14159, scalar2=-3.14159, op0=ALU.min, op1=ALU.max), reads=[t2[1]], writes=[t2[1]])
                    P.op("act", (lambda dst=dst: lambda e: e.activation(out=dst[:], in_=ang[:], func=AF.Sin))(), reads=[t2[1]], writes=[cs_tk])
                retire(t2)
            wq = psb("at_wq", [128, 8, 128], BF16)
            wk = psb("at_wk", [128, 8, 128], BF16)
            rot_bf = psb("at_rot_bf", [128, 128], BF16)
            P.op("dve", lambda e: e.tensor_copy(out=rot_bf[:], in_=cst["rot"][:]), reads=[cst_tk, cb_tk], writes=[cb_tk])
            qb_r = Ring([psb("at_qb%d" % i, [128, 512], BF16) for i in range(2)])
            wv = psb("at_wv", [128, 8, 128], BF16)
            wq_tk = tk(); wk_tk = tk(); wv_tk = tk()
            wo2 = [psb("at_wo%d" % i, [128, 2, D], BF16) for i in range(2)]; wo2_tk = [tk() for _ in range(2)]
            qT2 = [psb("at_qT%d" % i, [128, S], BF16) for i in range(2)]; qT2_tk = [tk() for _ in range(2)]
            kT2 = [psb("at_kT%d" % i, [128, S], BF16) for i in range(2)]; kT2_tk = [tk() for _ in range(2)]
            vt2 = [psb("at_vt%d" % i, [128, NT, 128], BF16) for i in range(2)]; vt2_tk = [tk() for _ in range(2)]
            aoT = psb("at_aoT", [128, 2, S], BF16); aoT_tk = tk()
            pT_r = Ring([psb("at_pT%d" % i, [128, 512], BF16) for i in range(4)])
            om_r = Ring([psb("at_om%d" % i, [128, 512]) for i in range(6)])
            rings = [pT_r, om_r, qb_r]

            def rot_weights(w, w_tk, wr, wr_tk):
                wv4 = w[:].rearrange("p c (g r d) -> p c g r d", g=2, r=2)
                wr4 = wr[:].rearrange("p c (g r d) -> p c g r d", g=2, r=2)
                for g in range(2):
                    P.op("pool", (lambda g=g: lambda e: e.tensor_scalar(out=wr4[:, :, g, 0, :], in0=wv4[:, :, g, 1, :], scalar1=-1.0, scalar2=None, op0=ALU.mult))(),
                         reads=[w_tk], writes=[wr_tk])
                    P.op("pool", (lambda g=g: lambda e: e.tensor_copy(out=wr4[:, :, g, 1, :], in_=wv4[:, :, g, 0, :]))(), reads=[w_tk, wr_tk], writes=[wr_tk])

            def prologue_items(h):
                par = h % 2
                hs = slice(h * 128, (h + 1) * 128)
                qT, qT_tk, kT, kT_tk, vt, vt_tk = qT2[par], qT2_tk[par], kT2[par], kT2_tk[par], vt2[par], vt2_tk[par]
                items = []

                def it_w():
                    load_w(wq, wq_tk, dr["b_w_q"][:, hs], 8, 128, 3, colgain=gcol[:, 0, :], colgain_tk=gcol_tk)
                    load_w(wk, wk_tk, dr["w_kv"][:, hs], 8, 128, 2, colgain=gcol[:, 1, :], colgain_tk=gcol_tk)
                    load_w(wv, wv_tk, dr["w_kv"][:, 1024 + h * 128:1024 + (h + 1) * 128], 8, 128, 2)
                    if h % 2 == 0:
                        load_w(wo2[(h // 2) % 2], wo2_tk[(h // 2) % 2], dr["b_w_out"][h * 128:(h + 2) * 128, :], 2, D, None)
                items += [it_w, None, None, None, None, None]
                for tg in range(4):
                    st = {}

                    def v1(tg=tg, st=st):
                        ps, ps_tk = PSb.next()
                        for j in range(4):
                            t = tg * 4 + j
                            for c in range(8):
                                P.op("pe", (lambda ps=ps, j=j, t=t, c=c: lambda e: e.matmul(
                                    ps[:, j * 128:(j + 1) * 128], xnT[:, c, t * 128:(t + 1) * 128], wv[:, c, :], start=(c == 0), stop=(c == 7)))(),
                                    reads=[xnT_tk[tg], wv_tk], writes=[ps_tk], sig=(j == 3 and c == 7))
                        st.update(ps=ps, ps_tk=ps_tk)

                    def v2(tg=tg, st=st):
                        ps, ps_tk = st["ps"], st["ps_tk"]
                        P.op("dve", (lambda ps=ps: lambda e: e.tensor_copy(out=vt[:, tg * 4:(tg + 1) * 4, :], in_=ps[:].rearrange("p (j n) -> p j n", j=4)))(),
                             reads=[ps_tk], writes=[vt_tk])
                    items += [v1, None, v2]
                for (w, w_tk, dst, dst_tk, gi) in ((wq, wq_tk, qT, qT_tk, 0), (wk, wk_tk, kT, kT_tk, 1)):
                    for tb in range(NB):
                        st = {}

                        def s1(w=w, w_tk=w_tk, tb=tb, st=st):
                            blk = slice(tb * 512, (tb + 1) * 512)
                            ps, ps_tk = PSb.next()
                            for c in range(8):
                                P.op("pe", (lambda ps=ps, c=c: lambda e: e.matmul(ps[:], w[:, c, :], xnT[:, c, blk], start=(c == 0), stop=(c == 7)))(),
                                     reads=[w_tk, xnT_tk[tb]], writes=[ps_tk], sig=(c == 7))
                            st.update(ps=ps, ps_tk=ps_tk)

                        def s2(tb=tb, st=st):
                            blk = slice(tb * 512, (tb + 1) * 512)
                            ps, ps_tk = st["ps"], st["ps_tk"]
                            sq, sq_tk = tmpf.next()
                            P.op("act", (lambda: lambda e: e.activation(out=bfv(sq), in_=ps[:], func=AF.Square))(), reads=[ps_tk], writes=[sq_tk])
                            qb, qb_tk = qb_r.next()
                            P.op("act", (lambda: lambda e: e.activation(out=qb[:], in_=ps[:], func=AF.Copy))(), reads=[ps_tk], writes=[qb_tk])
                            t1, t1_tk = om_r.next()
                            P.op("dve", (lambda: lambda e: e.tensor_tensor(out=t1[:], in0=ps[:], in1=cosT[:, blk], op=ALU.mult))(), reads=[ps_tk, cs_tk], writes=[t1_tk])
                            st.update(sq=sq, sq_tk=sq_tk, t1=t1, t1_tk=t1_tk, qb=qb, qb_tk=qb_tk)

                        def s3(st=st, gi=gi):
                            sq, sq_tk, qb, qb_tk = st["sq"], st["sq_tk"], st["qb"], st["qb_tk"]
                            ps2, ps2_tk = PSb.next()
                            P.op("pe", (lambda: lambda e: e.matmul(ps2[:], bo_bf[:, gi, :], bfv(sq), start=True, stop=True))(), reads=[sq_tk, bo_tk], writes=[ps2_tk])
                            psr, psr_tk = PSb.next()
                            P.op("pe", (lambda: lambda e: e.matmul(psr[:], rot_bf[:], qb[:], start=True, stop=True))(), reads=[qb_tk, cb_tk], writes=[psr_tk])
                            st.update(ps2=ps2, ps2_tk=ps2_tk, psr=psr, psr_tk=psr_tk)

                        def s4(tb=tb, st=st):
                            blk = slice(tb * 512, (tb + 1) * 512)
                            ps2, ps2_tk, psr, psr_tk = st["ps2"], st["ps2_tk"], st["psr"], st["psr_tk"]
                            rr, rr_tk = tmpf.next()
                            P.op("act", (lambda: lambda e: e.activation(out=rr[:], in_=ps2[:], func=AF.Ln, bias=epsb[:, 0:1], scale=1.0 / 64))(),
                                 reads=[ps2_tk, cst_tk], writes=[rr_tk])
                            P.op("act", (lambda: lambda e: e.activation(out=rr[:], in_=rr[:], func=AF.Exp, scale=-0.5))(), reads=[rr_tk], writes=[rr_tk])
                            t2_, t2_tk = om_r.next()
                            P.op("dve", (lambda: lambda e: e.tensor_tensor(out=t2_[:], in0=psr[:], in1=sinT[:, blk], op=ALU.mult))(), reads=[psr_tk, cs_tk], writes=[t2_tk])
                            st.update(rr=rr, rr_tk=rr_tk, t2=t2_, t2_tk=t2_tk)

                        def s5(tb=tb, st=st, dst=dst, dst_tk=dst_tk):
                            blk = slice(tb * 512, (tb + 1) * 512)
                            t1, t1_tk, t2_, t2_tk, rr, rr_tk = st["t1"], st["t1_tk"], st["t2"], st["t2_tk"], st["rr"], st["rr_tk"]
                            P.op("pool", (lambda: lambda e: e.tensor_tensor(out=t1[:], in0=t1[:], in1=t2_[:], op=ALU.add))(), reads=[t1_tk, t2_tk], writes=[t1_tk])
                            P.op("pool", (lambda: lambda e: e.tensor_tensor(out=dst[:, blk], in0=t1[:], in1=rr[:], op=ALU.mult))(), reads=[t1_tk, rr_tk], writes=[dst_tk])
                        items += [s1, None, s2, s3, None, s4, s5]
                return items

            def outproj_items(hp):
                items = []
                wo, wo_tk = wo2[hp % 2], wo2_tk[hp % 2]
                for tb in range(NB):
                    for n0 in range(0, 8, 2):
                        def it(tb=tb, n0=n0):
                            blk = slice(tb * 512, (tb + 1) * 512)
                            for n_ in range(n0, n0 + 2):
                                ps, ps_tk = PSb.next()
                                for hh in range(2):
                                    P.op("pe", (lambda ps=ps, hh=hh, n_=n_: lambda e: e.matmul(
                                        ps[:], wo[:, hh, n_ * 128:(n_ + 1) * 128], aoT[:, hh, blk], start=(hh == 0), stop=(hh == 1)))(),
                                        reads=[wo_tk, aoT_tk], writes=[ps_tk], sig=(hh == 1))
                                P.op("dve", (lambda ps=ps, n_=n_: lambda e: e.tensor_tensor(out=xT[:, n_, blk], in0=ps[:], in1=xT[:, n_, blk], op=ALU.add))(),
                                     reads=[ps_tk], writes=[xT_tk[n_][tb]])
                        it.is_outproj = True
                        items.append(it)
                return items

            for it in prologue_items(0):
                if it is not None:
                    it()
            for h in range(8):
                par = h % 2
                qT, qT_tk, kT, kT_tk, vt, vt_tk = qT2[par], qT2_tk[par], kT2[par], kT2_tk[par], vt2[par], vt2_tk[par]
                drain(10 ** 9)
                if h + 1 < 8:
                    pending.extend(prologue_items(h + 1))
                tiles = []
                for qb in range(NB):
                    for kt in range(4 * qb + 4):
                        for m in range(2):
                            tiles.append((qb, kt, m))

                def emit_score(i):
                    qb, kt, m = tiles[i]
                    j = kt - 4 * qb
                    q0 = 0 if j < 0 else 128 * j
                    n = 512 - q0
                    ms = slice(m * 64, (m + 1) * 64)
                    ps, ps_tk = PSa.next()
                    P.op("pe", (lambda ps=ps, kT=kT, qT=qT: lambda e: e.matmul(ps[:, 0:n], kT[ms, kt * 128:(kt + 1) * 128], qT[ms, qb * 512 + q0:(qb + 1) * 512], start=True, stop=True))(),
                         reads=[kT_tk, qT_tk], writes=[ps_tk])
                    return ps, ps_tk
                nxt = emit_score(0)
                for i, (qb, kt, m) in enumerate(tiles):
                    ps, ps_tk = nxt
                    if i + 1 < len(tiles):
                        nxt = emit_score(i + 1)
                    j = kt - 4 * qb
                    q0 = 0 if j < 0 else 128 * j
                    n = 512 - q0
                    nkt = 4 * qb + 4
                    pT, pT_tk = pT_r.next()
                    P.op("act", (lambda pT=pT, ps=ps, n=n: lambda e: e.activation(out=pT[:, 0:n], in_=ps[:, 0:n], func=AF.Exp))(), reads=[ps_tk], writes=[pT_tk])
                    if j >= 0:
                        P.op("dve", (lambda pT=pT: lambda e: e.tensor_tensor(out=pT[:, 0:128], in0=pT[:, 0:128], in1=tri_bf[:], op=ALU.mult))(),
                             reads=[pT_tk, cb_tk], writes=[pT_tk])
                    P.op("pe", (lambda pT=pT, m=m, kt=kt, q0=q0, n=n, nkt=nkt, vt=vt: lambda e: e.matmul(
                        acc_b[2 * m][:, q0:512], vt[:, kt, :], pT[:, 0:n], start=(kt == 0), stop=(kt == nkt - 1)))(),
                        reads=[pT_tk, vt_tk], writes=[acc_tk[2 * m]], sig=False)
                    P.op("pe", (lambda pT=pT, m=m, kt=kt, q0=q0, n=n, nkt=nkt: lambda e: e.matmul(
                        acc_b[2 * m + 1][:, q0:512], ones_bf[:], pT[:, 0:n], start=(kt == 0), stop=(kt == nkt - 1)))(),
                        reads=[pT_tk, cb_tk], writes=[acc_tk[2 * m + 1]], sig=True)
                    drain(1)
                    if kt == nkt - 1 and m == 1:
                        blk = slice(qb * 512, (qb + 1) * 512)
                        while any(getattr(it_, "is_epi", False) for it_ in pending):
                            drain(1)
                        o_s = []; rd_s = []
                        for mm in range(2):
                            o_, o_tk = om_r.next()
                            P.op("act", (lambda o_=o_, mm=mm: lambda e: e.activation(out=o_[:], in_=acc_b[2 * mm][:], func=AF.Copy))(), reads=[acc_tk[2 * mm]], writes=[o_tk])
                            rd, rd_tk = om_r.next()
                            P.op("dve", (lambda rd=rd, mm=mm: lambda e: e.tensor_copy(out=rd[:], in_=acc_b[2 * mm + 1][:]))(), reads=[acc_tk[2 * mm + 1]], writes=[rd_tk])
                            o_s.append((o_, o_tk)); rd_s.append((rd, rd_tk))
                        st = {}

                        def e0(rd_s=rd_s):
                            for mm in range(2):
                                rd, rd_tk = rd_s[mm]
                                P.op("act", (lambda rd=rd: lambda e: e.activation(out=rd[:], in_=rd[:], func=AF.Ln))(), reads=[rd_tk], writes=[rd_tk])
                                P.op("act", (lambda rd=rd: lambda e: e.activation(out=rd[:], in_=rd[:], func=AF.Exp, scale=-1.0))(), reads=[rd_tk], writes=[rd_tk])

                        def e1(o_s=o_s, rd_s=rd_s, st=st):
                            for mm in range(2):
                                P.op("pool", (lambda mm=mm: lambda e: e.tensor_tensor(out=o_s[mm][0][:], in0=o_s[mm][0][:], in1=rd_s[mm][0][:], op=ALU.mult))(),
                                     reads=[o_s[mm][1], rd_s[mm][1]], writes=[o_s[mm][1]])
                            df, df_tk = rd_s[0]
                            P.op("dve", (lambda df=df: lambda e: e.scalar_tensor_tensor(out=df[:], in0=o_s[1][0][:], scalar=prm[:, 3:4], in1=o_s[0][0][:], op0=ALU.mult, op1=ALU.add))(),
                                 reads=[o_s[0][1], o_s[1][1], prm_tk], writes=[df_tk])
                            st.update(df=df, df_tk=df_tk)

                        def e1b(rd_s=rd_s, st=st):
                            df, df_tk = st["df"], st["df_tk"]
                            sq, sq_tk = rd_s[1]
                            P.op("act", (lambda: lambda e: e.activation(out=bfv(sq), in_=df[:], func=AF.Square))(), reads=[df_tk], writes=[sq_tk])
                            st.update(sq=sq, sq_tk=sq_tk)

                        def e1c(st=st):
                            sq, sq_tk = st["sq"], st["sq_tk"]
                            ps2, ps2_tk = PSb.next()
                            P.op("pe", (lambda: lambda e: e.matmul(ps2[:], ones_bf_p[:], bfv(sq), start=True, stop=True))(), reads=[sq_tk, onesb_tk], writes=[ps2_tk])
                            st.update(ps2=ps2, ps2_tk=ps2_tk)

                        def e2(st=st, h=h, blk=blk):
                            df, df_tk, ps2, ps2_tk = st["df"], st["df_tk"], st["ps2"], st["ps2_tk"]
                            rr, rr_tk = tmpf.next()
                            P.op("act", (lambda: lambda e: e.activation(out=rr[:], in_=ps2[:], func=AF.Ln, bias=epsb[:, 0:1], scale=1.0 / 128))(),
                                 reads=[ps2_tk, cst_tk], writes=[rr_tk])
                            P.op("act", (lambda: lambda e: e.activation(out=rr[:], in_=rr[:], func=AF.Exp, scale=-0.5))(), reads=[rr_tk], writes=[rr_tk])
                            st.update(rr=rr, rr_tk=rr_tk)

                        def e3(st=st, h=h, blk=blk):
                            df, df_tk, rr, rr_tk = st["df"], st["df_tk"], st["rr"], st["rr_tk"]
                            P.op("dve", (lambda: lambda e: e.scalar_tensor_tensor(
                                out=aoT[:, h % 2, blk], in0=df[:], scalar=prm[:, 2:3], in1=rr[:], op0=ALU.mult, op1=ALU.mult))(),
                                reads=[df_tk, rr_tk, prm_tk], writes=[aoT_tk])
                        epi = [e0, e1, None, e1b, e1c, None, e2, e3]
                        for it_ in epi:
                            if it_ is not None:
                                it_.is_epi = True
                        lst = list(pending)
                        pos_ = 0
                        for ii, it_ in enumerate(lst):
                            if getattr(it_, "is_outproj", False):
                                pos_ = ii + 1
                        pending.clear()
                        pending.extend(lst[:pos_] + epi + lst[pos_:])
                        if qb == NB - 1 and h % 2 == 1:
                            lst = list(pending)
                            pos_ = lst.index(e3) + 1
                            pending.clear()
                            pending.extend(lst[:pos_] + outproj_items(h // 2) + lst[pos_:])
            drain(10 ** 9)
            for r in rings:
                mytks.extend(r.tks)
            retire(mytks)
            close_stage()

        if "gdn" in stages:
            rms_to_xnT()
            gdn()
        if "mlp0" in stages:
            rms_to_xnT()
            mlp(0, 1)
        if "attn" in stages:
            rms_to_xnT()
            attention()
        if "mlp1" in stages:
            rms_to_xnT()
            mlp(1, 4)

        osem = [P.new_dma_sem("out%d" % i) for i in range(2)]
        phf = ExitStack()
        open_stage(phf, 1024)
        stage = stage_box["ring"]
        for t in range(NT):
            tb = t // 4
            o, o_tk = stage.next()
            osi = (stage.i - 1) % 2
            for cg in range(2):
                ps, ps_tk = PS.next()
                for j in range(4):
                    c = cg * 4 + j
                    P.op("pe", (lambda ps=ps, c=c, j=j, t=t: lambda e: e.transpose(
                        ps[:, j * 128:(j + 1) * 128], xT[:, c, t * 128:(t + 1) * 128], cst["ident"][:]))(),
                        reads=[xT_tk[c][tb], cst_tk], writes=[ps_tk], sig=(j == 3))
                evac_copy(o[:, cg * 512:(cg + 1) * 512], ps[:], reads=[ps_tk], writes=[o_tk])
            P.dma("sp", osem[osi], (lambda o=o, t=t: lambda e: e.dma_start(out=out_d[t * 128:(t + 1) * 128, :], in_=o[:, 0:1024]))(), reads=[o_tk])
        P.final_wait("sp", osem[0])
        P.final_wait("sp", osem[1])
        phf.close()
        P.emit()
        nops = {e: len(P.ops[e]) for e in P.ENG}
        nops["splits"] = P.split_log
    return nc, nops


_CACHE = {}


def kernel(**inputs):
    B = inputs["x"].shape[0]
    consts = host_consts()
    if "nc" not in _CACHE:
        _CACHE["nc"] = build_program()
    nc, _ = _CACHE["nc"]
    shared = {}
    f32 = lambda a: np.ascontiguousarray(np.asarray(a, dtype=np.float32))
    shared["a_norm"] = f32(inputs["a_norm"]).reshape(1, D)
    shared["a_w_in"] = f32(inputs["a_w_in"]).reshape(D, 4112)
    shared["a_conv_w"] = f32(inputs["a_conv_w"]).reshape(4, 3072)
    shared["a_a_log"] = f32(inputs["a_a_log"]).reshape(1, 8)
    shared["a_dt_bias"] = f32(inputs["a_dt_bias"]).reshape(1, 8)
    shared["a_out_norm"] = f32(inputs["a_out_norm"]).reshape(1, 128)
    shared["a_w_out"] = f32(inputs["a_w_out"]).reshape(D, D)
    shared["kv_norm"] = f32(inputs["kv_norm"]).reshape(1, D)
    shared["w_kv"] = f32(inputs["w_kv"]).reshape(D, 2048)
    shared["k_norm"] = f32(inputs["k_norm"]).reshape(1, 64)
    shared["b_norm"] = f32(inputs["b_norm"]).reshape(1, D)
    shared["b_w_q"] = f32(inputs["b_w_q"]).reshape(D, D)
    shared["b_q_norm"] = f32(inputs["b_q_norm"]).reshape(1, 64)
    shared["b_lambda"] = f32(inputs["b_lambda"]).reshape(4, 64)
    shared["b_sub_norm"] = f32(inputs["b_sub_norm"]).reshape(1, 128)
    shared["b_w_out"] = f32(inputs["b_w_out"]).reshape(D, D)
    shared["mlp_norm"] = f32(inputs["mlp_norm"]).reshape(2, D)
    shared["mlp_w1"] = f32(inputs["mlp_w1"]).reshape(2 * D, DFF)
    shared["mlp_w2"] = f32(inputs["mlp_w2"]).reshape(2 * DFF, D)
    for n in CONST_NAMES:
        shared["c_" + n] = consts[n]
    x = f32(inputs["x"])
    pos = np.ascontiguousarray(np.asarray(inputs["positions"], dtype=np.int32))
    in_maps = []
    for b in range(B):
        m = dict(shared)
        m["x"] = x[b]
        m["positions"] = pos[b].reshape(1, S)
        in_maps.append(m)
    res = run_bass_kernel_spmd(nc, in_maps, core_ids=list(range(B)))
    return np.stack([np.asarray(r["out"], dtype=np.float32) for r in res.results], axis=0)
```

```python
import numpy as np
import concourse.bass as bass
import concourse.mybir as mybir
from concourse.bass_utils import run_bass_kernel_spmd
from contextlib import ExitStack

F32 = mybir.dt.float32
BF16 = mybir.dt.bfloat16
I32 = mybir.dt.int32
AF = mybir.ActivationFunctionType
ALU = mybir.AluOpType
AX = mybir.AxisListType

S = 2048
D = 1024
NT = 16
NB = 4
DFF = 4096
EPS = 1e-6


INHERIT = {}


class Tk:
    __slots__ = ("name", "w", "r", "excl")

    def __init__(self, name="", excl=False):
        self.name = name
        self.w = None
        self.r = dict(INHERIT)
        self.excl = excl


def retire(tks):
    for t in tks:
        if t.w is not None:
            k, v = t.w
            if INHERIT.get(k, 0) < v:
                INHERIT[k] = v
        for k, v in t.r.items():
            if INHERIT.get(k, 0) < v:
                INHERIT[k] = v


class Prog:
    ENG = ("pe", "act", "dve", "pool", "sp")

    def __init__(self, nc, es):
        self.nc = nc
        self.es = es
        self.ops = {e: [] for e in self.ENG}
        self.sems = {}
        self.cnt = {}
        for e in self.ENG:
            self.sems[e] = es.enter_context(nc.semaphore("s_" + e))
            self.cnt[e] = 0
        self.waited = {e: {} for e in self.ENG}
        self.pending_nosig = {e: False for e in self.ENG}

    def new_dma_sem(self, name):
        key = "dma_" + name
        self.sems[key] = self.es.enter_context(self.nc.semaphore(key))
        self.cnt[key] = 0
        return key

    def _deps(self, eng, reads, writes):
        deps = {}

        def add(d):
            if d is None:
                return
            k, v = d
            if deps.get(k, 0) < v:
                deps[k] = v
        for t in reads:
            add(t.w)
            if t.excl:
                for k, v in t.r.items():
                    if k != eng:
                        add((k, v))
        for t in writes:
            add(t.w)
            for k, v in t.r.items():
                add((k, v))
        waits = []
        wd = self.waited[eng]
        for k, v in deps.items():
            if k == "pe" and eng == "pe":
                continue
            if wd.get(k, 0) < v:
                wd[k] = v
                waits.append((k, v))
        return waits

    def op(self, eng, fn, reads=(), writes=(), sig=True):
        assert sig or eng == "pe"
        waits = self._deps(eng, reads, writes)
        if sig:
            self.cnt[eng] += 1
            val = self.cnt[eng]
        else:
            val = self.cnt[eng] + 1
        self.pending_nosig[eng] = not sig
        self.ops[eng].append((waits, fn, (eng, 1) if sig else None))
        for t in reads:
            if t.r.get(eng, 0) < val:
                t.r[eng] = val
        for t in writes:
            t.w = (eng, val)
            t.r = {}

    def dma(self, q, semkey, fn, reads=(), writes=()):
        assert q == "sp"
        waits = self._deps(q, reads, writes)
        self.cnt[semkey] += 1
        val = self.cnt[semkey]
        self.ops[q].append((waits, fn, (semkey, 16)))
        for t in reads:
            if t.r.get(semkey, 0) < val:
                t.r[semkey] = val
        for t in writes:
            t.w = (semkey, val)
            t.r = {}

    def final_wait(self, eng, semkey):
        if self.cnt[semkey] > 0:
            self.ops[eng].append(([(semkey, self.cnt[semkey])], None, None))

    def emit(self):
        nc = self.nc
        for e in self.ENG:
            assert not self.pending_nosig[e], e
        actual = {k: [0] for k in self.sems if k.startswith("dma_")}
        self.split_log = []
        with nc.Block() as block:
            def run(engname):
                def body(eng):
                    for waits, fn, inc in self.ops[engname]:
                        for k, v in waits:
                            eng.wait_ge(self.sems[k], actual[k][v] if k in actual else v)
                        if fn is not None:
                            n0 = nc.n_instructions()
                            ins = fn(eng)
                            if inc is not None:
                                ins.then_inc(self.sems[inc[0]], inc[1])
                                if inc[0] in actual:
                                    k_ = max(1, nc.n_instructions() - n0)
                                    if k_ != 1:
                                        self.split_log.append((inc[0], len(actual[inc[0]]), k_))
                                    actual[inc[0]].append(actual[inc[0]][-1] + 16 * k_)
                return body
            block.sync(run("sp"))
            block.tensor(run("pe"))
            block.scalar(run("act"))
            block.vector(run("dve"))
            block.gpsimd(run("pool"))


class Ring:
    def __init__(self, bufs, tks=None):
        self.bufs = bufs
        self.tks = tks if tks is not None else [Tk() for _ in bufs]
        self.i = 0

    def next(self):
        b, t = self.bufs[self.i], self.tks[self.i]
        self.i = (self.i + 1) % len(self.bufs)
        return b, t


def host_consts():
    c = {}
    i = np.arange(128)
    c["ident"] = np.eye(128, dtype=np.float32)
    c["ones"] = np.ones((128, 128), dtype=np.float32)
    bo = np.zeros((128, 128), np.float32); bo[:64, :64] = 1; bo[64:, 64:] = 1
    c["blockones"] = bo
    rm = np.zeros((128, 128), np.float32)
    for p in range(128):
        if p % 64 < 32:
            rm[p + 32, p] = -1.0
        else:
            rm[p - 32, p] = 1.0
    c["rot"] = rm
    fr = np.zeros((128, 128), np.float32)
    fr[:, 0] = (10000.0 ** (-(np.arange(128) % 32).astype(np.float32) / 32.0)).astype(np.float32)
    c["freq"] = fr
    c["trimask"] = (i[:, None] <= i[None, :]).astype(np.float32)
    c["mnegL"] = np.where(i[:, None] >= i[None, :], 0.0, -30000.0).astype(np.float32)
    c["mnegU"] = np.ascontiguousarray(c["mnegL"].T)
    c["mnegLs"] = np.where(i[:, None] > i[None, :], 0.0, -30000.0).astype(np.float32)
    c["strictL"] = (i[:, None] > i[None, :]).astype(np.float32)
    c["strictU"] = np.ascontiguousarray(c["strictL"].T)
    return c


CONST_NAMES = ["ident", "ones", "blockones", "rot", "freq", "trimask", "mnegLs", "mnegU", "strictU"]


def build_program(stages=("gdn", "mlp0", "attn", "mlp1"), debug=False):
    INHERIT.clear()
    nc = bass.Bass("TRN2", target_bir_lowering=False)
    dr = {}

    def din(name, shape, dt=F32):
        dr[name] = nc.dram_tensor(name, list(shape), dt, kind="ExternalInput").ap()
        return dr[name]
    x_d = din("x", [S, D])
    pos_d = din("positions", [1, S], I32)
    din("a_norm", [1, D]); din("a_w_in", [D, 4112]); din("a_conv_w", [4, 3072]); din("a_a_log", [1, 8])
    din("a_dt_bias", [1, 8]); din("a_out_norm", [1, 128]); din("a_w_out", [D, D])
    din("kv_norm", [1, D]); din("w_kv", [D, 2048]); din("k_norm", [1, 64])
    din("b_norm", [1, D]); din("b_w_q", [D, D]); din("b_q_norm", [1, 64]); din("b_lambda", [4, 64])
    din("b_sub_norm", [1, 128]); din("b_w_out", [D, D])
    din("mlp_norm", [2, D]); din("mlp_w1", [2 * D, DFF]); din("mlp_w2", [2 * DFF, D])
    for n in CONST_NAMES:
        din("c_" + n, [128, 128])
    out_d = nc.dram_tensor("out", [S, D], F32, kind="ExternalOutput").ap()

    with ExitStack() as es:
        P = Prog(nc, es)

        def sb(name, shape, dt=F32):
            return es.enter_context(nc.sbuf_tensor(name, list(shape), dt))

        xT = sb("xT", [128, 8, S])
        xT_tk = [[Tk() for _ in range(NB)] for _ in range(8)]
        xnT = sb("xnT", [128, 8, S], BF16)
        xnT_tk = [Tk() for _ in range(NB)]
        banks = [es.enter_context(nc.psum_tensor("bank%d" % i, [128, 512], F32)) for i in range(8)]
        PS = Ring(banks, [Tk("bank%d" % i, excl=True) for i in range(8)])
        stage_sem = [P.new_dma_sem("stage%d" % i) for i in range(2)]
        stage_box = {"gen": 0}

        def open_stage(ph_, width):
            g_ = stage_box["gen"]
            stage_box["gen"] = g_ + 1
            bufs = [ph_.enter_context(nc.sbuf_tensor("stage_%d_%d" % (g_, i), [128, width], F32)) for i in range(2)]
            stage_box["ring"] = Ring(bufs)
            stage_box["width"] = width

        def close_stage():
            retire(stage_box["ring"].tks)
        tmpf = Ring([sb("tmpf%d" % i, [128, 512]) for i in range(3)])
        rstd_r = Ring([sb("rstd%d" % i, [128, 512]) for i in range(1)])
        cst = {}
        cst_tk = Tk()
        csem = P.new_dma_sem("const")
        for n in CONST_NAMES:
            cst[n] = sb("cs_" + n, [128, 128])
            P.dma("sp", csem, (lambda n=n: lambda e: e.dma_start(out=cst[n][:], in_=dr["c_" + n]))(), writes=[cst_tk])
        gains = sb("gains", [128, 5, 8])
        gain_src = [dr["a_norm"][0, :], dr["mlp_norm"][0, :], dr["kv_norm"][0, :], dr["b_norm"][0, :], dr["mlp_norm"][1, :]]
        for gi, src in enumerate(gain_src):
            P.dma("sp", csem, (lambda gi=gi, src=src: lambda e: e.dma_start(
                out=gains[:, gi, :], in_=src.rearrange("(c p) -> p c", p=128), allow_slow_non_contiguous=True))(), writes=[cst_tk])
        epsb = sb("epsb", [128, 1])
        P.op("pool", lambda e: e.memset(epsb[:], EPS), writes=[cst_tk])
        ones_bf_p = sb("ones_bf_p", [128, 128], BF16)
        onesb_tk = Tk()
        P.op("dve", lambda e: e.tensor_copy(out=ones_bf_p[:], in_=cst["ones"][:]), reads=[cst_tk], writes=[onesb_tk])
        bfv = lambda t_: t_[:].bitcast(BF16)[:, 0:512]

        evac_flip = [0]

        def evac_copy(out_ap, in_ap, reads, writes):
            evac_flip[0] ^= 1
            if evac_flip[0]:
                P.op("act", lambda e: e.activation(out=out_ap, in_=in_ap, func=AF.Copy), reads=reads, writes=writes)
            else:
                P.op("dve", lambda e: e.tensor_copy(out=out_ap, in_=in_ap), reads=reads, writes=writes)

        ph0 = ExitStack()
        open_stage(ph0, 1024)
        stage = stage_box["ring"]
        for t in range(NT):
            st, st_tk = stage.next()
            si = (stage.i - 1) % 2
            P.dma("sp", stage_sem[si], (lambda st=st, t=t: lambda e: e.dma_start(
                out=st[:, 0:1024], in_=x_d[t * 128:(t + 1) * 128, :]))(), writes=[st_tk])
            for cg in range(2):
                ps, ps_tk = PS.next()
                for j in range(4):
                    c = cg * 4 + j
                    P.op("pe", (lambda ps=ps, st=st, c=c, j=j: lambda e: e.transpose(
                        ps[:, j * 128:(j + 1) * 128], st[:, c * 128:(c + 1) * 128], cst["ident"][:]))(),
                        reads=[st_tk, cst_tk], writes=[ps_tk], sig=(j == 3))
                tb = t // 4
                evac_copy(xT[:, cg * 4:(cg + 1) * 4, t * 128:(t + 1) * 128],
                          ps[:].rearrange("p (j n) -> p j n", j=4),
                          reads=[ps_tk], writes=[xT_tk[c][tb] for c in range(cg * 4, cg * 4 + 4)])
        close_stage()
        ph0.close()

        def rms_to_xnT():
            for tb in range(NB):
                blk = slice(tb * 512, (tb + 1) * 512)
                ps, ps_tk = PS.next()
                for c in range(8):
                    sq, sq_tk = tmpf.next()
                    P.op("act", (lambda sq=sq, c=c, blk=blk: lambda e: e.activation(out=bfv(sq), in_=xT[:, c, blk], func=AF.Square))(),
                         reads=[xT_tk[c][tb]], writes=[sq_tk])
                    P.op("pe", (lambda ps=ps, sq=sq, c=c: lambda e: e.matmul(ps[:], ones_bf_p[:], bfv(sq), start=(c == 0), stop=(c == 7)))(),
                         reads=[sq_tk, onesb_tk], writes=[ps_tk], sig=True)
                rs, rs_tk = tmpf.next()
                P.op("act", (lambda rs=rs, ps=ps: lambda e: e.activation(out=rs[:], in_=ps[:], func=AF.Ln, bias=epsb[:, 0:1], scale=1.0 / D))(),
                     reads=[ps_tk, cst_tk], writes=[rs_tk])
                rstd_bc, rstd_tk = rstd_r.next()
                P.op("act", (lambda rs=rs, rstd_bc=rstd_bc: lambda e: e.activation(out=rstd_bc[:], in_=rs[:], func=AF.Exp, scale=-0.5))(),
                     reads=[rs_tk], writes=[rstd_tk])
                for c in range(8):
                    eng = "dve" if c % 2 == 0 else "pool"
                    P.op(eng, (lambda c=c, blk=blk, rstd_bc=rstd_bc: lambda e: e.tensor_tensor(out=xnT[:, c, blk], in0=xT[:, c, blk], in1=rstd_bc[:], op=ALU.mult))(),
                         reads=[xT_tk[c][tb], rstd_tk], writes=[xnT_tk[tb]])

        wsem = [P.new_dma_sem("w%d" % i) for i in range(2)]

        def load_w(dst, dst_tk, src, kc, ncols, gain_idx, gain_c0=0, colgain=None, colgain_tk=None):
            stage = stage_box["ring"]
            per = max(1, stage_box["width"] // ncols)
            c0 = 0
            while c0 < kc:
                n = min(per, kc - c0)
                st, st_tk = stage.next()
                si = (stage.i - 1) % 2
                stv = st[:, 0:n * ncols].rearrange("p (c n) -> p c n", c=n)
                srcv = src[c0 * 128:(c0 + n) * 128, :].rearrange("(c p) n -> p c n", p=128)
                P.dma("sp", stage_sem[si], (lambda stv=stv, srcv=srcv: lambda e: e.dma_start(out=stv, in_=srcv))(), writes=[st_tk])
                dv = dst[:, c0:c0 + n, :]
                if gain_idx is None:
                    P.op("pool", (lambda dv=dv, stv=stv: lambda e: e.tensor_copy(out=dv, in_=stv))(), reads=[st_tk], writes=[dst_tk])
                elif colgain is None:
                    gv = gains[:, gain_idx, gain_c0 + c0:gain_c0 + c0 + n].unsqueeze(2).to_broadcast([128, n, ncols])
                    P.op("pool", (lambda dv=dv, stv=stv, gv=gv: lambda e: e.tensor_tensor(out=dv, in0=stv, in1=gv, op=ALU.mult))(),
                         reads=[st_tk, cst_tk], writes=[dst_tk])
                else:
                    gv = gains[:, gain_idx, gain_c0 + c0:gain_c0 + c0 + n].unsqueeze(2).to_broadcast([128, n, ncols])
                    cv = colgain.unsqueeze(1).to_broadcast([128, n, ncols])
                    P.op("pool", (lambda stv=stv, gv=gv: lambda e: e.tensor_tensor(out=stv, in0=stv, in1=gv, op=ALU.mult))(),
                         reads=[st_tk, cst_tk], writes=[st_tk])
                    P.op("pool", (lambda dv=dv, stv=stv, cv=cv: lambda e: e.tensor_tensor(out=dv, in0=stv, in1=cv, op=ALU.mult))(),
                         reads=[st_tk, colgain_tk], writes=[dst_tk])
                c0 += n

        def mlp(layer, gain_idx):
            with ExitStack() as ph:
                mlp_body(layer, gain_idx, ph)

        def mlp_body(layer, gain_idx, ph):
            FG = 512
            open_stage(ph, 2048)
            psb = lambda name, shape, dt=F32: ph.enter_context(nc.sbuf_tensor(name, list(shape), dt))
            w1r = Ring([psb("w1g%d_%d" % (layer, i), [128, 8, FG], BF16) for i in range(2)])
            w2r = Ring([psb("w2g%d_%d" % (layer, i), [128, FG // 128, D], BF16) for i in range(2)])
            hr = Ring([psb("hT%d_%d" % (layer, i), [128, FG // 128, 512], BF16) for i in range(2)])
            w1_d = dr["mlp_w1"][layer * D:(layer + 1) * D, :]
            w2_d = dr["mlp_w2"][layer * DFF:(layer + 1) * DFF, :]
            NG = DFF // FG
            wts_ = {}

            def load_group(g):
                w1, w1_tk = w1r.next()
                w2, w2_tk = w2r.next()
                load_w(w1, w1_tk, w1_d[:, g * FG:(g + 1) * FG], 8, FG, gain_idx)
                load_w(w2, w2_tk, w2_d[g * FG:(g + 1) * FG, :], FG // 128, D, None)
                wts_[g] = (w1, w1_tk, w2, w2_tk)

            def emit_h(g, tb):
                w1, w1_tk, _, _ = wts_[g]
                blk = slice(tb * 512, (tb + 1) * 512)
                hT, hT_tk = hr.next()
                for fc in range(FG // 128):
                    ps, ps_tk = PS.next()
                    for c in range(8):
                        P.op("pe", (lambda ps=ps, c=c, fc=fc: lambda e: e.matmul(
                            ps[:], w1[:, c, fc * 128:(fc + 1) * 128], xnT[:, c, blk], start=(c == 0), stop=(c == 7)))(),
                            reads=[w1_tk, xnT_tk[tb]], writes=[ps_tk], sig=(c == 7))
                    h1, h1_tk = tmpf.next()
                    P.op("act", (lambda h1=h1, ps=ps: lambda e: e.activation(out=h1[:], in_=ps[:], func=AF.Relu))(), reads=[ps_tk], writes=[h1_tk])
                    P.op("act", (lambda h1=h1, fc=fc: lambda e: e.activation(out=hT[:, fc, :], in_=h1[:], func=AF.Square))(), reads=[h1_tk], writes=[hT_tk])
                return hT, hT_tk

            def emit_o(g, tb, hT, hT_tk):
                _, _, w2, w2_tk = wts_[g]
                blk = slice(tb * 512, (tb + 1) * 512)
                for n in range(8):
                    ps, ps_tk = PS.next()
                    for fc in range(FG // 128):
                        P.op("pe", (lambda ps=ps, n=n, fc=fc: lambda e: e.matmul(
                            ps[:], w2[:, fc, n * 128:(n + 1) * 128], hT[:, fc, :], start=(fc == 0), stop=(fc == FG // 128 - 1)))(),
                            reads=[w2_tk, hT_tk], writes=[ps_tk], sig=(fc == FG // 128 - 1))
                    P.op("dve", (lambda ps=ps, n=n: lambda e: e.tensor_tensor(out=xT[:, n, blk], in0=ps[:], in1=xT[:, n, blk], op=ALU.add))(),
                         reads=[ps_tk], writes=[xT_tk[n][tb]])
            seq = [(g, tb) for g in range(NG) for tb in range(NB)]
            load_group(0)
            cur = emit_h(*seq[0])
            for i, (g, tb) in enumerate(seq):
                if tb == 0 and g + 1 < NG:
                    load_group(g + 1)
                nxt = emit_h(*seq[i + 1]) if i + 1 < len(seq) else None
                emit_o(g, tb, *cur)
                cur = nxt
            retire(w1r.tks + w2r.tks + hr.tks)
            close_stage()


        def gdn():
            with ExitStack() as ph:
                gdn_body(ph)

        def gdn_body(ph):
            import os
            open_stage(ph, 1024)
            psb = lambda name, shape, dt=F32: ph.enter_context(nc.sbuf_tensor(name, list(shape), dt))
            mytks = []

            def tk():
                t = Tk(); mytks.append(t); return t
            pb = banks[0:4]; pb_tk = PS.tks[0:4]
            qb_ = [banks[4], banks[6]]; qb_tk = [PS.tks[4], PS.tks[6]]
            PSs = Ring([banks[5]], [PS.tks[5]])
            OBk = [banks[7], banks[7]]; OB_tk = [PS.tks[7], PS.tks[7]]
            ident = cst["ident"]; onesf = cst["ones"]
            bc4 = lambda ap: ap.unsqueeze(1).to_broadcast([128, 4, 128])
            v3 = lambda ap: ap.rearrange("p (i e) -> p i e", i=4)
            ident_bf = psb("g_identbf", [128, 128], BF16); negones = psb("g_negones", [128, 128]); cb_tk = tk()
            P.op("dve", lambda e: e.tensor_copy(out=ident_bf[:], in_=ident[:]), reads=[cst_tk], writes=[cb_tk])
            P.op("dve", lambda e: e.tensor_scalar(out=negones[:], in0=onesf[:], scalar1=-1.0, scalar2=None, op0=ALU.mult), reads=[cst_tk, cb_tk], writes=[cb_tk])
            one_b = psb("g_oneb", [128, 1]); P.op("pool", lambda e: e.memset(one_b[:], 1.0), reads=[cb_tk], writes=[cb_tk])
            s1 = P.new_dma_sem("g_p1"); s2 = P.new_dma_sem("g_p2"); s3 = P.new_dma_sem("g_p3"); s4 = P.new_dma_sem("g_p4")
            alog = psb("g_alog", [128, 8]); dtb = psb("g_dtb", [128, 8]); onw = psb("g_onw", [128, 128]); cw = psb("g_cw", [128, 24, 4])
            alog_tk = tk(); dtb_tk = tk(); onw_tk = tk(); cw_tk = tk()
            P.dma("sp", s1, lambda e: e.dma_start(out=alog[:], in_=dr["a_a_log"][0, :].partition_broadcast(128)), writes=[alog_tk])
            P.dma("sp", s2, lambda e: e.dma_start(out=dtb[:], in_=dr["a_dt_bias"][0, :].partition_broadcast(128)), writes=[dtb_tk])
            P.dma("sp", s3, lambda e: e.dma_start(out=onw[:], in_=dr["a_out_norm"][0, :].partition_broadcast(128)), writes=[onw_tk])
            for j in range(4):
                P.dma("sp", s4, (lambda j=j: lambda e: e.dma_start(out=cw[:, :, j], in_=dr["a_conv_w"][j, :].rearrange("(c p) -> p c", p=128),
                      allow_slow_non_contiguous=True))(), writes=[cw_tk])
            P.op("act", lambda e: e.activation(out=alog[:], in_=alog[:], func=AF.Exp), reads=[alog_tk], writes=[alog_tk])
            P.op("dve", lambda e: e.tensor_scalar(out=alog[:], in0=alog[:], scalar1=-1.0, scalar2=None, op0=ALU.mult), reads=[alog_tk], writes=[alog_tk])
            wbg = psb("g_wbg", [128, 8, 16], BF16); wbg_tk = tk()
            load_w(wbg, wbg_tk, dr["a_w_in"][:, 4096:4112], 8, 16, 0)
            bg = psb("g_bg", [128, NT, 16]); bg_tk = tk()
            ps, ps_tk = pb[0], pb_tk[0]
            for t in range(NT):
                for c in range(8):
                    P.op("pe", (lambda t=t, c=c: lambda e: e.matmul(ps[:, t * 16:(t + 1) * 16], xnT[:, c, t * 128:(t + 1) * 128], wbg[:, c, :],
                         start=(c == 0), stop=(c == 7)))(), reads=[xnT_tk[t // 4], wbg_tk], writes=[ps_tk], sig=(t == NT - 1 and c == 7))
            P.op("dve", lambda e: e.tensor_copy(out=bg[:].rearrange("p t k -> p (t k)"), in_=ps[:, 0:256]), reads=[ps_tk], writes=[bg_tk])
            beta = psb("g_beta", [128, NT, 8]); gg = psb("g_g", [128, NT, 8]); par_tk = tk()
            P.op("act", lambda e: e.activation(out=beta[:], in_=bg[:, :, 0:8], func=AF.Exp, scale=-1.0), reads=[bg_tk], writes=[par_tk])
            P.op("act", lambda e: e.activation(out=beta[:], in_=beta[:], func=AF.Ln, bias=one_b[:, 0:1], scale=1.0), reads=[par_tk, cb_tk], writes=[par_tk])
            P.op("act", lambda e: e.activation(out=beta[:], in_=beta[:], func=AF.Exp, scale=-1.0), reads=[par_tk], writes=[par_tk])
            P.op("dve", lambda e: e.tensor_tensor(out=gg[:], in0=bg[:, :, 8:16], in1=dtb[:].unsqueeze(1).to_broadcast([128, NT, 8]), op=ALU.add),
                 reads=[bg_tk, dtb_tk, par_tk], writes=[par_tk])
            P.op("act", lambda e: e.activation(out=gg[:], in_=gg[:], func=AF.Exp), reads=[par_tk], writes=[par_tk])
            P.op("act", lambda e: e.activation(out=gg[:], in_=gg[:], func=AF.Ln, bias=one_b[:, 0:1], scale=1.0), reads=[par_tk, cb_tk], writes=[par_tk])
            P.op("dve", lambda e: e.tensor_tensor(out=gg[:], in0=gg[:], in1=alog[:].unsqueeze(1).to_broadcast([128, NT, 8]), op=ALU.mult),
                 reads=[par_tk, alog_tk], writes=[par_tk])
            gc = psb("g_gc", [128, NT, 8]); bgam = psb("g_bgam", [128, NT, 8]); kds = psb("g_kds", [128, NT, 8]); cd = psb("g_cd", [128, NT, 8])
            ps, ps_tk = pb[1], pb_tk[1]
            fl = lambda a_: a_[:].rearrange("p t k -> p (t k)")
            P.op("pe", lambda e: e.matmul(ps[:, 0:128], cst["trimask"][:], fl(gg), start=True, stop=True), reads=[par_tk, cst_tk], writes=[ps_tk], sig=False)
            P.op("pe", lambda e: e.matmul(ps[:, 128:256], onesf[:], fl(gg), start=True, stop=True), reads=[par_tk, cst_tk], writes=[ps_tk])
            P.op("dve", lambda e: e.tensor_copy(out=fl(gc), in_=ps[:, 0:128]), reads=[ps_tk], writes=[par_tk])
            P.op("dve", lambda e: e.tensor_tensor(out=fl(kds), in0=ps[:, 128:256], in1=fl(gc), op=ALU.subtract), reads=[ps_tk, par_tk], writes=[par_tk])
            P.op("act", lambda e: e.activation(out=fl(cd), in_=ps[:, 128:256], func=AF.Exp), reads=[ps_tk, par_tk], writes=[par_tk])
            P.op("act", lambda e: e.activation(out=fl(kds), in_=fl(kds), func=AF.Exp), reads=[par_tk], writes=[par_tk])
            P.op("act", lambda e: e.activation(out=fl(bgam), in_=fl(gc), func=AF.Exp), reads=[par_tk], writes=[par_tk])
            P.op("dve", lambda e: e.tensor_tensor(out=fl(bgam), in0=fl(bgam), in1=fl(beta), op=ALU.mult), reads=[par_tk], writes=[par_tk])
            wts = [psb("g_w%d" % k, [128, 8, 128], BF16) for k in range(3)]; wts_tk = [tk() for _ in range(3)]
            wz2 = [psb("g_wz%d" % k, [128, 8, 128], BF16) for k in range(2)]; wz2_tk = [tk() for _ in range(2)]
            wo2 = [psb("g_wo%d" % i, [128, 1, D], BF16) for i in range(2)]; wo2_tk = [tk() for _ in range(2)]
            cdiag = psb("g_cdiag", [128, 12, 128], BF16); cdiag_tk = tk()
            prjb = [psb("g_prjb%d" % k, [128, 516], BF16) for k in range(2)]; prjb_tk = [tk() for _ in range(2)]
            tails = psb("g_tails", [128, 3, 4], BF16); tails_tk = tk()
            qhT = psb("g_qhT", [128, S], BF16); khT = psb("g_khT", [128, S], BF16); vT = psb("g_vT", [128, S], BF16)
            qhT_tk = [tk() for _ in range(NB)]; khT_tk = [tk() for _ in range(NB)]; vT_tk = [tk() for _ in range(NB)]
            kbg2 = [psb("g_kbg%d" % i, [128, 512], BF16) for i in range(2)]; kdec2 = [psb("g_kdec%d" % i, [128, 512], BF16) for i in range(2)]
            bv2 = [psb("g_bv%d" % i, [128, 512], BF16) for i in range(2)]; zs2 = [psb("g_zs%d" % i, [128, 512], BF16) for i in range(2)]
            u2 = [psb("g_u%d" % i, [128, 512]) for i in range(2)]; wT2 = [psb("g_wT%d" % i, [128, 512], BF16) for i in range(2)]
            qdT2 = [psb("g_qdT%d" % i, [128, 512], BF16) for i in range(2)]; inT2 = [psb("g_inT%d" % i, [128, 512], BF16) for i in range(2)]
            goT2 = [psb("g_goT%d" % i, [128, 512], BF16) for i in range(2)]
            grp_tk = [[tk() for _ in range(10)] for _ in range(2)]
            KBG, KDEC, BV, ZS, U_, WT, QDT, INT, GOT = range(9)
            NF32 = int(os.environ.get("GDN_NF32", "6"))
            Pf = [psb("g_Pf%d" % a_, [128, 512]) for a_ in range(2)]; PTf = [psb("g_PTf%d" % a_, [128, 512]) for a_ in range(2)]
            Ph = [psb("g_Ph%d" % a_, [128, 512], BF16) for a_ in range(2)]; PTh = [psb("g_PTh%d" % a_, [128, 512], BF16) for a_ in range(2)]
            Rb = psb("g_R", [128, 512]); Rbf = psb("g_Rbf", [128, 512], BF16)
            Pb = Pf; PTb = PTf
            Ph_tk = [tk(), tk()]; PTh_tk = [tk(), tk()]; Rh_tk = tk()
            P_tk = [tk(), tk()]; PT_tk = [tk(), tk()]; R_tk = tk()
            gA = psb("g_gA", [128, 512]); gB = psb("g_gB", [128, 512]); gC = psb("g_gC", [128, 512]); gA_tk = tk(); gB_tk = tk(); gC_tk = tk()
            gD = psb("g_gD", [128, 512]); gD_tk = tk()
            zA = psb("g_zA", [128, 512]); zA_tk = tk()
            eA = psb("g_eA", [128, 512]); eA_tk = tk()
            go_b = psb("g_go", [128, 512], BF16); go_tk = tk()
            S_f = psb("g_Sf", [128, 128]); S_b = psb("g_Sb", [128, 128], BF16); S_tk = tk()
            vn_r = Ring([psb("g_vn%d" % i, [128, 128], BF16) for i in range(2)])
            st4 = psb("g_st4", [128, 8]); st4_tk = tk()
            if os.environ.get("SBUF_DBG"):
                print("GDN sbuf bytes remaining per partition:", nc.sbuf_bytes_remaining)

            def pstage_items(h, tb):
                items = []
                if tb == 0:
                    def it_w():
                        for k in range(3):
                            load_w(wts[k], wts_tk[k], dr["a_w_in"][:, k * 1024 + h * 128:k * 1024 + (h + 1) * 128], 8, 128, 0)
                        load_w(wz2[h % 2], wz2_tk[h % 2], dr["a_w_in"][:, 3 * 1024 + h * 128:3 * 1024 + (h + 1) * 128], 8, 128, 0)
                        load_w(wo2[h % 2], wo2_tk[h % 2], dr["a_w_out"][h * 128:(h + 1) * 128, :], 1, D, None)
                        for sec in range(3):
                            for j in range(4):
                                P.op("dve", (lambda sec=sec, j=j: lambda e: e.tensor_scalar(out=cdiag[:, sec * 4 + j, :], in0=ident[:], scalar1=cw[:, sec * 8 + h, j:j + 1], scalar2=None, op0=ALU.mult))(),
                                     reads=[cst_tk, cw_tk], writes=[cdiag_tk])
                    items.append(it_w)
                blk = slice(tb * 512, (tb + 1) * 512)
                stt = [dict() for _ in range(3)]

                def proj(sec):
                    k_ = sec % 2
                    pj, pj_tk = prjb[k_], prjb_tk[k_]
                    bk, bk_tk = qb_[0], qb_tk[0]
                    for c in range(8):
                        P.op("pe", (lambda c=c: lambda e: e.matmul(bk[:], wts[sec][:, c, :], xnT[:, c, blk], start=(c == 0), stop=(c == 7)))(),
                             reads=[wts_tk[sec], xnT_tk[tb]], writes=[bk_tk], sig=(c == 7))
                    P.op("act", (lambda: lambda e: e.activation(out=pj[:, 3:515], in_=bk[:], func=AF.Copy))(), reads=[bk_tk], writes=[pj_tk])
                    if tb == 0:
                        P.op("pool", (lambda: lambda e: e.memset(pj[:, 0:3], 0.0))(), reads=[pj_tk], writes=[pj_tk])
                    else:
                        P.op("pool", (lambda: lambda e: e.tensor_copy(out=pj[:, 0:3], in_=tails[:, sec, 0:3]))(), reads=[tails_tk, pj_tk], writes=[pj_tk])

                def conv(sec):
                    k_ = sec % 2
                    pj, pj_tk = prjb[k_], prjb_tk[k_]
                    bk, bk_tk = qb_[1], qb_tk[1]
                    for j in range(4):
                        P.op("pe", (lambda j=j: lambda e: e.matmul(bk[:], cdiag[:, sec * 4 + j, :], pj[:, j:j + 512], start=(j == 0), stop=(j == 3)))(),
                             reads=[cdiag_tk, pj_tk], writes=[bk_tk], sig=(j == 3))
                    P.op("pool", (lambda: lambda e: e.tensor_copy(out=tails[:, sec, 0:3], in_=pj[:, 512:515]))(), reads=[pj_tk, tails_tk], writes=[tails_tk])
                    e_, e_tk = tmpf.next()
                    P.op("act", (lambda: lambda e: e.activation(out=e_[:], in_=bk[:], func=AF.Exp, scale=-1.0))(), reads=[bk_tk], writes=[e_tk])
                    P.op("act", (lambda: lambda e: e.activation(out=e_[:], in_=e_[:], func=AF.Ln, bias=one_b[:, 0:1], scale=1.0))(), reads=[e_tk, cb_tk], writes=[e_tk])
                    P.op("act", (lambda: lambda e: e.activation(out=e_[:], in_=e_[:], func=AF.Exp, scale=-1.0))(), reads=[e_tk], writes=[e_tk])
                    stt[sec].update(e=e_, e_tk=e_tk)

                def fin1(sec):
                    bk, bk_tk = qb_[1], qb_tk[1]
                    e_, e_tk = stt[sec]["e"], stt[sec]["e_tk"]
                    if sec == 2:
                        P.op("dve", (lambda: lambda e: e.tensor_tensor(out=vT[:, blk], in0=bk[:], in1=e_[:], op=ALU.mult))(), reads=[bk_tk, e_tk], writes=[vT_tk[tb]])
                        return
                    P.op("dve", (lambda: lambda e: e.tensor_tensor(out=e_[:], in0=bk[:], in1=e_[:], op=ALU.mult))(), reads=[bk_tk, e_tk], writes=[e_tk])
                    sq, sq_tk = tmpf.next()
                    P.op("act", (lambda: lambda e: e.activation(out=bfv(sq), in_=e_[:], func=AF.Square))(), reads=[e_tk], writes=[sq_tk])
                    P.op("pe", (lambda: lambda e: e.matmul(bk[:], ones_bf_p[:], bfv(sq), start=True, stop=True))(), reads=[sq_tk, onesb_tk], writes=[bk_tk])
                    stt[sec].update(sq=sq, sq_tk=sq_tk)

                def fin2(sec):
                    if sec == 2:
                        return
                    bk, bk_tk = qb_[1], qb_tk[1]
                    e_, e_tk, sq, sq_tk = stt[sec]["e"], stt[sec]["e_tk"], stt[sec]["sq"], stt[sec]["sq_tk"]
                    P.op("act", (lambda: lambda e: e.activation(out=sq[:], in_=bk[:], func=AF.Ln, bias=epsb[:, 0:1], scale=1.0))(), reads=[bk_tk, cst_tk], writes=[sq_tk])
                    P.op("act", (lambda: lambda e: e.activation(out=sq[:], in_=sq[:], func=AF.Exp, scale=-0.5))(), reads=[sq_tk], writes=[sq_tk])
                    dst, dst_tk = (qhT, qhT_tk[tb]) if sec == 0 else (khT, khT_tk[tb])
                    sc = float(128 ** -0.5) if sec == 0 else 1.0
                    P.op("dve", (lambda: lambda e: e.scalar_tensor_tensor(out=dst[:, blk], in0=e_[:], scalar=sc, in1=sq[:], op0=ALU.mult, op1=ALU.mult))(),
                         reads=[e_tk, sq_tk], writes=[dst_tk])
                items.append(lambda: proj(0))
                for sec in range(3):
                    items.append((lambda sec=sec: lambda: conv(sec))())
                    if sec + 1 < 3:
                        items.append((lambda sec=sec: lambda: proj(sec + 1))())
                    items.append((lambda sec=sec: lambda: fin1(sec))())
                    items.append((lambda sec=sec: lambda: fin2(sec))())
                return items

            def prep_items(h, g):
                par = g % 2
                t0 = 4 * g
                blk = slice(g * 512, (g + 1) * 512)
                T_ = grp_tk[par]
                kbg, kdec, bv, zs, u_, wT, qdT, inT = kbg2[par], kdec2[par], bv2[par], zs2[par], u2[par], wT2[par], qdT2[par], inT2[par]
                scb = lambda X: X[:, t0:t0 + 4, h].unsqueeze(2).to_broadcast([128, 4, 128])
                tsl = lambda i: slice(i * 128, (i + 1) * 128)
                tok = lambda i: slice((t0 + i) * 128, (t0 + i + 1) * 128)
                items = []

                def m1():
                    P.op("pool", lambda e: e.tensor_tensor(out=v3(gC[:]), in0=bc4(ident[:]), in1=scb(beta), op=ALU.mult), reads=[par_tk, cst_tk], writes=[gC_tk])
                    P.op("pe", lambda e: e.matmul(pb[2][:], onesf[:], gC[:], start=True, stop=True), reads=[gC_tk, cst_tk], writes=[pb_tk[2]])

                def p5a():
                    for i in range(4):
                        P.op("pe", (lambda i=i: lambda e: e.matmul(pb[3][:, tsl(i)], khT[:, tok(i)], khT[:, tok(i)], start=True, stop=True))(), reads=[khT_tk[g]], writes=[pb_tk[3]], sig=(i == 3))
                    for i in range(4):
                        P.op("pe", (lambda i=i: lambda e: e.matmul(pb[0][:, tsl(i)], khT[:, tok(i)], qhT[:, tok(i)], start=True, stop=True))(), reads=[khT_tk[g], qhT_tk[g]], writes=[pb_tk[0]], sig=(i == 3))

                def p6():
                    P.op("dve", lambda e: e.tensor_tensor(out=inT[:], in0=pb[0][:], in1=gB[:], op=ALU.mult), reads=[pb_tk[0], gB_tk], writes=[T_[INT]])
                    P.op("pool", lambda e: e.tensor_tensor(out=qdT[:], in0=qhT[:, blk], in1=gD[:], op=ALU.mult), reads=[qhT_tk[g], gD_tk], writes=[T_[QDT]])
                    P.op("pool", lambda e: e.tensor_tensor(out=v3(gB[:]), in0=v3(gB[:]), in1=bc4(cst["strictU"][:]), op=ALU.mult), reads=[gB_tk, cst_tk, T_[INT]], writes=[gB_tk])

                def p7():
                    P.op("pool", lambda e: e.tensor_tensor(out=v3(gA[:]), in0=v3(gA[:]), in1=scb(beta), op=ALU.mult), reads=[gA_tk, par_tk], writes=[gA_tk])
                    P.op("dve", lambda e: e.tensor_tensor(out=gC[:], in0=pb[2][:], in1=gB[:], op=ALU.mult), reads=[pb_tk[2], gB_tk, gC_tk], writes=[gC_tk])

                def p8():
                    P.op("dve", lambda e: e.tensor_tensor(out=Pb[0][:], in0=pb[3][:], in1=gA[:], op=ALU.mult), reads=[pb_tk[3], gA_tk], writes=[P_tk[0]])
                    P.op("dve", lambda e: e.tensor_tensor(out=PTb[0][:], in0=pb[3][:], in1=gC[:], op=ALU.mult), reads=[pb_tk[3], gC_tk], writes=[PT_tk[0]])
                    P.op("pool", lambda e: e.tensor_tensor(out=v3(Rb[:]), in0=bc4(ident[:]), in1=v3(PTb[0][:]), op=ALU.subtract), reads=[cst_tk, PT_tk[0]], writes=[R_tk])

                def p5b():
                    for i in range(4):
                        P.op("pe", (lambda i=i: lambda e: e.matmul(pb[1][:, tsl(i)], khT[:, tok(i)], ident_bf[:], start=True, stop=True))(), reads=[khT_tk[g], cb_tk], writes=[pb_tk[1]], sig=(i == 3))
                    for i in range(4):
                        P.op("pe", (lambda i=i: lambda e: e.matmul(pb[2][:, tsl(i)], vT[:, tok(i)], ident_bf[:], start=True, stop=True))(), reads=[vT_tk[g], cb_tk], writes=[pb_tk[2]], sig=(i == 3))

                def p5b2():
                    P.op("dve", lambda e: e.tensor_tensor(out=v3(kbg[:]), in0=v3(pb[1][:]), in1=scb(bgam), op=ALU.mult), reads=[pb_tk[1], par_tk], writes=[T_[KBG]])
                    P.op("dve", lambda e: e.tensor_tensor(out=v3(kdec[:]), in0=v3(pb[1][:]), in1=scb(kds), op=ALU.mult), reads=[pb_tk[1], par_tk], writes=[T_[KDEC]])
                    P.op("dve", lambda e: e.tensor_tensor(out=v3(bv[:]), in0=v3(pb[2][:]), in1=scb(beta), op=ALU.mult), reads=[pb_tk[2], par_tk], writes=[T_[BV]])

                def p5c():
                    for i in range(4):
                        for c in range(8):
                            P.op("pe", (lambda i=i, c=c: lambda e: e.matmul(pb[0][:, tsl(i)], xnT[:, c, tok(i)], wz2[h % 2][:, c, :], start=(c == 0), stop=(c == 7)))(),
                                 reads=[xnT_tk[g], wz2_tk[h % 2]], writes=[pb_tk[0]], sig=(i == 3 and c == 7))

                def p5c2():
                    P.op("act", lambda e: e.activation(out=zA[:], in_=pb[0][:], func=AF.Exp, scale=-1.0), reads=[pb_tk[0]], writes=[zA_tk])
                    P.op("act", lambda e: e.activation(out=zA[:], in_=zA[:], func=AF.Ln, bias=one_b[:, 0:1], scale=1.0), reads=[zA_tk, cb_tk], writes=[zA_tk])
                    P.op("act", lambda e: e.activation(out=zA[:], in_=zA[:], func=AF.Exp, scale=-1.0), reads=[zA_tk], writes=[zA_tk])

                def p5c3():
                    P.op("dve", lambda e: e.tensor_tensor(out=zA[:], in0=pb[0][:], in1=zA[:], op=ALU.mult), reads=[pb_tk[0], zA_tk], writes=[zA_tk])
                    P.op("pool", lambda e: e.tensor_tensor(out=v3(zs[:]), in0=v3(zA[:]), in1=bc4(onw[:]), op=ALU.mult), reads=[zA_tk, onw_tk], writes=[T_[ZS]])
                items += [m1, p5a, p6, p7, p8, p5b, p5b2, p5c, p5c2, p5c3]
                n_front = len(items)
                if NF32 == 0:
                    def c0():
                        P.op("act", lambda e: e.activation(out=Ph[0][:], in_=Pf[0][:], func=AF.Copy), reads=[P_tk[0]], writes=[Ph_tk[0]])
                        P.op("act", lambda e: e.activation(out=PTh[0][:], in_=PTf[0][:], func=AF.Copy), reads=[PT_tk[0]], writes=[PTh_tk[0]])
                        P.op("act", lambda e: e.activation(out=Rbf[:], in_=Rb[:], func=AF.Copy), reads=[R_tk], writes=[Rh_tk])
                    items.append(c0)
                dbl_stages = []
                for lvl in range(1, 7):
                    cur = (lvl - 1) % 2
                    nxt = 1 - cur
                    f32lvl = lvl <= NF32

                    def d1(lvl=lvl, cur=cur, nxt=nxt, f32lvl=f32lvl):
                        A_, AT_, A_tk, AT_tk = (Pf, PTf, P_tk, PT_tk) if f32lvl else (Ph, PTh, Ph_tk, PTh_tk)
                        for i in range(4):
                            P.op("pe", (lambda i=i: lambda e: e.matmul(pb[1][:, tsl(i)], AT_[cur][:, tsl(i)], A_[cur][:, tsl(i)], start=True, stop=True))(),
                                 reads=[AT_tk[cur], A_tk[cur]], writes=[pb_tk[1]], sig=(i == 3))
                        if lvl < 6:
                            for i in range(4):
                                P.op("pe", (lambda i=i: lambda e: e.matmul(pb[2][:, tsl(i)], A_[cur][:, tsl(i)], AT_[cur][:, tsl(i)], start=True, stop=True))(),
                                     reads=[AT_tk[cur], A_tk[cur]], writes=[pb_tk[2]], sig=(i == 3))

                    def d2(lvl=lvl, cur=cur, nxt=nxt, f32lvl=f32lvl):
                        if f32lvl:
                            P.op("act", lambda e: e.activation(out=Pf[nxt][:], in_=pb[1][:], func=AF.Copy), reads=[pb_tk[1]], writes=[P_tk[nxt]])
                        if lvl >= NF32 and lvl < 6 or not f32lvl:
                            P.op("act", lambda e: e.activation(out=Ph[nxt][:], in_=pb[1][:], func=AF.Copy), reads=[pb_tk[1]], writes=[Ph_tk[nxt]])
                        if lvl < 6:
                            if lvl + 1 <= NF32:
                                P.op("dve", lambda e: e.tensor_copy(out=PTf[nxt][:], in_=pb[2][:]), reads=[pb_tk[2]], writes=[PT_tk[nxt]])
                            else:
                                P.op("dve", lambda e: e.tensor_copy(out=PTh[nxt][:], in_=pb[2][:]), reads=[pb_tk[2]], writes=[PTh_tk[nxt]])

                    def d3(lvl=lvl, cur=cur, nxt=nxt, f32lvl=f32lvl):
                        for i in range(4):
                            if f32lvl:
                                P.op("pe", (lambda i=i: lambda e: e.matmul(pb[3][:, tsl(i)], Pf[nxt][:, tsl(i)], Rb[:, tsl(i)], start=True, stop=True))(),
                                     reads=[P_tk[nxt], R_tk], writes=[pb_tk[3]], sig=(i == 3))
                            else:
                                P.op("pe", (lambda i=i: lambda e: e.matmul(pb[3][:, tsl(i)], Ph[nxt][:, tsl(i)], Rbf[:, tsl(i)], start=True, stop=True))(),
                                     reads=[Ph_tk[nxt], Rh_tk], writes=[pb_tk[3]], sig=(i == 3))

                    def d4(lvl=lvl, f32lvl=f32lvl):
                        if f32lvl:
                            P.op("dve", lambda e: e.tensor_tensor(out=Rb[:], in0=pb[3][:], in1=Rb[:], op=ALU.add), reads=[pb_tk[3], R_tk], writes=[R_tk])
                            if lvl == NF32:
                                P.op("act", lambda e: e.activation(out=Rbf[:], in_=Rb[:], func=AF.Copy), reads=[R_tk], writes=[Rh_tk])
                        else:
                            P.op("dve", lambda e: e.tensor_tensor(out=Rbf[:], in0=pb[3][:], in1=Rbf[:], op=ALU.add), reads=[pb_tk[3], Rh_tk], writes=[Rh_tk])
                    dbl_stages.append((d1, d2, d3, d4))
                for li in range(6):
                    d1_, d2_, d3_, d4_ = dbl_stages[li]
                    if li == 0:
                        items += [d1_, d2_]
                    if li + 1 < 6:
                        n1, n2, _, _ = dbl_stages[li + 1]
                        items += [n1, d3_, n2, d4_]
                    else:
                        items += [d3_, d4_]

                def f1():
                    if NF32 >= 6:
                        P.op("act", lambda e: e.activation(out=Rbf[:], in_=Rb[:], func=AF.Copy), reads=[R_tk], writes=[Rh_tk])

                def f2():
                    for i in range(4):
                        P.op("pe", (lambda i=i: lambda e: e.matmul(pb[1][:, tsl(i)], Rbf[:, tsl(i)], bv[:, tsl(i)], start=True, stop=True))(), reads=[Rh_tk, T_[BV]], writes=[pb_tk[1]], sig=(i == 3))
                    for i in range(4):
                        P.op("pe", (lambda i=i: lambda e: e.matmul(pb[2][:, tsl(i)], kbg[:, tsl(i)], Rbf[:, tsl(i)], start=True, stop=True))(), reads=[Rh_tk, T_[KBG]], writes=[pb_tk[2]], sig=(i == 3))

                def f3():
                    P.op("act", lambda e: e.activation(out=u_[:], in_=pb[1][:], func=AF.Copy), reads=[pb_tk[1]], writes=[T_[U_]])
                    P.op("dve", lambda e: e.tensor_copy(out=wT[:], in_=pb[2][:]), reads=[pb_tk[2]], writes=[T_[WT]])
                items += [f1, f2, f3]
                return items, n_front

            def early_items(h, g):
                t0 = 4 * g
                scb = lambda X: X[:, t0:t0 + 4, h].unsqueeze(2).to_broadcast([128, 4, 128])

                def e1():
                    P.op("pool", lambda e: e.tensor_copy(out=v3(gA[:]), in_=scb(gc)), reads=[par_tk], writes=[gA_tk])
                    P.op("pool", lambda e: e.tensor_tensor(out=v3(gD[:]), in0=bc4(ident[:]), in1=scb(gc), op=ALU.mult), reads=[par_tk, cst_tk], writes=[gD_tk])

                def e2():
                    P.op("pe", lambda e: e.matmul(pb[0][:], ident[:], gA[:], start=True, stop=False), reads=[gA_tk, cst_tk], writes=[pb_tk[0]], sig=False)
                    P.op("pe", lambda e: e.matmul(pb[0][:], negones[:], gD[:], start=False, stop=True), reads=[gD_tk, cb_tk], writes=[pb_tk[0]])

                def e3():
                    P.op("dve", lambda e: e.tensor_tensor(out=v3(gA[:]), in0=v3(pb[0][:]), in1=bc4(cst["mnegLs"][:]), op=ALU.add), reads=[pb_tk[0], cst_tk], writes=[gA_tk])
                    P.op("dve", lambda e: e.scalar_tensor_tensor(out=v3(gB[:]), in0=v3(pb[0][:]), scalar=-1.0, in1=bc4(cst["mnegU"][:]), op0=ALU.mult, op1=ALU.add),
                         reads=[pb_tk[0], cst_tk], writes=[gB_tk])

                def e4():
                    P.op("pe", lambda e: e.matmul(pb[0][:], onesf[:], gD[:], start=True, stop=True), reads=[gD_tk, cst_tk], writes=[pb_tk[0]])

                def e5():
                    P.op("act", lambda e: e.activation(out=gA[:], in_=gA[:], func=AF.Exp), reads=[gA_tk], writes=[gA_tk])
                    P.op("act", lambda e: e.activation(out=gB[:], in_=gB[:], func=AF.Exp), reads=[gB_tk], writes=[gB_tk])
                    P.op("act", lambda e: e.activation(out=gD[:], in_=pb[0][:], func=AF.Exp), reads=[pb_tk[0], gD_tk], writes=[gD_tk])
                return [e1, e2, e3, e4, e5]

            def scan_items(h, g):
                par = g % 2
                t0 = 4 * g
                blk = slice(g * 512, (g + 1) * 512)
                T_ = grp_tk[par]
                kdec, zs, u_, wT, qdT, inT, goT = kdec2[par], zs2[par], u2[par], wT2[par], qdT2[par], inT2[par], goT2[par]
                OB, OBt = OBk[par], OB_tk[par]
                tsl = lambda i: slice(i * 128, (i + 1) * 128)
                items = []
                if g == 0:
                    def s0():
                        P.op("pool", lambda e: e.memset(S_f[:], 0.0), reads=[S_tk], writes=[S_tk])
                        P.op("pool", lambda e: e.memset(S_b[:], 0.0), reads=[S_tk], writes=[S_tk])
                    items.append(s0)
                for i in range(4):
                    t = t0 + i
                    sd = {}

                    def sa(i=i, sd=sd):
                        wsp, wsp_tk = PSs.next()
                        P.op("pe", (lambda: lambda e: e.matmul(wsp[:, 0:128], wT[:, tsl(i)], S_b[:], start=True, stop=True))(), reads=[T_[WT], S_tk], writes=[wsp_tk])
                        sd.update(wsp=wsp, wsp_tk=wsp_tk)

                    def sb_(i=i, sd=sd):
                        wsp, wsp_tk = sd["wsp"], sd["wsp_tk"]
                        vn, vn_tk = vn_r.next()
                        P.op("dve", (lambda: lambda e: e.tensor_tensor(out=vn[:], in0=u_[:, tsl(i)], in1=wsp[:, 0:128], op=ALU.subtract))(), reads=[T_[U_], wsp_tk], writes=[vn_tk])
                        sd.update(vn=vn, vn_tk=vn_tk)

                    def sc_(i=i, sd=sd):
                        wsp, wsp_tk, vn, vn_tk = sd["wsp"], sd["wsp_tk"], sd["vn"], sd["vn_tk"]
                        P.op("pe", (lambda: lambda e: e.matmul(OB[:, tsl(i)], qdT[:, tsl(i)], S_b[:], start=True, stop=False))(), reads=[T_[QDT], S_tk], writes=[OBt], sig=False)
                        P.op("pe", (lambda: lambda e: e.matmul(OB[:, tsl(i)], inT[:, tsl(i)], vn[:], start=False, stop=True))(), reads=[T_[INT], vn_tk], writes=[OBt], sig=False)
                        P.op("pe", (lambda: lambda e: e.matmul(wsp[:, 128:256], kdec[:, tsl(i)], vn[:], start=True, stop=True))(), reads=[T_[KDEC], vn_tk], writes=[wsp_tk])

                    def sd_(i=i, sd=sd, t=t):
                        wsp, wsp_tk = sd["wsp"], sd["wsp_tk"]
                        P.op("dve", (lambda: lambda e: e.scalar_tensor_tensor(out=S_f[:], in0=S_f[:], scalar=cd[:, t, h:h + 1], in1=wsp[:, 128:256], op0=ALU.mult, op1=ALU.add))(),
                             reads=[wsp_tk, par_tk, S_tk], writes=[S_tk])
                        P.op("act", lambda e: e.activation(out=S_b[:], in_=S_f[:], func=AF.Copy), reads=[S_tk], writes=[S_tk])
                    items += [sa, sb_, sc_, sd_]

                def e1():
                    P.op("act", lambda e: e.activation(out=eA[:], in_=OB[:], func=AF.Square), reads=[OBt], writes=[eA_tk])

                def e2():
                    P.op("dve", lambda e: e.tensor_reduce(out=st4[:, 0:4], in_=v3(eA[:]), axis=AX.X, op=ALU.add), reads=[eA_tk, st4_tk], writes=[st4_tk])
                    P.op("dve", lambda e: e.tensor_tensor(out=eA[:], in0=OB[:], in1=zs[:], op=ALU.mult), reads=[OBt, T_[ZS], st4_tk], writes=[eA_tk])

                def e3():
                    P.op("act", lambda e: e.activation(out=st4[:, 4:8], in_=st4[:, 0:4], func=AF.Ln, bias=epsb[:, 0:1], scale=1.0 / 128), reads=[st4_tk, cst_tk], writes=[st4_tk])
                    P.op("act", lambda e: e.activation(out=st4[:, 4:8], in_=st4[:, 4:8], func=AF.Exp, scale=-0.5), reads=[st4_tk], writes=[st4_tk])

                def e4():
                    P.op("pool", lambda e: e.tensor_tensor(out=v3(go_b[:]), in0=v3(eA[:]), in1=st4[:, 4:8].unsqueeze(2).to_broadcast([128, 4, 128]), op=ALU.mult),
                         reads=[eA_tk, st4_tk], writes=[go_tk])

                def e5():
                    gp, gp_tk = PSs.next()
                    for i in range(4):
                        P.op("pe", (lambda i=i: lambda e: e.matmul(gp[:, tsl(i)], go_b[:, tsl(i)], ident_bf[:], start=True, stop=True))(), reads=[go_tk, cb_tk], writes=[gp_tk], sig=(i == 3))
                    P.op("act", (lambda: lambda e: e.activation(out=goT[:], in_=gp[:], func=AF.Copy))(), reads=[gp_tk], writes=[T_[GOT]])
                items += [e1, e2, e3, e4, e5]
                wo, wo_tk = wo2[h % 2], wo2_tk[h % 2]
                for n0 in range(0, 8, 2):
                    def op_(n0=n0):
                        for n_ in range(n0, n0 + 2):
                            ps_, ps_tk_ = PSs.next()
                            P.op("pe", (lambda ps_=ps_, n_=n_: lambda e: e.matmul(ps_[:], wo[:, 0, n_ * 128:(n_ + 1) * 128], goT[:], start=True, stop=True))(),
                                 reads=[wo_tk, T_[GOT]], writes=[ps_tk_])
                            P.op("dve", (lambda ps_=ps_, n_=n_: lambda e: e.tensor_tensor(out=xT[:, n_, blk], in0=ps_[:], in1=xT[:, n_, blk], op=ALU.add))(),
                                 reads=[ps_tk_], writes=[xT_tk[n_][g]])
                    items.append(op_)
                return items

            def merge(a_, b_):
                if not a_:
                    return list(b_)
                if not b_:
                    return list(a_)
                if len(a_) < len(b_):
                    a_, b_ = b_, a_
                out = []
                ratio = len(a_) / float(len(b_))
                bi = 0
                for i_, it_ in enumerate(a_):
                    out.append(it_)
                    while bi < len(b_) and (bi + 1) * ratio <= i_ + 1:
                        out.append(b_[bi]); bi += 1
                out.extend(b_[bi:])
                return out

            for tb_ in range(NB):
                for it_ in pstage_items(0, tb_):
                    it_()
            prev_scan = []
            seq = [(h_, g_) for h_ in range(8) for g_ in range(4)]
            for it_ in early_items(*seq[0]):
                it_()
            for k_, (h, g) in enumerate(seq):
                if g >= 1 and h + 1 < 8:
                    pit = pstage_items(h + 1, g - 1)
                elif g == 0 and h >= 1:
                    pit = pstage_items(h, 3)
                else:
                    pit = []
                main, nf = prep_items(h, g)
                if k_ + 1 < len(seq):
                    main = main[:nf] + merge(main[nf:], early_items(*seq[k_ + 1]))
                for it_ in merge(merge(main, prev_scan), pit):
                    it_()
                prev_scan = scan_items(h, g)
            for it_ in prev_scan:
                it_()
            for tl in grp_tk:
                pass
            mytks.extend(vn_r.tks)
            retire(mytks)
            close_stage()

        def attention():
            with ExitStack() as ph:
                attention_body(ph)

        def attention_body(ph):
            import math, os
            from collections import deque
            lam_init = 0.8 - 0.6 * math.exp(-0.3 * 1)
            open_stage(ph, 2048)
            psb = lambda name, shape, dt=F32: ph.enter_context(nc.sbuf_tensor(name, list(shape), dt))
            mytks = []

            def tk():
                t = Tk(); mytks.append(t); return t
            PSa = Ring(banks[0:2], PS.tks[0:2])
            PSb = Ring(banks[2:4], PS.tks[2:4])
            acc_b, acc_tk = banks[4:8], PS.tks[4:8]
            pending = deque()

            def drain(n):
                while n > 0 and pending:
                    it_ = pending.popleft()
                    if it_ is not None:
                        it_()
                    n -= 1
            psem = P.new_dma_sem("at_prm"); lsem = P.new_dma_sem("at_lamb"); possem = P.new_dma_sem("at_pos"); gsem = P.new_dma_sem("at_g")
            prm = psb("at_prm", [128, 8]); prm_tk = tk()
            for half in range(2):
                P.dma("sp", psem, (lambda half=half: lambda e: e.dma_start(out=prm[half * 64:(half + 1) * 64, 0:1],
                      in_=dr["b_q_norm"][0, :].rearrange("(p o) -> p o", o=1), allow_slow_non_contiguous=True))(), writes=[prm_tk])
                P.dma("sp", psem, (lambda half=half: lambda e: e.dma_start(out=prm[half * 64:(half + 1) * 64, 1:2],
                      in_=dr["k_norm"][0, :].rearrange("(p o) -> p o", o=1), allow_slow_non_contiguous=True))(), writes=[prm_tk])
            P.dma("sp", psem, lambda e: e.dma_start(out=prm[:, 2:3], in_=dr["b_sub_norm"][0, :].rearrange("(p o) -> p o", o=1),
                  allow_slow_non_contiguous=True), writes=[prm_tk])
            gcol = psb("at_gcol", [128, 2, 128]); gcol_tk = tk()
            for half in range(2):
                P.dma("sp", gsem, (lambda half=half: lambda e: e.dma_start(out=gcol[:, 0, half * 64:(half + 1) * 64], in_=dr["b_q_norm"][0, :].partition_broadcast(128)))(), writes=[gcol_tk])
                P.dma("sp", gsem, (lambda half=half: lambda e: e.dma_start(out=gcol[:, 1, half * 64:(half + 1) * 64], in_=dr["k_norm"][0, :].partition_broadcast(128)))(), writes=[gcol_tk])
            P.op("dve", lambda e: e.tensor_scalar(out=gcol[:, 0, :], in0=gcol[:, 0, :], scalar1=0.125, scalar2=None, op0=ALU.mult), reads=[gcol_tk], writes=[gcol_tk])
            P.op("dve", lambda e: e.tensor_scalar(out=prm[:, 0:1], in0=prm[:, 0:1], scalar1=0.125, scalar2=None, op0=ALU.mult), reads=[prm_tk], writes=[prm_tk])
            P.op("dve", lambda e: e.tensor_scalar(out=prm[:, 2:3], in0=prm[:, 2:3], scalar1=float(1.0 - lam_init), scalar2=None, op0=ALU.mult), reads=[prm_tk], writes=[prm_tk])
            bo = psb("at_bo", [128, 2, 128]); bo_tk = tk()
            ig = psb("at_ig", [128, 2])
            P.op("dve", lambda e: e.tensor_tensor(out=ig[:], in0=prm[:, 0:2], in1=prm[:, 0:2], op=ALU.mult), reads=[prm_tk], writes=[bo_tk])
            P.op("dve", lambda e: e.reciprocal(out=prm[:, 6:8], in_=ig[:]), reads=[bo_tk, prm_tk], writes=[prm_tk])
            for i_ in range(2):
                P.op("dve", (lambda i_=i_: lambda e: e.tensor_scalar(out=bo[:, i_, :], in0=cst["blockones"][:], scalar1=prm[:, 6 + i_:7 + i_], scalar2=None, op0=ALU.mult))(),
                     reads=[prm_tk, cst_tk, bo_tk], writes=[bo_tk])
            bo_bf = psb("at_bo_bf", [128, 2, 128], BF16)
            P.op("dve", lambda e: e.tensor_copy(out=bo_bf[:], in_=bo[:]), reads=[bo_tk], writes=[bo_tk])
            ones_bf = psb("at_ones_bf", [128, 128], BF16); tri_bf = psb("at_tri_bf", [128, 128], BF16)
            cb_tk = tk()
            P.op("dve", lambda e: e.tensor_copy(out=ones_bf[:], in_=cst["ones"][:]), reads=[cst_tk], writes=[cb_tk])
            P.op("dve", lambda e: e.tensor_copy(out=tri_bf[:], in_=cst["trimask"][:]), reads=[cst_tk, cb_tk], writes=[cb_tk])
            cosT = psb("at_cosT", [128, S], BF16); sinT = psb("at_sinT", [128, S], BF16); cs_tk = tk()
            with ExitStack() as ph2:
                lamb = ph2.enter_context(nc.sbuf_tensor("at_lamb", [128, 4, 64], F32)); lamb_tk = Tk()
                lp = ph2.enter_context(nc.sbuf_tensor("at_lp", [128, 2, 64], F32)); lp_tk = Tk()
                P.dma("sp", lsem, lambda e: e.dma_start(out=lamb[:].rearrange("p a b -> p (a b)"),
                      in_=dr["b_lambda"].rearrange("a b -> (a b)").partition_broadcast(128)), writes=[lamb_tk])
                P.op("dve", lambda e: e.tensor_tensor(out=lp[:, 0, :], in0=lamb[:, 0, :], in1=lamb[:, 1, :], op=ALU.mult), reads=[lamb_tk], writes=[lp_tk])
                P.op("dve", lambda e: e.tensor_tensor(out=lp[:, 1, :], in0=lamb[:, 2, :], in1=lamb[:, 3, :], op=ALU.mult), reads=[lamb_tk, lp_tk], writes=[lp_tk])
                P.op("dve", lambda e: e.tensor_reduce(out=prm[:, 4:6], in_=lp[:], axis=AX.X, op=ALU.add), reads=[lp_tk, prm_tk], writes=[prm_tk])
                P.op("act", lambda e: e.activation(out=prm[:, 4:6], in_=prm[:, 4:6], func=AF.Exp), reads=[prm_tk], writes=[prm_tk])
                P.op("dve", lambda e: e.tensor_tensor(out=prm[:, 3:4], in0=prm[:, 5:6], in1=prm[:, 4:5], op=ALU.subtract), reads=[prm_tk], writes=[prm_tk])
                P.op("dve", lambda e: e.tensor_scalar(out=prm[:, 3:4], in0=prm[:, 3:4], scalar1=float(-lam_init), scalar2=None, op0=ALU.add), reads=[prm_tk], writes=[prm_tk])
                retire([lamb_tk, lp_tk])
                posi = ph2.enter_context(nc.sbuf_tensor("at_posi", [128, S], I32))
                ang = ph2.enter_context(nc.sbuf_tensor("at_ang", [128, S], F32))
                kf = ph2.enter_context(nc.sbuf_tensor("at_kf", [128, S], F32))
                t2 = [Tk() for _ in range(3)]
                P.dma("sp", possem, lambda e: e.dma_start(out=posi[:], in_=pos_d[0, :].partition_broadcast(128)), writes=[t2[0]])
                for which, dst in ((0, sinT), (1, cosT)):
                    P.op("dve", lambda e: e.tensor_copy(out=ang[:], in_=posi[:]), reads=[t2[0]], writes=[t2[1]])
                    P.op("dve", (lambda which=which: lambda e: e.tensor_scalar(out=ang[:], in0=ang[:], scalar1=cst["freq"][:, 0:1], scalar2=float(which * np.pi / 2), op0=ALU.mult, op1=ALU.add))(),
                         reads=[t2[1], cst_tk], writes=[t2[1]])
                    P.op("dve", lambda e: e.tensor_scalar(out=kf[:], in0=ang[:], scalar1=float(1.0 / (2 * np.pi)), scalar2=None, op0=ALU.mult), reads=[t2[1]], writes=[t2[2]])
                    P.op("dve", lambda e: e.tensor_copy(out=kf[:].bitcast(I32), in_=kf[:]), reads=[t2[2]], writes=[t2[2]])
                    P.op("dve", lambda e: e.tensor_copy(out=kf[:], in_=kf[:].bitcast(I32)), reads=[t2[2]], writes=[t2[2]])
                    P.op("dve", lambda e: e.scalar_tensor_tensor(out=ang[:], in0=kf[:], scalar=-6.28125, in1=ang[:], op0=ALU.mult, op1=ALU.add), reads=[t2[1], t2[2]], writes=[t2[1]])
                    P.op("dve", lambda e: e.scalar_tensor_tensor(out=ang[:], in0=kf[:], scalar=-float(2 * np.pi - 6.28125), in1=ang[:], op0=ALU.mult, op1=ALU.add), reads=[t2[1], t2[2]], writes=[t2[1]])
                    P.op("dve", lambda e: e.tensor_scalar(out=ang[:], in0=ang[:], scalar1=3.14159, scalar2=-3.14159, op0=ALU.min, op1=ALU.max), reads=[t2[1]], writes=[t2[1]])
                    P.op("act", (lambda dst=dst: lambda e: e.activation(out=dst[:], in_=ang[:], func=AF.Sin))(), reads=[t2[1]], writes=[cs_tk])
                retire(t2)
            wq = psb("at_wq", [128, 8, 128], BF16)
            wk = psb("at_wk", [128, 8, 128], BF16)
            rot_bf = psb("at_rot_bf", [128, 128], BF16)
            P.op("dve", lambda e: e.tensor_copy(out=rot_bf[:], in_=cst["rot"][:]), reads=[cst_tk, cb_tk], writes=[cb_tk])
            qb_r = Ring([psb("at_qb%d" % i, [128, 512], BF16) for i in range(2)])
            wv = psb("at_wv", [128, 8, 128], BF16)
            wq_tk = tk(); wk_tk = tk(); wv_tk = tk()
            wo2 = [psb("at_wo%d" % i, [128, 2, D], BF16) for i in range(2)]; wo2_tk = [tk() for _ in range(2)]
            qT2 = [psb("at_qT%d" % i, [128, S], BF16) for i in range(2)]; qT2_tk = [tk() for _ in range(2)]
            kz2 = [[psb("at_kz%d_%d" % (i, m_), [128, S], BF16) for m_ in range(2)] for i in range(2)]; kT2_tk = [tk() for _ in range(2)]
            for i_ in range(2):
                for m_ in range(2):
                    P.op("pool", (lambda i_=i_, m_=m_: lambda e: e.memset(kz2[i_][m_][:], 0.0))(), writes=[kT2_tk[i_]])
            vt2 = [psb("at_vt%d" % i, [128, NT, 128], BF16) for i in range(2)]; vt2_tk = [tk() for _ in range(2)]
            aoT = psb("at_aoT", [128, 2, S], BF16); aoT_tk = tk()
            pT_r = Ring([psb("at_pT%d" % i, [128, 512], BF16) for i in range(3)])
            om_r = Ring([psb("at_om%d" % i, [128, 512]) for i in range(6)])
            rings = [pT_r, om_r, qb_r]
            if os.environ.get("SBUF_DBG"):
                print("ATTN sbuf bytes remaining per partition:", nc.sbuf_bytes_remaining)

            def rot_weights(w, w_tk, wr, wr_tk):
                wv4 = w[:].rearrange("p c (g r d) -> p c g r d", g=2, r=2)
                wr4 = wr[:].rearrange("p c (g r d) -> p c g r d", g=2, r=2)
                for g in range(2):
                    P.op("pool", (lambda g=g: lambda e: e.tensor_scalar(out=wr4[:, :, g, 0, :], in0=wv4[:, :, g, 1, :], scalar1=-1.0, scalar2=None, op0=ALU.mult))(),
                         reads=[w_tk], writes=[wr_tk])
                    P.op("pool", (lambda g=g: lambda e: e.tensor_copy(out=wr4[:, :, g, 1, :], in_=wv4[:, :, g, 0, :]))(), reads=[w_tk, wr_tk], writes=[wr_tk])

            def prologue_items(h):
                par = h % 2
                hs = slice(h * 128, (h + 1) * 128)
                qT, qT_tk, kT, kT_tk, vt, vt_tk = qT2[par], qT2_tk[par], kz2[par], kT2_tk[par], vt2[par], vt2_tk[par]
                items = []

                def it_w():
                    load_w(wq, wq_tk, dr["b_w_q"][:, hs], 8, 128, 3, colgain=gcol[:, 0, :], colgain_tk=gcol_tk)
                    load_w(wk, wk_tk, dr["w_kv"][:, hs], 8, 128, 2, colgain=gcol[:, 1, :], colgain_tk=gcol_tk)
                    load_w(wv, wv_tk, dr["w_kv"][:, 1024 + h * 128:1024 + (h + 1) * 128], 8, 128, 2)
                    if h % 2 == 0:
                        load_w(wo2[(h // 2) % 2], wo2_tk[(h // 2) % 2], dr["b_w_out"][h * 128:(h + 2) * 128, :], 2, D, None)
                items += [it_w, None, None, None, None, None]
                for tg in range(4):
                    st = {}

                    def v1(tg=tg, st=st):
                        ps, ps_tk = PSb.next()
                        for j in range(4):
                            t = tg * 4 + j
                            for c in range(8):
                                P.op("pe", (lambda ps=ps, j=j, t=t, c=c: lambda e: e.matmul(
                                    ps[:, j * 128:(j + 1) * 128], xnT[:, c, t * 128:(t + 1) * 128], wv[:, c, :], start=(c == 0), stop=(c == 7)))(),
                                    reads=[xnT_tk[tg], wv_tk], writes=[ps_tk], sig=(j == 3 and c == 7))
                        st.update(ps=ps, ps_tk=ps_tk)

                    def v2(tg=tg, st=st):
                        ps, ps_tk = st["ps"], st["ps_tk"]
                        P.op("dve", (lambda ps=ps: lambda e: e.tensor_copy(out=vt[:, tg * 4:(tg + 1) * 4, :], in_=ps[:].rearrange("p (j n) -> p j n", j=4)))(),
                             reads=[ps_tk], writes=[vt_tk])
                    items += [v1, None, v2]
                for (w, w_tk, dst, dst_tk, gi) in ((wq, wq_tk, qT, qT_tk, 0), (wk, wk_tk, kT, kT_tk, 1)):
                    for tb in range(NB):
                        st = {}

                        def s1(w=w, w_tk=w_tk, tb=tb, st=st):
                            blk = slice(tb * 512, (tb + 1) * 512)
                            ps, ps_tk = PSb.next()
                            for c in range(8):
                                P.op("pe", (lambda ps=ps, c=c: lambda e: e.matmul(ps[:], w[:, c, :], xnT[:, c, blk], start=(c == 0), stop=(c == 7)))(),
                                     reads=[w_tk, xnT_tk[tb]], writes=[ps_tk], sig=(c == 7))
                            st.update(ps=ps, ps_tk=ps_tk)

                        def s2(tb=tb, st=st):
                            blk = slice(tb * 512, (tb + 1) * 512)
                            ps, ps_tk = st["ps"], st["ps_tk"]
                            sq, sq_tk = tmpf.next()
                            P.op("act", (lambda: lambda e: e.activation(out=bfv(sq), in_=ps[:], func=AF.Square))(), reads=[ps_tk], writes=[sq_tk])
                            qb, qb_tk = qb_r.next()
                            P.op("act", (lambda: lambda e: e.activation(out=qb[:], in_=ps[:], func=AF.Copy))(), reads=[ps_tk], writes=[qb_tk])
                            t1, t1_tk = om_r.next()
                            P.op("dve", (lambda: lambda e: e.tensor_tensor(out=t1[:], in0=ps[:], in1=cosT[:, blk], op=ALU.mult))(), reads=[ps_tk, cs_tk], writes=[t1_tk])
                            st.update(sq=sq, sq_tk=sq_tk, t1=t1, t1_tk=t1_tk, qb=qb, qb_tk=qb_tk)

                        def s3(st=st, gi=gi):
                            sq, sq_tk, qb, qb_tk = st["sq"], st["sq_tk"], st["qb"], st["qb_tk"]
                            ps2, ps2_tk = PSb.next()
                            P.op("pe", (lambda: lambda e: e.matmul(ps2[:], bo_bf[:, gi, :], bfv(sq), start=True, stop=True))(), reads=[sq_tk, bo_tk], writes=[ps2_tk])
                            psr, psr_tk = PSb.next()
                            P.op("pe", (lambda: lambda e: e.matmul(psr[:], rot_bf[:], qb[:], start=True, stop=True))(), reads=[qb_tk, cb_tk], writes=[psr_tk])
                            st.update(ps2=ps2, ps2_tk=ps2_tk, psr=psr, psr_tk=psr_tk)

                        def s4(tb=tb, st=st):
                            blk = slice(tb * 512, (tb + 1) * 512)
                            ps2, ps2_tk, psr, psr_tk = st["ps2"], st["ps2_tk"], st["psr"], st["psr_tk"]
                            rr, rr_tk = tmpf.next()
                            P.op("act", (lambda: lambda e: e.activation(out=rr[:], in_=ps2[:], func=AF.Ln, bias=epsb[:, 0:1], scale=1.0 / 64))(),
                                 reads=[ps2_tk, cst_tk], writes=[rr_tk])
                            P.op("act", (lambda: lambda e: e.activation(out=rr[:], in_=rr[:], func=AF.Exp, scale=-0.5))(), reads=[rr_tk], writes=[rr_tk])
                            t2_, t2_tk = om_r.next()
                            P.op("dve", (lambda: lambda e: e.tensor_tensor(out=t2_[:], in0=psr[:], in1=sinT[:, blk], op=ALU.mult))(), reads=[psr_tk, cs_tk], writes=[t2_tk])
                            st.update(rr=rr, rr_tk=rr_tk, t2=t2_, t2_tk=t2_tk)

                        def s5(tb=tb, st=st, dst=dst, dst_tk=dst_tk):
                            blk = slice(tb * 512, (tb + 1) * 512)
                            t1, t1_tk, t2_, t2_tk, rr, rr_tk = st["t1"], st["t1_tk"], st["t2"], st["t2_tk"], st["rr"], st["rr_tk"]
                            P.op("pool", (lambda: lambda e: e.tensor_tensor(out=t1[:], in0=t1[:], in1=t2_[:], op=ALU.add))(), reads=[t1_tk, t2_tk], writes=[t1_tk])
                            if isinstance(dst, list):
                                for m_ in range(2):
                                    hs_ = slice(m_ * 64, (m_ + 1) * 64)
                                    P.op("pool", (lambda m_=m_, hs_=hs_: lambda e: e.tensor_tensor(out=dst[m_][hs_, blk], in0=t1[hs_, :], in1=rr[hs_, :], op=ALU.mult))(),
                                         reads=[t1_tk, rr_tk], writes=[dst_tk])
                            else:
                                P.op("pool", (lambda: lambda e: e.tensor_tensor(out=dst[:, blk], in0=t1[:], in1=rr[:], op=ALU.mult))(), reads=[t1_tk, rr_tk], writes=[dst_tk])
                        items += [s1, None, s2, s3, None, s4, s5]
                return items

            def outproj_items(hp):
                items = []
                wo, wo_tk = wo2[hp % 2], wo2_tk[hp % 2]
                for tb in range(NB):
                    for n0 in range(0, 8, 2):
                        def it(tb=tb, n0=n0):
                            blk = slice(tb * 512, (tb + 1) * 512)
                            for n_ in range(n0, n0 + 2):
                                ps, ps_tk = PSb.next()
                                for hh in range(2):
                                    P.op("pe", (lambda ps=ps, hh=hh, n_=n_: lambda e: e.matmul(
                                        ps[:], wo[:, hh, n_ * 128:(n_ + 1) * 128], aoT[:, hh, blk], start=(hh == 0), stop=(hh == 1)))(),
                                        reads=[wo_tk, aoT_tk], writes=[ps_tk], sig=(hh == 1))
                                P.op("dve", (lambda ps=ps, n_=n_: lambda e: e.tensor_tensor(out=xT[:, n_, blk], in0=ps[:], in1=xT[:, n_, blk], op=ALU.add))(),
                                     reads=[ps_tk], writes=[xT_tk[n_][tb]])
                        it.is_outproj = True
                        items.append(it)
                return items

            for it in prologue_items(0):
                if it is not None:
                    it()
            for h in range(8):
                par = h % 2
                qT, qT_tk, kT, kT_tk, vt, vt_tk = qT2[par], qT2_tk[par], kz2[par], kT2_tk[par], vt2[par], vt2_tk[par]
                drain(10 ** 9)
                if h + 1 < 8:
                    pending.extend(prologue_items(h + 1))
                tiles = []
                for qb in range(NB):
                    for kt in range(4 * qb + 4):
                        for m in range(2):
                            tiles.append((qb, kt, m))

                def emit_score(i):
                    qb, kt, m = tiles[i]
                    j = kt - 4 * qb
                    q0 = 0 if j < 0 else 128 * j
                    n = 512 - q0
                    ms = slice(m * 64, (m + 1) * 64)
                    ps, ps_tk = PSa.next()
                    P.op("pe", (lambda ps=ps, kT=kT, qT=qT: lambda e: e.matmul(ps[:, 0:n], kT[m][:, kt * 128:(kt + 1) * 128], qT[:, qb * 512 + q0:(qb + 1) * 512], start=True, stop=True))(),
                         reads=[kT_tk, qT_tk], writes=[ps_tk])
                    return ps, ps_tk
                nxt = emit_score(0)
                for i, (qb, kt, m) in enumerate(tiles):
                    ps, ps_tk = nxt
                    if i + 1 < len(tiles):
                        nxt = emit_score(i + 1)
                    j = kt - 4 * qb
                    q0 = 0 if j < 0 else 128 * j
                    n = 512 - q0
                    nkt = 4 * qb + 4
                    pT, pT_tk = pT_r.next()
                    P.op("act", (lambda pT=pT, ps=ps, n=n: lambda e: e.activation(out=pT[:, 0:n], in_=ps[:, 0:n], func=AF.Exp))(), reads=[ps_tk], writes=[pT_tk])
                    if j >= 0:
                        P.op("dve", (lambda pT=pT: lambda e: e.tensor_tensor(out=pT[:, 0:128], in0=pT[:, 0:128], in1=tri_bf[:], op=ALU.mult))(),
                             reads=[pT_tk, cb_tk], writes=[pT_tk])
                    P.op("pe", (lambda pT=pT, m=m, kt=kt, q0=q0, n=n, nkt=nkt, vt=vt: lambda e: e.matmul(
                        acc_b[2 * m][:, q0:512], vt[:, kt, :], pT[:, 0:n], start=(kt == 0), stop=(kt == nkt - 1)))(),
                        reads=[pT_tk, vt_tk], writes=[acc_tk[2 * m]], sig=False)
                    P.op("pe", (lambda pT=pT, m=m, kt=kt, q0=q0, n=n, nkt=nkt: lambda e: e.matmul(
                        acc_b[2 * m + 1][:, q0:512], ones_bf[:], pT[:, 0:n], start=(kt == 0), stop=(kt == nkt - 1)))(),
                        reads=[pT_tk, cb_tk], writes=[acc_tk[2 * m + 1]], sig=True)
                    drain(1)
                    if kt == nkt - 1 and m == 1:
                        blk = slice(qb * 512, (qb + 1) * 512)
                        while any(getattr(it_, "is_epi", False) for it_ in pending):
                            drain(1)
                        o_s = []; rd_s = []
                        for mm in range(2):
                            o_, o_tk = om_r.next()
                            P.op("act", (lambda o_=o_, mm=mm: lambda e: e.activation(out=o_[:], in_=acc_b[2 * mm][:], func=AF.Copy))(), reads=[acc_tk[2 * mm]], writes=[o_tk])
                            rd, rd_tk = om_r.next()
                            P.op("dve", (lambda rd=rd, mm=mm: lambda e: e.tensor_copy(out=rd[:], in_=acc_b[2 * mm + 1][:]))(), reads=[acc_tk[2 * mm + 1]], writes=[rd_tk])
                            o_s.append((o_, o_tk)); rd_s.append((rd, rd_tk))
                        st = {}

                        def e0(rd_s=rd_s):
                            for mm in range(2):
                                rd, rd_tk = rd_s[mm]
                                P.op("act", (lambda rd=rd: lambda e: e.activation(out=rd[:], in_=rd[:], func=AF.Ln))(), reads=[rd_tk], writes=[rd_tk])
                                P.op("act", (lambda rd=rd: lambda e: e.activation(out=rd[:], in_=rd[:], func=AF.Exp, scale=-1.0))(), reads=[rd_tk], writes=[rd_tk])

                        def e1(o_s=o_s, rd_s=rd_s, st=st):
                            for mm in range(2):
                                P.op("pool", (lambda mm=mm: lambda e: e.tensor_tensor(out=o_s[mm][0][:], in0=o_s[mm][0][:], in1=rd_s[mm][0][:], op=ALU.mult))(),
                                     reads=[o_s[mm][1], rd_s[mm][1]], writes=[o_s[mm][1]])
                            df, df_tk = rd_s[0]
                            P.op("dve", (lambda df=df: lambda e: e.scalar_tensor_tensor(out=df[:], in0=o_s[1][0][:], scalar=prm[:, 3:4], in1=o_s[0][0][:], op0=ALU.mult, op1=ALU.add))(),
                                 reads=[o_s[0][1], o_s[1][1], prm_tk], writes=[df_tk])
                            st.update(df=df, df_tk=df_tk)

                        def e1b(rd_s=rd_s, st=st):
                            df, df_tk = st["df"], st["df_tk"]
                            sq, sq_tk = rd_s[1]
                            P.op("act", (lambda: lambda e: e.activation(out=bfv(sq), in_=df[:], func=AF.Square))(), reads=[df_tk], writes=[sq_tk])
                            st.update(sq=sq, sq_tk=sq_tk)

                        def e1c(st=st):
                            sq, sq_tk = st["sq"], st["sq_tk"]
                            ps2, ps2_tk = PSb.next()
                            P.op("pe", (lambda: lambda e: e.matmul(ps2[:], ones_bf_p[:], bfv(sq), start=True, stop=True))(), reads=[sq_tk, onesb_tk], writes=[ps2_tk])
                            st.update(ps2=ps2, ps2_tk=ps2_tk)

                        def e2(st=st, h=h, blk=blk):
                            df, df_tk, ps2, ps2_tk = st["df"], st["df_tk"], st["ps2"], st["ps2_tk"]
                            rr, rr_tk = tmpf.next()
                            P.op("act", (lambda: lambda e: e.activation(out=rr[:], in_=ps2[:], func=AF.Ln, bias=epsb[:, 0:1], scale=1.0 / 128))(),
                                 reads=[ps2_tk, cst_tk], writes=[rr_tk])
                            P.op("act", (lambda: lambda e: e.activation(out=rr[:], in_=rr[:], func=AF.Exp, scale=-0.5))(), reads=[rr_tk], writes=[rr_tk])
                            st.update(rr=rr, rr_tk=rr_tk)

                        def e3(st=st, h=h, blk=blk):
                            df, df_tk, rr, rr_tk = st["df"], st["df_tk"], st["rr"], st["rr_tk"]
                            P.op("dve", (lambda: lambda e: e.scalar_tensor_tensor(
                                out=aoT[:, h % 2, blk], in0=df[:], scalar=prm[:, 2:3], in1=rr[:], op0=ALU.mult, op1=ALU.mult))(),
                                reads=[df_tk, rr_tk, prm_tk], writes=[aoT_tk])
                        epi = [e0, e1, None, e1b, e1c, None, e2, e3]
                        for it_ in epi:
                            if it_ is not None:
                                it_.is_epi = True
                        lst = list(pending)
                        pos_ = 0
                        for ii, it_ in enumerate(lst):
                            if getattr(it_, "is_outproj", False):
                                pos_ = ii + 1
                        pending.clear()
                        pending.extend(lst[:pos_] + epi + lst[pos_:])
                        if qb == NB - 1 and h % 2 == 1:
                            lst = list(pending)
                            pos_ = lst.index(e3) + 1
                            pending.clear()
                            pending.extend(lst[:pos_] + outproj_items(h // 2) + lst[pos_:])
            drain(10 ** 9)
            for r in rings:
                mytks.extend(r.tks)
            retire(mytks)
            close_stage()

        if "gdn" in stages:
            rms_to_xnT()
            gdn()
        if "mlp0" in stages:
            rms_to_xnT()
            mlp(0, 1)
        if "attn" in stages:
            rms_to_xnT()
            attention()
        if "mlp1" in stages:
            rms_to_xnT()
            mlp(1, 4)

        osem = [P.new_dma_sem("out%d" % i) for i in range(2)]
        phf = ExitStack()
        open_stage(phf, 1024)
        stage = stage_box["ring"]
        for t in range(NT):
            tb = t // 4
            o, o_tk = stage.next()
            osi = (stage.i - 1) % 2
            for cg in range(2):
                ps, ps_tk = PS.next()
                for j in range(4):
                    c = cg * 4 + j
                    P.op("pe", (lambda ps=ps, c=c, j=j, t=t: lambda e: e.transpose(
                        ps[:, j * 128:(j + 1) * 128], xT[:, c, t * 128:(t + 1) * 128], cst["ident"][:]))(),
                        reads=[xT_tk[c][tb], cst_tk], writes=[ps_tk], sig=(j == 3))
                evac_copy(o[:, cg * 512:(cg + 1) * 512], ps[:], reads=[ps_tk], writes=[o_tk])
            P.dma("sp", osem[osi], (lambda o=o, t=t: lambda e: e.dma_start(out=out_d[t * 128:(t + 1) * 128, :], in_=o[:, 0:1024]))(), reads=[o_tk])
        P.final_wait("sp", osem[0])
        P.final_wait("sp", osem[1])
        phf.close()
        P.emit()
        nops = {e: len(P.ops[e]) for e in P.ENG}
        nops["splits"] = P.split_log
    return nc, nops


_CACHE = {}


def kernel(**inputs):
    B = inputs["x"].shape[0]
    consts = host_consts()
    if "nc" not in _CACHE:
        _CACHE["nc"] = build_program()
    nc, _ = _CACHE["nc"]
    shared = {}
    f32 = lambda a: np.ascontiguousarray(np.asarray(a, dtype=np.float32))
    shared["a_norm"] = f32(inputs["a_norm"]).reshape(1, D)
    shared["a_w_in"] = f32(inputs["a_w_in"]).reshape(D, 4112)
    shared["a_conv_w"] = f32(inputs["a_conv_w"]).reshape(4, 3072)
    shared["a_a_log"] = f32(inputs["a_a_log"]).reshape(1, 8)
    shared["a_dt_bias"] = f32(inputs["a_dt_bias"]).reshape(1, 8)
    shared["a_out_norm"] = f32(inputs["a_out_norm"]).reshape(1, 128)
    shared["a_w_out"] = f32(inputs["a_w_out"]).reshape(D, D)
    shared["kv_norm"] = f32(inputs["kv_norm"]).reshape(1, D)
    shared["w_kv"] = f32(inputs["w_kv"]).reshape(D, 2048)
    shared["k_norm"] = f32(inputs["k_norm"]).reshape(1, 64)
    shared["b_norm"] = f32(inputs["b_norm"]).reshape(1, D)
    shared["b_w_q"] = f32(inputs["b_w_q"]).reshape(D, D)
    shared["b_q_norm"] = f32(inputs["b_q_norm"]).reshape(1, 64)
    shared["b_lambda"] = f32(inputs["b_lambda"]).reshape(4, 64)
    shared["b_sub_norm"] = f32(inputs["b_sub_norm"]).reshape(1, 128)
    shared["b_w_out"] = f32(inputs["b_w_out"]).reshape(D, D)
    shared["mlp_norm"] = f32(inputs["mlp_norm"]).reshape(2, D)
    shared["mlp_w1"] = f32(inputs["mlp_w1"]).reshape(2 * D, DFF)
    shared["mlp_w2"] = f32(inputs["mlp_w2"]).reshape(2 * DFF, D)
    for n in CONST_NAMES:
        shared["c_" + n] = consts[n]
    x = f32(inputs["x"])
    pos = np.ascontiguousarray(np.asarray(inputs["positions"], dtype=np.int32))
    in_maps = []
    for b in range(B):
        m = dict(shared)
        m["x"] = x[b]
        m["positions"] = pos[b].reshape(1, S)
        in_maps.append(m)
    res = run_bass_kernel_spmd(nc, in_maps, core_ids=list(range(B)))
    return np.stack([np.asarray(r["out"], dtype=np.float32) for r in res.results], axis=0)
```
